# Optimizing a Trainium2 kernel written in Bass

```python
import math
import jax, jax.numpy as jnp
from jax import lax
import numpy as np

D_MODEL = 1024
BATCH = 8
SEQ = 2048
DEPTH = 1

ATTN_HEADS = 8
ATTN_KV_HEADS = 2
HEAD_DIM = 64
ATTN_WIDTH = ATTN_HEADS * HEAD_DIM
KV_WIDTH = ATTN_KV_HEADS * HEAD_DIM
WINDOW = 128
BLOCK = 128
SSM_CH_PER_GROUP = 16
SSM_WIDTH = D_MODEL - ATTN_WIDTH
SSM_GROUPS = SSM_WIDTH // SSM_CH_PER_GROUP
SSM_STATE = 64
DT_MIN = 1e-3
DT_MAX = 1e-1
MIX_WIDTH = ATTN_WIDTH + SSM_WIDTH
IN_WIDTH = ATTN_WIDTH + 2 * KV_WIDTH + SSM_WIDTH
D_FF = 2816
EPS = 1e-6
NEG_INF = -1e30
LAMBDA_RE_MAX = -1e-4

kernel_name = "hymba_swa_s5_macaron_block"


def _alibi_slopes(n_heads):
    return jnp.asarray(2.0 ** (-8.0 * (np.arange(n_heads) + 1) / n_heads), dtype=jnp.float32)


def _rmsnorm(x, g):
    x32 = x.astype(jnp.float32)
    y = x32 * lax.rsqrt(jnp.mean(x32 * x32, axis=-1, keepdims=True) + EPS)
    return (y * g.astype(jnp.float32)).astype(x.dtype)


def _swiglu(x, w_gate, w_up, w_down):
    return (jax.nn.silu(x @ w_gate) * (x @ w_up)) @ w_down


def _window_attention(q, k, v, sinks):
    b, l = q.shape[0], q.shape[1]
    nb = l // BLOCK
    gq = ATTN_HEADS // ATTN_KV_HEADS
    qb = q.reshape(b, nb, BLOCK, ATTN_KV_HEADS, gq, HEAD_DIM)

    def band(t):
        tp = jnp.pad(t, ((0, 0), (BLOCK, BLOCK), (0, 0), (0, 0)))
        tp = tp.reshape(b, nb + 2, BLOCK, ATTN_KV_HEADS, HEAD_DIM)
        return jnp.concatenate([tp[:, :-2], tp[:, 1:-1], tp[:, 2:]], axis=2)

    kw, vw = band(k), band(v)
    scores = jnp.einsum('bnqkgd,bnskd->bnkgqs', qb, kw).astype(jnp.float32) * (HEAD_DIM ** -0.5)
    qi = jnp.arange(BLOCK)[:, None]
    kj = jnp.arange(3 * BLOCK)[None, :]
    rel = kj - BLOCK - qi
    key_pos = jnp.arange(nb)[:, None, None] * BLOCK - BLOCK + kj[None]
    valid = (jnp.abs(rel) <= WINDOW)[None] & (key_pos >= 0) & (key_pos < l)
    slopes = _alibi_slopes(ATTN_HEADS).reshape(ATTN_KV_HEADS, gq)
    alibi = -slopes[:, :, None, None] * jnp.abs(rel).astype(jnp.float32)
    scores = jnp.where(valid[None, :, None, None], scores + alibi, NEG_INF)
    sink = jnp.broadcast_to(sinks.astype(jnp.float32).reshape(1, 1, ATTN_KV_HEADS, gq, 1, 1),
                            scores.shape[:-1] + (1,))
    probs = jax.nn.softmax(jnp.concatenate([scores, sink], axis=-1), axis=-1)[..., :-1]
    out = jnp.einsum('bnkgqs,bnskd->bnqkgd', probs.astype(v.dtype), vw)
    return out.reshape(b, l, ATTN_WIDTH)


def _s5_direction(u, lam_re, lam_im, log_dt, b_re, b_im, c_re, c_im, reverse):
    f32 = jnp.float32
    lr = jnp.minimum(lam_re.astype(f32), LAMBDA_RE_MAX)
    li = lam_im.astype(f32)
    dt = jnp.exp(log_dt.astype(f32))[:, None]
    mag = jnp.exp(lr * dt)
    a_re = mag * jnp.cos(li * dt)
    a_im = mag * jnp.sin(li * dt)
    den = lr * lr + li * li
    coef_re = ((a_re - 1.0) * lr + a_im * li) / den
    coef_im = (a_im * lr - (a_re - 1.0) * li) / den
    br, bi = b_re.astype(f32), b_im.astype(f32)
    bb_re = coef_re[..., None] * br - coef_im[..., None] * bi
    bb_im = coef_re[..., None] * bi + coef_im[..., None] * br
    bu_re = jnp.einsum('blgh,gph->blgp', u, bb_re)
    bu_im = jnp.einsum('blgh,gph->blgp', u, bb_im)
    ar = jnp.broadcast_to(a_re, bu_re.shape)
    ai = jnp.broadcast_to(a_im, bu_re.shape)

    def combine(e1, e2):
        ar1, ai1, xr1, xi1 = e1
        ar2, ai2, xr2, xi2 = e2
        return (ar2 * ar1 - ai2 * ai1,
                ar2 * ai1 + ai2 * ar1,
                ar2 * xr1 - ai2 * xi1 + xr2,
                ar2 * xi1 + ai2 * xr1 + xi2)

    _, _, xr, xi = lax.associative_scan(combine, (ar, ai, bu_re, bu_im), reverse=reverse, axis=1)
    return (jnp.einsum('blgp,ghp->blgh', xr, c_re.astype(f32))
            - jnp.einsum('blgp,ghp->blgh', xi, c_im.astype(f32)))


def _s5_mixer(u, lam_re, lam_im, log_dt, b_re, b_im, c_re, c_im, d_skip, glu_w, glu_b):
    b, l, _ = u.shape
    ug = u.astype(jnp.float32).reshape(b, l, SSM_GROUPS, SSM_CH_PER_GROUP)
    y = d_skip.astype(jnp.float32) * ug
    for direction in range(2):
        y = y + _s5_direction(ug, lam_re[direction], lam_im[direction], log_dt[direction],
                              b_re[direction], b_im[direction], c_re[direction], c_im[direction],
                              reverse=(direction == 1))
    y = jax.nn.gelu(y.reshape(b, l, SSM_WIDTH)).astype(u.dtype)
    return y * jax.nn.sigmoid(y @ glu_w + glu_b)


def setup_inputs(seed: int = 0) -> dict:
    key = jax.random.key(seed)
    ks = iter(jax.random.split(key, 40))
    f32 = jnp.float32

    def nrm(shape, scale):
        return jax.random.normal(next(ks), shape, f32) * scale

    def gain(shape):
        return 1.0 + 0.01 * jax.random.normal(next(ks), shape, f32)

    L, G, P, Hc = DEPTH, SSM_GROUPS, SSM_STATE, SSM_CH_PER_GROUP
    lam_im_init = jnp.pi * jnp.arange(P, dtype=f32)
    return {
        "x": jax.random.normal(next(ks), (BATCH, SEQ, D_MODEL), f32),
        "norm_ffn1": gain((L, D_MODEL)),
        "ffn1_w_gate": nrm((L, D_MODEL, D_FF), D_MODEL ** -0.5),
        "ffn1_w_up": nrm((L, D_MODEL, D_FF), D_MODEL ** -0.5),
        "ffn1_w_down": nrm((L, D_FF, D_MODEL), D_FF ** -0.5),
        "norm_mix": gain((L, D_MODEL)),
        "w_in": nrm((L, D_MODEL, IN_WIDTH), D_MODEL ** -0.5),
        "attn_sinks": nrm((L, ATTN_HEADS), 0.5),
        "ssm_lambda_re": -0.5 + nrm((L, 2, G, P), 0.01),
        "ssm_lambda_im": lam_im_init + nrm((L, 2, G, P), 0.01),
        "ssm_log_dt": jax.random.uniform(next(ks), (L, 2, G), f32,
                                         minval=math.log(DT_MIN), maxval=math.log(DT_MAX)),
        "ssm_b_re": nrm((L, 2, G, P, Hc), (2.0 * Hc) ** -0.5),
        "ssm_b_im": nrm((L, 2, G, P, Hc), (2.0 * Hc) ** -0.5),
        "ssm_c_re": nrm((L, 2, G, Hc, P), (2.0 * P) ** -0.5),
        "ssm_c_im": nrm((L, 2, G, Hc, P), (2.0 * P) ** -0.5),
        "ssm_d": 1.0 + nrm((L, G, Hc), 0.1),
        "ssm_glu_w": nrm((L, SSM_WIDTH, SSM_WIDTH), SSM_WIDTH ** -0.5),
        "ssm_glu_b": nrm((L, SSM_WIDTH), 0.01),
        "attn_out_norm": gain((L, ATTN_WIDTH)),
        "ssm_out_norm": gain((L, SSM_WIDTH)),
        "w_out": nrm((L, MIX_WIDTH, D_MODEL), MIX_WIDTH ** -0.5),
        "norm_ffn2": gain((L, D_MODEL)),
        "ffn2_w_gate": nrm((L, D_MODEL, D_FF), D_MODEL ** -0.5),
        "ffn2_w_up": nrm((L, D_MODEL, D_FF), D_MODEL ** -0.5),
        "ffn2_w_down": nrm((L, D_FF, D_MODEL), D_FF ** -0.5),
        "final_norm": gain((D_MODEL,)),
    }


def reference(x, norm_ffn1, ffn1_w_gate, ffn1_w_up, ffn1_w_down, norm_mix, w_in, attn_sinks,
              ssm_lambda_re, ssm_lambda_im, ssm_log_dt, ssm_b_re, ssm_b_im, ssm_c_re, ssm_c_im,
              ssm_d, ssm_glu_w, ssm_glu_b, attn_out_norm, ssm_out_norm, w_out,
              norm_ffn2, ffn2_w_gate, ffn2_w_up, ffn2_w_down, final_norm):
    b, l, _ = x.shape
    for layer in range(DEPTH):
        x = x + 0.5 * _swiglu(_rmsnorm(x, norm_ffn1[layer]),
                              ffn1_w_gate[layer], ffn1_w_up[layer], ffn1_w_down[layer])
        h = _rmsnorm(x, norm_mix[layer])
        proj = h @ w_in[layer]
        q, k, v, u = jnp.split(proj, [ATTN_WIDTH, ATTN_WIDTH + KV_WIDTH,
                                      ATTN_WIDTH + 2 * KV_WIDTH], axis=-1)
        attn = _window_attention(q.reshape(b, l, ATTN_HEADS, HEAD_DIM),
                                 k.reshape(b, l, ATTN_KV_HEADS, HEAD_DIM),
                                 v.reshape(b, l, ATTN_KV_HEADS, HEAD_DIM),
                                 attn_sinks[layer])
        ssm = _s5_mixer(u, ssm_lambda_re[layer], ssm_lambda_im[layer], ssm_log_dt[layer],
                        ssm_b_re[layer], ssm_b_im[layer], ssm_c_re[layer], ssm_c_im[layer],
                        ssm_d[layer], ssm_glu_w[layer], ssm_glu_b[layer])
        mixed = jnp.concatenate([_rmsnorm(attn, attn_out_norm[layer]),
                                 _rmsnorm(ssm, ssm_out_norm[layer])], axis=-1)
        x = x + mixed @ w_out[layer]
        x = x + 0.5 * _swiglu(_rmsnorm(x, norm_ffn2[layer]),
                              ffn2_w_gate[layer], ffn2_w_up[layer], ffn2_w_down[layer])
    return _rmsnorm(x, final_norm)
```

```python
import numpy as np
from contextlib import ExitStack
import concourse.bass as bass
import concourse.mybir as mybir
from concourse.bass_utils import run_bass_kernel_spmd

F32 = mybir.dt.float32
BF16 = mybir.dt.bfloat16
I32 = mybir.dt.int32
AF = mybir.ActivationFunctionType
ALU = mybir.AluOpType

NCORES = 8
T = 2048
TB = 512
NTB = 4
D = 1024
KD = 8
FF = 2816
NF = 22
EPS = 1e-6
FGROUPS = [(0, 4), (4, 8), (8, 12), (12, 16), (16, 20), (20, 22)]
NCH = 256


class _Tile:
    __slots__ = ("name", "last_w", "rd_eng", "rd_dma")

    def __init__(self, name):
        self.name = name
        self.last_w = None
        self.rd_eng = {}
        self.rd_dma = []


class _Stream:
    def __init__(self, name, final_only=False):
        self.name = name
        self.final_only = final_only
        self.count = 0
        self.sem = None


class _Op:
    __slots__ = ("eng", "fn", "deps", "signal", "sigval", "stream", "sidx", "is_dma")

    def __init__(self, eng, fn, stream=None):
        self.eng = eng
        self.fn = fn
        self.deps = []
        self.signal = False
        self.sigval = 0
        self.stream = stream
        self.sidx = 0
        self.is_dma = stream is not None


class Prog:
    ENGS = ("pe", "act", "dve", "pool", "sp")

    def __init__(self, nc):
        self.nc = nc
        self.ops = {e: [] for e in self.ENGS}
        self.tiles = {}
        self.streams = []
        self.regions = {}

    def tile(self, *key):
        t = self.tiles.get(key)
        if t is None:
            t = _Tile(key)
            self.tiles[key] = t
        return t

    def rtile(self, region, view, *key):
        reg = self.regions.setdefault(region, {"view": None, "tiles": {}, "carry": ({}, [], [])})
        if reg["view"] != view:
            eng_last = dict(reg["carry"][0])
            dmas = list(reg["carry"][1])
            writers = list(reg["carry"][2])
            for t in reg["tiles"].values():
                if t.last_w is not None:
                    writers.append(t.last_w)
                for e, o in t.rd_eng.items():
                    if e not in eng_last or eng_last[e].sidx < o.sidx:
                        eng_last[e] = o
                dmas.extend(t.rd_dma)
            reg["view"] = view
            reg["tiles"] = {}
            reg["carry"] = (eng_last, dmas, writers)
        t = reg["tiles"].get(key)
        if t is None:
            t = _Tile((region, view) + key)
            eng_last, dmas, writers = reg["carry"]
            t.rd_eng = dict(eng_last)
            t.rd_dma = list(dmas) + list(writers)
            reg["tiles"][key] = t
        return t

    def stream(self, name, final_only=False):
        s = _Stream(name, final_only)
        self.streams.append(s)
        return s

    def _add(self, o, r, w):
        raw = set()
        deps = set()
        for t in r:
            if t.last_w is not None:
                raw.add(t.last_w)
        for t in w:
            if t.last_w is not None:
                deps.add(t.last_w)
            deps.update(t.rd_eng.values())
            deps.update(t.rd_dma)
        o.sidx = len(self.ops[o.eng])
        for t in r:
            if o.is_dma:
                t.rd_dma.append(o)
            else:
                t.rd_eng[o.eng] = o
        for t in w:
            t.last_w = o
            t.rd_eng = {}
            t.rd_dma = []
        raw.discard(o)
        deps.discard(o)
        for d in raw:
            o.deps.append(d)
        for d in deps:
            if d in raw:
                continue
            if d.is_dma and o.is_dma and d.stream is o.stream and d.stream.final_only:
                continue
            if d.is_dma or o.is_dma or d.eng != o.eng or o.eng != "pe":
                o.deps.append(d)
        self.ops[o.eng].append(o)
        return o

    def op(self, eng, fn, r=(), w=()):
        return self._add(_Op(eng, fn), list(r), list(w))

    def dma(self, q, out, in_, r=(), w=(), stream=None, **kw):
        if stream is None:
            stream = self.stream("anon")
        o = _Op(q, lambda e, out=out, in_=in_, kw=kw: e.dma_start(out=out, in_=in_, **kw), stream)
        stream.count += 1
        self._add(o, list(r), list(w))
        o.sigval = 16 * stream.count
        return o

    def emit(self, es: ExitStack, out_streams):
        nc = self.nc
        for e in self.ENGS:
            for o in self.ops[e]:
                for d in o.deps:
                    if not d.is_dma:
                        d.signal = True
        for e in self.ENGS:
            c = 0
            for o in self.ops[e]:
                if not o.is_dma and o.signal:
                    c += 1
                    o.sigval = c
        esem = {e: es.enter_context(nc.semaphore("sem_" + e)) for e in self.ENGS}
        for i, s in enumerate(self.streams):
            if s.count > 0:
                s.sem = es.enter_context(nc.semaphore("ds%d_%s" % (i, s.name)))
        block = es.enter_context(nc.Block())
        handles = {"pe": block.tensor, "act": block.scalar, "dve": block.vector,
                   "pool": block.gpsimd, "sp": block.sync}

        def run(engname, e):
            known = {}
            for o in self.ops[engname]:
                waits = {}
                for d in o.deps:
                    if d.is_dma:
                        sem = d.stream.sem
                        val = 16 * d.stream.count if d.stream.final_only else d.sigval
                    else:
                        sem = esem[d.eng]
                        val = d.sigval
                    k = id(sem)
                    if k not in waits or waits[k][1] < val:
                        waits[k] = (sem, val)
                for k, (sem, val) in waits.items():
                    if known.get(k, 0) < val:
                        e.wait_ge(sem, val)
                        known[k] = val
                ins = o.fn(e)
                if o.is_dma:
                    ins.then_inc(o.stream.sem, 16)
                elif o.signal:
                    ins.then_inc(esem[engname], 1)
            if engname == "sp":
                for s in out_streams:
                    if s.count > 0:
                        e.wait_ge(s.sem, 16 * s.count)

        for engname in self.ENGS:
            def _f(e, engname=engname):
                run(engname, e)
            handles[engname](_f)


def _ap(base, offset_elems, dims):
    return bass.AP(base.tensor, base.offset + offset_elems, [list(base.ap[0])] + [list(d) for d in dims])


def build_program(taps=(), stop_after=None):
    nc = bass.Bass("TRN2", target_bir_lowering=False)
    es = ExitStack()
    P = Prog(nc)
    taps = list(taps)
    tap_out = {}

    def din(name, shape, dt=F32):
        return nc.dram_tensor(name, list(shape), dt, kind="ExternalInput").ap()

    xT_d = din("xT", [D, T])
    outT_d = nc.dram_tensor("outT", [D, T], F32, kind="ExternalOutput").ap()
    w_d = {}
    for nm, shp in [("ffn1_w_gate", [D, FF]), ("ffn1_w_up", [D, FF]), ("ffn1_w_down", [FF, D]),
                    ("ffn2_w_gate", [D, FF]), ("ffn2_w_up", [D, FF]), ("ffn2_w_down", [FF, D]),
                    ("w_in", [D, 1280]), ("w_out", [D, D]), ("ssm_glu_w", [512, 512])]:
        w_d[nm] = din(nm, shp)
    p_d = {}
    for nm, shp in [("norm_ffn1", [D]), ("norm_mix", [D]), ("norm_ffn2", [D]), ("final_norm", [D]),
                    ("attn_out_norm", [512]), ("ssm_out_norm", [512]), ("ssm_glu_b", [512]),
                    ("attn_sinks", [8]), ("ssm_lambda_re", [2, 32, 64]), ("ssm_lambda_im", [2, 32, 64]),
                    ("ssm_log_dt", [2, 32]), ("ssm_b_re", [2, 32, 64, 16]), ("ssm_b_im", [2, 32, 64, 16]),
                    ("ssm_c_re", [2, 32, 16, 64]), ("ssm_c_im", [2, 32, 16, 64]), ("ssm_d", [32, 16]),
                    ("c_eb", [128, 24 * 128])]:
        p_d[nm] = din(nm, shp)

    def sb(name, shape, dt):
        return es.enter_context(nc.sbuf_tensor(name, list(shape), dt))

    xT = sb("xT_sb", [128, KD, T], F32)
    ring = sb("ring", [128, 4, 4096], BF16)
    RA = sb("RA", [128, 16384], BF16)
    RB = sb("RB", [128, 8192], BF16)
    RC = sb("RC", [128, 12288], BF16)
    RD = sb("RD", [128, 7168], BF16)
    gcols = sb("gcols", [128, 4, KD], F32)
    ones = sb("ones", [128, 128], BF16)
    tf = [sb("tf%d" % i, [128, TB], F32) for i in range(3)]
    tb16 = [sb("tb16_%d" % i, [128, TB], BF16) for i in range(2)]
    psb = [es.enter_context(nc.psum_tensor("ps%d" % b, [128, 512], F32)) for b in range(8)]

    def pst(b):
        return [P.tile("ps", b, 0), P.tile("ps", b, 1)]

    def psth(b, h):
        return [P.tile("ps", b, 0), P.tile("ps", b, 1)]

    s_small = P.stream("small", final_only=True)
    s_out = P.stream("out")
    tap_streams = []

    def tap(name, ap, shape, dt=F32, r=()):
        if name not in taps:
            return
        d = nc.dram_tensor("tap_" + name, list(shape), dt, kind="ExternalOutput").ap()
        tap_out[name] = d
        ts_ = P.stream("tap_" + name)
        tap_streams.append(ts_)
        P.dma("sp", d, ap, r=list(r), stream=ts_)

    t_consts = P.tile("consts")
    P.op("pool", lambda e: e.memset(ones[:], 1.0), w=[t_consts])
    nc_allow = es.enter_context(nc.allow_non_contiguous_dma(reason="tiny parameter loads"))
    for i, nm in enumerate(["norm_ffn1", "norm_mix", "norm_ffn2", "final_norm"]):
        P.dma("sp", gcols[:, i, :], p_d[nm].rearrange("(k p) -> p k", p=128), w=[t_consts], stream=s_small)

    xs = [P.stream("x%d" % k) for k in range(KD)]
    for k in range(KD):
        P.dma("sp", xT[:, k, :], xT_d[k * 128:(k + 1) * 128, :],
              w=[P.tile("xT", k, tb) for tb in range(NTB)], stream=xs[k])

    ring_state = {"n": 0}
    rstreams = {q: [P.stream("ring%s%d" % (q, i)) for i in range(4)] for q in ("pool", "sp")}

    def ring_load(parts, q="pool", r=()):
        s = ring_state["n"] % 4
        ring_state["n"] += 1
        t = P.tile("ring", s)
        slot = ring[:, s, :]
        for dst_fn, src in parts:
            P.dma(q, dst_fn(slot), src, r=list(r), w=[t], stream=rstreams[q][s])
        return slot, t

    psrot = {"gu": 0, "dn": 0}

    def rmsnorm(src_fn, src_tiles_fn, dst_fn, dst_tiles_fn, nchunk, width, gcol_fn, ones_ap=None, bank=None):
        if bank is None:
            b = 4 + (psrot["dn"] % 4)
            psrot["dn"] += 1
        else:
            b = bank
        ps = psb[b]
        oa = ones[:] if ones_ap is None else ones_ap
        for k in range(nchunk):
            sq = tb16[k % 2]
            tq = P.tile("tb16", k % 2)
            P.op("act", lambda e, sq=sq, k=k: e.activation(out=sq[:], in_=src_fn(k), func=AF.Square),
                 r=src_tiles_fn(k), w=[tq])
            P.op("pe", lambda e, sq=sq, k=k, ps=ps: e.matmul(ps[:], lhsT=oa, rhs=sq[:],
                                                             start=(k == 0), stop=(k == nchunk - 1)),
                 r=[tq, t_consts], w=pst(b))
        ttf = P.tile("tf", 0)
        P.op("act", lambda e, ps=ps: e.activation(out=tf[0][:], in_=ps[:], func=AF.Sqrt,
                                                  scale=1.0 / width, bias=epsc[:, 0:1]),
             r=pst(b) + [t_consts], w=[ttf])
        P.op("dve", lambda e, ps=ps: e.reciprocal(out=ps[:], in_=tf[0][:]), r=[ttf], w=pst(b))
        for k in range(nchunk):
            P.op("dve", lambda e, k=k, ps=ps: e.scalar_tensor_tensor(
                out=dst_fn(k), in0=src_fn(k), scalar=gcol_fn(k), in1=ps[:], op0=ALU.mult, op1=ALU.mult),
                r=src_tiles_fn(k) + pst(b) + [t_consts], w=dst_tiles_fn(k))

    epsc = sb("epsc", [128, 1], F32)
    P.op("pool", lambda e: e.memset(epsc[:], EPS), w=[t_consts])

    ebt = sb("ebt", [128, 24, 128], BF16)
    pcols = sb("pcols", [128, 16], F32)
    s_eb = P.stream("eb")
    t_eb = P.tile("ebt")
    P.dma("pool", ebt[:], p_d["c_eb"].rearrange("p (a q) -> p a q", a=24), w=[t_eb], stream=s_eb, max_dma_last_dim=4096)
    t_pc = P.tile("pcols")
    for kv in range(2):
        P.dma("sp", pcols[kv * 64:(kv + 1) * 64, 0:4],
              p_d["attn_out_norm"][kv * 256:(kv + 1) * 256].rearrange("(c d) -> d c", d=64), w=[t_pc], stream=s_small)
        sk = p_d["attn_sinks"]
        P.dma("sp", pcols[kv * 64:(kv + 1) * 64, 12:16],
              bass.AP(sk.tensor, sk.offset + 4 * kv, [[0, 64], [1, 4]]), w=[t_pc], stream=s_small)
    P.dma("sp", pcols[:, 4:8], p_d["ssm_out_norm"].rearrange("(c p) -> p c", p=128), w=[t_pc], stream=s_small)
    P.dma("sp", pcols[:, 8:12], p_d["ssm_glu_b"].rearrange("(c p) -> p c", p=128), w=[t_pc], stream=s_small)
    P.op("act", lambda e: e.activation(out=pcols[:, 12:16], in_=pcols[:, 12:16], func=AF.Exp), r=[t_pc], w=[t_pc])

    ssc = sb("ssc", [128, 24, 32], F32)
    ssci = sb("ssci", [128, 2, 32], I32)
    identb = sb("identb", [128, 128], BF16)
    identf = sb("identf", [128, 128], F32)
    rsel = sb("rsel", [128, 8, 240], BF16)
    dcol = sb("dcol", [128, 32], F32)
    arb = sb("arb", [128, 2, 2, 16], F32)
    aib = sb("aib", [128, 2, 2, 16], F32)
    sring = sb("sring", [128, 2, 8, 32], F32)
    scr_mg = nc.dram_tensor("scr_mg", [128, 32, 128], BF16, kind="Internal").ap()
    scr_sz = nc.dram_tensor("scr_sz", [128, 2, 32, 128], BF16, kind="Internal").ap()
    scr_so = nc.dram_tensor("scr_so", [128, 2, 2, 16, 128], BF16, kind="Internal").ap()
    t_id = P.tile("ident")
    for idt in (identb, identf):
        P.op("pool", lambda e, idt=idt: e.memset(idt[:], 0.0), w=[t_id])
        P.op("pool", lambda e, idt=idt: e.affine_select(out=idt[:], in_=idt[:], pattern=[[-1, 128]],
                                                       compare_op=ALU.not_equal, fill=1.0, base=0, channel_multiplier=1),
             r=[t_id], w=[t_id])
    t_rsel = P.tile("rsel")
    P.op("pool", lambda e: e.memset(rsel[:], 0.0), w=[t_rsel])
    for a0 in range(8):
        P.op("pool", lambda e, a0=a0: e.tensor_copy(out=rsel[:, a0, 112:128], in_=identb[:, a0 * 16:(a0 + 1) * 16]),
             r=[t_id], w=[t_rsel])
    t_ab = P.tile("arb")
    s_scr = P.stream("scr")
    t_scr = P.tile("scr")

    def ssm_setup():
        VW = "setup"
        def sc(i):
            return ssc[:, i, :]
        def st(i):
            return P.tile("ssc", i)
        def vtt(o, a, b, op):
            P.op("dve", lambda e: e.tensor_tensor(out=sc(o), in0=sc(a), in1=sc(b), op=op), r=[st(a), st(b)], w=[st(o)])
        def vts(o, a, s1, s2, op0, op1=None):
            if op1 is None:
                P.op("dve", lambda e: e.tensor_scalar(out=sc(o), in0=sc(a), scalar1=s1, scalar2=None, op0=op0), r=[st(a)], w=[st(o)])
            else:
                P.op("dve", lambda e: e.tensor_scalar(out=sc(o), in0=sc(a), scalar1=s1, scalar2=s2, op0=op0, op1=op1), r=[st(a)], w=[st(o)])
        def vstt(o, a, scal, b, op0, op1):
            P.op("dve", lambda e: e.scalar_tensor_tensor(out=sc(o), in0=sc(a), scalar=scal, in1=sc(b), op0=op0, op1=op1),
                 r=[st(a), st(b)], w=[st(o)])
        LR, LI, LDT, lr, dt, z, th, mag, sn, cs, Are, Aim = range(12)
        t0, t1, t2, t3, t4, t5, cre, cim, p2r, p2i, t6, t7 = range(12, 24)
        for par in range(2):
            for slot, nm in ((LR, "ssm_lambda_re"), (LI, "ssm_lambda_im")):
                src = p_d[nm]
                for d in range(2):
                    P.dma("sp", ssc[par * 64:(par + 1) * 64, slot, d * 16:(d + 1) * 16],
                          bass.AP(src.tensor, src.offset + d * 2048 + par * 16 * 64, [[1, 64], [64, 16]]),
                          w=[st(slot)], stream=s_small)
            src = p_d["ssm_log_dt"]
            P.dma("sp", ssc[par * 64:(par + 1) * 64, LDT, :].rearrange("p (d g) -> p d g", d=2),
                  bass.AP(src.tensor, src.offset + par * 16, [[0, 64], [32, 2], [1, 16]]), w=[st(LDT)], stream=s_small)
        for s_ in range(8):
            src = p_d["ssm_d"]
            P.dma("sp", dcol[s_ * 16:(s_ + 1) * 16, :], bass.AP(src.tensor, src.offset, [[1, 16], [16, 32]]),
                  w=[P.tile("dcol")], stream=s_small)
        big = RC[:, 0:12288].bitcast(F32).rearrange("p (j n) -> p j n", j=12)
        def bg(j):
            return big[:, j, :]
        def bgt(j):
            return P.rtile("RC", VW, "big", j)
        BRE, BIM, CRE, CIM, XR, XI, YR, YI, T0, T1, T2, T3 = range(12)
        for par in range(2):
            for d in range(2):
                for j, nm in ((BRE, "ssm_b_re"), (BIM, "ssm_b_im")):
                    src = p_d[nm]
                    P.dma("sp", big[par * 64:(par + 1) * 64, j, d * 256:(d + 1) * 256].rearrange("p (g h) -> p g h", g=16),
                          bass.AP(src.tensor, src.offset + d * 32768 + par * 16 * 1024, [[16, 64], [1024, 16], [1, 16]]),
                          w=[bgt(j)], stream=s_small)
        cnat = RA[:, 12288:14336].bitcast(F32).rearrange("p (r d b q c) -> p r d b q c", r=2, d=2, b=2, q=2)
        t_cnat = P.rtile("RAx", VW, "cnat")
        for ri, nm in enumerate(("ssm_c_re", "ssm_c_im")):
            src = p_d[nm]
            for d in range(2):
                for blk in range(2):
                    P.dma("sp", cnat[:, ri, d, blk, :, :],
                          bass.AP(src.tensor, src.offset + d * 32768 + blk * 8 * 1024, [[64, 128], [16384, 2], [1, 64]]),
                          w=[t_cnat], stream=s_small)
        vts(lr, LR, -1e-4, None, ALU.min)
        vts(t0, LDT, 1.4426950408889634, None, ALU.mult)
        P.op("dve", lambda e: e.tensor_copy(out=ssci[:, 0, :], in_=sc(t0)), r=[st(t0)], w=[P.tile("ssci", 0)])
        P.op("dve", lambda e: e.tensor_copy(out=sc(t1), in_=ssci[:, 0, :]), r=[P.tile("ssci", 0)], w=[st(t1)])
        vstt(t2, t1, -0.693145751953125, LDT, ALU.mult, ALU.add)
        vstt(t2, t1, -1.42860682030941723212e-6, t2, ALU.mult, ALU.add)
        vts(t3, t2, 1.0 / 362880.0, None, ALU.mult)
        for c in (1.0 / 40320, 1.0 / 5040, 1.0 / 720, 1.0 / 120, 1.0 / 24, 1.0 / 6, 0.5, 1.0):
            vstt(t3, t3, c, t2, ALU.add, ALU.mult)
        vts(t3, t3, 1.0, None, ALU.add)
        P.op("dve", lambda e: e.tensor_scalar(out=ssci[:, 1, :], in0=sc(t1), scalar1=127.0, scalar2=8388608.0,
                                              op0=ALU.add, op1=ALU.mult), r=[st(t1)], w=[P.tile("ssci", 1)])
        P.op("dve", lambda e: e.tensor_tensor(out=sc(dt), in0=sc(t3), in1=ssci[:, 1, :].bitcast(F32), op=ALU.mult),
             r=[st(t3), P.tile("ssci", 1)], w=[st(dt)])
        vtt(z, lr, dt, ALU.mult)
        vtt(th, LI, dt, ALU.mult)
        vts(t3, z, 1.0 / 5040.0, None, ALU.mult)
        for c in (1.0 / 720, 1.0 / 120, 1.0 / 24, 1.0 / 6, 0.5, 1.0):
            vstt(t3, t3, c, z, ALU.add, ALU.mult)
        vts(mag, t3, 1.0, None, ALU.add)
        PI_LO = 3.1415925
        vts(t0, th, 0.15915494309189535, None, ALU.mult)
        P.op("dve", lambda e: e.tensor_copy(out=ssci[:, 0, :], in_=sc(t0)), r=[st(t0)], w=[P.tile("ssci", 0)])
        P.op("dve", lambda e: e.tensor_copy(out=sc(t1), in_=ssci[:, 0, :]), r=[P.tile("ssci", 0)], w=[st(t1)])
        vstt(t2, t1, -6.28125, th, ALU.mult, ALU.add)
        vstt(t2, t1, -1.9353071795864769e-3, t2, ALU.mult, ALU.add)
        vts(t4, t2, -PI_LO, PI_LO, ALU.max, ALU.min)
        P.op("act", lambda e: e.activation(out=sc(sn), in_=sc(t4), func=AF.Sin), r=[st(t4)], w=[st(sn)])
        vts(t5, t2, 1.5707963267948966, None, ALU.add)
        vts(t0, t5, PI_LO, None, ALU.is_gt)
        vstt(t5, t0, -6.283185307179586, t5, ALU.mult, ALU.add)
        vts(t5, t5, -PI_LO, PI_LO, ALU.max, ALU.min)
        P.op("act", lambda e: e.activation(out=sc(cs), in_=sc(t5), func=AF.Sin), r=[st(t5)], w=[st(cs)])
        vtt(Are, mag, cs, ALU.mult)
        vtt(Aim, mag, sn, ALU.mult)
        vts(t0, Are, -1.0, None, ALU.add)
        vtt(t1, t0, lr, ALU.mult)
        vtt(t2, Aim, LI, ALU.mult)
        vtt(t1, t1, t2, ALU.add)
        vtt(t2, Aim, lr, ALU.mult)
        vtt(t3, t0, LI, ALU.mult)
        vtt(t2, t2, t3, ALU.subtract)
        vtt(t3, lr, lr, ALU.mult)
        vtt(t4, LI, LI, ALU.mult)
        vtt(t3, t3, t4, ALU.add)
        P.op("dve", lambda e: e.reciprocal(out=sc(t3), in_=sc(t3)), r=[st(t3)], w=[st(t3)])
        vtt(cre, t1, t3, ALU.mult)
        vtt(cim, t2, t3, ALU.mult)
        def csq(o_r, o_i, a_r, a_i):
            vtt(t0, a_r, a_r, ALU.mult)
            vtt(t1, a_i, a_i, ALU.mult)
            vtt(t2, a_r, a_i, ALU.mult)
            vtt(o_r, t0, t1, ALU.subtract)
            vts(o_i, t2, 2.0, None, ALU.mult)
        csq(p2r, p2i, Are, Aim)
        csq(t6, t7, p2r, p2i)
        csq(p2r, p2i, t6, t7)
        P.op("dve", lambda e: e.tensor_copy(out=arb[:], in_=_ap(sc(p2r), 0, [[16, 2], [0, 2], [1, 16]])), r=[st(p2r)], w=[t_ab])
        P.op("dve", lambda e: e.tensor_copy(out=aib[:, :, 1, :], in_=sc(p2i).rearrange("p (d g) -> p d g", d=2)), r=[st(p2i)], w=[t_ab])
        P.op("dve", lambda e: e.tensor_scalar(out=aib[:, :, 0, :], in0=sc(p2i).rearrange("p (d g) -> p d g", d=2),
                                              scalar1=-1.0, scalar2=None, op0=ALU.mult), r=[st(p2i)], w=[t_ab])
        for ri, cj in ((0, CRE), (1, CIM)):
            for d in range(2):
                for blk in range(2):
                    b = 4 + (psrot["dn"] % 4)
                    psrot["dn"] += 1
                    P.op("pe", lambda e, b=b, ri=ri, d=d, blk=blk: e.transpose(
                        psb[b][:, 0:128], cnat[:, ri, d, blk, :, :].rearrange("p q c -> p (q c)"), identf[:]),
                        r=[t_cnat, t_id], w=pst(b))
                    P.op("act", lambda e, b=b, cj=cj, d=d, blk=blk: e.activation(
                        out=bg(cj)[:, d * 256 + blk * 128:d * 256 + (blk + 1) * 128], in_=psb[b][:, 0:128], func=AF.Copy),
                        r=pst(b), w=[bgt(cj)])
        def bc(slot):
            return _ap(sc(slot), 0, [[1, 32], [0, 16]])
        def b3(j):
            return bg(j).rearrange("p (g h) -> p g h", h=16)
        def cmul_bc(o_r, o_i, a_r, a_i, x_r, x_i):
            P.op("dve", lambda e: e.tensor_tensor(out=b3(T0), in0=b3(x_r), in1=bc(a_r), op=ALU.mult), r=[bgt(x_r), st(a_r)], w=[bgt(T0)])
            P.op("dve", lambda e: e.tensor_tensor(out=b3(T1), in0=b3(x_i), in1=bc(a_i), op=ALU.mult), r=[bgt(x_i), st(a_i)], w=[bgt(T1)])
            P.op("dve", lambda e: e.tensor_tensor(out=b3(T2), in0=b3(x_i), in1=bc(a_r), op=ALU.mult), r=[bgt(x_i), st(a_r)], w=[bgt(T2)])
            P.op("dve", lambda e: e.tensor_tensor(out=b3(T3), in0=b3(x_r), in1=bc(a_i), op=ALU.mult), r=[bgt(x_r), st(a_i)], w=[bgt(T3)])
            P.op("dve", lambda e: e.tensor_tensor(out=bg(o_r), in0=bg(T0), in1=bg(T1), op=ALU.subtract), r=[bgt(T0), bgt(T1)], w=[bgt(o_r)])
            P.op("dve", lambda e: e.tensor_tensor(out=bg(o_i), in0=bg(T2), in1=bg(T3), op=ALU.add), r=[bgt(T2), bgt(T3)], w=[bgt(o_i)])
        so16 = RB[:, 0:8192].rearrange("p (d r g t h) -> p d r g t h", d=2, r=2, g=16, t=8)
        t_so = P.rtile("RB", "so16", "so")
        cur = (CRE, CIM)
        nxt = [(XR, XI), (YR, YI)]
        for k in range(1, 9):
            o = nxt[k % 2]
            cmul_bc(o[0], o[1], Are, Aim, cur[0], cur[1])
            cur = o
            for d in range(2):
                slot = (k - 1) if d == 0 else (8 - k)
                for ri in range(2):
                    P.op("act", lambda e, d=d, ri=ri, slot=slot, cur=cur: e.activation(
                        out=so16[:, d, ri, :, slot, :], in_=b3(cur[ri])[:, d * 16:(d + 1) * 16, :], func=AF.Copy,
                        scale=(1.0 if ri == 0 else -1.0)), r=[bgt(cur[ri])], w=[t_so])
        P.dma("sp", scr_so.rearrange("p d r g n -> p (d r g n)"), RB[:, 0:8192], r=[t_so], w=[t_scr], stream=s_scr)
        cpa = RA[:, 14336:15360].rearrange("p (r d g h) -> p r d g h", r=2, d=2, g=16)
        t_cpa = P.rtile("RAx", "cpa", "cpa")
        for ri, cj in ((0, CRE), (1, CIM)):
            P.op("act", lambda e, ri=ri, cj=cj: e.activation(out=cpa[:, ri].rearrange("p d g h -> p (d g) h"), in_=b3(cj), func=AF.Copy,
                                                           scale=(1.0 if ri == 0 else -1.0)), r=[bgt(cj)], w=[t_cpa])
        wa16 = RB[:, 0:8192].rearrange("p (r d g t h) -> p r d g t h", r=2, d=2, g=16, t=8)
        t_wa = P.rtile("RB", "wa16", "wa")
        cmul_bc(XR, XI, cre, cim, BRE, BIM)
        cur = (XR, XI)
        nxt = [(YR, YI), (XR, XI)]
        for tau in range(8):
            for d in range(2):
                slot = (7 - tau) if d == 0 else tau
                for ri in range(2):
                    P.op("act", lambda e, d=d, ri=ri, slot=slot, cur=cur: e.activation(
                        out=wa16[:, ri, d, :, slot, :], in_=b3(cur[ri])[:, d * 16:(d + 1) * 16, :], func=AF.Copy),
                        r=[bgt(cur[ri])], w=[t_wa])
            if tau < 7:
                o = nxt[tau % 2]
                cmul_bc(o[0], o[1], Are, Aim, cur[0], cur[1])
                cur = o
        V3 = "pe_stage"
        ww = RC[:, 0:7680].rearrange("p (b d g j h) -> p b d g j h", b=2, d=2, g=8, j=15)
        cpb = RC[:, 7680:8704].rearrange("p (d g h) -> p d g h", d=2, g=32)
        mgst = RC[:, 8704:10752].rearrange("p (b g n) -> p b g n", b=2, g=8)
        szst = RD[:, 0:4096].rearrange("p (b d g n) -> p b d g n", b=2, d=2, g=8)
        t_cpb = P.rtile("RC", V3, "cpb")
        s_asm = P.stream("asm", final_only=True)
        s_asmw = [P.stream("asmw%d" % i) for i in range(2)]
        for bi in range(2):
            P.op("pool", lambda e, bi=bi: e.memset(ww[:, bi].rearrange("p d g j h -> p (d g j h)"), 0.0),
                 w=[P.rtile("RC", V3, "ww", bi)])
        for ri in range(2):
            for par in range(2):
                P.dma("sp", cpb[ri * 64:(ri + 1) * 64, :, par * 16:(par + 1) * 16, :].rearrange("p d g h -> p d (g h)"),
                      cpa[par * 64:(par + 1) * 64, ri].rearrange("p d g h -> p d (g h)"), r=[t_cpa], w=[t_cpb], stream=s_asm)
        for blk in range(4):
            bi = blk % 2
            par, gp0 = blk // 2, (blk % 2) * 8
            t_ww = P.rtile("RC", V3, "ww", bi)
            for d in range(2):
                j0 = 0 if d == 0 else 7
                for ri in range(2):
                    P.dma("sp", ww[ri * 64:(ri + 1) * 64, bi, d, :, j0:j0 + 8, :].rearrange("p g j h -> p g (j h)"),
                          wa16[par * 64:(par + 1) * 64, ri, d, gp0:gp0 + 8, :, :].rearrange("p g t h -> p g (t h)"),
                          r=[t_wa], w=[t_ww], stream=s_asmw[bi])
            t_mg = P.rtile("RC", V3, "mgst", bi)
            t_sz = P.rtile("RD", V3, "szst", bi)
            for gl in range(8):
                g = blk * 8 + gl
                b = 4 + (psrot["dn"] % 4)
                psrot["dn"] += 1
                for t in range(8):
                    for d in range(2):
                        P.op("pe", lambda e, b=b, t=t, d=d, gl=gl, bi=bi, g=g: e.matmul(
                            psb[b][:, t * 16:(t + 1) * 16],
                            lhsT=ww[:, bi, d, gl, 7 - t:15 - t, :].rearrange("p j h -> p (j h)"),
                            rhs=cpb[:, d, g, :], start=(d == 0), stop=(d == 1)), r=[t_ww, t_cpb], w=pst(b))
                P.op("dve", lambda e, b=b, bi=bi, gl=gl, g=g: e.scalar_tensor_tensor(
                    out=mgst[:, bi, gl, :], in0=identb[:], scalar=dcol[:, g:g + 1], in1=psb[b][:, 0:128],
                    op0=ALU.mult, op1=ALU.add), r=pst(b) + [t_id, P.tile("dcol")], w=[t_mg])
                for d in range(2):
                    j0 = 0 if d == 0 else 7
                    b2 = 4 + (psrot["dn"] % 4)
                    psrot["dn"] += 1
                    P.op("pe", lambda e, b2=b2, bi=bi, d=d, gl=gl, j0=j0: e.transpose(
                        psb[b2][:, 0:64].bitcast(BF16), ww[:, bi, d, gl, j0:j0 + 8, :].rearrange("p j h -> p (j h)"), identb[:]),
                        r=[t_ww, t_id], w=pst(b2))
                    P.op("act", lambda e, b2=b2, bi=bi, d=d, gl=gl: e.activation(
                        out=szst[:, bi, d, gl, :], in_=psb[b2][:, 0:64].bitcast(BF16), func=AF.Copy), r=pst(b2), w=[t_sz])
            P.dma("sp", scr_mg[:, blk * 8:(blk + 1) * 8, :], mgst[:, bi], r=[t_mg], w=[t_scr], stream=s_scr)
            P.dma("sp", scr_sz[:, :, blk * 8:(blk + 1) * 8, :], szst[:, bi], r=[t_sz], w=[t_scr], stream=s_scr)
        tap("arb", arb[:], [128, 2, 2, 16], r=[t_ab])
        tap("aib", aib[:], [128, 2, 2, 16], r=[t_ab])
        tap("ssc", ssc[:], [128, 24, 32], r=[st(i) for i in range(24)])

    if "nossm" not in taps:
        ssm_setup()

    def ffn(idx, wg_d, wu_d, wd_d):
        for half in range(2):
            tbs = [2 * half, 2 * half + 1]
            xn = RA[:, 0:8192].rearrange("p (k t) -> p k t", k=KD)
            h1 = RA[:, 8192:12288].rearrange("p (f t) -> p f t", f=4)
            vname = "ffn%d_%d" % (idx, half)
            for tl, tb in enumerate(tbs):
                xnt = P.rtile("RA", vname, "xn", tl)
                rmsnorm(lambda k, tb=tb: xT[:, k, tb * TB:(tb + 1) * TB],
                        lambda k, tb=tb: [P.tile("xT", k, tb)],
                        lambda k, tl=tl: xn[:, k, tl * TB:(tl + 1) * TB],
                        lambda k, xnt=xnt, vname=vname: [xnt, P.rtile("RAx", vname, "x")], KD, float(D),
                        lambda k: gcols[:, 0 if idx == 1 else 2, k:k + 1])
            for (f0, f1) in FGROUPS:
                nf = f1 - f0
                wg, twg = ring_load([(lambda s, nf=nf: s[:, 0:KD * nf * 128].rearrange("p (k n) -> p k n", k=KD),
                                      wg_d[:, f0 * 128:f1 * 128].rearrange("(k p) n -> p k n", p=128))])
                wu, twu = ring_load([(lambda s, nf=nf: s[:, 0:KD * nf * 128].rearrange("p (k n) -> p k n", k=KD),
                                      wu_d[:, f0 * 128:f1 * 128].rearrange("(k p) n -> p k n", p=128))])
                wd, twd = ring_load([(lambda s, nf=nf: s[:, 0:nf * D].rearrange("p (k n) -> p k n", k=nf),
                                      wd_d[f0 * 128:f1 * 128, :].rearrange("(k p) n -> p k n", p=128))])
                wgv = wg[:, 0:KD * nf * 128].rearrange("p (k n) -> p k n", k=KD)
                wuv = wu[:, 0:KD * nf * 128].rearrange("p (k n) -> p k n", k=KD)
                wdv = wd[:, 0:nf * D].rearrange("p (k n) -> p k n", k=nf)
                for fi in range(nf):
                    for tl, tb in enumerate(tbs):
                        bg = (psrot["gu"] % 2) * 2
                        psrot["gu"] += 1
                        xnt = P.rtile("RA", vname, "xn", tl)
                        for which, (wv, tw, bb) in enumerate([(wgv, twg, bg), (wuv, twu, bg + 1)]):
                            for k in range(KD):
                                P.op("pe", lambda e, wv=wv, k=k, fi=fi, tl=tl, bb=bb: e.matmul(
                                    psb[bb][:], lhsT=wv[:, k, fi * 128:(fi + 1) * 128],
                                    rhs=xn[:, k, tl * TB:(tl + 1) * TB], start=(k == 0), stop=(k == KD - 1)),
                                    r=[tw, xnt], w=pst(bb))
                        ts = tf[1 + (psrot["gu"] % 2)]
                        tts = P.tile("tf", 1 + (psrot["gu"] % 2))
                        P.op("act", lambda e, ts=ts, bg=bg: e.activation(out=ts[:], in_=psb[bg][:], func=AF.Silu),
                             r=pst(bg), w=[tts])
                        h1t = P.rtile("RA", vname, "h1", fi, tl)
                        P.op("dve", lambda e, ts=ts, bg=bg, fi=fi, tl=tl: e.tensor_tensor(
                            out=h1[:, fi, tl * TB:(tl + 1) * TB], in0=ts[:], in1=psb[bg + 1][:], op=ALU.mult),
                            r=[tts] + pst(bg + 1), w=[h1t])
                for m in range(KD):
                    for tl, tb in enumerate(tbs):
                        b = 4 + (psrot["dn"] % 4)
                        psrot["dn"] += 1
                        for fi in range(nf):
                            P.op("pe", lambda e, fi=fi, m=m, tl=tl, b=b, wdv=wdv: e.matmul(
                                psb[b][:], lhsT=wdv[:, fi, m * 128:(m + 1) * 128],
                                rhs=h1[:, fi, tl * TB:(tl + 1) * TB], start=(fi == 0), stop=(fi == nf - 1)),
                                r=[twd, P.rtile("RA", vname, "h1", fi, tl)], w=pst(b))
                        xt = P.tile("xT", m, tb)
                        P.op("dve", lambda e, m=m, tb=tb, b=b: e.scalar_tensor_tensor(
                            out=xT[:, m, tb * TB:(tb + 1) * TB], in0=psb[b][:], scalar=0.5,
                            in1=xT[:, m, tb * TB:(tb + 1) * TB], op0=ALU.mult, op1=ALU.add),
                            r=pst(b) + [xt], w=[xt])

    if stop_after != "load" and "noffn1" not in taps:
        ffn(1, w_d["ffn1_w_gate"], w_d["ffn1_w_up"], w_d["ffn1_w_down"])
    tap("x1", xT[:], [128, KD, T], r=[P.tile("xT", k, tb) for k in range(KD) for tb in range(NTB)])


    def middle():
        hn = RA[:, 0:8192].rearrange("p (b k t) -> p b k t", b=2, k=KD)
        uT = RA[:, 8192:16384].rearrange("p (c t) -> p c t", c=4)
        qT = RC[:, 0:8192].rearrange("p (c t) -> p c t", c=4)
        kT = RC[:, 8192:10240]
        Vt = RC[:, 10240:12288].rearrange("p (b d) -> p b d", b=16)
        Et = RD[:, 0:1536].rearrange("p (i n) -> p i n", i=3)
        Pt = RD[:, 1536:4608].rearrange("p (i n) -> p i n", i=6)
        attnb = RD[:, 4608:6656].rearrange("p (c n) -> p c n", c=4)
        w_in = w_d["w_in"]
        def wq_parts():
            parts = []
            for kv in range(2):
                for c in range(4):
                    parts.append((lambda s, kv=kv, c=c: s.rearrange("p (k c v d) -> p k c v d", k=KD, c=4, v=2)[:, :, c, kv, :],
                                  w_in[:, kv * 256 + c * 64:kv * 256 + (c + 1) * 64].rearrange("(k p) d -> p k d", p=128)))
            return parts
        wq, twq = ring_load(wq_parts())
        wkv, twkv = ring_load([(lambda s: s[:, 0:2048].rearrange("p (k n) -> p k n", k=KD),
                                w_in[:, 512:768].rearrange("(k p) n -> p k n", p=128))])
        wu, twu = ring_load([(lambda s: s.rearrange("p (k n) -> p k n", k=KD),
                              w_in[:, 768:1280].rearrange("(k p) n -> p k n", p=128))])
        wqv = wq.rearrange("p (k n) -> p k n", k=KD)
        wkvv = wkv[:, 0:2048].rearrange("p (k n) -> p k n", k=KD)
        wuv = wu.rearrange("p (k n) -> p k n", k=KD)
        evac = {"n": 0}

        def evacuate(out_ap, in_ap, r, w):
            evac["n"] += 1
            if evac["n"] % 2:
                P.op("act", lambda e: e.activation(out=out_ap, in_=in_ap, func=AF.Copy), r=r, w=w)
            else:
                P.op("dve", lambda e: e.tensor_copy(out=out_ap, in_=in_ap), r=r, w=w)

        def nextbank(pool=(4, 5, 6, 7), key="dn"):
            b = pool[psrot.setdefault(key, 0) % len(pool)]
            psrot[key] += 1
            return b

        for tb in range(NTB):
            hb = tb % 2
            hnt = P.rtile("RA", "mid", "hn", hb)
            rmsnorm(lambda k, tb=tb: xT[:, k, tb * TB:(tb + 1) * TB],
                    lambda k, tb=tb: [P.tile("xT", k, tb)],
                    lambda k, hb=hb: hn[:, hb, k, :], lambda k, hnt=hnt: [hnt], KD, float(D),
                    lambda k: gcols[:, 1, k:k + 1])
            for c in range(4):
                b = nextbank()
                for k in range(KD):
                    P.op("pe", lambda e, b=b, k=k, c=c, hb=hb: e.matmul(
                        psb[b][:], lhsT=wqv[:, k, c * 128:(c + 1) * 128], rhs=hn[:, hb, k, :],
                        start=(k == 0), stop=(k == KD - 1)), r=[twq, hnt], w=pst(b))
                evacuate(qT[:, c, tb * TB:(tb + 1) * TB], psb[b][:], pst(b), [P.rtile("RC", "qkv", "q", tb)])
            b = nextbank()
            for k in range(KD):
                P.op("pe", lambda e, b=b, k=k, hb=hb: e.matmul(
                    psb[b][:], lhsT=wkvv[:, k, 0:128], rhs=hn[:, hb, k, :],
                    start=(k == 0), stop=(k == KD - 1)), r=[twkv, hnt], w=pst(b))
            evacuate(kT[:, tb * TB:(tb + 1) * TB], psb[b][:], pst(b), [P.rtile("RC", "qkv", "k", tb)])
            b = nextbank()
            for sub in range(4):
                for k in range(KD):
                    P.op("pe", lambda e, b=b, k=k, sub=sub, hb=hb: e.matmul(
                        psb[b][:, sub * 128:(sub + 1) * 128], lhsT=hn[:, hb, k, sub * 128:(sub + 1) * 128],
                        rhs=wkvv[:, k, 128:256], start=(k == 0), stop=(k == KD - 1)), r=[twkv, hnt], w=pst(b))
            evacuate(Vt[:, tb * 4:(tb + 1) * 4, :], psb[b][:].rearrange("p (s d) -> p s d", s=4), pst(b),
                     [P.rtile("RC", "qkv", "v", tb)])
            for c in range(4):
                b = nextbank()
                for k in range(KD):
                    P.op("pe", lambda e, b=b, k=k, c=c, hb=hb: e.matmul(
                        psb[b][:], lhsT=wuv[:, k, c * 128:(c + 1) * 128], rhs=hn[:, hb, k, :],
                        start=(k == 0), stop=(k == KD - 1)), r=[twu, hnt], w=pst(b))
                evacuate(uT[:, c, tb * TB:(tb + 1) * TB], psb[b][:], pst(b), [P.rtile("RA", "mid", "u", c, tb), P.rtile("RAx", "mid", "u")])
        tap("qT", qT, [128, 4, T], BF16, r=[P.rtile("RC", "qkv", "q", tb) for tb in range(NTB)])
        tap("kT", kT, [128, T], BF16, r=[P.rtile("RC", "qkv", "k", tb) for tb in range(NTB)])
        tap("Vt", Vt, [128, 16, 128], BF16, r=[P.rtile("RC", "qkv", "v", tb) for tb in range(NTB)])
        tap("uT", uT, [128, 4, T], BF16, r=[P.rtile("RA", "mid", "u", c, tb) for c in range(4) for tb in range(NTB)])

        w_out = w_d["w_out"]
        woa, twoa = ring_load([(lambda s, kv=kv: s.rearrange("p (c n) -> p c n", c=4)[kv * 64:(kv + 1) * 64],
                                w_out[kv * 256:(kv + 1) * 256, :].rearrange("(c d) n -> d c n", d=64)) for kv in range(2)])
        woav = woa.rearrange("p (c n) -> p c n", c=4)

        def attn_block(n):
            tb, nl = n // 4, n % 4
            bn = 4 + 2 * (n % 2)
            bd = bn + 1
            for kv in range(2):
                js = [j for j in (n - 1, n, n + 1) if 0 <= j < 16]
                for ji, j in enumerate(js):
                    dl = j - n + 1
                    bs = (psrot["gu"] % 3)
                    psrot["gu"] += 1
                    ei = psrot["gu"] % 3
                    pi = psrot["gu"] % 6
                    P.op("pe", lambda e, bs=bs, kv=kv, j=j, n=n: e.matmul(
                        psb[bs][:], lhsT=kT[kv * 64:(kv + 1) * 64, j * 128:(j + 1) * 128],
                        rhs=qT[kv * 64:(kv + 1) * 64, :, n * 128:(n + 1) * 128], start=True, stop=True),
                        r=[P.rtile("RC", "qkv", "k", j // 4), P.rtile("RC", "qkv", "q", tb)], w=pst(bs))
                    te = P.rtile("RD", "attn", "E", ei)
                    P.op("act", lambda e, bs=bs, ei=ei: e.activation(out=Et[:, ei, :], in_=psb[bs][:], func=AF.Exp, scale=0.125),
                         r=pst(bs), w=[te])
                    tp = P.rtile("RD", "attn", "P", pi)
                    P.op("pool", lambda e, ei=ei, pi=pi, kv=kv, dl=dl: e.tensor_tensor(
                        out=Pt[:, pi, :], in0=Et[:, ei, :],
                        in1=ebt[:, (kv * 3 + dl) * 4:(kv * 3 + dl) * 4 + 4, :].rearrange("p a q -> p (a q)"), op=ALU.mult),
                        r=[te, t_eb], w=[tp])
                    P.op("pe", lambda e, bn=bn, kv=kv, j=j, pi=pi, ji=ji, js=js: e.matmul(
                        psb[bn][kv * 64:(kv + 1) * 64, :], lhsT=Vt[:, j, kv * 64:(kv + 1) * 64], rhs=Pt[:, pi, :],
                        start=(ji == 0), stop=(ji == len(js) - 1)),
                        r=[tp, P.rtile("RC", "qkv", "v", j // 4)], w=pst(bn))
                    P.op("pe", lambda e, bd=bd, kv=kv, pi=pi, ji=ji, js=js: e.matmul(
                        psb[bd][kv * 64:(kv + 1) * 64, :], lhsT=ones[:, 0:64], rhs=Pt[:, pi, :],
                        start=(ji == 0), stop=(ji == len(js) - 1)), r=[tp, t_consts], w=pst(bd))
            return (n, bn, bd)

        def attn_finish(n, bn, bd):
            tb, nl = n // 4, n % 4
            tt = P.tile("tf", 0)
            P.op("dve", lambda e, bd=bd: e.tensor_tensor(
                out=tf[0][:].rearrange("p (a q) -> p a q", a=4), in0=psb[bd][:].rearrange("p (a q) -> p a q", a=4),
                in1=_ap(pcols[:, 12:16], 0, [[1, 4], [0, 128]]), op=ALU.add), r=pst(bd) + [t_pc], w=[tt])
            P.op("dve", lambda e: e.reciprocal(out=tf[0][:], in_=tf[0][:]), r=[tt], w=[tt])
            ta = P.rtile("RD", "attn", "attnb", nl)
            P.op("dve", lambda e, bn=bn, nl=nl: e.tensor_tensor(
                out=attnb[:, :, nl * 128:(nl + 1) * 128], in0=psb[bn][:].rearrange("p (a q) -> p a q", a=4),
                in1=tf[0][:].rearrange("p (a q) -> p a q", a=4), op=ALU.mult), r=pst(bn) + [tt], w=[ta])

        def attn_tb_out(tb):
            tas = [P.rtile("RD", "attn", "attnb", nl) for nl in range(4)]
            tap("attn%d" % tb, attnb, [128, 4, 512], BF16, r=tas)
            rmsnorm(lambda c: attnb[:, c, :], lambda c: tas, lambda c: attnb[:, c, :], lambda c: tas, 4, 512.0,
                    lambda c: pcols[:, c:c + 1], bank=3)
            for m in range(KD):
                b = 3
                for c in range(4):
                    P.op("pe", lambda e, b=b, c=c, m=m: e.matmul(
                        psb[b][:], lhsT=woav[:, c, m * 128:(m + 1) * 128], rhs=attnb[:, c, :],
                        start=(c == 0), stop=(c == 3)), r=[twoa] + tas, w=pst(b))
                xt = P.tile("xT", m, tb)
                P.op("dve", lambda e, b=b, m=m, tb=tb: e.tensor_tensor(
                    out=xT[:, m, tb * TB:(tb + 1) * TB], in0=psb[b][:], in1=xT[:, m, tb * TB:(tb + 1) * TB], op=ALU.add),
                    r=pst(b) + [xt], w=[xt])

        U8 = RB[:, 0:8192].rearrange("p (g c) -> p g c", g=32)
        ZX = RA[:, 0:16384].rearrange("p (d r g c) -> p d r g c", d=2, r=2, g=16)
        hb_state = {"n": 0}

        def halfbank(pool=(4, 5, 6, 7)):
            i = hb_state["n"]
            hb_state["n"] += 1
            return pool[(i // 2) % len(pool)], i % 2

        def u8t(g):
            return P.rtile("RB", "u8", g)

        do_ssm = "nossm" not in taps
        if do_ssm:
            for g in range(32):
                blk, gl = g // 8, g % 8
                b, h = halfbank()
                for s_ in range(8):
                    P.op("pe", lambda e, b=b, h=h, gl=gl, s_=s_, blk=blk: e.matmul(
                        psb[b][:, h * 256:(h + 1) * 256], lhsT=rsel[:, gl, (7 - s_) * 16:(15 - s_) * 16],
                        rhs=_ap(uT[:, blk, :], s_, [[8, 256]]), start=(s_ == 0), stop=(s_ == 7)),
                        r=[t_rsel] + [P.rtile("RA", "mid", "u", blk, tb) for tb in range(NTB)], w=psth(b, h))
                evacuate(U8[:, g, :], psb[b][:, h * 256:(h + 1) * 256], psth(b, h), [u8t(g)])
            tap("U8", U8, [128, 32, 256], BF16, r=[u8t(g) for g in range(32)])
            zxall = [P.rtile("RAx", "zx", "all")]
            def zxc(d, c):
                return P.rtile("RA", "zx", d, c)
            for d in range(2):
                szs, tsz = ring_load([(lambda s: s.rearrange("p (g n) -> p g n", g=32), scr_sz[:, d])], q="sp", r=[t_scr])
                szv = szs.rearrange("p (g n) -> p g n", g=32)
                for gp in range(16):
                    for ri in range(2):
                        b, h = halfbank()
                        for par in range(2):
                            g = 16 * par + gp
                            P.op("pe", lambda e, b=b, h=h, par=par, g=g, ri=ri, szv=szv: e.matmul(
                                psb[b][par * 64:(par + 1) * 64, h * 256:(h + 1) * 256],
                                lhsT=szv[:, g, ri * 64:(ri + 1) * 64], rhs=U8[:, g, :], start=True, stop=True),
                                r=[tsz, u8t(g)], w=psth(b, h))
                        P.op("act", lambda e, b=b, h=h, d=d, ri=ri, gp=gp: e.activation(
                            out=ZX[:, d, ri, gp, :], in_=psb[b][:, h * 256:(h + 1) * 256], func=AF.Copy),
                            r=psth(b, h), w=[zxc(d, c) for c in range(NCH)] + zxall)
            tap("Z", ZX, [128, 2, 2, 16, 256], BF16, r=[zxc(d, c) for d in range(2) for c in range(NCH)])
            for d in range(2):
                P.op("pool", lambda e, d=d: e.memset(sring[:, d, 7, :], 0.0), w=[P.tile("sr", d, 7)])
        st1 = ssc[:, 0:2, :]
        st2 = ssc[:, 2:4, :]

        def scan_step(step):
            for d in range(2):
                c = step if d == 0 else NCH - 1 - step
                ip, inw = (step - 1) % 8, step % 8
                tp_, tn_ = P.tile("sr", d, ip), P.tile("sr", d, inw)
                v3 = lambda ap: ap.rearrange("p (r g) -> p r g", r=2)
                t1, t2 = P.tile("ssc", d), P.tile("ssc", 2 + d)
                P.op("dve", lambda e, d=d, ip=ip: e.tensor_tensor(
                    out=v3(st1[:, d, :]), in0=v3(sring[:, d, ip, :]), in1=arb[:, d], op=ALU.mult), r=[tp_, t_ab], w=[t1])
                P.op("dve", lambda e, d=d, ip=ip: e.tensor_tensor(
                    out=v3(st2[:, d, :]), in0=_ap(sring[:, d, ip, :], 16, [[-16, 2], [1, 16]]), in1=aib[:, d], op=ALU.mult),
                    r=[tp_, t_ab], w=[t2])
                P.op("dve", lambda e, d=d: e.tensor_tensor(out=st1[:, d, :], in0=st1[:, d, :], in1=st2[:, d, :], op=ALU.add),
                     r=[t1, t2], w=[t1])
                P.op("dve", lambda e, d=d, inw=inw, c=c: e.tensor_tensor(
                    out=v3(sring[:, d, inw, :]), in0=v3(st1[:, d, :]), in1=ZX[:, d, :, :, c], op=ALU.add),
                    r=[t1, zxc(d, c)], w=[tn_])
                P.op("act", lambda e, d=d, inw=inw, c=c: e.activation(
                    out=ZX[:, d, :, :, c], in_=v3(sring[:, d, inw, :]), func=AF.Copy), r=[tn_], w=[zxc(d, c)])

        pend = []
        step = 0
        for n in range(16):
            pend.append(attn_block(n))
            if len(pend) > 1:
                attn_finish(*pend.pop(0))
            if n % 4 == 0 and n > 0:
                attn_tb_out(n // 4 - 1)
            if do_ssm:
                for _ in range(16):
                    scan_step(step)
                    step += 1
        attn_finish(*pend.pop(0))
        attn_tb_out(3)
        if do_ssm:
            tap("X", ZX, [128, 2, 2, 16, 256], BF16, r=[zxc(d, c) for d in range(2) for c in range(NCH)])
            ygT = RC[:, 0:8192].rearrange("p (c t) -> p c t", c=4)
            Yact = RC[:, 8192:12288].rearrange("p (b g c) -> p b g c", b=2, g=8)
            mgs, tmg = ring_load([(lambda s: s.rearrange("p (g n) -> p g n", g=32), scr_mg)], q="sp", r=[t_scr])
            mgv = mgs.rearrange("p (g n) -> p g n", g=32)
            sov, tso = [], []
            for d in range(2):
                sl, tt_ = ring_load([(lambda s: s.rearrange("p (r g n) -> p r g n", r=2, g=16), scr_so[:, d])], q="sp", r=[t_scr])
                sov.append(sl.rearrange("p (r g n) -> p r g n", r=2, g=16))
                tso.append(tt_)
            zr = [[zxc(d, c) for c in range(NCH)] for d in range(2)]
            for blk in range(4):
                yb = blk % 2
                tya = P.rtile("RC", "back", "yact", yb)
                for gl in range(8):
                    g = blk * 8 + gl
                    par, gp = g // 16, g % 16
                    b, h = halfbank()
                    c0 = h * 256
                    pr = slice(par * 64, (par + 1) * 64)
                    P.op("pe", lambda e, b=b, c0=c0, g=g: e.matmul(psb[b][:, c0:c0 + 256], lhsT=mgv[:, g, :], rhs=U8[:, g, :],
                                                                   start=True, stop=False), r=[tmg, u8t(g)], w=psth(b, h))
                    for ri in range(2):
                        P.op("pe", lambda e, b=b, c0=c0, pr=pr, ri=ri, gp=gp: e.matmul(
                            psb[b][:, c0 + 1:c0 + 256], lhsT=sov[0][pr, ri, gp, :], rhs=ZX[pr, 0, ri, gp, 0:255],
                            start=False, stop=False), r=[tso[0]] + zr[0], w=psth(b, h))
                    for ri in range(2):
                        P.op("pe", lambda e, b=b, c0=c0, pr=pr, ri=ri, gp=gp: e.matmul(
                            psb[b][:, c0:c0 + 255], lhsT=sov[1][pr, ri, gp, :], rhs=ZX[pr, 1, ri, gp, 1:256],
                            start=False, stop=(ri == 1)), r=[tso[1]] + zr[1], w=psth(b, h))
                    if "ypre" in taps and g in (0, 5, 17, 31):
                        P.op("act", lambda e, b=b, c0=c0, g=g: e.activation(out=tf[1][:, 0:256], in_=psb[b][:, c0:c0 + 256], func=AF.Copy),
                             r=psth(b, h), w=[P.tile("tf", 1)])
                        taps.append("ypre%d" % g)
                        tap("ypre%d" % g, tf[1][:, 0:256], [128, 256], r=[P.tile("tf", 1)])
                    P.op("act", lambda e, b=b, c0=c0, yb=yb, gl=gl: e.activation(
                        out=Yact[:, yb, gl, :], in_=psb[b][:, c0:c0 + 256], func=AF.Gelu_apprx_tanh), r=psth(b, h), w=[tya])
                for t in range(8):
                    b, h = halfbank()
                    c0 = h * 256
                    for gl in range(8):
                        P.op("pe", lambda e, b=b, c0=c0, t=t, gl=gl, yb=yb: e.matmul(
                            psb[b][:, c0:c0 + 256], lhsT=rsel[:, t, (7 - gl) * 16:(15 - gl) * 16], rhs=Yact[:, yb, gl, :],
                            start=(gl == 0), stop=(gl == 7)), r=[t_rsel, tya], w=psth(b, h))
                    evacuate(_ap(ygT[:, blk, :], t, [[8, 256]]), psb[b][:, c0:c0 + 256], psth(b, h), [P.rtile("RC", "back", "yg", blk)])
            tyg = [P.rtile("RC", "back", "yg", blk) for blk in range(4)]
            tap("ygT", ygT, [128, 4, T], BF16, r=tyg)
            glus, tglu = ring_load([(lambda s: s[:, 0:2048].rearrange("p (k n) -> p k n", k=4),
                                     w_d["ssm_glu_w"].rearrange("(k p) n -> p k n", p=128))])
            gluv = glus[:, 0:2048].rearrange("p (k n) -> p k n", k=4)
            wos, twos = ring_load([(lambda s: s.rearrange("p (c n) -> p c n", c=4),
                                    w_out[512:1024, :].rearrange("(c p) n -> p c n", p=128))])
            wosv = wos.rearrange("p (c n) -> p c n", c=4)
            tas = [P.rtile("RD", "attn", "attnb", nl) for nl in range(4)]
            for tb in range(NTB):
                for m in range(4):
                    b = nextbank()
                    for k in range(4):
                        P.op("pe", lambda e, b=b, k=k, m=m, tb=tb: e.matmul(
                            psb[b][:], lhsT=gluv[:, k, m * 128:(m + 1) * 128], rhs=ygT[:, k, tb * TB:(tb + 1) * TB],
                            start=(k == 0), stop=(k == 3)), r=[tglu] + tyg, w=pst(b))
                    tt1 = P.tile("tf", 1)
                    P.op("act", lambda e, b=b, m=m: e.activation(out=tf[1][:], in_=psb[b][:], func=AF.Sigmoid,
                                                                bias=pcols[:, 8 + m:9 + m]), r=pst(b) + [t_pc], w=[tt1])
                    P.op("dve", lambda e, m=m, tb=tb: e.tensor_tensor(
                        out=attnb[:, m, :], in0=ygT[:, m, tb * TB:(tb + 1) * TB], in1=tf[1][:], op=ALU.mult),
                        r=[tt1] + tyg, w=tas)
                tap("ssm%d" % tb, attnb, [128, 4, 512], BF16, r=tas)
                rmsnorm(lambda c: attnb[:, c, :], lambda c: tas, lambda c: attnb[:, c, :], lambda c: tas, 4, 512.0,
                        lambda c: pcols[:, 4 + c:5 + c])
                for m in range(KD):
                    b = nextbank()
                    for c in range(4):
                        P.op("pe", lambda e, b=b, c=c, m=m: e.matmul(
                            psb[b][:], lhsT=wosv[:, c, m * 128:(m + 1) * 128], rhs=attnb[:, c, :],
                            start=(c == 0), stop=(c == 3)), r=[twos] + tas, w=pst(b))
                    xt = P.tile("xT", m, tb)
                    P.op("dve", lambda e, b=b, m=m, tb=tb: e.tensor_tensor(
                        out=xT[:, m, tb * TB:(tb + 1) * TB], in0=psb[b][:], in1=xT[:, m, tb * TB:(tb + 1) * TB], op=ALU.add),
                        r=pst(b) + [xt], w=[xt])
        tap("x2", xT[:], [128, KD, T], r=[P.tile("xT", k, tb) for k in range(KD) for tb in range(NTB)])

    if stop_after not in ("ffn1", "load"):
        middle()

    if stop_after not in ("ffn1", "load", "attn"):
        ffn(2, w_d["ffn2_w_gate"], w_d["ffn2_w_up"], w_d["ffn2_w_down"])

    for tb in range(NTB):
        rmsnorm(lambda k, tb=tb: xT[:, k, tb * TB:(tb + 1) * TB],
                lambda k, tb=tb: [P.tile("xT", k, tb)],
                lambda k, tb=tb: xT[:, k, tb * TB:(tb + 1) * TB],
                lambda k, tb=tb: [P.tile("xT", k, tb)], KD, float(D),
                lambda k: gcols[:, 3, k:k + 1])
        for k in range(KD):
            P.dma("sp", outT_d[k * 128:(k + 1) * 128, tb * TB:(tb + 1) * TB], xT[:, k, tb * TB:(tb + 1) * TB],
                  r=[P.tile("xT", k, tb)], stream=s_out)

    P.emit(es, [s_out] + tap_streams)
    es.close()
    return nc, list(tap_out.keys())


def _alibi_eb():
    slopes = 2.0 ** (-8.0 * (np.arange(8) + 1) / 8.0)
    sp = np.arange(128)[:, None]
    tq = np.arange(128)[None, :]
    eb = np.zeros((128, 2, 3, 4, 128), np.float64)
    for kv in range(2):
        for dl in range(3):
            rel = 128 * (dl - 1) + sp - tq
            for hh in range(4):
                v = np.exp(-slopes[kv * 4 + hh] * np.abs(rel))
                eb[:, kv, dl, hh, :] = np.where(np.abs(rel) <= 128, v, 0.0)
    return eb.reshape(128, 24 * 128).astype(np.float32)


_PROG_CACHE = {}


def _in_maps(inputs, taps=(), stop_after=None, cores=NCORES):
    x = np.asarray(inputs["x"], np.float32)
    shared = {}
    for nm in ["ffn1_w_gate", "ffn1_w_up", "ffn1_w_down", "ffn2_w_gate", "ffn2_w_up", "ffn2_w_down",
               "w_in", "w_out", "ssm_glu_w", "norm_ffn1", "norm_mix", "norm_ffn2", "attn_out_norm",
               "ssm_out_norm", "ssm_glu_b", "attn_sinks", "ssm_lambda_re", "ssm_lambda_im", "ssm_log_dt",
               "ssm_b_re", "ssm_b_im", "ssm_c_re", "ssm_c_im", "ssm_d"]:
        a = np.asarray(inputs[nm], np.float32)
        shared[nm] = np.ascontiguousarray(a.reshape(a.shape[1:]))
    shared["final_norm"] = np.ascontiguousarray(np.asarray(inputs["final_norm"], np.float32))
    shared["c_eb"] = _alibi_eb()
    maps = []
    for c in range(cores):
        m = dict(shared)
        m["xT"] = np.ascontiguousarray(x[c].T)
        maps.append(m)
    return maps


def kernel(**inputs):
    key = "main"
    if key not in _PROG_CACHE:
        _PROG_CACHE[key] = build_program()
    nc, _ = _PROG_CACHE[key]
    maps = _in_maps(inputs)
    res = run_bass_kernel_spmd(nc, maps, core_ids=list(range(NCORES)))
    out = np.stack([np.ascontiguousarray(r["outT"].T) for r in res.results], axis=0)
    return out.astype(np.float32)
```

```python
import numpy as np
from contextlib import ExitStack
import concourse.bass as bass
import concourse.mybir as mybir
from concourse.bass_utils import run_bass_kernel_spmd

F32 = mybir.dt.float32
BF16 = mybir.dt.bfloat16
I32 = mybir.dt.int32
AF = mybir.ActivationFunctionType
ALU = mybir.AluOpType

NCORES = 8
T = 2048
TB = 512
NTB = 4
D = 1024
KD = 8
FF = 2816
NF = 22
EPS = 1e-6
FGROUPS = [(0, 4), (4, 8), (8, 12), (12, 16), (16, 20), (20, 22)]
NCH = 256


class _Tile:
    __slots__ = ("name", "last_w", "rd_eng", "rd_dma")

    def __init__(self, name):
        self.name = name
        self.last_w = None
        self.rd_eng = {}
        self.rd_dma = []


class _Stream:
    def __init__(self, name, final_only=False):
        self.name = name
        self.final_only = final_only
        self.count = 0
        self.sem = None


class _Op:
    __slots__ = ("eng", "fn", "deps", "signal", "sigval", "stream", "sidx", "is_dma")

    def __init__(self, eng, fn, stream=None):
        self.eng = eng
        self.fn = fn
        self.deps = []
        self.signal = False
        self.sigval = 0
        self.stream = stream
        self.sidx = 0
        self.is_dma = stream is not None


class Prog:
    ENGS = ("pe", "act", "dve", "pool", "sp")

    def __init__(self, nc):
        self.nc = nc
        self.ops = {e: [] for e in self.ENGS}
        self.tiles = {}
        self.streams = []
        self.regions = {}

    def tile(self, *key):
        t = self.tiles.get(key)
        if t is None:
            t = _Tile(key)
            self.tiles[key] = t
        return t

    def rtile(self, region, view, *key):
        reg = self.regions.setdefault(region, {"view": None, "tiles": {}, "carry": ({}, [], [])})
        if reg["view"] != view:
            eng_last = dict(reg["carry"][0])
            dmas = list(reg["carry"][1])
            writers = list(reg["carry"][2])
            for t in reg["tiles"].values():
                if t.last_w is not None:
                    writers.append(t.last_w)
                for e, o in t.rd_eng.items():
                    if e not in eng_last or eng_last[e].sidx < o.sidx:
                        eng_last[e] = o
                dmas.extend(t.rd_dma)
            reg["view"] = view
            reg["tiles"] = {}
            reg["carry"] = (eng_last, dmas, writers)
        t = reg["tiles"].get(key)
        if t is None:
            t = _Tile((region, view) + key)
            eng_last, dmas, writers = reg["carry"]
            t.rd_eng = dict(eng_last)
            t.rd_dma = list(dmas) + list(writers)
            reg["tiles"][key] = t
        return t

    def stream(self, name, final_only=False):
        s = _Stream(name, final_only)
        self.streams.append(s)
        return s

    def _add(self, o, r, w):
        raw = set()
        deps = set()
        for t in r:
            if t.last_w is not None:
                raw.add(t.last_w)
        for t in w:
            if t.last_w is not None:
                deps.add(t.last_w)
            deps.update(t.rd_eng.values())
            deps.update(t.rd_dma)
        o.sidx = len(self.ops[o.eng])
        for t in r:
            if o.is_dma:
                t.rd_dma.append(o)
            else:
                t.rd_eng[o.eng] = o
        for t in w:
            t.last_w = o
            t.rd_eng = {}
            t.rd_dma = []
        raw.discard(o)
        deps.discard(o)
        for d in raw:
            o.deps.append(d)
        for d in deps:
            if d in raw:
                continue
            if d.is_dma and o.is_dma and d.stream is o.stream and d.stream.final_only:
                continue
            if d.is_dma or o.is_dma or d.eng != o.eng or o.eng != "pe":
                o.deps.append(d)
        self.ops[o.eng].append(o)
        return o

    def op(self, eng, fn, r=(), w=()):
        return self._add(_Op(eng, fn), list(r), list(w))

    def dma(self, q, out, in_, r=(), w=(), stream=None, **kw):
        if stream is None:
            stream = self.stream("anon")
        o = _Op(q, lambda e, out=out, in_=in_, kw=kw: e.dma_start(out=out, in_=in_, **kw), stream)
        stream.count += 1
        self._add(o, list(r), list(w))
        o.sigval = 16 * stream.count
        return o

    def emit(self, es: ExitStack, out_streams):
        nc = self.nc
        for e in self.ENGS:
            for o in self.ops[e]:
                for d in o.deps:
                    if not d.is_dma:
                        d.signal = True
        for e in self.ENGS:
            c = 0
            for o in self.ops[e]:
                if not o.is_dma and o.signal:
                    c += 1
                    o.sigval = c
        esem = {e: es.enter_context(nc.semaphore("sem_" + e)) for e in self.ENGS}
        for i, s in enumerate(self.streams):
            if s.count > 0:
                s.sem = es.enter_context(nc.semaphore("ds%d_%s" % (i, s.name)))
        block = es.enter_context(nc.Block())
        handles = {"pe": block.tensor, "act": block.scalar, "dve": block.vector,
                   "pool": block.gpsimd, "sp": block.sync}

        def run(engname, e):
            known = {}
            for o in self.ops[engname]:
                waits = {}
                for d in o.deps:
                    if d.is_dma:
                        sem = d.stream.sem
                        val = 16 * d.stream.count if d.stream.final_only else d.sigval
                    else:
                        sem = esem[d.eng]
                        val = d.sigval
                    k = id(sem)
                    if k not in waits or waits[k][1] < val:
                        waits[k] = (sem, val)
                for k, (sem, val) in waits.items():
                    if known.get(k, 0) < val:
                        e.wait_ge(sem, val)
                        known[k] = val
                ins = o.fn(e)
                if o.is_dma:
                    ins.then_inc(o.stream.sem, 16)
                elif o.signal:
                    ins.then_inc(esem[engname], 1)
            if engname == "sp":
                for s in out_streams:
                    if s.count > 0:
                        e.wait_ge(s.sem, 16 * s.count)

        for engname in self.ENGS:
            def _f(e, engname=engname):
                run(engname, e)
            handles[engname](_f)


def _ap(base, offset_elems, dims):
    return bass.AP(base.tensor, base.offset + offset_elems, [list(base.ap[0])] + [list(d) for d in dims])


def build_program(taps=(), stop_after=None):
    nc = bass.Bass("TRN2", target_bir_lowering=False)
    es = ExitStack()
    P = Prog(nc)
    taps = list(taps)
    tap_out = {}

    def din(name, shape, dt=F32):
        return nc.dram_tensor(name, list(shape), dt, kind="ExternalInput").ap()

    xT_d = din("xT", [D, T])
    outT_d = nc.dram_tensor("outT", [D, T], F32, kind="ExternalOutput").ap()
    w_d = {}
    for nm, shp in [("ffn1_w_gate", [D, FF]), ("ffn1_w_up", [D, FF]), ("ffn1_w_down", [FF, D]),
                    ("ffn2_w_gate", [D, FF]), ("ffn2_w_up", [D, FF]), ("ffn2_w_down", [FF, D]),
                    ("w_in", [D, 1280]), ("w_out", [D, D]), ("ssm_glu_w", [512, 512])]:
        w_d[nm] = din(nm, shp)
    p_d = {}
    for nm, shp in [("norm_ffn1", [D]), ("norm_mix", [D]), ("norm_ffn2", [D]), ("final_norm", [D]),
                    ("attn_out_norm", [512]), ("ssm_out_norm", [512]), ("ssm_glu_b", [512]),
                    ("attn_sinks", [8]), ("ssm_lambda_re", [2, 32, 64]), ("ssm_lambda_im", [2, 32, 64]),
                    ("ssm_log_dt", [2, 32]), ("ssm_b_re", [2, 32, 64, 16]), ("ssm_b_im", [2, 32, 64, 16]),
                    ("ssm_c_re", [2, 32, 16, 64]), ("ssm_c_im", [2, 32, 16, 64]), ("ssm_d", [32, 16]),
                    ("c_eb", [128, 24 * 128])]:
        p_d[nm] = din(nm, shp)

    def sb(name, shape, dt):
        return es.enter_context(nc.sbuf_tensor(name, list(shape), dt))

    xT = sb("xT_sb", [128, KD, T], F32)
    ring = sb("ring", [128, 4, 4096], BF16)
    RA = sb("RA", [128, 16384], BF16)
    RB = sb("RB", [128, 8192], BF16)
    RC = sb("RC", [128, 12288], BF16)
    RD = sb("RD", [128, 7168], BF16)
    gcols = sb("gcols", [128, 4, KD], F32)
    ones = sb("ones", [128, 128], BF16)
    tf = [sb("tf%d" % i, [128, TB], F32) for i in range(3)]
    tb16 = [sb("tb16_%d" % i, [128, TB], BF16) for i in range(2)]
    psb = [es.enter_context(nc.psum_tensor("ps%d" % b, [128, 512], F32)) for b in range(8)]

    def pst(b):
        return [P.tile("ps", b, 0), P.tile("ps", b, 1)]

    def psth(b, h):
        return [P.tile("ps", b, 0), P.tile("ps", b, 1)]

    s_small = P.stream("small", final_only=True)
    s_out = P.stream("out")
    tap_streams = []

    def tap(name, ap, shape, dt=F32, r=()):
        if name not in taps:
            return
        d = nc.dram_tensor("tap_" + name, list(shape), dt, kind="ExternalOutput").ap()
        tap_out[name] = d
        ts_ = P.stream("tap_" + name)
        tap_streams.append(ts_)
        P.dma("sp", d, ap, r=list(r), stream=ts_)

    t_consts = P.tile("consts")
    P.op("pool", lambda e: e.memset(ones[:], 1.0), w=[t_consts])
    nc_allow = es.enter_context(nc.allow_non_contiguous_dma(reason="tiny parameter loads"))
    for i, nm in enumerate(["norm_ffn1", "norm_mix", "norm_ffn2", "final_norm"]):
        P.dma("sp", gcols[:, i, :], p_d[nm].rearrange("(k p) -> p k", p=128), w=[t_consts], stream=s_small)

    xs = [P.stream("x%d" % k) for k in range(KD)]
    for k in range(KD):
        P.dma("sp", xT[:, k, :], xT_d[k * 128:(k + 1) * 128, :],
              w=[P.tile("xT", k, tb) for tb in range(NTB)], stream=xs[k])

    ring_state = {"n": 0}
    rstreams = {q: [P.stream("ring%s%d" % (q, i)) for i in range(4)] for q in ("pool", "sp")}

    def ring_load(parts, q="pool", r=()):
        s = ring_state["n"] % 4
        ring_state["n"] += 1
        t = P.tile("ring", s)
        slot = ring[:, s, :]
        for dst_fn, src in parts:
            P.dma(q, dst_fn(slot), src, r=list(r), w=[t], stream=rstreams[q][s])
        return slot, t

    psrot = {"gu": 0, "dn": 0}

    def rmsnorm(src_fn, src_tiles_fn, dst_fn, dst_tiles_fn, nchunk, width, gcol_fn, ones_ap=None, bank=None, pre_dve=None):
        if bank is None:
            b = 4 + (psrot["dn"] % 4)
            psrot["dn"] += 1
        else:
            b = bank
        ps = psb[b]
        oa = ones[:] if ones_ap is None else ones_ap
        for k in range(nchunk):
            sq = tb16[k % 2]
            tq = P.tile("tb16", k % 2)
            P.op("act", lambda e, sq=sq, k=k: e.activation(out=sq[:], in_=src_fn(k), func=AF.Square),
                 r=src_tiles_fn(k), w=[tq])
            P.op("pe", lambda e, sq=sq, k=k, ps=ps: e.matmul(ps[:], lhsT=oa, rhs=sq[:],
                                                             start=(k == 0), stop=(k == nchunk - 1)),
                 r=[tq, t_consts], w=pst(b))
        ttf = P.tile("tf", 0)
        P.op("act", lambda e, ps=ps: e.activation(out=tf[0][:], in_=ps[:], func=AF.Sqrt,
                                                  scale=1.0 / width, bias=epsc[:, 0:1]),
             r=pst(b) + [t_consts], w=[ttf])
        if pre_dve is not None:
            pre_dve()
        P.op("dve", lambda e, ps=ps: e.reciprocal(out=ps[:], in_=tf[0][:]), r=[ttf], w=pst(b))
        for k in range(nchunk):
            if pre_dve is not None:
                pre_dve()
            P.op("dve", lambda e, k=k, ps=ps: e.scalar_tensor_tensor(
                out=dst_fn(k), in0=src_fn(k), scalar=gcol_fn(k), in1=ps[:], op0=ALU.mult, op1=ALU.mult),
                r=src_tiles_fn(k) + pst(b) + [t_consts], w=dst_tiles_fn(k))

    epsc = sb("epsc", [128, 1], F32)
    P.op("pool", lambda e: e.memset(epsc[:], EPS), w=[t_consts])

    ebt = sb("ebt", [128, 24, 128], BF16)
    pcols = sb("pcols", [128, 16], F32)
    s_eb = P.stream("eb")
    t_eb = P.tile("ebt")
    P.dma("pool", ebt[:], p_d["c_eb"].rearrange("p (a q) -> p a q", a=24), w=[t_eb], stream=s_eb, max_dma_last_dim=4096)
    t_pc = P.tile("pcols")
    for kv in range(2):
        P.dma("sp", pcols[kv * 64:(kv + 1) * 64, 0:4],
              p_d["attn_out_norm"][kv * 256:(kv + 1) * 256].rearrange("(c d) -> d c", d=64), w=[t_pc], stream=s_small)
        sk = p_d["attn_sinks"]
        P.dma("sp", pcols[kv * 64:(kv + 1) * 64, 12:16],
              bass.AP(sk.tensor, sk.offset + 4 * kv, [[0, 64], [1, 4]]), w=[t_pc], stream=s_small)
    P.dma("sp", pcols[:, 4:8], p_d["ssm_out_norm"].rearrange("(c p) -> p c", p=128), w=[t_pc], stream=s_small)
    P.dma("sp", pcols[:, 8:12], p_d["ssm_glu_b"].rearrange("(c p) -> p c", p=128), w=[t_pc], stream=s_small)
    P.op("act", lambda e: e.activation(out=pcols[:, 12:16], in_=pcols[:, 12:16], func=AF.Exp), r=[t_pc], w=[t_pc])

    ssc = sb("ssc", [128, 24, 32], F32)
    ssci = sb("ssci", [128, 2, 32], I32)
    identb = sb("identb", [128, 128], BF16)
    identf = sb("identf", [128, 128], F32)
    rsel = sb("rsel", [128, 8, 240], BF16)
    dcol = sb("dcol", [128, 32], F32)
    amat = sb("amat", [128, 2, 2, 16, 2], F32)
    sring = sb("sring", [128, 8, 64], F32)
    stt = sb("stt", [128, 2, 128], F32)
    scr_mg = nc.dram_tensor("scr_mg", [128, 32, 128], BF16, kind="Internal").ap()
    scr_sz = nc.dram_tensor("scr_sz", [128, 2, 32, 128], BF16, kind="Internal").ap()
    scr_so = nc.dram_tensor("scr_so", [128, 2, 2, 16, 128], BF16, kind="Internal").ap()
    t_id = P.tile("ident")
    for idt in (identb, identf):
        P.op("pool", lambda e, idt=idt: e.memset(idt[:], 0.0), w=[t_id])
        P.op("pool", lambda e, idt=idt: e.affine_select(out=idt[:], in_=idt[:], pattern=[[-1, 128]],
                                                       compare_op=ALU.not_equal, fill=1.0, base=0, channel_multiplier=1),
             r=[t_id], w=[t_id])
    t_rsel = P.tile("rsel")
    P.op("pool", lambda e: e.memset(rsel[:], 0.0), w=[t_rsel])
    for a0 in range(8):
        P.op("pool", lambda e, a0=a0: e.tensor_copy(out=rsel[:, a0, 112:128], in_=identb[:, a0 * 16:(a0 + 1) * 16]),
             r=[t_id], w=[t_rsel])
    t_ab = P.tile("arb")
    s_scr = P.stream("scr")
    t_scr = P.tile("scr")

    def ssm_setup():
        VW = "setup"
        def sc(i):
            return ssc[:, i, :]
        def st(i):
            return P.tile("ssc", i)
        def vtt(o, a, b, op):
            P.op("dve", lambda e: e.tensor_tensor(out=sc(o), in0=sc(a), in1=sc(b), op=op), r=[st(a), st(b)], w=[st(o)])
        def vts(o, a, s1, s2, op0, op1=None):
            if op1 is None:
                P.op("dve", lambda e: e.tensor_scalar(out=sc(o), in0=sc(a), scalar1=s1, scalar2=None, op0=op0), r=[st(a)], w=[st(o)])
            else:
                P.op("dve", lambda e: e.tensor_scalar(out=sc(o), in0=sc(a), scalar1=s1, scalar2=s2, op0=op0, op1=op1), r=[st(a)], w=[st(o)])
        def vstt(o, a, scal, b, op0, op1):
            P.op("dve", lambda e: e.scalar_tensor_tensor(out=sc(o), in0=sc(a), scalar=scal, in1=sc(b), op0=op0, op1=op1),
                 r=[st(a), st(b)], w=[st(o)])
        LR, LI, LDT, lr, dt, z, th, mag, sn, cs, Are, Aim = range(12)
        t0, t1, t2, t3, t4, t5, cre, cim, p2r, p2i, t6, t7 = range(12, 24)
        for par in range(2):
            for slot, nm in ((LR, "ssm_lambda_re"), (LI, "ssm_lambda_im")):
                src = p_d[nm]
                for d in range(2):
                    P.dma("sp", ssc[par * 64:(par + 1) * 64, slot, d * 16:(d + 1) * 16],
                          bass.AP(src.tensor, src.offset + d * 2048 + par * 16 * 64, [[1, 64], [64, 16]]),
                          w=[st(slot)], stream=s_small)
            src = p_d["ssm_log_dt"]
            P.dma("sp", ssc[par * 64:(par + 1) * 64, LDT, :].rearrange("p (d g) -> p d g", d=2),
                  bass.AP(src.tensor, src.offset + par * 16, [[0, 64], [32, 2], [1, 16]]), w=[st(LDT)], stream=s_small)
        for s_ in range(8):
            src = p_d["ssm_d"]
            P.dma("sp", dcol[s_ * 16:(s_ + 1) * 16, :], bass.AP(src.tensor, src.offset, [[1, 16], [16, 32]]),
                  w=[P.tile("dcol")], stream=s_small)
        big = RC[:, 0:12288].bitcast(F32).rearrange("p (j n) -> p j n", j=12)
        def bg(j):
            return big[:, j, :]
        def bgt(j):
            return P.rtile("RC", VW, "big", j)
        BRE, BIM, CRE, CIM, XR, XI, YR, YI, T0, T1, T2, T3 = range(12)
        for par in range(2):
            for d in range(2):
                for j, nm in ((BRE, "ssm_b_re"), (BIM, "ssm_b_im")):
                    src = p_d[nm]
                    P.dma("sp", big[par * 64:(par + 1) * 64, j, d * 256:(d + 1) * 256].rearrange("p (g h) -> p g h", g=16),
                          bass.AP(src.tensor, src.offset + d * 32768 + par * 16 * 1024, [[16, 64], [1024, 16], [1, 16]]),
                          w=[bgt(j)], stream=s_small)
        cnat = RA[:, 12288:14336].bitcast(F32).rearrange("p (r d b q c) -> p r d b q c", r=2, d=2, b=2, q=2)
        t_cnat = P.rtile("RAx", VW, "cnat")
        for ri, nm in enumerate(("ssm_c_re", "ssm_c_im")):
            src = p_d[nm]
            for d in range(2):
                for blk in range(2):
                    P.dma("sp", cnat[:, ri, d, blk, :, :],
                          bass.AP(src.tensor, src.offset + d * 32768 + blk * 8 * 1024, [[64, 128], [16384, 2], [1, 64]]),
                          w=[t_cnat], stream=s_small)
        vts(lr, LR, -1e-4, None, ALU.min)
        vts(t0, LDT, 1.4426950408889634, None, ALU.mult)
        P.op("dve", lambda e: e.tensor_copy(out=ssci[:, 0, :], in_=sc(t0)), r=[st(t0)], w=[P.tile("ssci", 0)])
        P.op("dve", lambda e: e.tensor_copy(out=sc(t1), in_=ssci[:, 0, :]), r=[P.tile("ssci", 0)], w=[st(t1)])
        vstt(t2, t1, -0.693145751953125, LDT, ALU.mult, ALU.add)
        vstt(t2, t1, -1.42860682030941723212e-6, t2, ALU.mult, ALU.add)
        vts(t3, t2, 1.0 / 362880.0, None, ALU.mult)
        for c in (1.0 / 40320, 1.0 / 5040, 1.0 / 720, 1.0 / 120, 1.0 / 24, 1.0 / 6, 0.5, 1.0):
            vstt(t3, t3, c, t2, ALU.add, ALU.mult)
        vts(t3, t3, 1.0, None, ALU.add)
        P.op("dve", lambda e: e.tensor_scalar(out=ssci[:, 1, :], in0=sc(t1), scalar1=127.0, scalar2=8388608.0,
                                              op0=ALU.add, op1=ALU.mult), r=[st(t1)], w=[P.tile("ssci", 1)])
        P.op("dve", lambda e: e.tensor_tensor(out=sc(dt), in0=sc(t3), in1=ssci[:, 1, :].bitcast(F32), op=ALU.mult),
             r=[st(t3), P.tile("ssci", 1)], w=[st(dt)])
        vtt(z, lr, dt, ALU.mult)
        vtt(th, LI, dt, ALU.mult)
        vts(t3, z, 1.0 / 5040.0, None, ALU.mult)
        for c in (1.0 / 720, 1.0 / 120, 1.0 / 24, 1.0 / 6, 0.5, 1.0):
            vstt(t3, t3, c, z, ALU.add, ALU.mult)
        vts(mag, t3, 1.0, None, ALU.add)
        PI_LO = 3.1415925
        vts(t0, th, 0.15915494309189535, None, ALU.mult)
        P.op("dve", lambda e: e.tensor_copy(out=ssci[:, 0, :], in_=sc(t0)), r=[st(t0)], w=[P.tile("ssci", 0)])
        P.op("dve", lambda e: e.tensor_copy(out=sc(t1), in_=ssci[:, 0, :]), r=[P.tile("ssci", 0)], w=[st(t1)])
        vstt(t2, t1, -6.28125, th, ALU.mult, ALU.add)
        vstt(t2, t1, -1.9353071795864769e-3, t2, ALU.mult, ALU.add)
        vts(t4, t2, -PI_LO, PI_LO, ALU.max, ALU.min)
        P.op("act", lambda e: e.activation(out=sc(sn), in_=sc(t4), func=AF.Sin), r=[st(t4)], w=[st(sn)])
        vts(t5, t2, 1.5707963267948966, None, ALU.add)
        vts(t0, t5, PI_LO, None, ALU.is_gt)
        vstt(t5, t0, -6.283185307179586, t5, ALU.mult, ALU.add)
        vts(t5, t5, -PI_LO, PI_LO, ALU.max, ALU.min)
        P.op("act", lambda e: e.activation(out=sc(cs), in_=sc(t5), func=AF.Sin), r=[st(t5)], w=[st(cs)])
        vtt(Are, mag, cs, ALU.mult)
        vtt(Aim, mag, sn, ALU.mult)
        vts(t0, Are, -1.0, None, ALU.add)
        vtt(t1, t0, lr, ALU.mult)
        vtt(t2, Aim, LI, ALU.mult)
        vtt(t1, t1, t2, ALU.add)
        vtt(t2, Aim, lr, ALU.mult)
        vtt(t3, t0, LI, ALU.mult)
        vtt(t2, t2, t3, ALU.subtract)
        vtt(t3, lr, lr, ALU.mult)
        vtt(t4, LI, LI, ALU.mult)
        vtt(t3, t3, t4, ALU.add)
        P.op("dve", lambda e: e.reciprocal(out=sc(t3), in_=sc(t3)), r=[st(t3)], w=[st(t3)])
        vtt(cre, t1, t3, ALU.mult)
        vtt(cim, t2, t3, ALU.mult)
        def csq(o_r, o_i, a_r, a_i):
            vtt(t0, a_r, a_r, ALU.mult)
            vtt(t1, a_i, a_i, ALU.mult)
            vtt(t2, a_r, a_i, ALU.mult)
            vtt(o_r, t0, t1, ALU.subtract)
            vts(o_i, t2, 2.0, None, ALU.mult)
        csq(p2r, p2i, Are, Aim)
        csq(t6, t7, p2r, p2i)
        csq(p2r, p2i, t6, t7)
        for ro, ri_, slot_, sgn in ((0, 0, p2r, 1.0), (1, 1, p2r, 1.0), (0, 1, p2i, -1.0), (1, 0, p2i, 1.0)):
            P.op("dve", lambda e, ro=ro, ri_=ri_, slot_=slot_, sgn=sgn: e.tensor_scalar(
                out=amat[:, :, ro, :, ri_], in0=sc(slot_).rearrange("p (d g) -> p d g", d=2), scalar1=sgn, scalar2=None,
                op0=ALU.mult), r=[st(slot_)], w=[t_ab])
        for ri, cj in ((0, CRE), (1, CIM)):
            for d in range(2):
                for blk in range(2):
                    b = 4 + (psrot["dn"] % 4)
                    psrot["dn"] += 1
                    P.op("pe", lambda e, b=b, ri=ri, d=d, blk=blk: e.transpose(
                        psb[b][:, 0:128], cnat[:, ri, d, blk, :, :].rearrange("p q c -> p (q c)"), identf[:]),
                        r=[t_cnat, t_id], w=pst(b))
                    P.op("act", lambda e, b=b, cj=cj, d=d, blk=blk: e.activation(
                        out=bg(cj)[:, d * 256 + blk * 128:d * 256 + (blk + 1) * 128], in_=psb[b][:, 0:128], func=AF.Copy),
                        r=pst(b), w=[bgt(cj)])
        def bc(slot):
            return _ap(sc(slot), 0, [[1, 32], [0, 16]])
        def b3(j):
            return bg(j).rearrange("p (g h) -> p g h", h=16)
        def cmul_bc(o_r, o_i, a_r, a_i, x_r, x_i):
            P.op("dve", lambda e: e.tensor_tensor(out=b3(T0), in0=b3(x_r), in1=bc(a_r), op=ALU.mult), r=[bgt(x_r), st(a_r)], w=[bgt(T0)])
            P.op("dve", lambda e: e.tensor_tensor(out=b3(T1), in0=b3(x_i), in1=bc(a_i), op=ALU.mult), r=[bgt(x_i), st(a_i)], w=[bgt(T1)])
            P.op("dve", lambda e: e.tensor_tensor(out=b3(T2), in0=b3(x_i), in1=bc(a_r), op=ALU.mult), r=[bgt(x_i), st(a_r)], w=[bgt(T2)])
            P.op("dve", lambda e: e.tensor_tensor(out=b3(T3), in0=b3(x_r), in1=bc(a_i), op=ALU.mult), r=[bgt(x_r), st(a_i)], w=[bgt(T3)])
            P.op("dve", lambda e: e.tensor_tensor(out=bg(o_r), in0=bg(T0), in1=bg(T1), op=ALU.subtract), r=[bgt(T0), bgt(T1)], w=[bgt(o_r)])
            P.op("dve", lambda e: e.tensor_tensor(out=bg(o_i), in0=bg(T2), in1=bg(T3), op=ALU.add), r=[bgt(T2), bgt(T3)], w=[bgt(o_i)])
        so16 = RB[:, 0:8192].rearrange("p (d r g t h) -> p d r g t h", d=2, r=2, g=16, t=8)
        t_so = P.rtile("RB", "so16", "so")
        cur = (CRE, CIM)
        nxt = [(XR, XI), (YR, YI)]
        for k in range(1, 9):
            o = nxt[k % 2]
            cmul_bc(o[0], o[1], Are, Aim, cur[0], cur[1])
            cur = o
            for d in range(2):
                slot = (k - 1) if d == 0 else (8 - k)
                for ri in range(2):
                    P.op("act", lambda e, d=d, ri=ri, slot=slot, cur=cur: e.activation(
                        out=so16[:, d, ri, :, slot, :], in_=b3(cur[ri])[:, d * 16:(d + 1) * 16, :], func=AF.Copy,
                        scale=(1.0 if ri == 0 else -1.0)), r=[bgt(cur[ri])], w=[t_so])
        P.dma("sp", scr_so.rearrange("p d r g n -> p (d r g n)"), RB[:, 0:8192], r=[t_so], w=[t_scr], stream=s_scr)
        cpa = RA[:, 14336:15360].rearrange("p (r d g h) -> p r d g h", r=2, d=2, g=16)
        t_cpa = P.rtile("RAx", "cpa", "cpa")
        for ri, cj in ((0, CRE), (1, CIM)):
            P.op("act", lambda e, ri=ri, cj=cj: e.activation(out=cpa[:, ri].rearrange("p d g h -> p (d g) h"), in_=b3(cj), func=AF.Copy,
                                                           scale=(1.0 if ri == 0 else -1.0)), r=[bgt(cj)], w=[t_cpa])
        wa16 = RB[:, 0:8192].rearrange("p (r d g t h) -> p r d g t h", r=2, d=2, g=16, t=8)
        t_wa = P.rtile("RB", "wa16", "wa")
        cmul_bc(XR, XI, cre, cim, BRE, BIM)
        cur = (XR, XI)
        nxt = [(YR, YI), (XR, XI)]
        for tau in range(8):
            for d in range(2):
                slot = (7 - tau) if d == 0 else tau
                for ri in range(2):
                    P.op("act", lambda e, d=d, ri=ri, slot=slot, cur=cur: e.activation(
                        out=wa16[:, ri, d, :, slot, :], in_=b3(cur[ri])[:, d * 16:(d + 1) * 16, :], func=AF.Copy),
                        r=[bgt(cur[ri])], w=[t_wa])
            if tau < 7:
                o = nxt[tau % 2]
                cmul_bc(o[0], o[1], Are, Aim, cur[0], cur[1])
                cur = o
        V3 = "pe_stage"
        ww = RC[:, 0:7680].rearrange("p (b d g j h) -> p b d g j h", b=2, d=2, g=8, j=15)
        cpb = RC[:, 7680:8704].rearrange("p (d g h) -> p d g h", d=2, g=32)
        mgst = RC[:, 8704:10752].rearrange("p (b g n) -> p b g n", b=2, g=8)
        szst = RD[:, 0:4096].rearrange("p (b d g n) -> p b d g n", b=2, d=2, g=8)
        t_cpb = P.rtile("RC", V3, "cpb")
        s_asm = P.stream("asm", final_only=True)
        s_asmw = [P.stream("asmw%d" % i) for i in range(2)]
        for bi in range(2):
            P.op("pool", lambda e, bi=bi: e.memset(ww[:, bi].rearrange("p d g j h -> p (d g j h)"), 0.0),
                 w=[P.rtile("RC", V3, "ww", bi)])
        for ri in range(2):
            for par in range(2):
                P.dma("sp", cpb[ri * 64:(ri + 1) * 64, :, par * 16:(par + 1) * 16, :].rearrange("p d g h -> p d (g h)"),
                      cpa[par * 64:(par + 1) * 64, ri].rearrange("p d g h -> p d (g h)"), r=[t_cpa], w=[t_cpb], stream=s_asm)
        for blk in range(4):
            bi = blk % 2
            par, gp0 = blk // 2, (blk % 2) * 8
            t_ww = P.rtile("RC", V3, "ww", bi)
            for d in range(2):
                j0 = 0 if d == 0 else 7
                for ri in range(2):
                    P.dma("sp", ww[ri * 64:(ri + 1) * 64, bi, d, :, j0:j0 + 8, :].rearrange("p g j h -> p g (j h)"),
                          wa16[par * 64:(par + 1) * 64, ri, d, gp0:gp0 + 8, :, :].rearrange("p g t h -> p g (t h)"),
                          r=[t_wa], w=[t_ww], stream=s_asmw[bi])
            t_mg = P.rtile("RC", V3, "mgst", bi)
            t_sz = P.rtile("RD", V3, "szst", bi)
            for gl in range(8):
                g = blk * 8 + gl
                b = 4 + (psrot["dn"] % 4)
                psrot["dn"] += 1
                for t in range(8):
                    for d in range(2):
                        P.op("pe", lambda e, b=b, t=t, d=d, gl=gl, bi=bi, g=g: e.matmul(
                            psb[b][:, t * 16:(t + 1) * 16],
                            lhsT=ww[:, bi, d, gl, 7 - t:15 - t, :].rearrange("p j h -> p (j h)"),
                            rhs=cpb[:, d, g, :], start=(d == 0), stop=(d == 1)), r=[t_ww, t_cpb], w=pst(b))
                P.op("dve", lambda e, b=b, bi=bi, gl=gl, g=g: e.scalar_tensor_tensor(
                    out=mgst[:, bi, gl, :], in0=identb[:], scalar=dcol[:, g:g + 1], in1=psb[b][:, 0:128],
                    op0=ALU.mult, op1=ALU.add), r=pst(b) + [t_id, P.tile("dcol")], w=[t_mg])
                for d in range(2):
                    j0 = 0 if d == 0 else 7
                    b2 = 4 + (psrot["dn"] % 4)
                    psrot["dn"] += 1
                    P.op("pe", lambda e, b2=b2, bi=bi, d=d, gl=gl, j0=j0: e.transpose(
                        psb[b2][:, 0:64].bitcast(BF16), ww[:, bi, d, gl, j0:j0 + 8, :].rearrange("p j h -> p (j h)"), identb[:]),
                        r=[t_ww, t_id], w=pst(b2))
                    P.op("act", lambda e, b2=b2, bi=bi, d=d, gl=gl: e.activation(
                        out=szst[:, bi, d, gl, :], in_=psb[b2][:, 0:64].bitcast(BF16), func=AF.Copy), r=pst(b2), w=[t_sz])
            P.dma("sp", scr_mg[:, blk * 8:(blk + 1) * 8, :], mgst[:, bi], r=[t_mg], w=[t_scr], stream=s_scr)
            P.dma("sp", scr_sz[:, :, blk * 8:(blk + 1) * 8, :], szst[:, bi], r=[t_sz], w=[t_scr], stream=s_scr)
        tap("amat", amat[:], [128, 2, 2, 16, 2], r=[t_ab])
        tap("ssc", ssc[:], [128, 24, 32], r=[st(i) for i in range(24)])

    if "nossm" not in taps:
        ssm_setup()

    def ffn(idx, wg_d, wu_d, wd_d):
        for half in range(2):
            tbs = [2 * half, 2 * half + 1]
            xn = RA[:, 0:8192].rearrange("p (k t) -> p k t", k=KD)
            h1 = RA[:, 8192:12288].rearrange("p (f t) -> p f t", f=4)
            vname = "ffn%d_%d" % (idx, half)
            for tl, tb in enumerate(tbs):
                xnt = P.rtile("RA", vname, "xn", tl)
                rmsnorm(lambda k, tb=tb: xT[:, k, tb * TB:(tb + 1) * TB],
                        lambda k, tb=tb: [P.tile("xT", k, tb)],
                        lambda k, tl=tl: xn[:, k, tl * TB:(tl + 1) * TB],
                        lambda k, xnt=xnt, vname=vname: [xnt, P.rtile("RAx", vname, "x")], KD, float(D),
                        lambda k: gcols[:, 0 if idx == 1 else 2, k:k + 1])
            for (f0, f1) in FGROUPS:
                nf = f1 - f0
                wg, twg = ring_load([(lambda s, nf=nf: s[:, 0:KD * nf * 128].rearrange("p (k n) -> p k n", k=KD),
                                      wg_d[:, f0 * 128:f1 * 128].rearrange("(k p) n -> p k n", p=128))])
                wu, twu = ring_load([(lambda s, nf=nf: s[:, 0:KD * nf * 128].rearrange("p (k n) -> p k n", k=KD),
                                      wu_d[:, f0 * 128:f1 * 128].rearrange("(k p) n -> p k n", p=128))])
                wd, twd = ring_load([(lambda s, nf=nf: s[:, 0:nf * D].rearrange("p (k n) -> p k n", k=nf),
                                      wd_d[f0 * 128:f1 * 128, :].rearrange("(k p) n -> p k n", p=128))])
                wgv = wg[:, 0:KD * nf * 128].rearrange("p (k n) -> p k n", k=KD)
                wuv = wu[:, 0:KD * nf * 128].rearrange("p (k n) -> p k n", k=KD)
                wdv = wd[:, 0:nf * D].rearrange("p (k n) -> p k n", k=nf)
                for fi in range(nf):
                    for tl, tb in enumerate(tbs):
                        bg = (psrot["gu"] % 2) * 2
                        psrot["gu"] += 1
                        xnt = P.rtile("RA", vname, "xn", tl)
                        for which, (wv, tw, bb) in enumerate([(wgv, twg, bg), (wuv, twu, bg + 1)]):
                            for k in range(KD):
                                P.op("pe", lambda e, wv=wv, k=k, fi=fi, tl=tl, bb=bb: e.matmul(
                                    psb[bb][:], lhsT=wv[:, k, fi * 128:(fi + 1) * 128],
                                    rhs=xn[:, k, tl * TB:(tl + 1) * TB], start=(k == 0), stop=(k == KD - 1)),
                                    r=[tw, xnt], w=pst(bb))
                        ts = tf[1 + (psrot["gu"] % 2)]
                        tts = P.tile("tf", 1 + (psrot["gu"] % 2))
                        P.op("act", lambda e, ts=ts, bg=bg: e.activation(out=ts[:], in_=psb[bg][:], func=AF.Silu),
                             r=pst(bg), w=[tts])
                        h1t = P.rtile("RA", vname, "h1", fi, tl)
                        P.op("dve", lambda e, ts=ts, bg=bg, fi=fi, tl=tl: e.tensor_tensor(
                            out=h1[:, fi, tl * TB:(tl + 1) * TB], in0=ts[:], in1=psb[bg + 1][:], op=ALU.mult),
                            r=[tts] + pst(bg + 1), w=[h1t])
                for m in range(KD):
                    for tl, tb in enumerate(tbs):
                        b = 4 + (psrot["dn"] % 4)
                        psrot["dn"] += 1
                        for fi in range(nf):
                            P.op("pe", lambda e, fi=fi, m=m, tl=tl, b=b, wdv=wdv: e.matmul(
                                psb[b][:], lhsT=wdv[:, fi, m * 128:(m + 1) * 128],
                                rhs=h1[:, fi, tl * TB:(tl + 1) * TB], start=(fi == 0), stop=(fi == nf - 1)),
                                r=[twd, P.rtile("RA", vname, "h1", fi, tl)], w=pst(b))
                        xt = P.tile("xT", m, tb)
                        P.op("dve", lambda e, m=m, tb=tb, b=b: e.scalar_tensor_tensor(
                            out=xT[:, m, tb * TB:(tb + 1) * TB], in0=psb[b][:], scalar=0.5,
                            in1=xT[:, m, tb * TB:(tb + 1) * TB], op0=ALU.mult, op1=ALU.add),
                            r=pst(b) + [xt], w=[xt])

    if stop_after != "load" and "noffn1" not in taps:
        ffn(1, w_d["ffn1_w_gate"], w_d["ffn1_w_up"], w_d["ffn1_w_down"])
    tap("x1", xT[:], [128, KD, T], r=[P.tile("xT", k, tb) for k in range(KD) for tb in range(NTB)])


    def middle():
        hn = RA[:, 0:8192].rearrange("p (b k t) -> p b k t", b=2, k=KD)
        uT = RA[:, 8192:16384].rearrange("p (c t) -> p c t", c=4)
        qT = RC[:, 0:8192].rearrange("p (c t) -> p c t", c=4)
        kT = RC[:, 8192:10240]
        Vt = RC[:, 10240:12288].rearrange("p (b d) -> p b d", b=16)
        Et = RD[:, 0:1536].rearrange("p (i n) -> p i n", i=3)
        Pt = RD[:, 1536:4608].rearrange("p (i n) -> p i n", i=6)
        attnb = RD[:, 4608:6656].rearrange("p (c n) -> p c n", c=4)
        w_in = w_d["w_in"]
        def wq_parts():
            parts = []
            for kv in range(2):
                for c in range(4):
                    parts.append((lambda s, kv=kv, c=c: s.rearrange("p (k c v d) -> p k c v d", k=KD, c=4, v=2)[:, :, c, kv, :],
                                  w_in[:, kv * 256 + c * 64:kv * 256 + (c + 1) * 64].rearrange("(k p) d -> p k d", p=128)))
            return parts
        wq, twq = ring_load(wq_parts())
        wkv, twkv = ring_load([(lambda s: s[:, 0:2048].rearrange("p (k n) -> p k n", k=KD),
                                w_in[:, 512:768].rearrange("(k p) n -> p k n", p=128))])
        wu, twu = ring_load([(lambda s: s.rearrange("p (k n) -> p k n", k=KD),
                              w_in[:, 768:1280].rearrange("(k p) n -> p k n", p=128))])
        wqv = wq.rearrange("p (k n) -> p k n", k=KD)
        wkvv = wkv[:, 0:2048].rearrange("p (k n) -> p k n", k=KD)
        wuv = wu.rearrange("p (k n) -> p k n", k=KD)
        evac = {"n": 0}

        def evacuate(out_ap, in_ap, r, w):
            evac["n"] += 1
            if evac["n"] % 2:
                P.op("act", lambda e: e.activation(out=out_ap, in_=in_ap, func=AF.Copy), r=r, w=w)
            else:
                P.op("dve", lambda e: e.tensor_copy(out=out_ap, in_=in_ap), r=r, w=w)

        def nextbank(pool=(4, 5, 6, 7), key="dn"):
            b = pool[psrot.setdefault(key, 0) % len(pool)]
            psrot[key] += 1
            return b

        for tb in range(NTB):
            hb = tb % 2
            hnt = P.rtile("RA", "mid", "hn", hb)
            rmsnorm(lambda k, tb=tb: xT[:, k, tb * TB:(tb + 1) * TB],
                    lambda k, tb=tb: [P.tile("xT", k, tb)],
                    lambda k, hb=hb: hn[:, hb, k, :], lambda k, hnt=hnt: [hnt], KD, float(D),
                    lambda k: gcols[:, 1, k:k + 1])
            for c in range(4):
                b = nextbank()
                for k in range(KD):
                    P.op("pe", lambda e, b=b, k=k, c=c, hb=hb: e.matmul(
                        psb[b][:], lhsT=wqv[:, k, c * 128:(c + 1) * 128], rhs=hn[:, hb, k, :],
                        start=(k == 0), stop=(k == KD - 1)), r=[twq, hnt], w=pst(b))
                evacuate(qT[:, c, tb * TB:(tb + 1) * TB], psb[b][:], pst(b), [P.rtile("RC", "qkv", "q", tb)])
            b = nextbank()
            for k in range(KD):
                P.op("pe", lambda e, b=b, k=k, hb=hb: e.matmul(
                    psb[b][:], lhsT=wkvv[:, k, 0:128], rhs=hn[:, hb, k, :],
                    start=(k == 0), stop=(k == KD - 1)), r=[twkv, hnt], w=pst(b))
            evacuate(kT[:, tb * TB:(tb + 1) * TB], psb[b][:], pst(b), [P.rtile("RC", "qkv", "k", tb)])
            b = nextbank()
            for sub in range(4):
                for k in range(KD):
                    P.op("pe", lambda e, b=b, k=k, sub=sub, hb=hb: e.matmul(
                        psb[b][:, sub * 128:(sub + 1) * 128], lhsT=hn[:, hb, k, sub * 128:(sub + 1) * 128],
                        rhs=wkvv[:, k, 128:256], start=(k == 0), stop=(k == KD - 1)), r=[twkv, hnt], w=pst(b))
            evacuate(Vt[:, tb * 4:(tb + 1) * 4, :], psb[b][:].rearrange("p (s d) -> p s d", s=4), pst(b),
                     [P.rtile("RC", "qkv", "v", tb)])
            for c in range(4):
                b = nextbank()
                for k in range(KD):
                    P.op("pe", lambda e, b=b, k=k, c=c, hb=hb: e.matmul(
                        psb[b][:], lhsT=wuv[:, k, c * 128:(c + 1) * 128], rhs=hn[:, hb, k, :],
                        start=(k == 0), stop=(k == KD - 1)), r=[twu, hnt], w=pst(b))
                evacuate(uT[:, c, tb * TB:(tb + 1) * TB], psb[b][:], pst(b), [P.rtile("RA", "mid", "u", c, tb), P.rtile("RAx", "mid", "u")])
        tap("qT", qT, [128, 4, T], BF16, r=[P.rtile("RC", "qkv", "q", tb) for tb in range(NTB)])
        tap("kT", kT, [128, T], BF16, r=[P.rtile("RC", "qkv", "k", tb) for tb in range(NTB)])
        tap("Vt", Vt, [128, 16, 128], BF16, r=[P.rtile("RC", "qkv", "v", tb) for tb in range(NTB)])
        tap("uT", uT, [128, 4, T], BF16, r=[P.rtile("RA", "mid", "u", c, tb) for c in range(4) for tb in range(NTB)])

        w_out = w_d["w_out"]
        woa, twoa = ring_load([(lambda s, kv=kv: s.rearrange("p (c n) -> p c n", c=4)[kv * 64:(kv + 1) * 64],
                                w_out[kv * 256:(kv + 1) * 256, :].rearrange("(c d) n -> d c n", d=64)) for kv in range(2)])
        woav = woa.rearrange("p (c n) -> p c n", c=4)

        pump_state = {"acc": 0.0, "step": 0, "on": False}

        def pump():
            if not pump_state["on"]:
                return
            pump_state["acc"] += NCH / 96.0
            while pump_state["acc"] >= 1.0 and pump_state["step"] < NCH:
                scan_step(pump_state["step"])
                pump_state["step"] += 1
                pump_state["acc"] -= 1.0

        def attn_block(n):
            tb, nl = n // 4, n % 4
            bn = 4 + 2 * (n % 2)
            bd = bn + 1
            js = [j for j in (n - 1, n, n + 1) if 0 <= j < 16]
            tiles_ = [(kv, ji, j) for kv in range(2) for ji, j in enumerate(js)]
            pis = {}

            def front(kv, ji, j):
                dl = j - n + 1
                bs = (psrot["gu"] % 3)
                psrot["gu"] += 1
                ei = psrot["gu"] % 3
                pi = psrot["gu"] % 6
                pis[(kv, ji)] = pi
                P.op("pe", lambda e, bs=bs, kv=kv, j=j, n=n: e.matmul(
                    psb[bs][:], lhsT=kT[kv * 64:(kv + 1) * 64, j * 128:(j + 1) * 128],
                    rhs=qT[kv * 64:(kv + 1) * 64, :, n * 128:(n + 1) * 128], start=True, stop=True),
                    r=[P.rtile("RC", "qkv", "k", j // 4), P.rtile("RC", "qkv", "q", tb)], w=pst(bs))
                te = P.rtile("RD", "attn", "E", ei)
                P.op("act", lambda e, bs=bs, ei=ei: e.activation(out=Et[:, ei, :], in_=psb[bs][:], func=AF.Exp, scale=0.125),
                     r=pst(bs), w=[te])
                tp = P.rtile("RD", "attn", "P", pi)
                P.op("pool", lambda e, ei=ei, pi=pi, kv=kv, dl=dl: e.tensor_tensor(
                    out=Pt[:, pi, :], in0=Et[:, ei, :],
                    in1=ebt[:, (kv * 3 + dl) * 4:(kv * 3 + dl) * 4 + 4, :].rearrange("p a q -> p (a q)"), op=ALU.mult),
                    r=[te, t_eb], w=[tp])

            def back(kv, ji, j):
                pi = pis[(kv, ji)]
                tp = P.rtile("RD", "attn", "P", pi)
                P.op("pe", lambda e, bn=bn, kv=kv, j=j, pi=pi, ji=ji: e.matmul(
                    psb[bn][kv * 64:(kv + 1) * 64, :], lhsT=Vt[:, j, kv * 64:(kv + 1) * 64], rhs=Pt[:, pi, :],
                    start=(ji == 0), stop=(ji == len(js) - 1)),
                    r=[tp, P.rtile("RC", "qkv", "v", j // 4)], w=pst(bn))
                P.op("pe", lambda e, bd=bd, kv=kv, pi=pi, ji=ji: e.matmul(
                    psb[bd][kv * 64:(kv + 1) * 64, :], lhsT=ones[:, 0:64], rhs=Pt[:, pi, :],
                    start=(ji == 0), stop=(ji == len(js) - 1)), r=[tp, t_consts], w=pst(bd))

            DEPTH = 2
            for i in range(len(tiles_) + DEPTH):
                if i < len(tiles_):
                    front(*tiles_[i])
                if i >= DEPTH:
                    back(*tiles_[i - DEPTH])
            return (n, bn, bd)

        def attn_finish(n, bn, bd):
            tb, nl = n // 4, n % 4
            tt = P.tile("tf", 0)
            pump()
            P.op("dve", lambda e, bd=bd: e.tensor_tensor(
                out=tf[0][:].rearrange("p (a q) -> p a q", a=4), in0=psb[bd][:].rearrange("p (a q) -> p a q", a=4),
                in1=_ap(pcols[:, 12:16], 0, [[1, 4], [0, 128]]), op=ALU.add), r=pst(bd) + [t_pc], w=[tt])
            P.op("dve", lambda e: e.reciprocal(out=tf[0][:], in_=tf[0][:]), r=[tt], w=[tt])
            pump()
            ta = P.rtile("RD", "attn", "attnb", nl)
            P.op("dve", lambda e, bn=bn, nl=nl: e.tensor_tensor(
                out=attnb[:, :, nl * 128:(nl + 1) * 128], in0=psb[bn][:].rearrange("p (a q) -> p a q", a=4),
                in1=tf[0][:].rearrange("p (a q) -> p a q", a=4), op=ALU.mult), r=pst(bn) + [tt], w=[ta])

        def attn_tb_out(tb):
            tas = [P.rtile("RD", "attn", "attnb", nl) for nl in range(4)]
            tap("attn%d" % tb, attnb, [128, 4, 512], BF16, r=tas)
            rmsnorm(lambda c: attnb[:, c, :], lambda c: tas, lambda c: attnb[:, c, :], lambda c: tas, 4, 512.0,
                    lambda c: pcols[:, c:c + 1], bank=3, pre_dve=pump)
            for m in range(KD):
                b = 3
                for c in range(4):
                    P.op("pe", lambda e, b=b, c=c, m=m: e.matmul(
                        psb[b][:], lhsT=woav[:, c, m * 128:(m + 1) * 128], rhs=attnb[:, c, :],
                        start=(c == 0), stop=(c == 3)), r=[twoa] + tas, w=pst(b))
                xt = P.tile("xT", m, tb)
                P.op("dve", lambda e, b=b, m=m, tb=tb: e.tensor_tensor(
                    out=xT[:, m, tb * TB:(tb + 1) * TB], in0=psb[b][:], in1=xT[:, m, tb * TB:(tb + 1) * TB], op=ALU.add),
                    r=pst(b) + [xt], w=[xt])

        U8 = RB[:, 0:8192].rearrange("p (g c) -> p g c", g=32)
        ZX = RA[:, 0:16384].rearrange("p (d r g c) -> p d r g c", d=2, r=2, g=16)
        hb_state = {"n": 0}

        def halfbank(pool=(4, 5, 6, 7)):
            i = hb_state["n"]
            hb_state["n"] += 1
            return pool[(i // 2) % len(pool)], i % 2

        def u8t(g):
            return P.rtile("RB", "u8", g)

        do_ssm = "nossm" not in taps
        if do_ssm:
            for g in range(32):
                blk, gl = g // 8, g % 8
                b, h = halfbank()
                for s_ in range(8):
                    P.op("pe", lambda e, b=b, h=h, gl=gl, s_=s_, blk=blk: e.matmul(
                        psb[b][:, h * 256:(h + 1) * 256], lhsT=rsel[:, gl, (7 - s_) * 16:(15 - s_) * 16],
                        rhs=_ap(uT[:, blk, :], s_, [[8, 256]]), start=(s_ == 0), stop=(s_ == 7)),
                        r=[t_rsel] + [P.rtile("RA", "mid", "u", blk, tb) for tb in range(NTB)], w=psth(b, h))
                evacuate(U8[:, g, :], psb[b][:, h * 256:(h + 1) * 256], psth(b, h), [u8t(g)])
            tap("U8", U8, [128, 32, 256], BF16, r=[u8t(g) for g in range(32)])
            zxall = [P.rtile("RAx", "zx", "all")]
            def zxc(d, c):
                return P.rtile("RA", "zx", d, c)
            for d in range(2):
                szs, tsz = ring_load([(lambda s: s.rearrange("p (g n) -> p g n", g=32), scr_sz[:, d])], q="sp", r=[t_scr])
                szv = szs.rearrange("p (g n) -> p g n", g=32)
                for gp in range(16):
                    for ri in range(2):
                        b, h = halfbank()
                        for par in range(2):
                            g = 16 * par + gp
                            P.op("pe", lambda e, b=b, h=h, par=par, g=g, ri=ri, szv=szv: e.matmul(
                                psb[b][par * 64:(par + 1) * 64, h * 256:(h + 1) * 256],
                                lhsT=szv[:, g, ri * 64:(ri + 1) * 64], rhs=U8[:, g, :], start=True, stop=True),
                                r=[tsz, u8t(g)], w=psth(b, h))
                        P.op("act", lambda e, b=b, h=h, d=d, ri=ri, gp=gp: e.activation(
                            out=ZX[:, d, ri, gp, :], in_=psb[b][:, h * 256:(h + 1) * 256], func=AF.Copy),
                            r=psth(b, h), w=[zxc(d, c) for c in range(NCH)] + zxall)
            tap("Z", ZX, [128, 2, 2, 16, 256], BF16, r=[zxc(d, c) for d in range(2) for c in range(NCH)])
            P.op("pool", lambda e: e.memset(ssc[:, 14:16, :].rearrange("p a b -> p (a b)"), 0.0), w=[P.tile("sr", 15)])
        RAb = RA[:, 0:16384]
        ring2 = ssc[:, 0:16, :].rearrange("p (s a) b -> p s (a b)", s=8)

        def rslot(i):
            i = i % 16
            return sring[:, i, :] if i < 8 else ring2[:, i - 8, :]

        def scan_step(step):
            c0, c1 = step, NCH - 1 - step
            ip, inw, tb_ = (step - 1) % 16, step % 16, step % 2
            tp_, tn_ = P.tile("sr", ip), P.tile("sr", inw)
            tt_ = P.tile("stt", tb_)
            zt = [zxc(0, c0), zxc(1, c1)]
            zap = _ap(RAb, c0, [[8192 + c1 - c0, 2], [256, 16], [4096, 2]])
            P.op("dve", lambda e, ip=ip, tb_=tb_: e.tensor_tensor(
                out=stt[:, tb_, :].rearrange("p (d r b) -> p d r b", d=2, r=2),
                in0=_ap(rslot(ip), 0, [[32, 2], [0, 2], [1, 32]]),
                in1=amat[:].rearrange("p d r g i -> p d r (g i)"), op=ALU.mult), r=[tp_, t_ab], w=[tt_])
            P.op("dve", lambda e, inw=inw, tb_=tb_: e.tensor_tensor(
                out=_ap(rslot(inw), 0, [[32, 2], [1, 2], [2, 16]]),
                in0=_ap(stt[:, tb_, :], 0, [[64, 2], [32, 2], [2, 16]]),
                in1=_ap(stt[:, tb_, :], 1, [[64, 2], [32, 2], [2, 16]]), op=ALU.add), r=[tt_], w=[tn_])
            P.op("dve", lambda e, inw=inw, zap=zap: e.tensor_tensor(
                out=rslot(inw).rearrange("p (d g i) -> p d g i", d=2, i=2),
                in0=rslot(inw).rearrange("p (d g i) -> p d g i", d=2, i=2), in1=zap, op=ALU.add),
                r=[tn_] + zt, w=[tn_])
            if step % 8 == 7:
                k8 = step - 7
                base = rslot(k8)
                for d in range(2):
                    if d == 0:
                        oap = _ap(RAb, k8, [[1, 8], [256, 16], [4096, 2]])
                        cols = [zxc(0, k8 + j) for j in range(8)]
                    else:
                        oap = _ap(RAb, 8192 + NCH - 1 - k8, [[-1, 8], [256, 16], [4096, 2]])
                        cols = [zxc(1, NCH - 1 - k8 - j) for j in range(8)]
                    P.op("act", lambda e, d=d, oap=oap, base=base: e.activation(
                        out=oap, in_=_ap(base, d * 32, [[64, 8], [2, 16], [1, 2]]), func=AF.Copy),
                        r=[P.tile("sr", (k8 + j) % 16) for j in range(8)], w=cols)

        pend = []
        pump_state["on"] = do_ssm
        for n in range(16):
            pend.append(attn_block(n))
            if len(pend) > 1:
                attn_finish(*pend.pop(0))
            if n % 4 == 0 and n > 0:
                attn_tb_out(n // 4 - 1)
        attn_finish(*pend.pop(0))
        attn_tb_out(3)
        while do_ssm and pump_state["step"] < NCH:
            scan_step(pump_state["step"])
            pump_state["step"] += 1
        if do_ssm:
            tap("X", ZX, [128, 2, 2, 16, 256], BF16, r=[zxc(d, c) for d in range(2) for c in range(NCH)])
            ygT = RC[:, 0:8192].rearrange("p (c t) -> p c t", c=4)
            Yact = RC[:, 8192:12288].rearrange("p (b g c) -> p b g c", b=2, g=8)
            mgs, tmg = ring_load([(lambda s: s.rearrange("p (g n) -> p g n", g=32), scr_mg)], q="sp", r=[t_scr])
            mgv = mgs.rearrange("p (g n) -> p g n", g=32)
            sov, tso = [], []
            for d in range(2):
                sl, tt_ = ring_load([(lambda s: s.rearrange("p (r g n) -> p r g n", r=2, g=16), scr_so[:, d])], q="sp", r=[t_scr])
                sov.append(sl.rearrange("p (r g n) -> p r g n", r=2, g=16))
                tso.append(tt_)
            zr = [[zxc(d, c) for c in range(NCH)] for d in range(2)]
            for blk in range(4):
                yb = blk % 2
                tya = P.rtile("RC", "back", "yact", yb)
                for gl in range(8):
                    g = blk * 8 + gl
                    par, gp = g // 16, g % 16
                    b, h = halfbank()
                    c0 = h * 256
                    pr = slice(par * 64, (par + 1) * 64)
                    P.op("pe", lambda e, b=b, c0=c0, g=g: e.matmul(psb[b][:, c0:c0 + 256], lhsT=mgv[:, g, :], rhs=U8[:, g, :],
                                                                   start=True, stop=False), r=[tmg, u8t(g)], w=psth(b, h))
                    for ri in range(2):
                        P.op("pe", lambda e, b=b, c0=c0, pr=pr, ri=ri, gp=gp: e.matmul(
                            psb[b][:, c0 + 1:c0 + 256], lhsT=sov[0][pr, ri, gp, :], rhs=ZX[pr, 0, ri, gp, 0:255],
                            start=False, stop=False), r=[tso[0]] + zr[0], w=psth(b, h))
                    for ri in range(2):
                        P.op("pe", lambda e, b=b, c0=c0, pr=pr, ri=ri, gp=gp: e.matmul(
                            psb[b][:, c0:c0 + 255], lhsT=sov[1][pr, ri, gp, :], rhs=ZX[pr, 1, ri, gp, 1:256],
                            start=False, stop=(ri == 1)), r=[tso[1]] + zr[1], w=psth(b, h))
                    if "ypre" in taps and g in (0, 5, 17, 31):
                        P.op("act", lambda e, b=b, c0=c0, g=g: e.activation(out=tf[1][:, 0:256], in_=psb[b][:, c0:c0 + 256], func=AF.Copy),
                             r=psth(b, h), w=[P.tile("tf", 1)])
                        taps.append("ypre%d" % g)
                        tap("ypre%d" % g, tf[1][:, 0:256], [128, 256], r=[P.tile("tf", 1)])
                    P.op("act", lambda e, b=b, c0=c0, yb=yb, gl=gl: e.activation(
                        out=Yact[:, yb, gl, :], in_=psb[b][:, c0:c0 + 256], func=AF.Gelu_apprx_tanh), r=psth(b, h), w=[tya])
                for t in range(8):
                    b, h = halfbank()
                    c0 = h * 256
                    for gl in range(8):
                        P.op("pe", lambda e, b=b, c0=c0, t=t, gl=gl, yb=yb: e.matmul(
                            psb[b][:, c0:c0 + 256], lhsT=rsel[:, t, (7 - gl) * 16:(15 - gl) * 16], rhs=Yact[:, yb, gl, :],
                            start=(gl == 0), stop=(gl == 7)), r=[t_rsel, tya], w=psth(b, h))
                    evacuate(_ap(ygT[:, blk, :], t, [[8, 256]]), psb[b][:, c0:c0 + 256], psth(b, h), [P.rtile("RC", "back", "yg", blk)])
            tyg = [P.rtile("RC", "back", "yg", blk) for blk in range(4)]
            tap("ygT", ygT, [128, 4, T], BF16, r=tyg)
            glus, tglu = ring_load([(lambda s: s[:, 0:2048].rearrange("p (k n) -> p k n", k=4),
                                     w_d["ssm_glu_w"].rearrange("(k p) n -> p k n", p=128))])
            gluv = glus[:, 0:2048].rearrange("p (k n) -> p k n", k=4)
            wos, twos = ring_load([(lambda s: s.rearrange("p (c n) -> p c n", c=4),
                                    w_out[512:1024, :].rearrange("(c p) n -> p c n", p=128))])
            wosv = wos.rearrange("p (c n) -> p c n", c=4)
            tas = [P.rtile("RD", "attn", "attnb", nl) for nl in range(4)]
            for tb in range(NTB):
                for m in range(4):
                    b = nextbank()
                    for k in range(4):
                        P.op("pe", lambda e, b=b, k=k, m=m, tb=tb: e.matmul(
                            psb[b][:], lhsT=gluv[:, k, m * 128:(m + 1) * 128], rhs=ygT[:, k, tb * TB:(tb + 1) * TB],
                            start=(k == 0), stop=(k == 3)), r=[tglu] + tyg, w=pst(b))
                    tt1 = P.tile("tf", 1)
                    P.op("act", lambda e, b=b, m=m: e.activation(out=tf[1][:], in_=psb[b][:], func=AF.Sigmoid,
                                                                bias=pcols[:, 8 + m:9 + m]), r=pst(b) + [t_pc], w=[tt1])
                    P.op("dve", lambda e, m=m, tb=tb: e.tensor_tensor(
                        out=attnb[:, m, :], in0=ygT[:, m, tb * TB:(tb + 1) * TB], in1=tf[1][:], op=ALU.mult),
                        r=[tt1] + tyg, w=tas)
                tap("ssm%d" % tb, attnb, [128, 4, 512], BF16, r=tas)
                rmsnorm(lambda c: attnb[:, c, :], lambda c: tas, lambda c: attnb[:, c, :], lambda c: tas, 4, 512.0,
                        lambda c: pcols[:, 4 + c:5 + c])
                for m in range(KD):
                    b = nextbank()
                    for c in range(4):
                        P.op("pe", lambda e, b=b, c=c, m=m: e.matmul(
                            psb[b][:], lhsT=wosv[:, c, m * 128:(m + 1) * 128], rhs=attnb[:, c, :],
                            start=(c == 0), stop=(c == 3)), r=[twos] + tas, w=pst(b))
                    xt = P.tile("xT", m, tb)
                    P.op("dve", lambda e, b=b, m=m, tb=tb: e.tensor_tensor(
                        out=xT[:, m, tb * TB:(tb + 1) * TB], in0=psb[b][:], in1=xT[:, m, tb * TB:(tb + 1) * TB], op=ALU.add),
                        r=pst(b) + [xt], w=[xt])
        tap("x2", xT[:], [128, KD, T], r=[P.tile("xT", k, tb) for k in range(KD) for tb in range(NTB)])

    if stop_after not in ("ffn1", "load"):
        middle()

    if stop_after not in ("ffn1", "load", "attn"):
        ffn(2, w_d["ffn2_w_gate"], w_d["ffn2_w_up"], w_d["ffn2_w_down"])

    for tb in range(NTB):
        rmsnorm(lambda k, tb=tb: xT[:, k, tb * TB:(tb + 1) * TB],
                lambda k, tb=tb: [P.tile("xT", k, tb)],
                lambda k, tb=tb: xT[:, k, tb * TB:(tb + 1) * TB],
                lambda k, tb=tb: [P.tile("xT", k, tb)], KD, float(D),
                lambda k: gcols[:, 3, k:k + 1])
        for k in range(KD):
            P.dma("sp", outT_d[k * 128:(k + 1) * 128, tb * TB:(tb + 1) * TB], xT[:, k, tb * TB:(tb + 1) * TB],
                  r=[P.tile("xT", k, tb)], stream=s_out)

    P.emit(es, [s_out] + tap_streams)
    es.close()
    return nc, list(tap_out.keys())


def _alibi_eb():
    slopes = 2.0 ** (-8.0 * (np.arange(8) + 1) / 8.0)
    sp = np.arange(128)[:, None]
    tq = np.arange(128)[None, :]
    eb = np.zeros((128, 2, 3, 4, 128), np.float64)
    for kv in range(2):
        for dl in range(3):
            rel = 128 * (dl - 1) + sp - tq
            for hh in range(4):
                v = np.exp(-slopes[kv * 4 + hh] * np.abs(rel))
                eb[:, kv, dl, hh, :] = np.where(np.abs(rel) <= 128, v, 0.0)
    return eb.reshape(128, 24 * 128).astype(np.float32)


_PROG_CACHE = {}


def _in_maps(inputs, taps=(), stop_after=None, cores=NCORES):
    x = np.asarray(inputs["x"], np.float32)
    shared = {}
    for nm in ["ffn1_w_gate", "ffn1_w_up", "ffn1_w_down", "ffn2_w_gate", "ffn2_w_up", "ffn2_w_down",
               "w_in", "w_out", "ssm_glu_w", "norm_ffn1", "norm_mix", "norm_ffn2", "attn_out_norm",
               "ssm_out_norm", "ssm_glu_b", "attn_sinks", "ssm_lambda_re", "ssm_lambda_im", "ssm_log_dt",
               "ssm_b_re", "ssm_b_im", "ssm_c_re", "ssm_c_im", "ssm_d"]:
        a = np.asarray(inputs[nm], np.float32)
        shared[nm] = np.ascontiguousarray(a.reshape(a.shape[1:]))
    shared["final_norm"] = np.ascontiguousarray(np.asarray(inputs["final_norm"], np.float32))
    shared["c_eb"] = _alibi_eb()
    maps = []
    for c in range(cores):
        m = dict(shared)
        m["xT"] = np.ascontiguousarray(x[c].T)
        maps.append(m)
    return maps


def kernel(**inputs):
    key = "main"
    if key not in _PROG_CACHE:
        _PROG_CACHE[key] = build_program()
    nc, _ = _PROG_CACHE[key]
    maps = _in_maps(inputs)
    res = run_bass_kernel_spmd(nc, maps, core_ids=list(range(NCORES)))
    out = np.stack([np.ascontiguousarray(r["outT"].T) for r in res.results], axis=0)
    return out.astype(np.float32)
```

```python
import numpy as np
from contextlib import ExitStack
import concourse.bass as bass
import concourse.mybir as mybir
from concourse.bass_utils import run_bass_kernel_spmd

F32 = mybir.dt.float32
BF16 = mybir.dt.bfloat16
I32 = mybir.dt.int32
AF = mybir.ActivationFunctionType
ALU = mybir.AluOpType

NCORES = 8
T = 2048
TB = 512
NTB = 4
D = 1024
KD = 8
FF = 2816
NF = 22
EPS = 1e-6
FGROUPS = [(0, 4), (4, 8), (8, 12), (12, 16), (16, 20), (20, 22)]
NCH = 256


class _Tile:
    __slots__ = ("name", "last_w", "rd_eng", "rd_dma")

    def __init__(self, name):
        self.name = name
        self.last_w = None
        self.rd_eng = {}
        self.rd_dma = []


class _Stream:
    def __init__(self, name, final_only=False):
        self.name = name
        self.final_only = final_only
        self.count = 0
        self.sem = None


class _Op:
    __slots__ = ("eng", "fn", "deps", "signal", "sigval", "stream", "sidx", "is_dma")

    def __init__(self, eng, fn, stream=None):
        self.eng = eng
        self.fn = fn
        self.deps = []
        self.signal = False
        self.sigval = 0
        self.stream = stream
        self.sidx = 0
        self.is_dma = stream is not None


class _Lazy:
    __slots__ = ("region", "view", "key")

    def __init__(self, region, view, key):
        self.region = region
        self.view = view
        self.key = key


class Prog:
    ENGS = ("pe", "act", "dve", "pool", "sp")

    def __init__(self, nc):
        self.nc = nc
        self.ops = {e: [] for e in self.ENGS}
        self.tiles = {}
        self.streams = []
        self.regions = {}
        self.deferred = None
        self.parked = []

    def tile(self, *key):
        t = self.tiles.get(key)
        if t is None:
            t = _Tile(key)
            self.tiles[key] = t
        return t

    def rtile(self, region, view, *key):
        return _Lazy(region, view, key)

    def _res(self, t):
        return self._rtile(t.region, t.view, *t.key) if isinstance(t, _Lazy) else t

    def drain(self, n=None):
        d = self.parked
        if not d:
            return
        assert self.deferred is None
        k = len(d) if n is None else min(n, len(d))
        for ent in d[:k]:
            if ent[0] == "op":
                self.op(*ent[1:])
            else:
                self.dma(*ent[1:6], stream=ent[6], **ent[7])
        del d[:k]

    def _rtile(self, region, view, *key):
        reg = self.regions.setdefault(region, {"view": None, "tiles": {}, "carry": ({}, [], [])})
        if reg["view"] != view:
            eng_last = dict(reg["carry"][0])
            dmas = list(reg["carry"][1])
            writers = list(reg["carry"][2])
            for t in reg["tiles"].values():
                if t.last_w is not None:
                    writers.append(t.last_w)
                for e, o in t.rd_eng.items():
                    if e not in eng_last or eng_last[e].sidx < o.sidx:
                        eng_last[e] = o
                dmas.extend(t.rd_dma)
            reg["view"] = view
            reg["tiles"] = {}
            reg["carry"] = (eng_last, dmas, writers)
        t = reg["tiles"].get(key)
        if t is None:
            t = _Tile((region, view) + key)
            eng_last, dmas, writers = reg["carry"]
            t.rd_eng = dict(eng_last)
            t.rd_dma = list(dmas) + list(writers)
            reg["tiles"][key] = t
        return t

    def stream(self, name, final_only=False):
        s = _Stream(name, final_only)
        self.streams.append(s)
        return s

    def _add(self, o, r, w):
        r = [self._res(t) for t in r]
        w = [self._res(t) for t in w]
        raw = set()
        deps = set()
        for t in r:
            if t.last_w is not None:
                raw.add(t.last_w)
        for t in w:
            if t.last_w is not None:
                deps.add(t.last_w)
            deps.update(t.rd_eng.values())
            deps.update(t.rd_dma)
        o.sidx = len(self.ops[o.eng])
        for t in r:
            if o.is_dma:
                t.rd_dma.append(o)
            else:
                t.rd_eng[o.eng] = o
        for t in w:
            t.last_w = o
            t.rd_eng = {}
            t.rd_dma = []
        raw.discard(o)
        deps.discard(o)
        for d in raw:
            o.deps.append(d)
        for d in deps:
            if d in raw:
                continue
            if d.is_dma and o.is_dma and d.stream is o.stream and d.stream.final_only:
                continue
            if d.is_dma or o.is_dma or d.eng != o.eng or o.eng != "pe":
                o.deps.append(d)
        self.ops[o.eng].append(o)
        return o

    def op(self, eng, fn, r=(), w=()):
        if self.deferred is not None:
            self.deferred.append(("op", eng, fn, list(r), list(w)))
            return None
        return self._add(_Op(eng, fn), list(r), list(w))

    def dma(self, q, out, in_, r=(), w=(), stream=None, **kw):
        if stream is None:
            stream = self.stream("anon")
        if self.deferred is not None:
            self.deferred.append(("dma", q, out, in_, list(r), list(w), stream, kw))
            return None
        o = _Op(q, lambda e, out=out, in_=in_, kw=kw: e.dma_start(out=out, in_=in_, **kw), stream)
        stream.count += 1
        self._add(o, list(r), list(w))
        o.sigval = 16 * stream.count
        return o

    def emit(self, es: ExitStack, out_streams):
        nc = self.nc
        for e in self.ENGS:
            for o in self.ops[e]:
                for d in o.deps:
                    if not d.is_dma:
                        d.signal = True
        for e in self.ENGS:
            c = 0
            for o in self.ops[e]:
                if not o.is_dma and o.signal:
                    c += 1
                    o.sigval = c
        esem = {e: es.enter_context(nc.semaphore("sem_" + e)) for e in self.ENGS}
        for i, s in enumerate(self.streams):
            if s.count > 0:
                s.sem = es.enter_context(nc.semaphore("ds%d_%s" % (i, s.name)))
        block = es.enter_context(nc.Block())
        handles = {"pe": block.tensor, "act": block.scalar, "dve": block.vector,
                   "pool": block.gpsimd, "sp": block.sync}

        def run(engname, e):
            known = {}
            for o in self.ops[engname]:
                waits = {}
                for d in o.deps:
                    if d.is_dma:
                        sem = d.stream.sem
                        val = 16 * d.stream.count if d.stream.final_only else d.sigval
                    else:
                        sem = esem[d.eng]
                        val = d.sigval
                    k = id(sem)
                    if k not in waits or waits[k][1] < val:
                        waits[k] = (sem, val)
                for k, (sem, val) in waits.items():
                    if known.get(k, 0) < val:
                        e.wait_ge(sem, val)
                        known[k] = val
                ins = o.fn(e)
                if o.is_dma:
                    ins.then_inc(o.stream.sem, 16)
                elif o.signal:
                    ins.then_inc(esem[engname], 1)
            if engname == "sp":
                for s in out_streams:
                    if s.count > 0:
                        e.wait_ge(s.sem, 16 * s.count)

        for engname in self.ENGS:
            def _f(e, engname=engname):
                run(engname, e)
            handles[engname](_f)


def _ap(base, offset_elems, dims):
    return bass.AP(base.tensor, base.offset + offset_elems, [list(base.ap[0])] + [list(d) for d in dims])


def build_program(taps=(), stop_after=None):
    nc = bass.Bass("TRN2", target_bir_lowering=False)
    es = ExitStack()
    P = Prog(nc)
    taps = list(taps)
    tap_out = {}

    def din(name, shape, dt=F32):
        return nc.dram_tensor(name, list(shape), dt, kind="ExternalInput").ap()

    xT_d = din("xT", [D, T])
    outT_d = nc.dram_tensor("outT", [D, T], F32, kind="ExternalOutput").ap()
    w_d = {}
    for nm, shp in [("ffn1_w_gate", [D, FF]), ("ffn1_w_up", [D, FF]), ("ffn1_w_down", [FF, D]),
                    ("ffn2_w_gate", [D, FF]), ("ffn2_w_up", [D, FF]), ("ffn2_w_down", [FF, D]),
                    ("w_in", [D, 1280]), ("w_out", [D, D]), ("ssm_glu_w", [512, 512])]:
        w_d[nm] = din(nm, shp)
    p_d = {}
    for nm, shp in [("norm_ffn1", [D]), ("norm_mix", [D]), ("norm_ffn2", [D]), ("final_norm", [D]),
                    ("attn_out_norm", [512]), ("ssm_out_norm", [512]), ("ssm_glu_b", [512]),
                    ("attn_sinks", [8]), ("ssm_lambda_re", [2, 32, 64]), ("ssm_lambda_im", [2, 32, 64]),
                    ("ssm_log_dt", [2, 32]), ("ssm_b_re", [2, 32, 64, 16]), ("ssm_b_im", [2, 32, 64, 16]),
                    ("ssm_c_re", [2, 32, 16, 64]), ("ssm_c_im", [2, 32, 16, 64]), ("ssm_d", [32, 16]),
                    ("c_eb", [128, 24 * 128])]:
        p_d[nm] = din(nm, shp)

    def sb(name, shape, dt):
        return es.enter_context(nc.sbuf_tensor(name, list(shape), dt))

    xT = sb("xT_sb", [128, KD, T], F32)
    ring = sb("ring", [128, 4, 4096], BF16)
    RA = sb("RA", [128, 16384], BF16)
    RB = sb("RB", [128, 8192], BF16)
    RC = sb("RC", [128, 12288], BF16)
    RD = sb("RD", [128, 7168], BF16)
    gcols = sb("gcols", [128, 4, KD], F32)
    ones = sb("ones", [128, 128], BF16)
    tf = [sb("tf%d" % i, [128, TB], F32) for i in range(3)]
    tb16 = [sb("tb16_%d" % i, [128, TB], BF16) for i in range(2)]
    psb = [es.enter_context(nc.psum_tensor("ps%d" % b, [128, 512], F32)) for b in range(8)]

    def pst(b):
        return [P.tile("ps", b, 0), P.tile("ps", b, 1)]

    def psth(b, h):
        return [P.tile("ps", b, 0), P.tile("ps", b, 1)]

    s_small = P.stream("small", final_only=True)
    s_out = P.stream("out")
    tap_streams = []

    def tap(name, ap, shape, dt=F32, r=()):
        if name not in taps:
            return
        d = nc.dram_tensor("tap_" + name, list(shape), dt, kind="ExternalOutput").ap()
        tap_out[name] = d
        ts_ = P.stream("tap_" + name)
        tap_streams.append(ts_)
        P.dma("sp", d, ap, r=list(r), stream=ts_)

    t_consts = P.tile("consts")
    P.op("pool", lambda e: e.memset(ones[:], 1.0), w=[t_consts])
    nc_allow = es.enter_context(nc.allow_non_contiguous_dma(reason="tiny parameter loads"))
    for i, nm in enumerate(["norm_ffn1", "norm_mix", "norm_ffn2", "final_norm"]):
        P.dma("sp", gcols[:, i, :], p_d[nm].rearrange("(k p) -> p k", p=128), w=[t_consts], stream=s_small)

    xs = [P.stream("x%d" % k) for k in range(KD)]
    for k in range(KD):
        P.dma("sp", xT[:, k, :], xT_d[k * 128:(k + 1) * 128, :],
              w=[P.tile("xT", k, tb) for tb in range(NTB)], stream=xs[k])

    ring_state = {"n": 0}
    rstreams = {q: [P.stream("ring%s%d" % (q, i)) for i in range(4)] for q in ("pool", "sp")}

    def ring_load(parts, q="pool", r=()):
        s = ring_state["n"] % 4
        ring_state["n"] += 1
        t = P.tile("ring", s)
        slot = ring[:, s, :]
        for dst_fn, src in parts:
            P.dma(q, dst_fn(slot), src, r=list(r), w=[t], stream=rstreams[q][s])
        return slot, t

    psrot = {"gu": 0, "dn": 0}
    dnpool = [4, 5, 6, 7]

    def dnbank():
        b = dnpool[psrot["dn"] % len(dnpool)]
        psrot["dn"] += 1
        return b

    def rmsnorm(src_fn, src_tiles_fn, dst_fn, dst_tiles_fn, nchunk, width, gcol_fn, ones_ap=None, bank=None, pre_dve=None):
        b = dnbank() if bank is None else bank
        ps = psb[b]
        oa = ones[:] if ones_ap is None else ones_ap
        for k in range(nchunk):
            sq = tb16[k % 2]
            tq = P.tile("tb16", k % 2)
            P.op("act", lambda e, sq=sq, k=k: e.activation(out=sq[:], in_=src_fn(k), func=AF.Square),
                 r=src_tiles_fn(k), w=[tq])
            P.op("pe", lambda e, sq=sq, k=k, ps=ps: e.matmul(ps[:], lhsT=oa, rhs=sq[:],
                                                             start=(k == 0), stop=(k == nchunk - 1)),
                 r=[tq, t_consts], w=pst(b))
        ttf = P.tile("tf", 0)
        P.op("act", lambda e, ps=ps: e.activation(out=tf[0][:], in_=ps[:], func=AF.Sqrt,
                                                  scale=1.0 / width, bias=epsc[:, 0:1]),
             r=pst(b) + [t_consts], w=[ttf])
        if pre_dve is not None:
            pre_dve()
        P.op("dve", lambda e, ps=ps: e.reciprocal(out=ps[:], in_=tf[0][:]), r=[ttf], w=pst(b))
        for k in range(nchunk):
            if pre_dve is not None:
                pre_dve()
            P.op("dve", lambda e, k=k, ps=ps: e.scalar_tensor_tensor(
                out=dst_fn(k), in0=src_fn(k), scalar=gcol_fn(k), in1=ps[:], op0=ALU.mult, op1=ALU.mult),
                r=src_tiles_fn(k) + pst(b) + [t_consts], w=dst_tiles_fn(k))

    epsc = sb("epsc", [128, 1], F32)
    P.op("pool", lambda e: e.memset(epsc[:], EPS), w=[t_consts])

    ebt = sb("ebt", [128, 24, 128], BF16)
    pcols = sb("pcols", [128, 16], F32)
    s_eb = P.stream("eb")
    t_eb = P.tile("ebt")
    P.dma("pool", ebt[:], p_d["c_eb"].rearrange("p (a q) -> p a q", a=24), w=[t_eb], stream=s_eb, max_dma_last_dim=4096)
    t_pc = P.tile("pcols")
    for kv in range(2):
        P.dma("sp", pcols[kv * 64:(kv + 1) * 64, 0:4],
              p_d["attn_out_norm"][kv * 256:(kv + 1) * 256].rearrange("(c d) -> d c", d=64), w=[t_pc], stream=s_small)
        sk = p_d["attn_sinks"]
        P.dma("sp", pcols[kv * 64:(kv + 1) * 64, 12:16],
              bass.AP(sk.tensor, sk.offset + 4 * kv, [[0, 64], [1, 4]]), w=[t_pc], stream=s_small)
    P.dma("sp", pcols[:, 4:8], p_d["ssm_out_norm"].rearrange("(c p) -> p c", p=128), w=[t_pc], stream=s_small)
    P.dma("sp", pcols[:, 8:12], p_d["ssm_glu_b"].rearrange("(c p) -> p c", p=128), w=[t_pc], stream=s_small)
    P.op("act", lambda e: e.activation(out=pcols[:, 12:16], in_=pcols[:, 12:16], func=AF.Exp), r=[t_pc], w=[t_pc])

    ssc = sb("ssc", [128, 24, 32], F32)
    ssci = sb("ssci", [128, 2, 32], I32)
    identb = sb("identb", [128, 128], BF16)
    identf = sb("identf", [128, 128], F32)
    rsel = sb("rsel", [128, 8, 240], BF16)
    dcol = sb("dcol", [128, 32], F32)
    amat = sb("amat", [128, 2, 2, 16, 2], F32)
    sring = sb("sring", [128, 8, 64], F32)
    stt = sb("stt", [128, 2, 128], F32)
    scr_mg = nc.dram_tensor("scr_mg", [128, 32, 128], BF16, kind="Internal").ap()
    scr_sz = nc.dram_tensor("scr_sz", [128, 2, 32, 128], BF16, kind="Internal").ap()
    scr_so = nc.dram_tensor("scr_so", [128, 2, 2, 16, 128], BF16, kind="Internal").ap()
    t_id = P.tile("ident")
    for idt in (identb, identf):
        P.op("pool", lambda e, idt=idt: e.memset(idt[:], 0.0), w=[t_id])
        P.op("pool", lambda e, idt=idt: e.affine_select(out=idt[:], in_=idt[:], pattern=[[-1, 128]],
                                                       compare_op=ALU.not_equal, fill=1.0, base=0, channel_multiplier=1),
             r=[t_id], w=[t_id])
    t_rsel = P.tile("rsel")
    P.op("pool", lambda e: e.memset(rsel[:], 0.0), w=[t_rsel])
    for a0 in range(8):
        P.op("pool", lambda e, a0=a0: e.tensor_copy(out=rsel[:, a0, 112:128], in_=identb[:, a0 * 16:(a0 + 1) * 16]),
             r=[t_id], w=[t_rsel])
    t_ab = P.tile("arb")
    s_scr = P.stream("scr")
    t_scr = P.tile("scr")

    setup_mark = []
    setup_b = []

    def ssm_setup():
        VW = "setup"
        def sc(i):
            return ssc[:, i, :]
        def st(i):
            return P.tile("ssc", i)
        def vtt(o, a, b, op):
            P.op("dve", lambda e: e.tensor_tensor(out=sc(o), in0=sc(a), in1=sc(b), op=op), r=[st(a), st(b)], w=[st(o)])
        def vts(o, a, s1, s2, op0, op1=None):
            if op1 is None:
                P.op("dve", lambda e: e.tensor_scalar(out=sc(o), in0=sc(a), scalar1=s1, scalar2=None, op0=op0), r=[st(a)], w=[st(o)])
            else:
                P.op("dve", lambda e: e.tensor_scalar(out=sc(o), in0=sc(a), scalar1=s1, scalar2=s2, op0=op0, op1=op1), r=[st(a)], w=[st(o)])
        def vstt(o, a, scal, b, op0, op1):
            P.op("dve", lambda e: e.scalar_tensor_tensor(out=sc(o), in0=sc(a), scalar=scal, in1=sc(b), op0=op0, op1=op1),
                 r=[st(a), st(b)], w=[st(o)])
        LR, LI, LDT, lr, dt, z, th, mag, sn, cs, Are, Aim = range(12)
        t0, t1, t2, t3, t4, t5, cre, cim, p2r, p2i, t6, t7 = range(12, 24)
        for par in range(2):
            for slot, nm in ((LR, "ssm_lambda_re"), (LI, "ssm_lambda_im")):
                src = p_d[nm]
                for d in range(2):
                    P.dma("sp", ssc[par * 64:(par + 1) * 64, slot, d * 16:(d + 1) * 16],
                          bass.AP(src.tensor, src.offset + d * 2048 + par * 16 * 64, [[1, 64], [64, 16]]),
                          w=[st(slot)], stream=s_small)
            src = p_d["ssm_log_dt"]
            P.dma("sp", ssc[par * 64:(par + 1) * 64, LDT, :].rearrange("p (d g) -> p d g", d=2),
                  bass.AP(src.tensor, src.offset + par * 16, [[0, 64], [32, 2], [1, 16]]), w=[st(LDT)], stream=s_small)
        for s_ in range(8):
            src = p_d["ssm_d"]
            P.dma("sp", dcol[s_ * 16:(s_ + 1) * 16, :], bass.AP(src.tensor, src.offset, [[1, 16], [16, 32]]),
                  w=[P.tile("dcol")], stream=s_small)
        big = RC[:, 0:12288].bitcast(F32).rearrange("p (j n) -> p j n", j=12)
        def bg(j):
            return big[:, j, :]
        def bgt(j):
            return P.rtile("RC", VW, "big", j)
        BRE, BIM, CRE, CIM, XR, XI, YR, YI, T0, T1, T2, T3 = range(12)
        for par in range(2):
            for d in range(2):
                for j, nm in ((BRE, "ssm_b_re"), (BIM, "ssm_b_im")):
                    src = p_d[nm]
                    P.dma("sp", big[par * 64:(par + 1) * 64, j, d * 256:(d + 1) * 256].rearrange("p (g h) -> p g h", g=16),
                          bass.AP(src.tensor, src.offset + d * 32768 + par * 16 * 1024, [[16, 64], [1024, 16], [1, 16]]),
                          w=[bgt(j)], stream=s_small)
        cnat = RA[:, 12288:14336].bitcast(F32).rearrange("p (r d b q c) -> p r d b q c", r=2, d=2, b=2, q=2)
        t_cnat = P.rtile("RAx", VW, "cnat")
        for ri, nm in enumerate(("ssm_c_re", "ssm_c_im")):
            src = p_d[nm]
            for d in range(2):
                for blk in range(2):
                    P.dma("sp", cnat[:, ri, d, blk, :, :],
                          bass.AP(src.tensor, src.offset + d * 32768 + blk * 8 * 1024, [[64, 128], [16384, 2], [1, 64]]),
                          w=[t_cnat], stream=s_small)
        P.deferred = []
        vts(lr, LR, -1e-4, None, ALU.min)
        vts(t0, LDT, 1.4426950408889634, None, ALU.mult)
        P.op("dve", lambda e: e.tensor_copy(out=ssci[:, 0, :], in_=sc(t0)), r=[st(t0)], w=[P.tile("ssci", 0)])
        P.op("dve", lambda e: e.tensor_copy(out=sc(t1), in_=ssci[:, 0, :]), r=[P.tile("ssci", 0)], w=[st(t1)])
        vstt(t2, t1, -0.693145751953125, LDT, ALU.mult, ALU.add)
        vstt(t2, t1, -1.42860682030941723212e-6, t2, ALU.mult, ALU.add)
        vts(t3, t2, 1.0 / 362880.0, None, ALU.mult)
        for c in (1.0 / 40320, 1.0 / 5040, 1.0 / 720, 1.0 / 120, 1.0 / 24, 1.0 / 6, 0.5, 1.0):
            vstt(t3, t3, c, t2, ALU.add, ALU.mult)
        vts(t3, t3, 1.0, None, ALU.add)
        P.op("dve", lambda e: e.tensor_scalar(out=ssci[:, 1, :], in0=sc(t1), scalar1=127.0, scalar2=8388608.0,
                                              op0=ALU.add, op1=ALU.mult), r=[st(t1)], w=[P.tile("ssci", 1)])
        P.op("dve", lambda e: e.tensor_tensor(out=sc(dt), in0=sc(t3), in1=ssci[:, 1, :].bitcast(F32), op=ALU.mult),
             r=[st(t3), P.tile("ssci", 1)], w=[st(dt)])
        vtt(z, lr, dt, ALU.mult)
        vtt(th, LI, dt, ALU.mult)
        vts(t3, z, 1.0 / 5040.0, None, ALU.mult)
        for c in (1.0 / 720, 1.0 / 120, 1.0 / 24, 1.0 / 6, 0.5, 1.0):
            vstt(t3, t3, c, z, ALU.add, ALU.mult)
        vts(mag, t3, 1.0, None, ALU.add)
        PI_LO = 3.1415925
        vts(t0, th, 0.15915494309189535, None, ALU.mult)
        P.op("dve", lambda e: e.tensor_copy(out=ssci[:, 0, :], in_=sc(t0)), r=[st(t0)], w=[P.tile("ssci", 0)])
        P.op("dve", lambda e: e.tensor_copy(out=sc(t1), in_=ssci[:, 0, :]), r=[P.tile("ssci", 0)], w=[st(t1)])
        vstt(t2, t1, -6.28125, th, ALU.mult, ALU.add)
        vstt(t2, t1, -1.9353071795864769e-3, t2, ALU.mult, ALU.add)
        vts(t4, t2, -PI_LO, PI_LO, ALU.max, ALU.min)
        P.op("act", lambda e: e.activation(out=sc(sn), in_=sc(t4), func=AF.Sin), r=[st(t4)], w=[st(sn)])
        vts(t5, t2, 1.5707963267948966, None, ALU.add)
        vts(t0, t5, PI_LO, None, ALU.is_gt)
        vstt(t5, t0, -6.283185307179586, t5, ALU.mult, ALU.add)
        vts(t5, t5, -PI_LO, PI_LO, ALU.max, ALU.min)
        P.op("act", lambda e: e.activation(out=sc(cs), in_=sc(t5), func=AF.Sin), r=[st(t5)], w=[st(cs)])
        vtt(Are, mag, cs, ALU.mult)
        vtt(Aim, mag, sn, ALU.mult)
        vts(t0, Are, -1.0, None, ALU.add)
        vtt(t1, t0, lr, ALU.mult)
        vtt(t2, Aim, LI, ALU.mult)
        vtt(t1, t1, t2, ALU.add)
        vtt(t2, Aim, lr, ALU.mult)
        vtt(t3, t0, LI, ALU.mult)
        vtt(t2, t2, t3, ALU.subtract)
        vtt(t3, lr, lr, ALU.mult)
        vtt(t4, LI, LI, ALU.mult)
        vtt(t3, t3, t4, ALU.add)
        P.op("dve", lambda e: e.reciprocal(out=sc(t3), in_=sc(t3)), r=[st(t3)], w=[st(t3)])
        vtt(cre, t1, t3, ALU.mult)
        vtt(cim, t2, t3, ALU.mult)
        def csq(o_r, o_i, a_r, a_i):
            vtt(t0, a_r, a_r, ALU.mult)
            vtt(t1, a_i, a_i, ALU.mult)
            vtt(t2, a_r, a_i, ALU.mult)
            vtt(o_r, t0, t1, ALU.subtract)
            vts(o_i, t2, 2.0, None, ALU.mult)
        csq(p2r, p2i, Are, Aim)
        csq(t6, t7, p2r, p2i)
        csq(p2r, p2i, t6, t7)
        for ro, ri_, slot_, sgn in ((0, 0, p2r, 1.0), (1, 1, p2r, 1.0), (0, 1, p2i, -1.0), (1, 0, p2i, 1.0)):
            P.op("dve", lambda e, ro=ro, ri_=ri_, slot_=slot_, sgn=sgn: e.tensor_scalar(
                out=amat[:, :, ro, :, ri_], in0=sc(slot_).rearrange("p (d g) -> p d g", d=2), scalar1=sgn, scalar2=None,
                op0=ALU.mult), r=[st(slot_)], w=[t_ab])
        for ri, cj in ((0, CRE), (1, CIM)):
            for d in range(2):
                for blk in range(2):
                    b = 7
                    P.op("pe", lambda e, b=b, ri=ri, d=d, blk=blk: e.transpose(
                        psb[b][:, 0:128], cnat[:, ri, d, blk, :, :].rearrange("p q c -> p (q c)"), identf[:]),
                        r=[t_cnat, t_id], w=pst(b))
                    P.op("dve", lambda e, b=b, cj=cj, d=d, blk=blk: e.tensor_copy(
                        out=bg(cj)[:, d * 256 + blk * 128:d * 256 + (blk + 1) * 128], in_=psb[b][:, 0:128]),
                        r=pst(b), w=[bgt(cj)])
        def bc(slot):
            return _ap(sc(slot), 0, [[1, 32], [0, 16]])
        def b3(j):
            return bg(j).rearrange("p (g h) -> p g h", h=16)
        def cmul_bc(o_r, o_i, a_r, a_i, x_r, x_i):
            P.op("dve", lambda e: e.tensor_tensor(out=b3(T0), in0=b3(x_r), in1=bc(a_r), op=ALU.mult), r=[bgt(x_r), st(a_r)], w=[bgt(T0)])
            P.op("dve", lambda e: e.tensor_tensor(out=b3(T1), in0=b3(x_i), in1=bc(a_i), op=ALU.mult), r=[bgt(x_i), st(a_i)], w=[bgt(T1)])
            P.op("dve", lambda e: e.tensor_tensor(out=b3(T2), in0=b3(x_i), in1=bc(a_r), op=ALU.mult), r=[bgt(x_i), st(a_r)], w=[bgt(T2)])
            P.op("dve", lambda e: e.tensor_tensor(out=b3(T3), in0=b3(x_r), in1=bc(a_i), op=ALU.mult), r=[bgt(x_r), st(a_i)], w=[bgt(T3)])
            P.op("dve", lambda e: e.tensor_tensor(out=bg(o_r), in0=bg(T0), in1=bg(T1), op=ALU.subtract), r=[bgt(T0), bgt(T1)], w=[bgt(o_r)])
            P.op("dve", lambda e: e.tensor_tensor(out=bg(o_i), in0=bg(T2), in1=bg(T3), op=ALU.add), r=[bgt(T2), bgt(T3)], w=[bgt(o_i)])
        so16 = RB[:, 0:8192].rearrange("p (d r g t h) -> p d r g t h", d=2, r=2, g=16, t=8)
        t_so = P.rtile("RB", "so16", "so")
        cur = (CRE, CIM)
        nxt = [(XR, XI), (YR, YI)]
        for k in range(1, 9):
            o = nxt[k % 2]
            cmul_bc(o[0], o[1], Are, Aim, cur[0], cur[1])
            cur = o
            for d in range(2):
                slot = (k - 1) if d == 0 else (8 - k)
                for ri in range(2):
                    P.op("dve", lambda e, d=d, ri=ri, slot=slot, cur=cur: e.tensor_scalar(
                        out=so16[:, d, ri, :, slot, :], in0=b3(cur[ri])[:, d * 16:(d + 1) * 16, :],
                        scalar1=(1.0 if ri == 0 else -1.0), scalar2=None, op0=ALU.mult), r=[bgt(cur[ri])], w=[t_so])
        P.dma("sp", scr_so.rearrange("p d r g n -> p (d r g n)"), RB[:, 0:8192], r=[t_so], w=[t_scr], stream=s_scr)
        cpa = RA[:, 14336:15360].rearrange("p (r d g h) -> p r d g h", r=2, d=2, g=16)
        t_cpa = P.rtile("RAx", "cpa", "cpa")
        for ri, cj in ((0, CRE), (1, CIM)):
            P.op("dve", lambda e, ri=ri, cj=cj: e.tensor_scalar(out=cpa[:, ri].rearrange("p d g h -> p (d g) h"), in0=b3(cj),
                                                             scalar1=(1.0 if ri == 0 else -1.0), scalar2=None, op0=ALU.mult),
                 r=[bgt(cj)], w=[t_cpa])
        wa16 = RB[:, 0:8192].rearrange("p (r d g t h) -> p r d g t h", r=2, d=2, g=16, t=8)
        t_wa = P.rtile("RB", "wa16", "wa")
        cmul_bc(XR, XI, cre, cim, BRE, BIM)
        cur = (XR, XI)
        nxt = [(YR, YI), (XR, XI)]
        for tau in range(8):
            for d in range(2):
                slot = (7 - tau) if d == 0 else tau
                for ri in range(2):
                    P.op("dve", lambda e, d=d, ri=ri, slot=slot, cur=cur: e.tensor_copy(
                        out=wa16[:, ri, d, :, slot, :], in_=b3(cur[ri])[:, d * 16:(d + 1) * 16, :]),
                        r=[bgt(cur[ri])], w=[t_wa])
            if tau < 7:
                o = nxt[tau % 2]
                cmul_bc(o[0], o[1], Are, Aim, cur[0], cur[1])
                cur = o
        setup_mark.append(len(P.deferred))
        V3 = "pe_stage"
        ww = RC[:, 0:7680].rearrange("p (b d g j h) -> p b d g j h", b=2, d=2, g=8, j=15)
        cpb = RC[:, 7680:8704].rearrange("p (d g h) -> p d g h", d=2, g=32)
        mgst = RC[:, 8704:10752].rearrange("p (b g n) -> p b g n", b=2, g=8)
        szst = RD[:, 0:4096].rearrange("p (b d g n) -> p b d g n", b=2, d=2, g=8)
        t_cpb = P.rtile("RC", V3, "cpb")
        s_asm = P.stream("asm", final_only=True)
        s_asmw = [P.stream("asmw%d" % i) for i in range(2)]
        for bi in range(2):
            P.op("pool", lambda e, bi=bi: e.memset(ww[:, bi].rearrange("p d g j h -> p (d g j h)"), 0.0),
                 w=[P.rtile("RC", V3, "ww", bi)])
        for ri in range(2):
            for par in range(2):
                P.dma("sp", cpb[ri * 64:(ri + 1) * 64, :, par * 16:(par + 1) * 16, :].rearrange("p d g h -> p d (g h)"),
                      cpa[par * 64:(par + 1) * 64, ri].rearrange("p d g h -> p d (g h)"), r=[t_cpa], w=[t_cpb], stream=s_asm)
        for blk in range(4):
            bi = blk % 2
            par, gp0 = blk // 2, (blk % 2) * 8
            t_ww = P.rtile("RC", V3, "ww", bi)
            for d in range(2):
                j0 = 0 if d == 0 else 7
                for ri in range(2):
                    P.dma("sp", ww[ri * 64:(ri + 1) * 64, bi, d, :, j0:j0 + 8, :].rearrange("p g j h -> p g (j h)"),
                          wa16[par * 64:(par + 1) * 64, ri, d, gp0:gp0 + 8, :, :].rearrange("p g t h -> p g (t h)"),
                          r=[t_wa], w=[t_ww], stream=s_asmw[bi])
            t_mg = P.rtile("RC", V3, "mgst", bi)
            t_sz = P.rtile("RD", V3, "szst", bi)
            for gl in range(8):
                g = blk * 8 + gl
                b = 4 + (g % 2)
                for t in range(8):
                    for d in range(2):
                        P.op("pe", lambda e, b=b, t=t, d=d, gl=gl, bi=bi, g=g: e.matmul(
                            psb[b][:, t * 16:(t + 1) * 16],
                            lhsT=ww[:, bi, d, gl, 7 - t:15 - t, :].rearrange("p j h -> p (j h)"),
                            rhs=cpb[:, d, g, :], start=(d == 0), stop=(d == 1)), r=[t_ww, t_cpb], w=pst(b))
                P.op("dve", lambda e, b=b, bi=bi, gl=gl, g=g: e.scalar_tensor_tensor(
                    out=mgst[:, bi, gl, :], in0=identb[:], scalar=dcol[:, g:g + 1], in1=psb[b][:, 0:128],
                    op0=ALU.mult, op1=ALU.add), r=pst(b) + [t_id, P.tile("dcol")], w=[t_mg])
                for d in range(2):
                    j0 = 0 if d == 0 else 7
                    b2 = 6 + d
                    P.op("pe", lambda e, b2=b2, bi=bi, d=d, gl=gl, j0=j0: e.transpose(
                        psb[b2][:, 0:64].bitcast(BF16), ww[:, bi, d, gl, j0:j0 + 8, :].rearrange("p j h -> p (j h)"), identb[:]),
                        r=[t_ww, t_id], w=pst(b2))
                    P.op("act", lambda e, b2=b2, bi=bi, d=d, gl=gl: e.activation(
                        out=szst[:, bi, d, gl, :], in_=psb[b2][:, 0:64].bitcast(BF16), func=AF.Copy), r=pst(b2), w=[t_sz])
            P.dma("sp", scr_mg[:, blk * 8:(blk + 1) * 8, :], mgst[:, bi], r=[t_mg], w=[t_scr], stream=s_scr)
            P.dma("sp", scr_sz[:, :, blk * 8:(blk + 1) * 8, :], szst[:, bi], r=[t_sz], w=[t_scr], stream=s_scr)
        tap("amat", amat[:], [128, 2, 2, 16, 2], r=[t_ab])
        tap("ssc", ssc[:], [128, 24, 32], r=[st(i) for i in range(24)])

    if "nossm" not in taps:
        ssm_setup()
        P.parked = P.deferred[:setup_mark[0]]
        setup_b.extend(P.deferred[setup_mark[0]:])
        P.deferred = None

    def ffn(idx, wg_d, wu_d, wd_d):
        for half in range(2):
            tbs = [2 * half, 2 * half + 1]
            xn = RA[:, 0:8192].rearrange("p (k t) -> p k t", k=KD)
            h1 = RA[:, 8192:12288].rearrange("p (f t) -> p f t", f=4)
            vname = "ffn%d_%d" % (idx, half)
            for tl, tb in enumerate(tbs):
                xnt = P.rtile("RA", vname, "xn", tl)
                rmsnorm(lambda k, tb=tb: xT[:, k, tb * TB:(tb + 1) * TB],
                        lambda k, tb=tb: [P.tile("xT", k, tb)],
                        lambda k, tl=tl: xn[:, k, tl * TB:(tl + 1) * TB],
                        lambda k, xnt=xnt: [xnt], KD, float(D),
                        lambda k: gcols[:, 0 if idx == 1 else 2, k:k + 1])
            for (f0, f1) in FGROUPS:
                nf = f1 - f0
                wg, twg = ring_load([(lambda s, nf=nf: s[:, 0:KD * nf * 128].rearrange("p (k n) -> p k n", k=KD),
                                      wg_d[:, f0 * 128:f1 * 128].rearrange("(k p) n -> p k n", p=128))])
                wu, twu = ring_load([(lambda s, nf=nf: s[:, 0:KD * nf * 128].rearrange("p (k n) -> p k n", k=KD),
                                      wu_d[:, f0 * 128:f1 * 128].rearrange("(k p) n -> p k n", p=128))])
                wd, twd = ring_load([(lambda s, nf=nf: s[:, 0:nf * D].rearrange("p (k n) -> p k n", k=nf),
                                      wd_d[f0 * 128:f1 * 128, :].rearrange("(k p) n -> p k n", p=128))])
                wgv = wg[:, 0:KD * nf * 128].rearrange("p (k n) -> p k n", k=KD)
                wuv = wu[:, 0:KD * nf * 128].rearrange("p (k n) -> p k n", k=KD)
                wdv = wd[:, 0:nf * D].rearrange("p (k n) -> p k n", k=nf)
                for fi in range(nf):
                    for tl, tb in enumerate(tbs):
                        bg = (psrot["gu"] % 2) * 2
                        psrot["gu"] += 1
                        xnt = P.rtile("RA", vname, "xn", tl)
                        for which, (wv, tw, bb) in enumerate([(wgv, twg, bg), (wuv, twu, bg + 1)]):
                            for k in range(KD):
                                P.op("pe", lambda e, wv=wv, k=k, fi=fi, tl=tl, bb=bb: e.matmul(
                                    psb[bb][:], lhsT=wv[:, k, fi * 128:(fi + 1) * 128],
                                    rhs=xn[:, k, tl * TB:(tl + 1) * TB], start=(k == 0), stop=(k == KD - 1)),
                                    r=[tw, xnt], w=pst(bb))
                        ts = tf[1 + (psrot["gu"] % 2)]
                        tts = P.tile("tf", 1 + (psrot["gu"] % 2))
                        P.op("act", lambda e, ts=ts, bg=bg: e.activation(out=ts[:], in_=psb[bg][:], func=AF.Silu),
                             r=pst(bg), w=[tts])
                        h1t = P.rtile("RA", vname, "h1", fi, tl)
                        P.op("dve", lambda e, ts=ts, bg=bg, fi=fi, tl=tl: e.tensor_tensor(
                            out=h1[:, fi, tl * TB:(tl + 1) * TB], in0=ts[:], in1=psb[bg + 1][:], op=ALU.mult),
                            r=[tts] + pst(bg + 1), w=[h1t])
                        P.drain(2)
                for m in range(KD):
                    for tl, tb in enumerate(tbs):
                        b = dnbank()
                        for fi in range(nf):
                            P.op("pe", lambda e, fi=fi, m=m, tl=tl, b=b, wdv=wdv, nf=nf: e.matmul(
                                psb[b][:], lhsT=wdv[:, fi, m * 128:(m + 1) * 128],
                                rhs=h1[:, fi, tl * TB:(tl + 1) * TB], start=(fi == 0), stop=(fi == nf - 1)),
                                r=[twd, P.rtile("RA", vname, "h1", fi, tl)], w=pst(b))
                        xt = P.tile("xT", m, tb)
                        P.op("dve", lambda e, m=m, tb=tb, b=b: e.scalar_tensor_tensor(
                            out=xT[:, m, tb * TB:(tb + 1) * TB], in0=psb[b][:], scalar=0.5,
                            in1=xT[:, m, tb * TB:(tb + 1) * TB], op0=ALU.mult, op1=ALU.add),
                            r=pst(b) + [xt], w=[xt])
                        P.drain(2)

    if stop_after != "load" and "noffn1" not in taps:
        dnpool[:] = [4, 5, 6]
        ffn(1, w_d["ffn1_w_gate"], w_d["ffn1_w_up"], w_d["ffn1_w_down"])
        dnpool[:] = [4, 5, 6, 7]
    P.drain()
    P.parked = list(setup_b)
    P.drain()
    tap("x1", xT[:], [128, KD, T], r=[P.tile("xT", k, tb) for k in range(KD) for tb in range(NTB)])


    def middle():
        hn = RA[:, 0:8192].rearrange("p (b k t) -> p b k t", b=2, k=KD)
        uT = RA[:, 8192:16384].rearrange("p (c t) -> p c t", c=4)
        qT = RC[:, 0:8192].rearrange("p (c t) -> p c t", c=4)
        kT = RC[:, 8192:10240]
        Vt = RC[:, 10240:12288].rearrange("p (b d) -> p b d", b=16)
        Et = RD[:, 0:1536].rearrange("p (i n) -> p i n", i=3)
        Pt = RD[:, 1536:4608].rearrange("p (i n) -> p i n", i=6)
        attnb = RD[:, 4608:6656].rearrange("p (c n) -> p c n", c=4)
        w_in = w_d["w_in"]
        def wq_parts():
            parts = []
            for kv in range(2):
                for c in range(4):
                    parts.append((lambda s, kv=kv, c=c: s.rearrange("p (k c v d) -> p k c v d", k=KD, c=4, v=2)[:, :, c, kv, :],
                                  w_in[:, kv * 256 + c * 64:kv * 256 + (c + 1) * 64].rearrange("(k p) d -> p k d", p=128)))
            return parts
        wq, twq = ring_load(wq_parts())
        wkv, twkv = ring_load([(lambda s: s[:, 0:2048].rearrange("p (k n) -> p k n", k=KD),
                                w_in[:, 512:768].rearrange("(k p) n -> p k n", p=128))])
        wu, twu = ring_load([(lambda s: s.rearrange("p (k n) -> p k n", k=KD),
                              w_in[:, 768:1280].rearrange("(k p) n -> p k n", p=128))])
        wqv = wq.rearrange("p (k n) -> p k n", k=KD)
        wkvv = wkv[:, 0:2048].rearrange("p (k n) -> p k n", k=KD)
        wuv = wu.rearrange("p (k n) -> p k n", k=KD)
        evac = {"n": 0}

        def evacuate(out_ap, in_ap, r, w):
            P.drain(evac.get("k", 0))
            evac["n"] += 1
            if evac["n"] % 2:
                P.op("act", lambda e: e.activation(out=out_ap, in_=in_ap, func=AF.Copy), r=r, w=w)
            else:
                P.op("dve", lambda e: e.tensor_copy(out=out_ap, in_=in_ap), r=r, w=w)

        def nextbank(pool=None, key="dn"):
            pool = tuple(dnpool) if pool is None else pool
            b = pool[psrot.setdefault(key, 0) % len(pool)]
            psrot[key] += 1
            return b

        for tb in range(NTB):
            hb = tb % 2
            hnt = P.rtile("RA", "mid", "hn", hb)
            rmsnorm(lambda k, tb=tb: xT[:, k, tb * TB:(tb + 1) * TB],
                    lambda k, tb=tb: [P.tile("xT", k, tb)],
                    lambda k, hb=hb: hn[:, hb, k, :], lambda k, hnt=hnt: [hnt], KD, float(D),
                    lambda k: gcols[:, 1, k:k + 1])
            for c in range(4):
                b = nextbank()
                for k in range(KD):
                    P.op("pe", lambda e, b=b, k=k, c=c, hb=hb: e.matmul(
                        psb[b][:], lhsT=wqv[:, k, c * 128:(c + 1) * 128], rhs=hn[:, hb, k, :],
                        start=(k == 0), stop=(k == KD - 1)), r=[twq, hnt], w=pst(b))
                evacuate(qT[:, c, tb * TB:(tb + 1) * TB], psb[b][:], pst(b), [P.rtile("RC", "qkv", "q", tb)])
            b = nextbank()
            for k in range(KD):
                P.op("pe", lambda e, b=b, k=k, hb=hb: e.matmul(
                    psb[b][:], lhsT=wkvv[:, k, 0:128], rhs=hn[:, hb, k, :],
                    start=(k == 0), stop=(k == KD - 1)), r=[twkv, hnt], w=pst(b))
            evacuate(kT[:, tb * TB:(tb + 1) * TB], psb[b][:], pst(b), [P.rtile("RC", "qkv", "k", tb)])
            b = nextbank()
            for sub in range(4):
                for k in range(KD):
                    P.op("pe", lambda e, b=b, k=k, sub=sub, hb=hb: e.matmul(
                        psb[b][:, sub * 128:(sub + 1) * 128], lhsT=hn[:, hb, k, sub * 128:(sub + 1) * 128],
                        rhs=wkvv[:, k, 128:256], start=(k == 0), stop=(k == KD - 1)), r=[twkv, hnt], w=pst(b))
            evacuate(Vt[:, tb * 4:(tb + 1) * 4, :], psb[b][:].rearrange("p (s d) -> p s d", s=4), pst(b),
                     [P.rtile("RC", "qkv", "v", tb)])
            for c in range(4):
                b = nextbank()
                for k in range(KD):
                    P.op("pe", lambda e, b=b, k=k, c=c, hb=hb: e.matmul(
                        psb[b][:], lhsT=wuv[:, k, c * 128:(c + 1) * 128], rhs=hn[:, hb, k, :],
                        start=(k == 0), stop=(k == KD - 1)), r=[twu, hnt], w=pst(b))
                evacuate(uT[:, c, tb * TB:(tb + 1) * TB], psb[b][:], pst(b), [P.rtile("RA", "mid", "u", c, tb), P.rtile("RAx", "mid", "u")])
        tap("qT", qT, [128, 4, T], BF16, r=[P.rtile("RC", "qkv", "q", tb) for tb in range(NTB)])
        tap("kT", kT, [128, T], BF16, r=[P.rtile("RC", "qkv", "k", tb) for tb in range(NTB)])
        tap("Vt", Vt, [128, 16, 128], BF16, r=[P.rtile("RC", "qkv", "v", tb) for tb in range(NTB)])
        tap("uT", uT, [128, 4, T], BF16, r=[P.rtile("RA", "mid", "u", c, tb) for c in range(4) for tb in range(NTB)])

        w_out = w_d["w_out"]
        woa, twoa = ring_load([(lambda s, kv=kv: s.rearrange("p (c n) -> p c n", c=4)[kv * 64:(kv + 1) * 64],
                                w_out[kv * 256:(kv + 1) * 256, :].rearrange("(c d) n -> d c n", d=64)) for kv in range(2)])
        woav = woa.rearrange("p (c n) -> p c n", c=4)

        pump_state = {"acc": 0.0, "step": 0, "on": False}

        def pump():
            if not pump_state["on"]:
                return
            pump_state["acc"] += NCH / 96.0
            while pump_state["acc"] >= 1.0 and pump_state["step"] < NCH:
                scan_step(pump_state["step"])
                pump_state["step"] += 1
                pump_state["acc"] -= 1.0

        def attn_block(n):
            tb, nl = n // 4, n % 4
            bn = 4 + 2 * (n % 2)
            bd = bn + 1
            js = [j for j in (n - 1, n, n + 1) if 0 <= j < 16]
            tiles_ = [(kv, ji, j) for kv in range(2) for ji, j in enumerate(js)]
            pis = {}

            def front(kv, ji, j):
                dl = j - n + 1
                bs = (psrot["gu"] % 3)
                psrot["gu"] += 1
                ei = psrot["gu"] % 3
                pi = psrot["gu"] % 6
                pis[(kv, ji)] = pi
                P.op("pe", lambda e, bs=bs, kv=kv, j=j, n=n: e.matmul(
                    psb[bs][:], lhsT=kT[kv * 64:(kv + 1) * 64, j * 128:(j + 1) * 128],
                    rhs=qT[kv * 64:(kv + 1) * 64, :, n * 128:(n + 1) * 128], start=True, stop=True),
                    r=[P.rtile("RC", "qkv", "k", j // 4), P.rtile("RC", "qkv", "q", tb)], w=pst(bs))
                te = P.rtile("RD", "attn", "E", ei)
                P.op("act", lambda e, bs=bs, ei=ei: e.activation(out=Et[:, ei, :], in_=psb[bs][:], func=AF.Exp, scale=0.125),
                     r=pst(bs), w=[te])
                tp = P.rtile("RD", "attn", "P", pi)
                P.op("pool", lambda e, ei=ei, pi=pi, kv=kv, dl=dl: e.tensor_tensor(
                    out=Pt[:, pi, :], in0=Et[:, ei, :],
                    in1=ebt[:, (kv * 3 + dl) * 4:(kv * 3 + dl) * 4 + 4, :].rearrange("p a q -> p (a q)"), op=ALU.mult),
                    r=[te, t_eb], w=[tp])

            def back(kv, ji, j):
                pi = pis[(kv, ji)]
                tp = P.rtile("RD", "attn", "P", pi)
                P.op("pe", lambda e, bn=bn, kv=kv, j=j, pi=pi, ji=ji: e.matmul(
                    psb[bn][kv * 64:(kv + 1) * 64, :], lhsT=Vt[:, j, kv * 64:(kv + 1) * 64], rhs=Pt[:, pi, :],
                    start=(ji == 0), stop=(ji == len(js) - 1)),
                    r=[tp, P.rtile("RC", "qkv", "v", j // 4)], w=pst(bn))
                P.op("pe", lambda e, bd=bd, kv=kv, pi=pi, ji=ji: e.matmul(
                    psb[bd][kv * 64:(kv + 1) * 64, :], lhsT=ones[:, 0:64], rhs=Pt[:, pi, :],
                    start=(ji == 0), stop=(ji == len(js) - 1)), r=[tp, t_consts], w=pst(bd))

            DEPTH = 2
            for i in range(len(tiles_) + DEPTH):
                if i < len(tiles_):
                    front(*tiles_[i])
                if i >= DEPTH:
                    back(*tiles_[i - DEPTH])
            return (n, bn, bd)

        def attn_finish(n, bn, bd):
            tb, nl = n // 4, n % 4
            tt = P.tile("tf", 0)
            pump()
            P.op("dve", lambda e, bd=bd: e.tensor_tensor(
                out=tf[0][:].rearrange("p (a q) -> p a q", a=4), in0=psb[bd][:].rearrange("p (a q) -> p a q", a=4),
                in1=_ap(pcols[:, 12:16], 0, [[1, 4], [0, 128]]), op=ALU.add), r=pst(bd) + [t_pc], w=[tt])
            P.op("dve", lambda e: e.reciprocal(out=tf[0][:], in_=tf[0][:]), r=[tt], w=[tt])
            pump()
            ta = P.rtile("RD", "attn", "attnb", nl)
            P.op("dve", lambda e, bn=bn, nl=nl: e.tensor_tensor(
                out=attnb[:, :, nl * 128:(nl + 1) * 128], in0=psb[bn][:].rearrange("p (a q) -> p a q", a=4),
                in1=tf[0][:].rearrange("p (a q) -> p a q", a=4), op=ALU.mult), r=pst(bn) + [tt], w=[ta])

        def attn_tb_out(tb):
            tas = [P.rtile("RD", "attn", "attnb", nl) for nl in range(4)]
            tap("attn%d" % tb, attnb, [128, 4, 512], BF16, r=tas)
            rmsnorm(lambda c: attnb[:, c, :], lambda c: tas, lambda c: attnb[:, c, :], lambda c: tas, 4, 512.0,
                    lambda c: pcols[:, c:c + 1], bank=3, pre_dve=pump)
            for m in range(KD):
                b = 3
                for c in range(4):
                    P.op("pe", lambda e, b=b, c=c, m=m: e.matmul(
                        psb[b][:], lhsT=woav[:, c, m * 128:(m + 1) * 128], rhs=attnb[:, c, :],
                        start=(c == 0), stop=(c == 3)), r=[twoa] + tas, w=pst(b))
                xt = P.tile("xT", m, tb)
                P.op("dve", lambda e, b=b, m=m, tb=tb: e.tensor_tensor(
                    out=xT[:, m, tb * TB:(tb + 1) * TB], in0=psb[b][:], in1=xT[:, m, tb * TB:(tb + 1) * TB], op=ALU.add),
                    r=pst(b) + [xt], w=[xt])

        U8 = RB[:, 0:8192].rearrange("p (g c) -> p g c", g=32)
        ZX = RA[:, 0:16384].rearrange("p (d r g c) -> p d r g c", d=2, r=2, g=16)
        hb_state = {"n": 0}

        def halfbank(pool=(4, 5, 6, 7)):
            i = hb_state["n"]
            hb_state["n"] += 1
            return pool[(i // 2) % len(pool)], i % 2

        def u8t(g):
            return P.rtile("RB", "u8", g)

        do_ssm = "nossm" not in taps
        if do_ssm:
            for g in range(32):
                blk, gl = g // 8, g % 8
                b, h = halfbank()
                for s_ in range(8):
                    P.op("pe", lambda e, b=b, h=h, gl=gl, s_=s_, blk=blk: e.matmul(
                        psb[b][:, h * 256:(h + 1) * 256], lhsT=rsel[:, gl, (7 - s_) * 16:(15 - s_) * 16],
                        rhs=_ap(uT[:, blk, :], s_, [[8, 256]]), start=(s_ == 0), stop=(s_ == 7)),
                        r=[t_rsel] + [P.rtile("RA", "mid", "u", blk, tb) for tb in range(NTB)], w=psth(b, h))
                evacuate(U8[:, g, :], psb[b][:, h * 256:(h + 1) * 256], psth(b, h), [u8t(g)])
            tap("U8", U8, [128, 32, 256], BF16, r=[u8t(g) for g in range(32)])
            zxall = [P.rtile("RAx", "zx", "all")]
            def zxc(d, c):
                return P.rtile("RA", "zx", d, c)
            for d in range(2):
                szs, tsz = ring_load([(lambda s: s.rearrange("p (g n) -> p g n", g=32), scr_sz[:, d])], q="sp", r=[t_scr])
                szv = szs.rearrange("p (g n) -> p g n", g=32)
                for gp in range(16):
                    for ri in range(2):
                        b, h = halfbank()
                        for par in range(2):
                            g = 16 * par + gp
                            P.op("pe", lambda e, b=b, h=h, par=par, g=g, ri=ri, szv=szv: e.matmul(
                                psb[b][par * 64:(par + 1) * 64, h * 256:(h + 1) * 256],
                                lhsT=szv[:, g, ri * 64:(ri + 1) * 64], rhs=U8[:, g, :], start=True, stop=True),
                                r=[tsz, u8t(g)], w=psth(b, h))
                        P.op("act", lambda e, b=b, h=h, d=d, ri=ri, gp=gp: e.activation(
                            out=ZX[:, d, ri, gp, :], in_=psb[b][:, h * 256:(h + 1) * 256], func=AF.Copy),
                            r=psth(b, h), w=[zxc(d, c) for c in range(NCH)] + zxall)
            tap("Z", ZX, [128, 2, 2, 16, 256], BF16, r=[zxc(d, c) for d in range(2) for c in range(NCH)])
            P.op("pool", lambda e: e.memset(ssc[:, 14:16, :].rearrange("p a b -> p (a b)"), 0.0), w=[P.tile("sr", 15)])
        RAb = RA[:, 0:16384]
        ring2 = ssc[:, 0:16, :].rearrange("p (s a) b -> p s (a b)", s=8)

        def rslot(i):
            i = i % 16
            return sring[:, i, :] if i < 8 else ring2[:, i - 8, :]

        def scan_step(step):
            c0, c1 = step, NCH - 1 - step
            ip, inw, tb_ = (step - 1) % 16, step % 16, step % 2
            tp_, tn_ = P.tile("sr", ip), P.tile("sr", inw)
            tt_ = P.tile("stt", tb_)
            zt = [zxc(0, c0), zxc(1, c1)]
            zap = _ap(RAb, c0, [[8192 + c1 - c0, 2], [256, 16], [4096, 2]])
            P.op("dve", lambda e, ip=ip, tb_=tb_: e.tensor_tensor(
                out=stt[:, tb_, :].rearrange("p (d r b) -> p d r b", d=2, r=2),
                in0=_ap(rslot(ip), 0, [[32, 2], [0, 2], [1, 32]]),
                in1=amat[:].rearrange("p d r g i -> p d r (g i)"), op=ALU.mult), r=[tp_, t_ab], w=[tt_])
            P.op("dve", lambda e, inw=inw, tb_=tb_: e.tensor_tensor(
                out=_ap(rslot(inw), 0, [[32, 2], [1, 2], [2, 16]]),
                in0=_ap(stt[:, tb_, :], 0, [[64, 2], [32, 2], [2, 16]]),
                in1=_ap(stt[:, tb_, :], 1, [[64, 2], [32, 2], [2, 16]]), op=ALU.add), r=[tt_], w=[tn_])
            P.op("dve", lambda e, inw=inw, zap=zap: e.tensor_tensor(
                out=rslot(inw).rearrange("p (d g i) -> p d g i", d=2, i=2),
                in0=rslot(inw).rearrange("p (d g i) -> p d g i", d=2, i=2), in1=zap, op=ALU.add),
                r=[tn_] + zt, w=[tn_])
            if step % 8 == 7:
                k8 = step - 7
                base = rslot(k8)
                for d in range(2):
                    if d == 0:
                        oap = _ap(RAb, k8, [[1, 8], [256, 16], [4096, 2]])
                        cols = [zxc(0, k8 + j) for j in range(8)]
                    else:
                        oap = _ap(RAb, 8192 + NCH - 1 - k8, [[-1, 8], [256, 16], [4096, 2]])
                        cols = [zxc(1, NCH - 1 - k8 - j) for j in range(8)]
                    P.op("act", lambda e, d=d, oap=oap, base=base: e.activation(
                        out=oap, in_=_ap(base, d * 32, [[64, 8], [2, 16], [1, 2]]), func=AF.Copy),
                        r=[P.tile("sr", (k8 + j) % 16) for j in range(8)], w=cols)

        pend = []
        pump_state["on"] = do_ssm
        for n in range(16):
            pend.append(attn_block(n))
            if len(pend) > 1:
                attn_finish(*pend.pop(0))
            if n % 4 == 0 and n > 0:
                attn_tb_out(n // 4 - 1)
        attn_finish(*pend.pop(0))
        attn_tb_out(3)
        while do_ssm and pump_state["step"] < NCH:
            scan_step(pump_state["step"])
            pump_state["step"] += 1
        if do_ssm:
            tap("X", ZX, [128, 2, 2, 16, 256], BF16, r=[zxc(d, c) for d in range(2) for c in range(NCH)])
            ygT = RC[:, 0:8192].rearrange("p (c t) -> p c t", c=4)
            Yact = RC[:, 8192:12288].rearrange("p (b g c) -> p b g c", b=2, g=8)
            mgs, tmg = ring_load([(lambda s: s.rearrange("p (g n) -> p g n", g=32), scr_mg)], q="sp", r=[t_scr])
            mgv = mgs.rearrange("p (g n) -> p g n", g=32)
            sov, tso = [], []
            for d in range(2):
                sl, tt_ = ring_load([(lambda s: s.rearrange("p (r g n) -> p r g n", r=2, g=16), scr_so[:, d])], q="sp", r=[t_scr])
                sov.append(sl.rearrange("p (r g n) -> p r g n", r=2, g=16))
                tso.append(tt_)
            zr = [[zxc(d, c) for c in range(NCH)] for d in range(2)]
            for blk in range(4):
                yb = blk % 2
                tya = P.rtile("RC", "back", "yact", yb)
                for gl in range(8):
                    g = blk * 8 + gl
                    par, gp = g // 16, g % 16
                    b, h = halfbank()
                    c0 = h * 256
                    pr = slice(par * 64, (par + 1) * 64)
                    P.op("pe", lambda e, b=b, c0=c0, g=g: e.matmul(psb[b][:, c0:c0 + 256], lhsT=mgv[:, g, :], rhs=U8[:, g, :],
                                                                   start=True, stop=False), r=[tmg, u8t(g)], w=psth(b, h))
                    for ri in range(2):
                        P.op("pe", lambda e, b=b, c0=c0, pr=pr, ri=ri, gp=gp: e.matmul(
                            psb[b][:, c0 + 1:c0 + 256], lhsT=sov[0][pr, ri, gp, :], rhs=ZX[pr, 0, ri, gp, 0:255],
                            start=False, stop=False), r=[tso[0]] + zr[0], w=psth(b, h))
                    for ri in range(2):
                        P.op("pe", lambda e, b=b, c0=c0, pr=pr, ri=ri, gp=gp: e.matmul(
                            psb[b][:, c0:c0 + 255], lhsT=sov[1][pr, ri, gp, :], rhs=ZX[pr, 1, ri, gp, 1:256],
                            start=False, stop=(ri == 1)), r=[tso[1]] + zr[1], w=psth(b, h))
                    if "ypre" in taps and g in (0, 5, 17, 31):
                        P.op("act", lambda e, b=b, c0=c0, g=g: e.activation(out=tf[1][:, 0:256], in_=psb[b][:, c0:c0 + 256], func=AF.Copy),
                             r=psth(b, h), w=[P.tile("tf", 1)])
                        taps.append("ypre%d" % g)
                        tap("ypre%d" % g, tf[1][:, 0:256], [128, 256], r=[P.tile("tf", 1)])
                    P.op("act", lambda e, b=b, c0=c0, yb=yb, gl=gl: e.activation(
                        out=Yact[:, yb, gl, :], in_=psb[b][:, c0:c0 + 256], func=AF.Gelu_apprx_tanh), r=psth(b, h), w=[tya])
                for t in range(8):
                    b, h = halfbank()
                    c0 = h * 256
                    for gl in range(8):
                        P.op("pe", lambda e, b=b, c0=c0, t=t, gl=gl, yb=yb: e.matmul(
                            psb[b][:, c0:c0 + 256], lhsT=rsel[:, t, (7 - gl) * 16:(15 - gl) * 16], rhs=Yact[:, yb, gl, :],
                            start=(gl == 0), stop=(gl == 7)), r=[t_rsel, tya], w=psth(b, h))
                    evacuate(_ap(ygT[:, blk, :], t, [[8, 256]]), psb[b][:, c0:c0 + 256], psth(b, h), [P.rtile("RC", "back", "yg", blk)])
            tyg = [P.rtile("RC", "back", "yg", blk) for blk in range(4)]
            tap("ygT", ygT, [128, 4, T], BF16, r=tyg)
            glus, tglu = ring_load([(lambda s: s[:, 0:2048].rearrange("p (k n) -> p k n", k=4),
                                     w_d["ssm_glu_w"].rearrange("(k p) n -> p k n", p=128))])
            gluv = glus[:, 0:2048].rearrange("p (k n) -> p k n", k=4)
            wos, twos = ring_load([(lambda s: s.rearrange("p (c n) -> p c n", c=4),
                                    w_out[512:1024, :].rearrange("(c p) n -> p c n", p=128))])
            wosv = wos.rearrange("p (c n) -> p c n", c=4)
            tas = [P.rtile("RD", "attn", "attnb", nl) for nl in range(4)]
            for tb in range(NTB):
                for m in range(4):
                    b = nextbank()
                    for k in range(4):
                        P.op("pe", lambda e, b=b, k=k, m=m, tb=tb: e.matmul(
                            psb[b][:], lhsT=gluv[:, k, m * 128:(m + 1) * 128], rhs=ygT[:, k, tb * TB:(tb + 1) * TB],
                            start=(k == 0), stop=(k == 3)), r=[tglu] + tyg, w=pst(b))
                    tt1 = P.tile("tf", 1)
                    P.op("act", lambda e, b=b, m=m: e.activation(out=tf[1][:], in_=psb[b][:], func=AF.Sigmoid,
                                                                bias=pcols[:, 8 + m:9 + m]), r=pst(b) + [t_pc], w=[tt1])
                    P.op("dve", lambda e, m=m, tb=tb: e.tensor_tensor(
                        out=attnb[:, m, :], in0=ygT[:, m, tb * TB:(tb + 1) * TB], in1=tf[1][:], op=ALU.mult),
                        r=[tt1] + tyg, w=tas)
                tap("ssm%d" % tb, attnb, [128, 4, 512], BF16, r=tas)
                rmsnorm(lambda c: attnb[:, c, :], lambda c: tas, lambda c: attnb[:, c, :], lambda c: tas, 4, 512.0,
                        lambda c: pcols[:, 4 + c:5 + c])
                for m in range(KD):
                    b = nextbank()
                    for c in range(4):
                        P.op("pe", lambda e, b=b, c=c, m=m: e.matmul(
                            psb[b][:], lhsT=wosv[:, c, m * 128:(m + 1) * 128], rhs=attnb[:, c, :],
                            start=(c == 0), stop=(c == 3)), r=[twos] + tas, w=pst(b))
                    xt = P.tile("xT", m, tb)
                    P.op("dve", lambda e, b=b, m=m, tb=tb: e.tensor_tensor(
                        out=xT[:, m, tb * TB:(tb + 1) * TB], in0=psb[b][:], in1=xT[:, m, tb * TB:(tb + 1) * TB], op=ALU.add),
                        r=pst(b) + [xt], w=[xt])
        tap("x2", xT[:], [128, KD, T], r=[P.tile("xT", k, tb) for k in range(KD) for tb in range(NTB)])

    if stop_after not in ("ffn1", "load"):
        middle()

    if stop_after not in ("ffn1", "load", "attn"):
        ffn(2, w_d["ffn2_w_gate"], w_d["ffn2_w_up"], w_d["ffn2_w_down"])

    for tb in range(NTB):
        rmsnorm(lambda k, tb=tb: xT[:, k, tb * TB:(tb + 1) * TB],
                lambda k, tb=tb: [P.tile("xT", k, tb)],
                lambda k, tb=tb: xT[:, k, tb * TB:(tb + 1) * TB],
                lambda k, tb=tb: [P.tile("xT", k, tb)], KD, float(D),
                lambda k: gcols[:, 3, k:k + 1])
        for k in range(KD):
            P.dma("sp", outT_d[k * 128:(k + 1) * 128, tb * TB:(tb + 1) * TB], xT[:, k, tb * TB:(tb + 1) * TB],
                  r=[P.tile("xT", k, tb)], stream=s_out)

    P.emit(es, [s_out] + tap_streams)
    es.close()
    return nc, list(tap_out.keys())


def _alibi_eb():
    slopes = 2.0 ** (-8.0 * (np.arange(8) + 1) / 8.0)
    sp = np.arange(128)[:, None]
    tq = np.arange(128)[None, :]
    eb = np.zeros((128, 2, 3, 4, 128), np.float64)
    for kv in range(2):
        for dl in range(3):
            rel = 128 * (dl - 1) + sp - tq
            for hh in range(4):
                v = np.exp(-slopes[kv * 4 + hh] * np.abs(rel))
                eb[:, kv, dl, hh, :] = np.where(np.abs(rel) <= 128, v, 0.0)
    return eb.reshape(128, 24 * 128).astype(np.float32)


_PROG_CACHE = {}


def _in_maps(inputs, taps=(), stop_after=None, cores=NCORES):
    x = np.asarray(inputs["x"], np.float32)
    shared = {}
    for nm in ["ffn1_w_gate", "ffn1_w_up", "ffn1_w_down", "ffn2_w_gate", "ffn2_w_up", "ffn2_w_down",
               "w_in", "w_out", "ssm_glu_w", "norm_ffn1", "norm_mix", "norm_ffn2", "attn_out_norm",
               "ssm_out_norm", "ssm_glu_b", "attn_sinks", "ssm_lambda_re", "ssm_lambda_im", "ssm_log_dt",
               "ssm_b_re", "ssm_b_im", "ssm_c_re", "ssm_c_im", "ssm_d"]:
        a = np.asarray(inputs[nm], np.float32)
        shared[nm] = np.ascontiguousarray(a.reshape(a.shape[1:]))
    shared["final_norm"] = np.ascontiguousarray(np.asarray(inputs["final_norm"], np.float32))
    shared["c_eb"] = _alibi_eb()
    maps = []
    for c in range(cores):
        m = dict(shared)
        m["xT"] = np.ascontiguousarray(x[c].T)
        maps.append(m)
    return maps


def kernel(**inputs):
    key = "main"
    if key not in _PROG_CACHE:
        _PROG_CACHE[key] = build_program()
    nc, _ = _PROG_CACHE[key]
    maps = _in_maps(inputs)
    res = run_bass_kernel_spmd(nc, maps, core_ids=list(range(NCORES)))
    out = np.stack([np.ascontiguousarray(r["outT"].T) for r in res.results], axis=0)
    return out.astype(np.float32)
```

```python
import numpy as np
from contextlib import ExitStack
import concourse.bass as bass
import concourse.mybir as mybir
from concourse.bass_utils import run_bass_kernel_spmd

F32 = mybir.dt.float32
BF16 = mybir.dt.bfloat16
I32 = mybir.dt.int32
AF = mybir.ActivationFunctionType
ALU = mybir.AluOpType

NCORES = 8
T = 2048
TB = 512
NTB = 4
D = 1024
KD = 8
FF = 2816
NF = 22
EPS = 1e-6
FGROUPS = [(0, 4), (4, 8), (8, 12), (12, 16), (16, 20), (20, 22)]
NCH = 256


class _Tile:
    __slots__ = ("name", "last_w", "rd_eng", "rd_dma")

    def __init__(self, name):
        self.name = name
        self.last_w = None
        self.rd_eng = {}
        self.rd_dma = []


class _Stream:
    def __init__(self, name, final_only=False):
        self.name = name
        self.final_only = final_only
        self.count = 0
        self.sem = None


class _Op:
    __slots__ = ("eng", "fn", "deps", "signal", "sigval", "stream", "sidx", "is_dma")

    def __init__(self, eng, fn, stream=None):
        self.eng = eng
        self.fn = fn
        self.deps = []
        self.signal = False
        self.sigval = 0
        self.stream = stream
        self.sidx = 0
        self.is_dma = stream is not None


class _Lazy:
    __slots__ = ("region", "view", "key")

    def __init__(self, region, view, key):
        self.region = region
        self.view = view
        self.key = key


class Prog:
    ENGS = ("pe", "act", "dve", "pool", "sp")

    def __init__(self, nc):
        self.nc = nc
        self.ops = {e: [] for e in self.ENGS}
        self.tiles = {}
        self.streams = []
        self.regions = {}
        self.deferred = None
        self.parked = []

    def tile(self, *key):
        t = self.tiles.get(key)
        if t is None:
            t = _Tile(key)
            self.tiles[key] = t
        return t

    def rtile(self, region, view, *key):
        return _Lazy(region, view, key)

    def _res(self, t):
        return self._rtile(t.region, t.view, *t.key) if isinstance(t, _Lazy) else t

    def drain(self, n=None):
        d = self.parked
        if not d:
            return
        assert self.deferred is None
        k = len(d) if n is None else min(n, len(d))
        for ent in d[:k]:
            if ent[0] == "op":
                self.op(*ent[1:])
            else:
                self.dma(*ent[1:6], stream=ent[6], **ent[7])
        del d[:k]

    def _rtile(self, region, view, *key):
        reg = self.regions.setdefault(region, {"view": None, "tiles": {}, "carry": ({}, [], [])})
        if reg["view"] != view:
            eng_last = dict(reg["carry"][0])
            dmas = list(reg["carry"][1])
            writers = list(reg["carry"][2])
            for t in reg["tiles"].values():
                if t.last_w is not None:
                    writers.append(t.last_w)
                for e, o in t.rd_eng.items():
                    if e not in eng_last or eng_last[e].sidx < o.sidx:
                        eng_last[e] = o
                dmas.extend(t.rd_dma)
            reg["view"] = view
            reg["tiles"] = {}
            reg["carry"] = (eng_last, dmas, writers)
        t = reg["tiles"].get(key)
        if t is None:
            t = _Tile((region, view) + key)
            eng_last, dmas, writers = reg["carry"]
            t.rd_eng = dict(eng_last)
            t.rd_dma = list(dmas) + list(writers)
            reg["tiles"][key] = t
        return t

    def stream(self, name, final_only=False):
        s = _Stream(name, final_only)
        self.streams.append(s)
        return s

    def _add(self, o, r, w):
        r = [self._res(t) for t in r]
        w = [self._res(t) for t in w]
        raw = set()
        deps = set()
        for t in r:
            if t.last_w is not None:
                raw.add(t.last_w)
        for t in w:
            if t.last_w is not None:
                deps.add(t.last_w)
            deps.update(t.rd_eng.values())
            deps.update(t.rd_dma)
        o.sidx = len(self.ops[o.eng])
        for t in r:
            if o.is_dma:
                t.rd_dma.append(o)
            else:
                t.rd_eng[o.eng] = o
        for t in w:
            t.last_w = o
            t.rd_eng = {}
            t.rd_dma = []
        raw.discard(o)
        deps.discard(o)
        for d in raw:
            o.deps.append(d)
        for d in deps:
            if d in raw:
                continue
            if d.is_dma and o.is_dma and d.stream is o.stream and d.stream.final_only:
                continue
            if d.is_dma or o.is_dma or d.eng != o.eng or o.eng != "pe":
                o.deps.append(d)
        self.ops[o.eng].append(o)
        return o

    def op(self, eng, fn, r=(), w=()):
        if self.deferred is not None:
            self.deferred.append(("op", eng, fn, list(r), list(w)))
            return None
        return self._add(_Op(eng, fn), list(r), list(w))

    def dma(self, q, out, in_, r=(), w=(), stream=None, **kw):
        if stream is None:
            stream = self.stream("anon")
        if self.deferred is not None:
            self.deferred.append(("dma", q, out, in_, list(r), list(w), stream, kw))
            return None
        o = _Op(q, lambda e, out=out, in_=in_, kw=kw: e.dma_start(out=out, in_=in_, **kw), stream)
        stream.count += 1
        self._add(o, list(r), list(w))
        o.sigval = 16 * stream.count
        return o

    def emit(self, es: ExitStack, out_streams):
        nc = self.nc
        for e in self.ENGS:
            for o in self.ops[e]:
                for d in o.deps:
                    if not d.is_dma:
                        d.signal = True
        for e in self.ENGS:
            c = 0
            for o in self.ops[e]:
                if not o.is_dma and o.signal:
                    c += 1
                    o.sigval = c
        esem = {e: es.enter_context(nc.semaphore("sem_" + e)) for e in self.ENGS}
        for i, s in enumerate(self.streams):
            if s.count > 0:
                s.sem = es.enter_context(nc.semaphore("ds%d_%s" % (i, s.name)))
        block = es.enter_context(nc.Block())
        handles = {"pe": block.tensor, "act": block.scalar, "dve": block.vector,
                   "pool": block.gpsimd, "sp": block.sync}

        def run(engname, e):
            known = {}
            for o in self.ops[engname]:
                waits = {}
                for d in o.deps:
                    if d.is_dma:
                        sem = d.stream.sem
                        val = 16 * d.stream.count if d.stream.final_only else d.sigval
                    else:
                        sem = esem[d.eng]
                        val = d.sigval
                    k = id(sem)
                    if k not in waits or waits[k][1] < val:
                        waits[k] = (sem, val)
                for k, (sem, val) in waits.items():
                    if known.get(k, 0) < val:
                        e.wait_ge(sem, val)
                        known[k] = val
                ins = o.fn(e)
                if o.is_dma:
                    ins.then_inc(o.stream.sem, 16)
                elif o.signal:
                    ins.then_inc(esem[engname], 1)
            if engname == "sp":
                for s in out_streams:
                    if s.count > 0:
                        e.wait_ge(s.sem, 16 * s.count)

        for engname in self.ENGS:
            def _f(e, engname=engname):
                run(engname, e)
            handles[engname](_f)


def _ap(base, offset_elems, dims):
    return bass.AP(base.tensor, base.offset + offset_elems, [list(base.ap[0])] + [list(d) for d in dims])


def build_program(taps=(), stop_after=None):
    nc = bass.Bass("TRN2", target_bir_lowering=False)
    es = ExitStack()
    P = Prog(nc)
    taps = list(taps)
    tap_out = {}

    def din(name, shape, dt=F32):
        return nc.dram_tensor(name, list(shape), dt, kind="ExternalInput").ap()

    xT_d = din("xT", [D, T])
    outT_d = nc.dram_tensor("outT", [D, T], F32, kind="ExternalOutput").ap()
    w_d = {}
    for nm, shp in [("ffn1_w_gate", [D, FF]), ("ffn1_w_up", [D, FF]), ("ffn1_w_down", [FF, D]),
                    ("ffn2_w_gate", [D, FF]), ("ffn2_w_up", [D, FF]), ("ffn2_w_down", [FF, D]),
                    ("w_in", [D, 1280]), ("w_out", [D, D]), ("ssm_glu_w", [512, 512])]:
        w_d[nm] = din(nm, shp)
    p_d = {}
    for nm, shp in [("norm_ffn1", [D]), ("norm_mix", [D]), ("norm_ffn2", [D]), ("final_norm", [D]),
                    ("attn_out_norm", [512]), ("ssm_out_norm", [512]), ("ssm_glu_b", [512]),
                    ("attn_sinks", [8]), ("ssm_lambda_re", [2, 32, 64]), ("ssm_lambda_im", [2, 32, 64]),
                    ("ssm_log_dt", [2, 32]), ("ssm_b_re", [2, 32, 64, 16]), ("ssm_b_im", [2, 32, 64, 16]),
                    ("ssm_c_re", [2, 32, 16, 64]), ("ssm_c_im", [2, 32, 16, 64]), ("ssm_d", [32, 16]),
                    ("c_eb", [128, 24 * 128])]:
        p_d[nm] = din(nm, shp)

    def sb(name, shape, dt):
        return es.enter_context(nc.sbuf_tensor(name, list(shape), dt))

    xT = sb("xT_sb", [128, KD, T], F32)
    ring = sb("ring", [128, 4, 4096], BF16)
    RA = sb("RA", [128, 16384], BF16)
    RB = sb("RB", [128, 8192], BF16)
    RC = sb("RC", [128, 12288], BF16)
    RD = sb("RD", [128, 7168], BF16)
    gcols = sb("gcols", [128, 4, KD], F32)
    ones = sb("ones", [128, 128], BF16)
    tf = [sb("tf%d" % i, [128, TB], F32) for i in range(3)]
    tb16 = [sb("tb16_%d" % i, [128, TB], BF16) for i in range(2)]
    psb = [es.enter_context(nc.psum_tensor("ps%d" % b, [128, 512], F32)) for b in range(8)]

    def pst(b):
        return [P.tile("ps", b, 0), P.tile("ps", b, 1)]

    def psth(b, h):
        return [P.tile("ps", b, 0), P.tile("ps", b, 1)]

    s_small = P.stream("small", final_only=True)
    s_out = P.stream("out")
    tap_streams = []

    def tap(name, ap, shape, dt=F32, r=()):
        if name not in taps:
            return
        d = nc.dram_tensor("tap_" + name, list(shape), dt, kind="ExternalOutput").ap()
        tap_out[name] = d
        ts_ = P.stream("tap_" + name)
        tap_streams.append(ts_)
        P.dma("sp", d, ap, r=list(r), stream=ts_)

    t_consts = P.tile("consts")
    P.op("pool", lambda e: e.memset(ones[:], 1.0), w=[t_consts])
    nc_allow = es.enter_context(nc.allow_non_contiguous_dma(reason="tiny parameter loads"))
    for i, nm in enumerate(["norm_ffn1", "norm_mix", "norm_ffn2", "final_norm"]):
        P.dma("sp", gcols[:, i, :], p_d[nm].rearrange("(k p) -> p k", p=128), w=[t_consts], stream=s_small)

    xs = [P.stream("x%d" % k) for k in range(KD)]
    for k in range(KD):
        P.dma("sp", xT[:, k, :], xT_d[k * 128:(k + 1) * 128, :],
              w=[P.tile("xT", k, tb) for tb in range(NTB)], stream=xs[k])

    ring_state = {"n": 0}
    rstreams = {q: [P.stream("ring%s%d" % (q, i)) for i in range(4)] for q in ("pool", "sp")}

    def ring_load(parts, q="pool", r=()):
        s = ring_state["n"] % 4
        ring_state["n"] += 1
        t = P.tile("ring", s)
        slot = ring[:, s, :]
        for dst_fn, src in parts:
            P.dma(q, dst_fn(slot), src, r=list(r), w=[t], stream=rstreams[q][s])
        return slot, t

    psrot = {"gu": 0, "dn": 0}
    dnpool = [4, 5, 6, 7]

    def dnbank():
        b = dnpool[psrot["dn"] % len(dnpool)]
        psrot["dn"] += 1
        return b

    def rmsnorm(src_fn, src_tiles_fn, dst_fn, dst_tiles_fn, nchunk, width, gcol_fn, ones_ap=None, bank=None, pre_dve=None):
        b = dnbank() if bank is None else bank
        ps = psb[b]
        oa = ones[:] if ones_ap is None else ones_ap
        for k in range(nchunk):
            sq = tb16[k % 2]
            tq = P.tile("tb16", k % 2)
            P.op("act", lambda e, sq=sq, k=k: e.activation(out=sq[:], in_=src_fn(k), func=AF.Square),
                 r=src_tiles_fn(k), w=[tq])
            P.op("pe", lambda e, sq=sq, k=k, ps=ps: e.matmul(ps[:], lhsT=oa, rhs=sq[:],
                                                             start=(k == 0), stop=(k == nchunk - 1)),
                 r=[tq, t_consts], w=pst(b))
        ttf = P.tile("tf", 0)
        P.op("act", lambda e, ps=ps: e.activation(out=tf[0][:], in_=ps[:], func=AF.Sqrt,
                                                  scale=1.0 / width, bias=epsc[:, 0:1]),
             r=pst(b) + [t_consts], w=[ttf])
        if pre_dve is not None:
            pre_dve()
        P.op("dve", lambda e, ps=ps: e.reciprocal(out=ps[:], in_=tf[0][:]), r=[ttf], w=pst(b))
        for k in range(nchunk):
            if pre_dve is not None:
                pre_dve()
            P.op("dve", lambda e, k=k, ps=ps: e.scalar_tensor_tensor(
                out=dst_fn(k), in0=src_fn(k), scalar=gcol_fn(k), in1=ps[:], op0=ALU.mult, op1=ALU.mult),
                r=src_tiles_fn(k) + pst(b) + [t_consts], w=dst_tiles_fn(k))

    epsc = sb("epsc", [128, 1], F32)
    P.op("pool", lambda e: e.memset(epsc[:], EPS), w=[t_consts])

    ebt = sb("ebt", [128, 24, 128], BF16)
    pcols = sb("pcols", [128, 16], F32)
    s_eb = P.stream("eb")
    t_eb = P.tile("ebt")
    P.dma("pool", ebt[:], p_d["c_eb"].rearrange("p (a q) -> p a q", a=24), w=[t_eb], stream=s_eb, max_dma_last_dim=4096)
    t_pc = P.tile("pcols")
    for kv in range(2):
        P.dma("sp", pcols[kv * 64:(kv + 1) * 64, 0:4],
              p_d["attn_out_norm"][kv * 256:(kv + 1) * 256].rearrange("(c d) -> d c", d=64), w=[t_pc], stream=s_small)
        sk = p_d["attn_sinks"]
        P.dma("sp", pcols[kv * 64:(kv + 1) * 64, 12:16],
              bass.AP(sk.tensor, sk.offset + 4 * kv, [[0, 64], [1, 4]]), w=[t_pc], stream=s_small)
    P.dma("sp", pcols[:, 4:8], p_d["ssm_out_norm"].rearrange("(c p) -> p c", p=128), w=[t_pc], stream=s_small)
    P.dma("sp", pcols[:, 8:12], p_d["ssm_glu_b"].rearrange("(c p) -> p c", p=128), w=[t_pc], stream=s_small)
    P.op("act", lambda e: e.activation(out=pcols[:, 12:16], in_=pcols[:, 12:16], func=AF.Exp), r=[t_pc], w=[t_pc])

    ssc = sb("ssc", [128, 24, 32], F32)
    ssci = sb("ssci", [128, 2, 32], I32)
    identb = sb("identb", [128, 128], BF16)
    identf = sb("identf", [128, 128], F32)
    rsel = sb("rsel", [128, 8, 240], BF16)
    dcol = sb("dcol", [128, 32], F32)
    amat = sb("amat", [128, 2, 2, 16, 2], F32)
    sring = sb("sring", [128, 8, 64], F32)
    stt = sb("stt", [128, 2, 128], F32)
    scr_mg = nc.dram_tensor("scr_mg", [128, 32, 128], BF16, kind="Internal").ap()
    scr_sz = nc.dram_tensor("scr_sz", [128, 2, 32, 128], BF16, kind="Internal").ap()
    scr_so = nc.dram_tensor("scr_so", [128, 2, 2, 16, 128], BF16, kind="Internal").ap()
    t_id = P.tile("ident")
    for idt in (identb, identf):
        P.op("pool", lambda e, idt=idt: e.memset(idt[:], 0.0), w=[t_id])
        P.op("pool", lambda e, idt=idt: e.affine_select(out=idt[:], in_=idt[:], pattern=[[-1, 128]],
                                                       compare_op=ALU.not_equal, fill=1.0, base=0, channel_multiplier=1),
             r=[t_id], w=[t_id])
    t_rsel = P.tile("rsel")
    P.op("pool", lambda e: e.memset(rsel[:], 0.0), w=[t_rsel])
    for a0 in range(8):
        P.op("pool", lambda e, a0=a0: e.tensor_copy(out=rsel[:, a0, 112:128], in_=identb[:, a0 * 16:(a0 + 1) * 16]),
             r=[t_id], w=[t_rsel])
    t_ab = P.tile("arb")
    s_scr = P.stream("scr")
    t_scr = P.tile("scr")

    setup_mark = []
    setup_b = []

    def ssm_setup():
        VW = "setup"
        def sc(i):
            return ssc[:, i, :]
        def st(i):
            return P.tile("ssc", i)
        def vtt(o, a, b, op):
            P.op("dve", lambda e: e.tensor_tensor(out=sc(o), in0=sc(a), in1=sc(b), op=op), r=[st(a), st(b)], w=[st(o)])
        def vts(o, a, s1, s2, op0, op1=None):
            if op1 is None:
                P.op("dve", lambda e: e.tensor_scalar(out=sc(o), in0=sc(a), scalar1=s1, scalar2=None, op0=op0), r=[st(a)], w=[st(o)])
            else:
                P.op("dve", lambda e: e.tensor_scalar(out=sc(o), in0=sc(a), scalar1=s1, scalar2=s2, op0=op0, op1=op1), r=[st(a)], w=[st(o)])
        def vstt(o, a, scal, b, op0, op1):
            P.op("dve", lambda e: e.scalar_tensor_tensor(out=sc(o), in0=sc(a), scalar=scal, in1=sc(b), op0=op0, op1=op1),
                 r=[st(a), st(b)], w=[st(o)])
        LR, LI, LDT, lr, dt, z, th, mag, sn, cs, Are, Aim = range(12)
        t0, t1, t2, t3, t4, t5, cre, cim, p2r, p2i, t6, t7 = range(12, 24)
        for par in range(2):
            for slot, nm in ((LR, "ssm_lambda_re"), (LI, "ssm_lambda_im")):
                src = p_d[nm]
                for d in range(2):
                    P.dma("sp", ssc[par * 64:(par + 1) * 64, slot, d * 16:(d + 1) * 16],
                          bass.AP(src.tensor, src.offset + d * 2048 + par * 16 * 64, [[1, 64], [64, 16]]),
                          w=[st(slot)], stream=s_small)
            src = p_d["ssm_log_dt"]
            P.dma("sp", ssc[par * 64:(par + 1) * 64, LDT, :].rearrange("p (d g) -> p d g", d=2),
                  bass.AP(src.tensor, src.offset + par * 16, [[0, 64], [32, 2], [1, 16]]), w=[st(LDT)], stream=s_small)
        for s_ in range(8):
            src = p_d["ssm_d"]
            P.dma("sp", dcol[s_ * 16:(s_ + 1) * 16, :], bass.AP(src.tensor, src.offset, [[1, 16], [16, 32]]),
                  w=[P.tile("dcol")], stream=s_small)
        big = RC[:, 0:12288].bitcast(F32).rearrange("p (j n) -> p j n", j=12)
        def bg(j):
            return big[:, j, :]
        def bgt(j):
            return P.rtile("RC", VW, "big", j)
        BRE, BIM, CRE, CIM, XR, XI, YR, YI, T0, T1, T2, T3 = range(12)
        for par in range(2):
            for d in range(2):
                for j, nm in ((BRE, "ssm_b_re"), (BIM, "ssm_b_im")):
                    src = p_d[nm]
                    P.dma("sp", big[par * 64:(par + 1) * 64, j, d * 256:(d + 1) * 256].rearrange("p (g h) -> p g h", g=16),
                          bass.AP(src.tensor, src.offset + d * 32768 + par * 16 * 1024, [[16, 64], [1024, 16], [1, 16]]),
                          w=[bgt(j)], stream=s_small)
        cnat = RA[:, 12288:14336].bitcast(F32).rearrange("p (r d b q c) -> p r d b q c", r=2, d=2, b=2, q=2)
        t_cnat = P.rtile("RAx", VW, "cnat")
        for ri, nm in enumerate(("ssm_c_re", "ssm_c_im")):
            src = p_d[nm]
            for d in range(2):
                for blk in range(2):
                    P.dma("sp", cnat[:, ri, d, blk, :, :],
                          bass.AP(src.tensor, src.offset + d * 32768 + blk * 8 * 1024, [[64, 128], [16384, 2], [1, 64]]),
                          w=[t_cnat], stream=s_small)
        P.deferred = []
        vts(lr, LR, -1e-4, None, ALU.min)
        vts(t0, LDT, 1.4426950408889634, None, ALU.mult)
        P.op("dve", lambda e: e.tensor_copy(out=ssci[:, 0, :], in_=sc(t0)), r=[st(t0)], w=[P.tile("ssci", 0)])
        P.op("dve", lambda e: e.tensor_copy(out=sc(t1), in_=ssci[:, 0, :]), r=[P.tile("ssci", 0)], w=[st(t1)])
        vstt(t2, t1, -0.693145751953125, LDT, ALU.mult, ALU.add)
        vstt(t2, t1, -1.42860682030941723212e-6, t2, ALU.mult, ALU.add)
        vts(t3, t2, 1.0 / 362880.0, None, ALU.mult)
        for c in (1.0 / 40320, 1.0 / 5040, 1.0 / 720, 1.0 / 120, 1.0 / 24, 1.0 / 6, 0.5, 1.0):
            vstt(t3, t3, c, t2, ALU.add, ALU.mult)
        vts(t3, t3, 1.0, None, ALU.add)
        P.op("dve", lambda e: e.tensor_scalar(out=ssci[:, 1, :], in0=sc(t1), scalar1=127.0, scalar2=8388608.0,
                                              op0=ALU.add, op1=ALU.mult), r=[st(t1)], w=[P.tile("ssci", 1)])
        P.op("dve", lambda e: e.tensor_tensor(out=sc(dt), in0=sc(t3), in1=ssci[:, 1, :].bitcast(F32), op=ALU.mult),
             r=[st(t3), P.tile("ssci", 1)], w=[st(dt)])
        vtt(z, lr, dt, ALU.mult)
        vtt(th, LI, dt, ALU.mult)
        vts(t3, z, 1.0 / 5040.0, None, ALU.mult)
        for c in (1.0 / 720, 1.0 / 120, 1.0 / 24, 1.0 / 6, 0.5, 1.0):
            vstt(t3, t3, c, z, ALU.add, ALU.mult)
        vts(mag, t3, 1.0, None, ALU.add)
        PI_LO = 3.1415925
        vts(t0, th, 0.15915494309189535, None, ALU.mult)
        P.op("dve", lambda e: e.tensor_copy(out=ssci[:, 0, :], in_=sc(t0)), r=[st(t0)], w=[P.tile("ssci", 0)])
        P.op("dve", lambda e: e.tensor_copy(out=sc(t1), in_=ssci[:, 0, :]), r=[P.tile("ssci", 0)], w=[st(t1)])
        vstt(t2, t1, -6.28125, th, ALU.mult, ALU.add)
        vstt(t2, t1, -1.9353071795864769e-3, t2, ALU.mult, ALU.add)
        vts(t4, t2, -PI_LO, PI_LO, ALU.max, ALU.min)
        P.op("act", lambda e: e.activation(out=sc(sn), in_=sc(t4), func=AF.Sin), r=[st(t4)], w=[st(sn)])
        vts(t5, t2, 1.5707963267948966, None, ALU.add)
        vts(t0, t5, PI_LO, None, ALU.is_gt)
        vstt(t5, t0, -6.283185307179586, t5, ALU.mult, ALU.add)
        vts(t5, t5, -PI_LO, PI_LO, ALU.max, ALU.min)
        P.op("act", lambda e: e.activation(out=sc(cs), in_=sc(t5), func=AF.Sin), r=[st(t5)], w=[st(cs)])
        vtt(Are, mag, cs, ALU.mult)
        vtt(Aim, mag, sn, ALU.mult)
        vts(t0, Are, -1.0, None, ALU.add)
        vtt(t1, t0, lr, ALU.mult)
        vtt(t2, Aim, LI, ALU.mult)
        vtt(t1, t1, t2, ALU.add)
        vtt(t2, Aim, lr, ALU.mult)
        vtt(t3, t0, LI, ALU.mult)
        vtt(t2, t2, t3, ALU.subtract)
        vtt(t3, lr, lr, ALU.mult)
        vtt(t4, LI, LI, ALU.mult)
        vtt(t3, t3, t4, ALU.add)
        P.op("dve", lambda e: e.reciprocal(out=sc(t3), in_=sc(t3)), r=[st(t3)], w=[st(t3)])
        vtt(cre, t1, t3, ALU.mult)
        vtt(cim, t2, t3, ALU.mult)
        def csq(o_r, o_i, a_r, a_i):
            vtt(t0, a_r, a_r, ALU.mult)
            vtt(t1, a_i, a_i, ALU.mult)
            vtt(t2, a_r, a_i, ALU.mult)
            vtt(o_r, t0, t1, ALU.subtract)
            vts(o_i, t2, 2.0, None, ALU.mult)
        csq(p2r, p2i, Are, Aim)
        csq(t6, t7, p2r, p2i)
        csq(p2r, p2i, t6, t7)
        for ro, ri_, slot_, sgn in ((0, 0, p2r, 1.0), (1, 1, p2r, 1.0), (0, 1, p2i, -1.0), (1, 0, p2i, 1.0)):
            P.op("dve", lambda e, ro=ro, ri_=ri_, slot_=slot_, sgn=sgn: e.tensor_scalar(
                out=amat[:, :, ro, :, ri_], in0=sc(slot_).rearrange("p (d g) -> p d g", d=2), scalar1=sgn, scalar2=None,
                op0=ALU.mult), r=[st(slot_)], w=[t_ab])
        for ri, cj in ((0, CRE), (1, CIM)):
            for d in range(2):
                for blk in range(2):
                    b = 7
                    P.op("pe", lambda e, b=b, ri=ri, d=d, blk=blk: e.transpose(
                        psb[b][:, 0:128], cnat[:, ri, d, blk, :, :].rearrange("p q c -> p (q c)"), identf[:]),
                        r=[t_cnat, t_id], w=pst(b))
                    P.op("dve", lambda e, b=b, cj=cj, d=d, blk=blk: e.tensor_copy(
                        out=bg(cj)[:, d * 256 + blk * 128:d * 256 + (blk + 1) * 128], in_=psb[b][:, 0:128]),
                        r=pst(b), w=[bgt(cj)])
        def bc(slot):
            return _ap(sc(slot), 0, [[1, 32], [0, 16]])
        def b3(j):
            return bg(j).rearrange("p (g h) -> p g h", h=16)
        def cmul_bc(o_r, o_i, a_r, a_i, x_r, x_i):
            P.op("dve", lambda e: e.tensor_tensor(out=b3(T0), in0=b3(x_r), in1=bc(a_r), op=ALU.mult), r=[bgt(x_r), st(a_r)], w=[bgt(T0)])
            P.op("dve", lambda e: e.tensor_tensor(out=b3(T1), in0=b3(x_i), in1=bc(a_i), op=ALU.mult), r=[bgt(x_i), st(a_i)], w=[bgt(T1)])
            P.op("dve", lambda e: e.tensor_tensor(out=b3(T2), in0=b3(x_i), in1=bc(a_r), op=ALU.mult), r=[bgt(x_i), st(a_r)], w=[bgt(T2)])
            P.op("dve", lambda e: e.tensor_tensor(out=b3(T3), in0=b3(x_r), in1=bc(a_i), op=ALU.mult), r=[bgt(x_r), st(a_i)], w=[bgt(T3)])
            P.op("dve", lambda e: e.tensor_tensor(out=bg(o_r), in0=bg(T0), in1=bg(T1), op=ALU.subtract), r=[bgt(T0), bgt(T1)], w=[bgt(o_r)])
            P.op("dve", lambda e: e.tensor_tensor(out=bg(o_i), in0=bg(T2), in1=bg(T3), op=ALU.add), r=[bgt(T2), bgt(T3)], w=[bgt(o_i)])
        so16 = RB[:, 0:8192].rearrange("p (d r g t h) -> p d r g t h", d=2, r=2, g=16, t=8)
        t_so = P.rtile("RB", "so16", "so")
        cur = (CRE, CIM)
        nxt = [(XR, XI), (YR, YI)]
        for k in range(1, 9):
            o = nxt[k % 2]
            cmul_bc(o[0], o[1], Are, Aim, cur[0], cur[1])
            cur = o
            for d in range(2):
                slot = (k - 1) if d == 0 else (8 - k)
                for ri in range(2):
                    P.op("dve", lambda e, d=d, ri=ri, slot=slot, cur=cur: e.tensor_scalar(
                        out=so16[:, d, ri, :, slot, :], in0=b3(cur[ri])[:, d * 16:(d + 1) * 16, :],
                        scalar1=(1.0 if ri == 0 else -1.0), scalar2=None, op0=ALU.mult), r=[bgt(cur[ri])], w=[t_so])
        P.dma("sp", scr_so.rearrange("p d r g n -> p (d r g n)"), RB[:, 0:8192], r=[t_so], w=[t_scr], stream=s_scr)
        cpa = RA[:, 14336:15360].rearrange("p (r d g h) -> p r d g h", r=2, d=2, g=16)
        t_cpa = P.rtile("RAx", "cpa", "cpa")
        for ri, cj in ((0, CRE), (1, CIM)):
            P.op("dve", lambda e, ri=ri, cj=cj: e.tensor_scalar(out=cpa[:, ri].rearrange("p d g h -> p (d g) h"), in0=b3(cj),
                                                             scalar1=(1.0 if ri == 0 else -1.0), scalar2=None, op0=ALU.mult),
                 r=[bgt(cj)], w=[t_cpa])
        wa16 = RB[:, 0:8192].rearrange("p (r d g t h) -> p r d g t h", r=2, d=2, g=16, t=8)
        t_wa = P.rtile("RB", "wa16", "wa")
        cmul_bc(XR, XI, cre, cim, BRE, BIM)
        cur = (XR, XI)
        nxt = [(YR, YI), (XR, XI)]
        for tau in range(8):
            for d in range(2):
                slot = (7 - tau) if d == 0 else tau
                for ri in range(2):
                    P.op("dve", lambda e, d=d, ri=ri, slot=slot, cur=cur: e.tensor_copy(
                        out=wa16[:, ri, d, :, slot, :], in_=b3(cur[ri])[:, d * 16:(d + 1) * 16, :]),
                        r=[bgt(cur[ri])], w=[t_wa])
            if tau < 7:
                o = nxt[tau % 2]
                cmul_bc(o[0], o[1], Are, Aim, cur[0], cur[1])
                cur = o
        setup_mark.append(len(P.deferred))
        V3 = "pe_stage"
        ww = RC[:, 0:7680].rearrange("p (b d g j h) -> p b d g j h", b=2, d=2, g=8, j=15)
        cpb = RC[:, 7680:8704].rearrange("p (d g h) -> p d g h", d=2, g=32)
        mgst = RC[:, 8704:10752].rearrange("p (b g n) -> p b g n", b=2, g=8)
        szst = RD[:, 0:4096].rearrange("p (b d g n) -> p b d g n", b=2, d=2, g=8)
        t_cpb = P.rtile("RC", V3, "cpb")
        s_asm = P.stream("asm", final_only=True)
        s_asmw = [P.stream("asmw%d" % i) for i in range(2)]
        for bi in range(2):
            P.op("pool", lambda e, bi=bi: e.memset(ww[:, bi].rearrange("p d g j h -> p (d g j h)"), 0.0),
                 w=[P.rtile("RC", V3, "ww", bi)])
        for ri in range(2):
            for par in range(2):
                P.dma("sp", cpb[ri * 64:(ri + 1) * 64, :, par * 16:(par + 1) * 16, :].rearrange("p d g h -> p d (g h)"),
                      cpa[par * 64:(par + 1) * 64, ri].rearrange("p d g h -> p d (g h)"), r=[t_cpa], w=[t_cpb], stream=s_asm)
        for blk in range(4):
            bi = blk % 2
            par, gp0 = blk // 2, (blk % 2) * 8
            t_ww = P.rtile("RC", V3, "ww", bi)
            for d in range(2):
                j0 = 0 if d == 0 else 7
                for ri in range(2):
                    P.dma("sp", ww[ri * 64:(ri + 1) * 64, bi, d, :, j0:j0 + 8, :].rearrange("p g j h -> p g (j h)"),
                          wa16[par * 64:(par + 1) * 64, ri, d, gp0:gp0 + 8, :, :].rearrange("p g t h -> p g (t h)"),
                          r=[t_wa], w=[t_ww], stream=s_asmw[bi])
            t_mg = P.rtile("RC", V3, "mgst", bi)
            t_sz = P.rtile("RD", V3, "szst", bi)
            for gl in range(8):
                g = blk * 8 + gl
                b = 4 + (g % 2)
                for t in range(8):
                    for d in range(2):
                        P.op("pe", lambda e, b=b, t=t, d=d, gl=gl, bi=bi, g=g: e.matmul(
                            psb[b][:, t * 16:(t + 1) * 16],
                            lhsT=ww[:, bi, d, gl, 7 - t:15 - t, :].rearrange("p j h -> p (j h)"),
                            rhs=cpb[:, d, g, :], start=(d == 0), stop=(d == 1)), r=[t_ww, t_cpb], w=pst(b))
                P.op("dve", lambda e, b=b, bi=bi, gl=gl, g=g: e.scalar_tensor_tensor(
                    out=mgst[:, bi, gl, :], in0=identb[:], scalar=dcol[:, g:g + 1], in1=psb[b][:, 0:128],
                    op0=ALU.mult, op1=ALU.add), r=pst(b) + [t_id, P.tile("dcol")], w=[t_mg])
                for d in range(2):
                    j0 = 0 if d == 0 else 7
                    b2 = 6 + d
                    P.op("pe", lambda e, b2=b2, bi=bi, d=d, gl=gl, j0=j0: e.transpose(
                        psb[b2][:, 0:64].bitcast(BF16), ww[:, bi, d, gl, j0:j0 + 8, :].rearrange("p j h -> p (j h)"), identb[:]),
                        r=[t_ww, t_id], w=pst(b2))
                    P.op("act", lambda e, b2=b2, bi=bi, d=d, gl=gl: e.activation(
                        out=szst[:, bi, d, gl, :], in_=psb[b2][:, 0:64].bitcast(BF16), func=AF.Copy), r=pst(b2), w=[t_sz])
            P.dma("sp", scr_mg[:, blk * 8:(blk + 1) * 8, :], mgst[:, bi], r=[t_mg], w=[t_scr], stream=s_scr)
            P.dma("sp", scr_sz[:, :, blk * 8:(blk + 1) * 8, :], szst[:, bi], r=[t_sz], w=[t_scr], stream=s_scr)
        tap("amat", amat[:], [128, 2, 2, 16, 2], r=[t_ab])
        tap("ssc", ssc[:], [128, 24, 32], r=[st(i) for i in range(24)])

    if "nossm" not in taps:
        ssm_setup()
        P.parked = P.deferred[:setup_mark[0]]
        setup_b.extend(P.deferred[setup_mark[0]:])
        P.deferred = None

    def ffn(idx, wg_d, wu_d, wd_d):
        for half in range(2):
            tbs = [2 * half, 2 * half + 1]
            xn = RA[:, 0:8192].rearrange("p (k t) -> p k t", k=KD)
            h1 = RA[:, 8192:12288].rearrange("p (f t) -> p f t", f=4)
            vname = "ffn%d_%d" % (idx, half)
            for tl, tb in enumerate(tbs):
                xnt = P.rtile("RA", vname, "xn", tl)
                rmsnorm(lambda k, tb=tb: xT[:, k, tb * TB:(tb + 1) * TB],
                        lambda k, tb=tb: [P.tile("xT", k, tb)],
                        lambda k, tl=tl: xn[:, k, tl * TB:(tl + 1) * TB],
                        lambda k, xnt=xnt: [xnt], KD, float(D),
                        lambda k: gcols[:, 0 if idx == 1 else 2, k:k + 1])
            for (f0, f1) in FGROUPS:
                nf = f1 - f0
                wg, twg = ring_load([(lambda s, nf=nf: s[:, 0:KD * nf * 128].rearrange("p (k n) -> p k n", k=KD),
                                      wg_d[:, f0 * 128:f1 * 128].rearrange("(k p) n -> p k n", p=128))])
                wu, twu = ring_load([(lambda s, nf=nf: s[:, 0:KD * nf * 128].rearrange("p (k n) -> p k n", k=KD),
                                      wu_d[:, f0 * 128:f1 * 128].rearrange("(k p) n -> p k n", p=128))])
                wd, twd = ring_load([(lambda s, nf=nf: s[:, 0:nf * D].rearrange("p (k n) -> p k n", k=nf),
                                      wd_d[f0 * 128:f1 * 128, :].rearrange("(k p) n -> p k n", p=128))])
                wgv = wg[:, 0:KD * nf * 128].rearrange("p (k n) -> p k n", k=KD)
                wuv = wu[:, 0:KD * nf * 128].rearrange("p (k n) -> p k n", k=KD)
                wdv = wd[:, 0:nf * D].rearrange("p (k n) -> p k n", k=nf)
                for fi in range(nf):
                    for tl, tb in enumerate(tbs):
                        bg = (psrot["gu"] % 2) * 2
                        psrot["gu"] += 1
                        xnt = P.rtile("RA", vname, "xn", tl)
                        for which, (wv, tw, bb) in enumerate([(wgv, twg, bg), (wuv, twu, bg + 1)]):
                            for k in range(KD):
                                P.op("pe", lambda e, wv=wv, k=k, fi=fi, tl=tl, bb=bb: e.matmul(
                                    psb[bb][:], lhsT=wv[:, k, fi * 128:(fi + 1) * 128],
                                    rhs=xn[:, k, tl * TB:(tl + 1) * TB], start=(k == 0), stop=(k == KD - 1)),
                                    r=[tw, xnt], w=pst(bb))
                        ts = tf[1 + (psrot["gu"] % 2)]
                        tts = P.tile("tf", 1 + (psrot["gu"] % 2))
                        P.op("act", lambda e, ts=ts, bg=bg: e.activation(out=ts[:], in_=psb[bg][:], func=AF.Silu),
                             r=pst(bg), w=[tts])
                        h1t = P.rtile("RA", vname, "h1", fi, tl)
                        P.op("dve", lambda e, ts=ts, bg=bg, fi=fi, tl=tl: e.tensor_tensor(
                            out=h1[:, fi, tl * TB:(tl + 1) * TB], in0=ts[:], in1=psb[bg + 1][:], op=ALU.mult),
                            r=[tts] + pst(bg + 1), w=[h1t])
                        P.drain(2)
                for m in range(KD):
                    for tl, tb in enumerate(tbs):
                        b = dnbank()
                        for fi in range(nf):
                            P.op("pe", lambda e, fi=fi, m=m, tl=tl, b=b, wdv=wdv, nf=nf: e.matmul(
                                psb[b][:], lhsT=wdv[:, fi, m * 128:(m + 1) * 128],
                                rhs=h1[:, fi, tl * TB:(tl + 1) * TB], start=(fi == 0), stop=(fi == nf - 1)),
                                r=[twd, P.rtile("RA", vname, "h1", fi, tl)], w=pst(b))
                        xt = P.tile("xT", m, tb)
                        P.op("dve", lambda e, m=m, tb=tb, b=b: e.scalar_tensor_tensor(
                            out=xT[:, m, tb * TB:(tb + 1) * TB], in0=psb[b][:], scalar=0.5,
                            in1=xT[:, m, tb * TB:(tb + 1) * TB], op0=ALU.mult, op1=ALU.add),
                            r=pst(b) + [xt], w=[xt])
                        P.drain(2)

    if stop_after != "load" and "noffn1" not in taps:
        dnpool[:] = [4, 5, 6]
        ffn(1, w_d["ffn1_w_gate"], w_d["ffn1_w_up"], w_d["ffn1_w_down"])
        dnpool[:] = [4, 5, 6, 7]
    P.drain()
    P.parked = list(setup_b)
    P.drain()
    tap("x1", xT[:], [128, KD, T], r=[P.tile("xT", k, tb) for k in range(KD) for tb in range(NTB)])


    def middle():
        hn = RA[:, 0:8192].rearrange("p (b k t) -> p b k t", b=2, k=KD)
        uT = RA[:, 8192:16384].rearrange("p (c s n) -> p c s n", c=4, s=8)
        qT = RC[:, 0:8192].rearrange("p (c t) -> p c t", c=4)
        kT = RC[:, 8192:10240]
        Vt = RC[:, 10240:12288].rearrange("p (b d) -> p b d", b=16)
        Et = RD[:, 0:1536].rearrange("p (i n) -> p i n", i=3)
        Pt = RD[:, 1536:4608].rearrange("p (i n) -> p i n", i=6)
        attnb = RD[:, 4608:6656].rearrange("p (c n) -> p c n", c=4)
        w_in = w_d["w_in"]
        def wq_parts():
            parts = []
            for kv in range(2):
                for c in range(4):
                    parts.append((lambda s, kv=kv, c=c: s.rearrange("p (k c v d) -> p k c v d", k=KD, c=4, v=2)[:, :, c, kv, :],
                                  w_in[:, kv * 256 + c * 64:kv * 256 + (c + 1) * 64].rearrange("(k p) d -> p k d", p=128)))
            return parts
        wq, twq = ring_load(wq_parts())
        wkv, twkv = ring_load([(lambda s: s[:, 0:2048].rearrange("p (k n) -> p k n", k=KD),
                                w_in[:, 512:768].rearrange("(k p) n -> p k n", p=128))])
        wu, twu = ring_load([(lambda s: s.rearrange("p (k n) -> p k n", k=KD),
                              w_in[:, 768:1280].rearrange("(k p) n -> p k n", p=128))])
        wqv = wq.rearrange("p (k n) -> p k n", k=KD)
        wkvv = wkv[:, 0:2048].rearrange("p (k n) -> p k n", k=KD)
        wuv = wu.rearrange("p (k n) -> p k n", k=KD)
        evac = {"n": 0}

        def evacuate(out_ap, in_ap, r, w):
            P.drain(evac.get("k", 0))
            evac["n"] += 1
            if evac["n"] % 2:
                P.op("act", lambda e: e.activation(out=out_ap, in_=in_ap, func=AF.Copy), r=r, w=w)
            else:
                P.op("dve", lambda e: e.tensor_copy(out=out_ap, in_=in_ap), r=r, w=w)

        def nextbank(pool=None, key="dn"):
            pool = tuple(dnpool) if pool is None else pool
            b = pool[psrot.setdefault(key, 0) % len(pool)]
            psrot[key] += 1
            return b

        def mixnorm(tb):
            hb = tb % 2
            hnt = P.rtile("RA", "mid", "hn", hb)
            rmsnorm(lambda k, tb=tb: xT[:, k, tb * TB:(tb + 1) * TB],
                    lambda k, tb=tb: [P.tile("xT", k, tb)],
                    lambda k, hb=hb: hn[:, hb, k, :], lambda k, hnt=hnt: [hnt], KD, float(D),
                    lambda k: gcols[:, 1, k:k + 1], bank=3)

        mixnorm(0)
        for tb in range(NTB):
            hb = tb % 2
            hnt = P.rtile("RA", "mid", "hn", hb)
            if tb + 1 < NTB:
                mixnorm(tb + 1)
            for c in range(4):
                b = nextbank()
                for k in range(KD):
                    P.op("pe", lambda e, b=b, k=k, c=c, hb=hb: e.matmul(
                        psb[b][:], lhsT=wqv[:, k, c * 128:(c + 1) * 128], rhs=hn[:, hb, k, :],
                        start=(k == 0), stop=(k == KD - 1)), r=[twq, hnt], w=pst(b))
                evacuate(qT[:, c, tb * TB:(tb + 1) * TB], psb[b][:], pst(b), [P.rtile("RC", "qkv", "q", tb)])
            b = nextbank()
            for k in range(KD):
                P.op("pe", lambda e, b=b, k=k, hb=hb: e.matmul(
                    psb[b][:], lhsT=wkvv[:, k, 0:128], rhs=hn[:, hb, k, :],
                    start=(k == 0), stop=(k == KD - 1)), r=[twkv, hnt], w=pst(b))
            evacuate(kT[:, tb * TB:(tb + 1) * TB], psb[b][:], pst(b), [P.rtile("RC", "qkv", "k", tb)])
            b = nextbank()
            for sub in range(4):
                for k in range(KD):
                    P.op("pe", lambda e, b=b, k=k, sub=sub, hb=hb: e.matmul(
                        psb[b][:, sub * 128:(sub + 1) * 128], lhsT=hn[:, hb, k, sub * 128:(sub + 1) * 128],
                        rhs=wkvv[:, k, 128:256], start=(k == 0), stop=(k == KD - 1)), r=[twkv, hnt], w=pst(b))
            evacuate(Vt[:, tb * 4:(tb + 1) * 4, :], psb[b][:].rearrange("p (s d) -> p s d", s=4), pst(b),
                     [P.rtile("RC", "qkv", "v", tb)])
            for c in range(4):
                b = nextbank()
                for k in range(KD):
                    P.op("pe", lambda e, b=b, k=k, c=c, hb=hb: e.matmul(
                        psb[b][:], lhsT=wuv[:, k, c * 128:(c + 1) * 128], rhs=hn[:, hb, k, :],
                        start=(k == 0), stop=(k == KD - 1)), r=[twu, hnt], w=pst(b))
                evacuate(uT[:, c, :, tb * 64:(tb + 1) * 64], psb[b][:].rearrange("p (n s) -> p s n", s=8), pst(b),
                         [P.rtile("RA", "mid", "u", c, tb), P.rtile("RAx", "mid", "u")])
        tap("qT", qT, [128, 4, T], BF16, r=[P.rtile("RC", "qkv", "q", tb) for tb in range(NTB)])
        tap("kT", kT, [128, T], BF16, r=[P.rtile("RC", "qkv", "k", tb) for tb in range(NTB)])
        tap("Vt", Vt, [128, 16, 128], BF16, r=[P.rtile("RC", "qkv", "v", tb) for tb in range(NTB)])

        w_out = w_d["w_out"]
        woa, twoa = ring_load([(lambda s, kv=kv: s.rearrange("p (c n) -> p c n", c=4)[kv * 64:(kv + 1) * 64],
                                w_out[kv * 256:(kv + 1) * 256, :].rearrange("(c d) n -> d c n", d=64)) for kv in range(2)])
        woav = woa.rearrange("p (c n) -> p c n", c=4)

        pump_state = {"acc": 0.0, "step": 0, "on": False}

        def pump():
            if not pump_state["on"]:
                return
            pump_state["acc"] += NCH / 96.0
            while pump_state["acc"] >= 1.0 and pump_state["step"] < NCH:
                scan_step(pump_state["step"])
                pump_state["step"] += 1
                pump_state["acc"] -= 1.0

        def attn_block(n):
            tb, nl = n // 4, n % 4
            bn = 4 + 2 * (n % 2)
            bd = bn + 1
            js = [j for j in (n - 1, n, n + 1) if 0 <= j < 16]
            tiles_ = [(kv, ji, j) for kv in range(2) for ji, j in enumerate(js)]
            pis = {}

            def front(kv, ji, j):
                dl = j - n + 1
                bs = (psrot["gu"] % 3)
                psrot["gu"] += 1
                ei = psrot["gu"] % 3
                pi = psrot["gu"] % 6
                pis[(kv, ji)] = pi
                P.op("pe", lambda e, bs=bs, kv=kv, j=j, n=n: e.matmul(
                    psb[bs][:], lhsT=kT[kv * 64:(kv + 1) * 64, j * 128:(j + 1) * 128],
                    rhs=qT[kv * 64:(kv + 1) * 64, :, n * 128:(n + 1) * 128], start=True, stop=True),
                    r=[P.rtile("RC", "qkv", "k", j // 4), P.rtile("RC", "qkv", "q", tb)], w=pst(bs))
                te = P.rtile("RD", "attn", "E", ei)
                P.op("act", lambda e, bs=bs, ei=ei: e.activation(out=Et[:, ei, :], in_=psb[bs][:], func=AF.Exp, scale=0.125),
                     r=pst(bs), w=[te])
                tp = P.rtile("RD", "attn", "P", pi)
                P.op("pool", lambda e, ei=ei, pi=pi, kv=kv, dl=dl: e.tensor_tensor(
                    out=Pt[:, pi, :], in0=Et[:, ei, :],
                    in1=ebt[:, (kv * 3 + dl) * 4:(kv * 3 + dl) * 4 + 4, :].rearrange("p a q -> p (a q)"), op=ALU.mult),
                    r=[te, t_eb], w=[tp])

            def back(kv, ji, j):
                pi = pis[(kv, ji)]
                tp = P.rtile("RD", "attn", "P", pi)
                P.op("pe", lambda e, bn=bn, kv=kv, j=j, pi=pi, ji=ji: e.matmul(
                    psb[bn][kv * 64:(kv + 1) * 64, :], lhsT=Vt[:, j, kv * 64:(kv + 1) * 64], rhs=Pt[:, pi, :],
                    start=(ji == 0), stop=(ji == len(js) - 1)),
                    r=[tp, P.rtile("RC", "qkv", "v", j // 4)], w=pst(bn))
                P.op("pe", lambda e, bd=bd, kv=kv, pi=pi, ji=ji: e.matmul(
                    psb[bd][kv * 64:(kv + 1) * 64, :], lhsT=ones[:, 0:64], rhs=Pt[:, pi, :],
                    start=(ji == 0), stop=(ji == len(js) - 1)), r=[tp, t_consts], w=pst(bd))

            DEPTH = 2
            for i in range(len(tiles_) + DEPTH):
                if i < len(tiles_):
                    front(*tiles_[i])
                if i >= DEPTH:
                    back(*tiles_[i - DEPTH])
            return (n, bn, bd)

        def attn_finish(n, bn, bd):
            tb, nl = n // 4, n % 4
            tt = P.tile("tf", 0)
            pump()
            P.op("dve", lambda e, bd=bd: e.tensor_tensor(
                out=tf[0][:].rearrange("p (a q) -> p a q", a=4), in0=psb[bd][:].rearrange("p (a q) -> p a q", a=4),
                in1=_ap(pcols[:, 12:16], 0, [[1, 4], [0, 128]]), op=ALU.add), r=pst(bd) + [t_pc], w=[tt])
            P.op("dve", lambda e: e.reciprocal(out=tf[0][:], in_=tf[0][:]), r=[tt], w=[tt])
            pump()
            ta = P.rtile("RD", "attn", "attnb", nl)
            P.op("dve", lambda e, bn=bn, nl=nl: e.tensor_tensor(
                out=attnb[:, :, nl * 128:(nl + 1) * 128], in0=psb[bn][:].rearrange("p (a q) -> p a q", a=4),
                in1=tf[0][:].rearrange("p (a q) -> p a q", a=4), op=ALU.mult), r=pst(bn) + [tt], w=[ta])

        def attn_tb_out(tb):
            tas = [P.rtile("RD", "attn", "attnb", nl) for nl in range(4)]
            tap("attn%d" % tb, attnb, [128, 4, 512], BF16, r=tas)
            rmsnorm(lambda c: attnb[:, c, :], lambda c: tas, lambda c: attnb[:, c, :], lambda c: tas, 4, 512.0,
                    lambda c: pcols[:, c:c + 1], bank=3, pre_dve=pump)
            for m in range(KD):
                b = 3
                for c in range(4):
                    P.op("pe", lambda e, b=b, c=c, m=m: e.matmul(
                        psb[b][:], lhsT=woav[:, c, m * 128:(m + 1) * 128], rhs=attnb[:, c, :],
                        start=(c == 0), stop=(c == 3)), r=[twoa] + tas, w=pst(b))
                xt = P.tile("xT", m, tb)
                P.op("dve", lambda e, b=b, m=m, tb=tb: e.tensor_tensor(
                    out=xT[:, m, tb * TB:(tb + 1) * TB], in0=psb[b][:], in1=xT[:, m, tb * TB:(tb + 1) * TB], op=ALU.add),
                    r=pst(b) + [xt], w=[xt])

        U8 = RB[:, 0:8192].rearrange("p (g c) -> p g c", g=32)
        ZX = RA[:, 0:16384].rearrange("p (d r g c) -> p d r g c", d=2, r=2, g=16)
        hb_state = {"n": 0}

        def halfbank(pool=(4, 5, 6, 7)):
            i = hb_state["n"]
            hb_state["n"] += 1
            return pool[(i // 2) % len(pool)], i % 2

        def u8t(g):
            return P.rtile("RB", "u8", g)

        do_ssm = "nossm" not in taps
        if do_ssm:
            for g in range(32):
                blk, gl = g // 8, g % 8
                b, h = halfbank()
                for s_ in range(8):
                    P.op("pe", lambda e, b=b, h=h, gl=gl, s_=s_, blk=blk: e.matmul(
                        psb[b][:, h * 256:(h + 1) * 256], lhsT=rsel[:, gl, (7 - s_) * 16:(15 - s_) * 16],
                        rhs=uT[:, blk, s_, :], start=(s_ == 0), stop=(s_ == 7)),
                        r=[t_rsel] + [P.rtile("RA", "mid", "u", blk, tb) for tb in range(NTB)], w=psth(b, h))
                evacuate(U8[:, g, :], psb[b][:, h * 256:(h + 1) * 256], psth(b, h), [u8t(g)])
            tap("U8", U8, [128, 32, 256], BF16, r=[u8t(g) for g in range(32)])
            zxall = [P.rtile("RAx", "zx", "all")]
            def zxc(d, c):
                return P.rtile("RA", "zx", d, c)
            for d in range(2):
                szs, tsz = ring_load([(lambda s: s.rearrange("p (g n) -> p g n", g=32), scr_sz[:, d])], q="sp", r=[t_scr])
                szv = szs.rearrange("p (g n) -> p g n", g=32)
                for gp in range(16):
                    for ri in range(2):
                        b, h = halfbank()
                        for par in range(2):
                            g = 16 * par + gp
                            P.op("pe", lambda e, b=b, h=h, par=par, g=g, ri=ri, szv=szv: e.matmul(
                                psb[b][par * 64:(par + 1) * 64, h * 256:(h + 1) * 256],
                                lhsT=szv[:, g, ri * 64:(ri + 1) * 64], rhs=U8[:, g, :], start=True, stop=True),
                                r=[tsz, u8t(g)], w=psth(b, h))
                        P.op("act", lambda e, b=b, h=h, d=d, ri=ri, gp=gp: e.activation(
                            out=ZX[:, d, ri, gp, :], in_=psb[b][:, h * 256:(h + 1) * 256], func=AF.Copy),
                            r=psth(b, h), w=[zxc(d, c) for c in range(NCH)] + zxall)
            tap("Z", ZX, [128, 2, 2, 16, 256], BF16, r=[zxc(d, c) for d in range(2) for c in range(NCH)])
            P.op("pool", lambda e: e.memset(ssc[:, 14:16, :].rearrange("p a b -> p (a b)"), 0.0), w=[P.tile("sr", 15)])
        RAb = RA[:, 0:16384]
        ring2 = ssc[:, 0:16, :].rearrange("p (s a) b -> p s (a b)", s=8)

        def rslot(i):
            i = i % 16
            return sring[:, i, :] if i < 8 else ring2[:, i - 8, :]

        def scan_step(step):
            c0, c1 = step, NCH - 1 - step
            ip, inw, tb_ = (step - 1) % 16, step % 16, step % 2
            tp_, tn_ = P.tile("sr", ip), P.tile("sr", inw)
            tt_ = P.tile("stt", tb_)
            zt = [zxc(0, c0), zxc(1, c1)]
            zap = _ap(RAb, c0, [[8192 + c1 - c0, 2], [256, 16], [4096, 2]])
            P.op("dve", lambda e, ip=ip, tb_=tb_: e.tensor_tensor(
                out=stt[:, tb_, :].rearrange("p (d r b) -> p d r b", d=2, r=2),
                in0=_ap(rslot(ip), 0, [[32, 2], [0, 2], [1, 32]]),
                in1=amat[:].rearrange("p d r g i -> p d r (g i)"), op=ALU.mult), r=[tp_, t_ab], w=[tt_])
            P.op("dve", lambda e, inw=inw, tb_=tb_: e.tensor_tensor(
                out=_ap(rslot(inw), 0, [[32, 2], [1, 2], [2, 16]]),
                in0=_ap(stt[:, tb_, :], 0, [[64, 2], [32, 2], [2, 16]]),
                in1=_ap(stt[:, tb_, :], 1, [[64, 2], [32, 2], [2, 16]]), op=ALU.add), r=[tt_], w=[tn_])
            P.op("dve", lambda e, inw=inw, zap=zap: e.tensor_tensor(
                out=rslot(inw).rearrange("p (d g i) -> p d g i", d=2, i=2),
                in0=rslot(inw).rearrange("p (d g i) -> p d g i", d=2, i=2), in1=zap, op=ALU.add),
                r=[tn_] + zt, w=[tn_])
            if step % 8 == 7:
                k8 = step - 7
                base = rslot(k8)
                for d in range(2):
                    if d == 0:
                        oap = _ap(RAb, k8, [[1, 8], [256, 16], [4096, 2]])
                        cols = [zxc(0, k8 + j) for j in range(8)]
                    else:
                        oap = _ap(RAb, 8192 + NCH - 1 - k8, [[-1, 8], [256, 16], [4096, 2]])
                        cols = [zxc(1, NCH - 1 - k8 - j) for j in range(8)]
                    P.op("act", lambda e, d=d, oap=oap, base=base: e.activation(
                        out=oap, in_=_ap(base, d * 32, [[64, 8], [2, 16], [1, 2]]), func=AF.Copy),
                        r=[P.tile("sr", (k8 + j) % 16) for j in range(8)], w=cols)

        pend = []
        pump_state["on"] = do_ssm
        for n in range(16):
            pend.append(attn_block(n))
            if len(pend) > 1:
                attn_finish(*pend.pop(0))
            if n % 4 == 0 and n > 0:
                attn_tb_out(n // 4 - 1)
        attn_finish(*pend.pop(0))
        attn_tb_out(3)
        while do_ssm and pump_state["step"] < NCH:
            scan_step(pump_state["step"])
            pump_state["step"] += 1
        if do_ssm:
            tap("X", ZX, [128, 2, 2, 16, 256], BF16, r=[zxc(d, c) for d in range(2) for c in range(NCH)])
            ygT = RC[:, 0:8192].rearrange("p (c t) -> p c t", c=4)
            Yact = RC[:, 8192:12288].rearrange("p (b g c) -> p b g c", b=2, g=8)
            mgs, tmg = ring_load([(lambda s: s.rearrange("p (g n) -> p g n", g=32), scr_mg)], q="sp", r=[t_scr])
            mgv = mgs.rearrange("p (g n) -> p g n", g=32)
            sov, tso = [], []
            for d in range(2):
                sl, tt_ = ring_load([(lambda s: s.rearrange("p (r g n) -> p r g n", r=2, g=16), scr_so[:, d])], q="sp", r=[t_scr])
                sov.append(sl.rearrange("p (r g n) -> p r g n", r=2, g=16))
                tso.append(tt_)
            zr = [[zxc(d, c) for c in range(NCH)] for d in range(2)]
            for blk in range(4):
                yb = blk % 2
                tya = P.rtile("RC", "back", "yact", yb)
                for gl in range(8):
                    g = blk * 8 + gl
                    par, gp = g // 16, g % 16
                    b, h = nextbank((0, 1, 2, 3, 4, 5), "yb"), 0
                    c0 = h * 256
                    pr = slice(par * 64, (par + 1) * 64)
                    P.op("pe", lambda e, b=b, c0=c0, g=g: e.matmul(psb[b][:, c0:c0 + 256], lhsT=mgv[:, g, :], rhs=U8[:, g, :],
                                                                   start=True, stop=False), r=[tmg, u8t(g)], w=psth(b, h))
                    for ri in range(2):
                        P.op("pe", lambda e, b=b, c0=c0, pr=pr, ri=ri, gp=gp: e.matmul(
                            psb[b][:, c0 + 1:c0 + 256], lhsT=sov[0][pr, ri, gp, :], rhs=ZX[pr, 0, ri, gp, 0:255],
                            start=False, stop=False), r=[tso[0]] + zr[0], w=psth(b, h))
                    for ri in range(2):
                        P.op("pe", lambda e, b=b, c0=c0, pr=pr, ri=ri, gp=gp: e.matmul(
                            psb[b][:, c0:c0 + 255], lhsT=sov[1][pr, ri, gp, :], rhs=ZX[pr, 1, ri, gp, 1:256],
                            start=False, stop=(ri == 1)), r=[tso[1]] + zr[1], w=psth(b, h))
                    if "ypre" in taps and g in (0, 5, 17, 31):
                        P.op("act", lambda e, b=b, c0=c0, g=g: e.activation(out=tf[1][:, 0:256], in_=psb[b][:, c0:c0 + 256], func=AF.Copy),
                             r=psth(b, h), w=[P.tile("tf", 1)])
                        taps.append("ypre%d" % g)
                        tap("ypre%d" % g, tf[1][:, 0:256], [128, 256], r=[P.tile("tf", 1)])
                    P.op("act", lambda e, b=b, c0=c0, yb=yb, gl=gl: e.activation(
                        out=Yact[:, yb, gl, :], in_=psb[b][:, c0:c0 + 256], func=AF.Gelu_apprx_tanh), r=psth(b, h), w=[tya])
                for t in range(8):
                    b, h = nextbank((6, 7, 0, 1, 2, 3, 4, 5), "rs"), 0
                    c0 = h * 256
                    for gl in range(8):
                        P.op("pe", lambda e, b=b, c0=c0, t=t, gl=gl, yb=yb: e.matmul(
                            psb[b][:, c0:c0 + 256], lhsT=rsel[:, t, (7 - gl) * 16:(15 - gl) * 16], rhs=Yact[:, yb, gl, :],
                            start=(gl == 0), stop=(gl == 7)), r=[t_rsel, tya], w=psth(b, h))
                    evacuate(_ap(ygT[:, blk, :], t, [[8, 256]]), psb[b][:, c0:c0 + 256], psth(b, h), [P.rtile("RC", "back", "yg", blk)])
            tyg = [P.rtile("RC", "back", "yg", blk) for blk in range(4)]
            tap("ygT", ygT, [128, 4, T], BF16, r=tyg)
            glus, tglu = ring_load([(lambda s: s[:, 0:2048].rearrange("p (k n) -> p k n", k=4),
                                     w_d["ssm_glu_w"].rearrange("(k p) n -> p k n", p=128))])
            gluv = glus[:, 0:2048].rearrange("p (k n) -> p k n", k=4)
            wos, twos = ring_load([(lambda s: s.rearrange("p (c n) -> p c n", c=4),
                                    w_out[512:1024, :].rearrange("(c p) n -> p c n", p=128))])
            wosv = wos.rearrange("p (c n) -> p c n", c=4)
            ssmb2 = [attnb, RD[:, 0:2048].rearrange("p (c n) -> p c n", c=4)]
            for tb in range(NTB):
                sb_ = ssmb2[tb % 2]
                tas = [P.rtile("RD", "back", "ssmb", tb % 2)]
                for m in range(4):
                    b = nextbank()
                    for k in range(4):
                        P.op("pe", lambda e, b=b, k=k, m=m, tb=tb: e.matmul(
                            psb[b][:], lhsT=gluv[:, k, m * 128:(m + 1) * 128], rhs=ygT[:, k, tb * TB:(tb + 1) * TB],
                            start=(k == 0), stop=(k == 3)), r=[tglu] + tyg, w=pst(b))
                    ti = 1 + (m % 2)
                    tt1 = P.tile("tf", ti)
                    P.op("act", lambda e, b=b, m=m, ti=ti: e.activation(out=tf[ti][:], in_=psb[b][:], func=AF.Sigmoid,
                                                                       bias=pcols[:, 8 + m:9 + m]), r=pst(b) + [t_pc], w=[tt1])
                    P.op("dve", lambda e, m=m, tb=tb, ti=ti, sb_=sb_: e.tensor_tensor(
                        out=sb_[:, m, :], in0=ygT[:, m, tb * TB:(tb + 1) * TB], in1=tf[ti][:], op=ALU.mult),
                        r=[tt1] + tyg, w=tas)
                tap("ssm%d" % tb, sb_, [128, 4, 512], BF16, r=tas)
                rmsnorm(lambda c, sb_=sb_: sb_[:, c, :], lambda c, tas=tas: tas, lambda c, sb_=sb_: sb_[:, c, :],
                        lambda c, tas=tas: tas, 4, 512.0, lambda c: pcols[:, 4 + c:5 + c])
                for m in range(KD):
                    b = nextbank()
                    for c in range(4):
                        P.op("pe", lambda e, b=b, c=c, m=m, sb_=sb_: e.matmul(
                            psb[b][:], lhsT=wosv[:, c, m * 128:(m + 1) * 128], rhs=sb_[:, c, :],
                            start=(c == 0), stop=(c == 3)), r=[twos] + tas, w=pst(b))
                    xt = P.tile("xT", m, tb)
                    P.op("dve", lambda e, b=b, m=m, tb=tb: e.tensor_tensor(
                        out=xT[:, m, tb * TB:(tb + 1) * TB], in0=psb[b][:], in1=xT[:, m, tb * TB:(tb + 1) * TB], op=ALU.add),
                        r=pst(b) + [xt], w=[xt])
        tap("x2", xT[:], [128, KD, T], r=[P.tile("xT", k, tb) for k in range(KD) for tb in range(NTB)])

    if stop_after not in ("ffn1", "load"):
        middle()

    if stop_after not in ("ffn1", "load", "attn"):
        ffn(2, w_d["ffn2_w_gate"], w_d["ffn2_w_up"], w_d["ffn2_w_down"])

    for tb in range(NTB):
        rmsnorm(lambda k, tb=tb: xT[:, k, tb * TB:(tb + 1) * TB],
                lambda k, tb=tb: [P.tile("xT", k, tb)],
                lambda k, tb=tb: xT[:, k, tb * TB:(tb + 1) * TB],
                lambda k, tb=tb: [P.tile("xT", k, tb)], KD, float(D),
                lambda k: gcols[:, 3, k:k + 1])
        for k in range(KD):
            P.dma("sp", outT_d[k * 128:(k + 1) * 128, tb * TB:(tb + 1) * TB], xT[:, k, tb * TB:(tb + 1) * TB],
                  r=[P.tile("xT", k, tb)], stream=s_out)

    P.emit(es, [s_out] + tap_streams)
    es.close()
    return nc, list(tap_out.keys())


def _alibi_eb():
    slopes = 2.0 ** (-8.0 * (np.arange(8) + 1) / 8.0)
    sp = np.arange(128)[:, None]
    tq = np.arange(128)[None, :]
    eb = np.zeros((128, 2, 3, 4, 128), np.float64)
    for kv in range(2):
        for dl in range(3):
            rel = 128 * (dl - 1) + sp - tq
            for hh in range(4):
                v = np.exp(-slopes[kv * 4 + hh] * np.abs(rel))
                eb[:, kv, dl, hh, :] = np.where(np.abs(rel) <= 128, v, 0.0)
    return eb.reshape(128, 24 * 128).astype(np.float32)


_PROG_CACHE = {}


def _in_maps(inputs, taps=(), stop_after=None, cores=NCORES):
    x = np.asarray(inputs["x"], np.float32)
    shared = {}
    for nm in ["ffn1_w_gate", "ffn1_w_up", "ffn1_w_down", "ffn2_w_gate", "ffn2_w_up", "ffn2_w_down",
               "w_in", "w_out", "ssm_glu_w", "norm_ffn1", "norm_mix", "norm_ffn2", "attn_out_norm",
               "ssm_out_norm", "ssm_glu_b", "attn_sinks", "ssm_lambda_re", "ssm_lambda_im", "ssm_log_dt",
               "ssm_b_re", "ssm_b_im", "ssm_c_re", "ssm_c_im", "ssm_d"]:
        a = np.asarray(inputs[nm], np.float32)
        shared[nm] = np.ascontiguousarray(a.reshape(a.shape[1:]))
    shared["final_norm"] = np.ascontiguousarray(np.asarray(inputs["final_norm"], np.float32))
    shared["c_eb"] = _alibi_eb()
    maps = []
    for c in range(cores):
        m = dict(shared)
        m["xT"] = np.ascontiguousarray(x[c].T)
        maps.append(m)
    return maps


def kernel(**inputs):
    key = "main"
    if key not in _PROG_CACHE:
        _PROG_CACHE[key] = build_program()
    nc, _ = _PROG_CACHE[key]
    maps = _in_maps(inputs)
    res = run_bass_kernel_spmd(nc, maps, core_ids=list(range(NCORES)))
    out = np.stack([np.ascontiguousarray(r["outT"].T) for r in res.results], axis=0)
    return out.astype(np.float32)
```

```python
import numpy as np
from contextlib import ExitStack
import concourse.bass as bass
import concourse.mybir as mybir
from concourse.bass_utils import run_bass_kernel_spmd

F32 = mybir.dt.float32
BF16 = mybir.dt.bfloat16
I32 = mybir.dt.int32
AF = mybir.ActivationFunctionType
ALU = mybir.AluOpType

NCORES = 8
T = 2048
TB = 512
NTB = 4
D = 1024
KD = 8
FF = 2816
NF = 22
EPS = 1e-6
FGROUPS = [(0, 4), (4, 8), (8, 12), (12, 16), (16, 20), (20, 22)]
NCH = 256


class _Tile:
    __slots__ = ("name", "last_w", "rd_eng", "rd_dma")

    def __init__(self, name):
        self.name = name
        self.last_w = None
        self.rd_eng = {}
        self.rd_dma = []


class _Stream:
    def __init__(self, name, final_only=False):
        self.name = name
        self.final_only = final_only
        self.count = 0
        self.sem = None


class _Op:
    __slots__ = ("eng", "fn", "deps", "signal", "sigval", "stream", "sidx", "is_dma")

    def __init__(self, eng, fn, stream=None):
        self.eng = eng
        self.fn = fn
        self.deps = []
        self.signal = False
        self.sigval = 0
        self.stream = stream
        self.sidx = 0
        self.is_dma = stream is not None


class _Lazy:
    __slots__ = ("region", "view", "key")

    def __init__(self, region, view, key):
        self.region = region
        self.view = view
        self.key = key


class Prog:
    ENGS = ("pe", "act", "dve", "pool", "sp")

    def __init__(self, nc):
        self.nc = nc
        self.ops = {e: [] for e in self.ENGS}
        self.tiles = {}
        self.streams = []
        self.regions = {}
        self.deferred = None
        self.parked = []

    def tile(self, *key):
        t = self.tiles.get(key)
        if t is None:
            t = _Tile(key)
            self.tiles[key] = t
        return t

    def rtile(self, region, view, *key):
        return _Lazy(region, view, key)

    def _res(self, t):
        return self._rtile(t.region, t.view, *t.key) if isinstance(t, _Lazy) else t

    def drain(self, n=None):
        d = self.parked
        if not d:
            return
        assert self.deferred is None
        k = len(d) if n is None else min(n, len(d))
        for ent in d[:k]:
            if ent[0] == "op":
                self.op(*ent[1:])
            else:
                self.dma(*ent[1:6], stream=ent[6], **ent[7])
        del d[:k]

    def _rtile(self, region, view, *key):
        reg = self.regions.setdefault(region, {"view": None, "tiles": {}, "carry": ({}, [], [])})
        if reg["view"] != view:
            eng_last = dict(reg["carry"][0])
            dmas = list(reg["carry"][1])
            writers = list(reg["carry"][2])
            for t in reg["tiles"].values():
                if t.last_w is not None:
                    writers.append(t.last_w)
                for e, o in t.rd_eng.items():
                    if e not in eng_last or eng_last[e].sidx < o.sidx:
                        eng_last[e] = o
                dmas.extend(t.rd_dma)
            reg["view"] = view
            reg["tiles"] = {}
            reg["carry"] = (eng_last, dmas, writers)
        t = reg["tiles"].get(key)
        if t is None:
            t = _Tile((region, view) + key)
            eng_last, dmas, writers = reg["carry"]
            t.rd_eng = dict(eng_last)
            t.rd_dma = list(dmas) + list(writers)
            reg["tiles"][key] = t
        return t

    def stream(self, name, final_only=False):
        s = _Stream(name, final_only)
        self.streams.append(s)
        return s

    def _add(self, o, r, w):
        r = [self._res(t) for t in r]
        w = [self._res(t) for t in w]
        raw = set()
        deps = set()
        for t in r:
            if t.last_w is not None:
                raw.add(t.last_w)
        for t in w:
            if t.last_w is not None:
                deps.add(t.last_w)
            deps.update(t.rd_eng.values())
            deps.update(t.rd_dma)
        o.sidx = len(self.ops[o.eng])
        for t in r:
            if o.is_dma:
                t.rd_dma.append(o)
            else:
                t.rd_eng[o.eng] = o
        for t in w:
            t.last_w = o
            t.rd_eng = {}
            t.rd_dma = []
        raw.discard(o)
        deps.discard(o)
        for d in raw:
            o.deps.append(d)
        for d in deps:
            if d in raw:
                continue
            if d.is_dma and o.is_dma and d.stream is o.stream and d.stream.final_only:
                continue
            if d.is_dma or o.is_dma or d.eng != o.eng or o.eng != "pe":
                o.deps.append(d)
        self.ops[o.eng].append(o)
        return o

    def op(self, eng, fn, r=(), w=()):
        if self.deferred is not None:
            self.deferred.append(("op", eng, fn, list(r), list(w)))
            return None
        return self._add(_Op(eng, fn), list(r), list(w))

    def dma(self, q, out, in_, r=(), w=(), stream=None, **kw):
        if stream is None:
            stream = self.stream("anon")
        if self.deferred is not None:
            self.deferred.append(("dma", q, out, in_, list(r), list(w), stream, kw))
            return None
        o = _Op(q, lambda e, out=out, in_=in_, kw=kw: e.dma_start(out=out, in_=in_, **kw), stream)
        stream.count += 1
        self._add(o, list(r), list(w))
        o.sigval = 16 * stream.count
        return o

    def emit(self, es: ExitStack, out_streams):
        nc = self.nc
        for e in self.ENGS:
            for o in self.ops[e]:
                for d in o.deps:
                    if not d.is_dma:
                        d.signal = True
        for e in self.ENGS:
            c = 0
            for o in self.ops[e]:
                if not o.is_dma and o.signal:
                    c += 1
                    o.sigval = c
        esem = {e: es.enter_context(nc.semaphore("sem_" + e)) for e in self.ENGS}
        for i, s in enumerate(self.streams):
            if s.count > 0:
                s.sem = es.enter_context(nc.semaphore("ds%d_%s" % (i, s.name)))
        block = es.enter_context(nc.Block())
        handles = {"pe": block.tensor, "act": block.scalar, "dve": block.vector,
                   "pool": block.gpsimd, "sp": block.sync}

        def run(engname, e):
            known = {}
            for o in self.ops[engname]:
                waits = {}
                for d in o.deps:
                    if d.is_dma:
                        sem = d.stream.sem
                        val = 16 * d.stream.count if d.stream.final_only else d.sigval
                    else:
                        sem = esem[d.eng]
                        val = d.sigval
                    k = id(sem)
                    if k not in waits or waits[k][1] < val:
                        waits[k] = (sem, val)
                for k, (sem, val) in waits.items():
                    if known.get(k, 0) < val:
                        e.wait_ge(sem, val)
                        known[k] = val
                ins = o.fn(e)
                if o.is_dma:
                    ins.then_inc(o.stream.sem, 16)
                elif o.signal:
                    ins.then_inc(esem[engname], 1)
            if engname == "sp":
                for s in out_streams:
                    if s.count > 0:
                        e.wait_ge(s.sem, 16 * s.count)

        for engname in self.ENGS:
            def _f(e, engname=engname):
                run(engname, e)
            handles[engname](_f)


def _ap(base, offset_elems, dims):
    return bass.AP(base.tensor, base.offset + offset_elems, [list(base.ap[0])] + [list(d) for d in dims])


def build_program(taps=(), stop_after=None):
    nc = bass.Bass("TRN2", target_bir_lowering=False)
    es = ExitStack()
    P = Prog(nc)
    taps = list(taps)
    tap_out = {}

    def din(name, shape, dt=F32):
        return nc.dram_tensor(name, list(shape), dt, kind="ExternalInput").ap()

    xT_d = din("xT", [D, T])
    outT_d = nc.dram_tensor("outT", [D, T], F32, kind="ExternalOutput").ap()
    w_d = {}
    for nm, shp in [("ffn1_w_gate", [D, FF]), ("ffn1_w_up", [D, FF]), ("ffn1_w_down", [FF, D]),
                    ("ffn2_w_gate", [D, FF]), ("ffn2_w_up", [D, FF]), ("ffn2_w_down", [FF, D]),
                    ("w_in", [D, 1280]), ("w_out", [D, D]), ("ssm_glu_w", [512, 512])]:
        w_d[nm] = din(nm, shp)
    p_d = {}
    for nm, shp in [("norm_ffn1", [D]), ("norm_mix", [D]), ("norm_ffn2", [D]), ("final_norm", [D]),
                    ("attn_out_norm", [512]), ("ssm_out_norm", [512]), ("ssm_glu_b", [512]),
                    ("attn_sinks", [8]), ("ssm_lambda_re", [2, 32, 64]), ("ssm_lambda_im", [2, 32, 64]),
                    ("ssm_log_dt", [2, 32]), ("ssm_b_re", [2, 32, 64, 16]), ("ssm_b_im", [2, 32, 64, 16]),
                    ("ssm_c_re", [2, 32, 16, 64]), ("ssm_c_im", [2, 32, 16, 64]), ("ssm_d", [32, 16]),
                    ("c_eb", [128, 24 * 128])]:
        p_d[nm] = din(nm, shp)

    def sb(name, shape, dt):
        return es.enter_context(nc.sbuf_tensor(name, list(shape), dt))

    xT = sb("xT_sb", [128, KD, T], F32)
    ring = sb("ring", [128, 4, 4096], BF16)
    RA = sb("RA", [128, 16384], BF16)
    RB = sb("RB", [128, 8192], BF16)
    RC = sb("RC", [128, 12288], BF16)
    RD = sb("RD", [128, 7168], BF16)
    gcols = sb("gcols", [128, 4, KD], F32)
    ones = sb("ones", [128, 128], BF16)
    tf = [sb("tf%d" % i, [128, TB], F32) for i in range(3)]
    tb16 = [sb("tb16_%d" % i, [128, TB], BF16) for i in range(2)]
    psb = [es.enter_context(nc.psum_tensor("ps%d" % b, [128, 512], F32)) for b in range(8)]

    def pst(b):
        return [P.tile("ps", b, 0), P.tile("ps", b, 1)]

    def psth(b, h):
        return [P.tile("ps", b, 0), P.tile("ps", b, 1)]

    s_small = P.stream("small", final_only=True)
    s_out = P.stream("out")
    tap_streams = []

    def tap(name, ap, shape, dt=F32, r=()):
        if name not in taps:
            return
        d = nc.dram_tensor("tap_" + name, list(shape), dt, kind="ExternalOutput").ap()
        tap_out[name] = d
        ts_ = P.stream("tap_" + name)
        tap_streams.append(ts_)
        P.dma("sp", d, ap, r=list(r), stream=ts_)

    t_consts = P.tile("consts")
    P.op("pool", lambda e: e.memset(ones[:], 1.0), w=[t_consts])
    nc_allow = es.enter_context(nc.allow_non_contiguous_dma(reason="tiny parameter loads"))
    for i, nm in enumerate(["norm_ffn1", "norm_mix", "norm_ffn2", "final_norm"]):
        P.dma("sp", gcols[:, i, :], p_d[nm].rearrange("(k p) -> p k", p=128), w=[t_consts], stream=s_small)

    xs = [P.stream("x%d" % i) for i in range(2 * KD)]
    for hf in range(2):
        for k in range(KD):
            P.dma("sp", xT[:, k, hf * 1024:(hf + 1) * 1024], xT_d[k * 128:(k + 1) * 128, hf * 1024:(hf + 1) * 1024],
                  w=[P.tile("xT", k, 2 * hf), P.tile("xT", k, 2 * hf + 1)], stream=xs[hf * KD + k])

    ring_state = {"n": 0}
    rstreams = {q: [P.stream("ring%s%d" % (q, i)) for i in range(4)] for q in ("pool", "sp")}

    def ring_load(parts, q="pool", r=()):
        s = ring_state["n"] % 4
        ring_state["n"] += 1
        t = P.tile("ring", s)
        slot = ring[:, s, :]
        for dst_fn, src in parts:
            P.dma(q, dst_fn(slot), src, r=list(r), w=[t], stream=rstreams[q][s])
        return slot, t

    psrot = {"gu": 0, "dn": 0}
    dnpool = [4, 5, 6, 7]

    def dnbank():
        b = dnpool[psrot["dn"] % len(dnpool)]
        psrot["dn"] += 1
        return b

    def rmsnorm(src_fn, src_tiles_fn, dst_fn, dst_tiles_fn, nchunk, width, gcol_fn, ones_ap=None, bank=None, pre_dve=None):
        b = dnbank() if bank is None else bank
        ps = psb[b]
        oa = ones[:] if ones_ap is None else ones_ap
        for k in range(nchunk):
            sq = tb16[k % 2]
            tq = P.tile("tb16", k % 2)
            P.op("act", lambda e, sq=sq, k=k: e.activation(out=sq[:], in_=src_fn(k), func=AF.Square),
                 r=src_tiles_fn(k), w=[tq])
            P.op("pe", lambda e, sq=sq, k=k, ps=ps: e.matmul(ps[:], lhsT=oa, rhs=sq[:],
                                                             start=(k == 0), stop=(k == nchunk - 1)),
                 r=[tq, t_consts], w=pst(b))
        ttf = P.tile("tf", 0)
        P.op("act", lambda e, ps=ps: e.activation(out=tf[0][:], in_=ps[:], func=AF.Sqrt,
                                                  scale=1.0 / width, bias=epsc[:, 0:1]),
             r=pst(b) + [t_consts], w=[ttf])
        if pre_dve is not None:
            pre_dve()
        P.op("dve", lambda e, ps=ps: e.reciprocal(out=ps[:], in_=tf[0][:]), r=[ttf], w=pst(b))
        for k in range(nchunk):
            if pre_dve is not None:
                pre_dve()
            P.op("dve", lambda e, k=k, ps=ps: e.scalar_tensor_tensor(
                out=dst_fn(k), in0=src_fn(k), scalar=gcol_fn(k), in1=ps[:], op0=ALU.mult, op1=ALU.mult),
                r=src_tiles_fn(k) + pst(b) + [t_consts], w=dst_tiles_fn(k))

    epsc = sb("epsc", [128, 1], F32)
    P.op("pool", lambda e: e.memset(epsc[:], EPS), w=[t_consts])

    ebt = sb("ebt", [128, 24, 128], BF16)
    pcols = sb("pcols", [128, 16], F32)
    s_eb = P.stream("eb")
    t_eb = P.tile("ebt")
    P.dma("pool", ebt[:], p_d["c_eb"].rearrange("p (a q) -> p a q", a=24), w=[t_eb], stream=s_eb, max_dma_last_dim=4096)
    t_pc = P.tile("pcols")
    for kv in range(2):
        P.dma("sp", pcols[kv * 64:(kv + 1) * 64, 0:4],
              p_d["attn_out_norm"][kv * 256:(kv + 1) * 256].rearrange("(c d) -> d c", d=64), w=[t_pc], stream=s_small)
        sk = p_d["attn_sinks"]
        P.dma("sp", pcols[kv * 64:(kv + 1) * 64, 12:16],
              bass.AP(sk.tensor, sk.offset + 4 * kv, [[0, 64], [1, 4]]), w=[t_pc], stream=s_small)
    P.dma("sp", pcols[:, 4:8], p_d["ssm_out_norm"].rearrange("(c p) -> p c", p=128), w=[t_pc], stream=s_small)
    P.dma("sp", pcols[:, 8:12], p_d["ssm_glu_b"].rearrange("(c p) -> p c", p=128), w=[t_pc], stream=s_small)
    P.op("act", lambda e: e.activation(out=pcols[:, 12:16], in_=pcols[:, 12:16], func=AF.Exp), r=[t_pc], w=[t_pc])

    ssc = sb("ssc", [128, 24, 32], F32)
    ssci = sb("ssci", [128, 2, 32], I32)
    identb = sb("identb", [128, 128], BF16)
    identf = sb("identf", [128, 128], F32)
    rsel = sb("rsel", [128, 8, 240], BF16)
    dcol = sb("dcol", [128, 32], F32)
    amat = sb("amat", [128, 2, 2, 16, 2], F32)
    sring = sb("sring", [128, 8, 64], F32)
    stt = sb("stt", [128, 2, 128], F32)
    scr_mg = nc.dram_tensor("scr_mg", [128, 32, 128], BF16, kind="Internal").ap()
    scr_sz = nc.dram_tensor("scr_sz", [128, 2, 32, 128], BF16, kind="Internal").ap()
    scr_so = nc.dram_tensor("scr_so", [128, 2, 2, 16, 128], BF16, kind="Internal").ap()
    t_id = P.tile("ident")
    for idt in (identb, identf):
        P.op("pool", lambda e, idt=idt: e.memset(idt[:], 0.0), w=[t_id])
        P.op("pool", lambda e, idt=idt: e.affine_select(out=idt[:], in_=idt[:], pattern=[[-1, 128]],
                                                       compare_op=ALU.not_equal, fill=1.0, base=0, channel_multiplier=1),
             r=[t_id], w=[t_id])
    t_rsel = P.tile("rsel")
    P.op("pool", lambda e: e.memset(rsel[:], 0.0), w=[t_rsel])
    for a0 in range(8):
        P.op("pool", lambda e, a0=a0: e.tensor_copy(out=rsel[:, a0, 112:128], in_=identb[:, a0 * 16:(a0 + 1) * 16]),
             r=[t_id], w=[t_rsel])
    t_ab = P.tile("arb")
    s_scr = P.stream("scr")
    t_scr = P.tile("scr")

    setup_mark = []
    setup_b = []

    def ssm_setup():
        VW = "setup"
        def sc(i):
            return ssc[:, i, :]
        def st(i):
            return P.tile("ssc", i)
        def vtt(o, a, b, op):
            P.op("dve", lambda e: e.tensor_tensor(out=sc(o), in0=sc(a), in1=sc(b), op=op), r=[st(a), st(b)], w=[st(o)])
        def vts(o, a, s1, s2, op0, op1=None):
            if op1 is None:
                P.op("dve", lambda e: e.tensor_scalar(out=sc(o), in0=sc(a), scalar1=s1, scalar2=None, op0=op0), r=[st(a)], w=[st(o)])
            else:
                P.op("dve", lambda e: e.tensor_scalar(out=sc(o), in0=sc(a), scalar1=s1, scalar2=s2, op0=op0, op1=op1), r=[st(a)], w=[st(o)])
        def vstt(o, a, scal, b, op0, op1):
            P.op("dve", lambda e: e.scalar_tensor_tensor(out=sc(o), in0=sc(a), scalar=scal, in1=sc(b), op0=op0, op1=op1),
                 r=[st(a), st(b)], w=[st(o)])
        LR, LI, LDT, lr, dt, z, th, mag, sn, cs, Are, Aim = range(12)
        t0, t1, t2, t3, t4, t5, cre, cim, p2r, p2i, t6, t7 = range(12, 24)
        for par in range(2):
            for slot, nm in ((LR, "ssm_lambda_re"), (LI, "ssm_lambda_im")):
                src = p_d[nm]
                for d in range(2):
                    P.dma("sp", ssc[par * 64:(par + 1) * 64, slot, d * 16:(d + 1) * 16],
                          bass.AP(src.tensor, src.offset + d * 2048 + par * 16 * 64, [[1, 64], [64, 16]]),
                          w=[st(slot)], stream=s_small)
            src = p_d["ssm_log_dt"]
            P.dma("sp", ssc[par * 64:(par + 1) * 64, LDT, :].rearrange("p (d g) -> p d g", d=2),
                  bass.AP(src.tensor, src.offset + par * 16, [[0, 64], [32, 2], [1, 16]]), w=[st(LDT)], stream=s_small)
        for s_ in range(8):
            src = p_d["ssm_d"]
            P.dma("sp", dcol[s_ * 16:(s_ + 1) * 16, :], bass.AP(src.tensor, src.offset, [[1, 16], [16, 32]]),
                  w=[P.tile("dcol")], stream=s_small)
        big = RC[:, 0:12288].bitcast(F32).rearrange("p (j n) -> p j n", j=12)
        def bg(j):
            return big[:, j, :]
        def bgt(j):
            return P.rtile("RC", VW, "big", j)
        BRE, BIM, CRE, CIM, XR, XI, YR, YI, T0, T1, T2, T3 = range(12)
        for par in range(2):
            for d in range(2):
                for j, nm in ((BRE, "ssm_b_re"), (BIM, "ssm_b_im")):
                    src = p_d[nm]
                    P.dma("sp", big[par * 64:(par + 1) * 64, j, d * 256:(d + 1) * 256].rearrange("p (g h) -> p g h", g=16),
                          bass.AP(src.tensor, src.offset + d * 32768 + par * 16 * 1024, [[16, 64], [1024, 16], [1, 16]]),
                          w=[bgt(j)], stream=s_small)
        cnat = RA[:, 12288:14336].bitcast(F32).rearrange("p (r d b q c) -> p r d b q c", r=2, d=2, b=2, q=2)
        t_cnat = P.rtile("RAx", VW, "cnat")
        for ri, nm in enumerate(("ssm_c_re", "ssm_c_im")):
            src = p_d[nm]
            for d in range(2):
                for blk in range(2):
                    P.dma("sp", cnat[:, ri, d, blk, :, :],
                          bass.AP(src.tensor, src.offset + d * 32768 + blk * 8 * 1024, [[64, 128], [16384, 2], [1, 64]]),
                          w=[t_cnat], stream=s_small)
        P.deferred = []
        vts(lr, LR, -1e-4, None, ALU.min)
        vts(t0, LDT, 1.4426950408889634, None, ALU.mult)
        P.op("dve", lambda e: e.tensor_copy(out=ssci[:, 0, :], in_=sc(t0)), r=[st(t0)], w=[P.tile("ssci", 0)])
        P.op("dve", lambda e: e.tensor_copy(out=sc(t1), in_=ssci[:, 0, :]), r=[P.tile("ssci", 0)], w=[st(t1)])
        vstt(t2, t1, -0.693145751953125, LDT, ALU.mult, ALU.add)
        vstt(t2, t1, -1.42860682030941723212e-6, t2, ALU.mult, ALU.add)
        vts(t3, t2, 1.0 / 362880.0, None, ALU.mult)
        for c in (1.0 / 40320, 1.0 / 5040, 1.0 / 720, 1.0 / 120, 1.0 / 24, 1.0 / 6, 0.5, 1.0):
            vstt(t3, t3, c, t2, ALU.add, ALU.mult)
        vts(t3, t3, 1.0, None, ALU.add)
        P.op("dve", lambda e: e.tensor_scalar(out=ssci[:, 1, :], in0=sc(t1), scalar1=127.0, scalar2=8388608.0,
                                              op0=ALU.add, op1=ALU.mult), r=[st(t1)], w=[P.tile("ssci", 1)])
        P.op("dve", lambda e: e.tensor_tensor(out=sc(dt), in0=sc(t3), in1=ssci[:, 1, :].bitcast(F32), op=ALU.mult),
             r=[st(t3), P.tile("ssci", 1)], w=[st(dt)])
        vtt(z, lr, dt, ALU.mult)
        vtt(th, LI, dt, ALU.mult)
        vts(t3, z, 1.0 / 5040.0, None, ALU.mult)
        for c in (1.0 / 720, 1.0 / 120, 1.0 / 24, 1.0 / 6, 0.5, 1.0):
            vstt(t3, t3, c, z, ALU.add, ALU.mult)
        vts(mag, t3, 1.0, None, ALU.add)
        PI_LO = 3.1415925
        vts(t0, th, 0.15915494309189535, None, ALU.mult)
        P.op("dve", lambda e: e.tensor_copy(out=ssci[:, 0, :], in_=sc(t0)), r=[st(t0)], w=[P.tile("ssci", 0)])
        P.op("dve", lambda e: e.tensor_copy(out=sc(t1), in_=ssci[:, 0, :]), r=[P.tile("ssci", 0)], w=[st(t1)])
        vstt(t2, t1, -6.28125, th, ALU.mult, ALU.add)
        vstt(t2, t1, -1.9353071795864769e-3, t2, ALU.mult, ALU.add)
        vts(t4, t2, -PI_LO, PI_LO, ALU.max, ALU.min)
        P.op("act", lambda e: e.activation(out=sc(sn), in_=sc(t4), func=AF.Sin), r=[st(t4)], w=[st(sn)])
        vts(t5, t2, 1.5707963267948966, None, ALU.add)
        vts(t0, t5, PI_LO, None, ALU.is_gt)
        vstt(t5, t0, -6.283185307179586, t5, ALU.mult, ALU.add)
        vts(t5, t5, -PI_LO, PI_LO, ALU.max, ALU.min)
        P.op("act", lambda e: e.activation(out=sc(cs), in_=sc(t5), func=AF.Sin), r=[st(t5)], w=[st(cs)])
        vtt(Are, mag, cs, ALU.mult)
        vtt(Aim, mag, sn, ALU.mult)
        vts(t0, Are, -1.0, None, ALU.add)
        vtt(t1, t0, lr, ALU.mult)
        vtt(t2, Aim, LI, ALU.mult)
        vtt(t1, t1, t2, ALU.add)
        vtt(t2, Aim, lr, ALU.mult)
        vtt(t3, t0, LI, ALU.mult)
        vtt(t2, t2, t3, ALU.subtract)
        vtt(t3, lr, lr, ALU.mult)
        vtt(t4, LI, LI, ALU.mult)
        vtt(t3, t3, t4, ALU.add)
        P.op("dve", lambda e: e.reciprocal(out=sc(t3), in_=sc(t3)), r=[st(t3)], w=[st(t3)])
        vtt(cre, t1, t3, ALU.mult)
        vtt(cim, t2, t3, ALU.mult)
        def csq(o_r, o_i, a_r, a_i):
            vtt(t0, a_r, a_r, ALU.mult)
            vtt(t1, a_i, a_i, ALU.mult)
            vtt(t2, a_r, a_i, ALU.mult)
            vtt(o_r, t0, t1, ALU.subtract)
            vts(o_i, t2, 2.0, None, ALU.mult)
        csq(p2r, p2i, Are, Aim)
        csq(t6, t7, p2r, p2i)
        csq(p2r, p2i, t6, t7)
        for ro, ri_, slot_, sgn in ((0, 0, p2r, 1.0), (1, 1, p2r, 1.0), (0, 1, p2i, -1.0), (1, 0, p2i, 1.0)):
            P.op("dve", lambda e, ro=ro, ri_=ri_, slot_=slot_, sgn=sgn: e.tensor_scalar(
                out=amat[:, :, ro, :, ri_], in0=sc(slot_).rearrange("p (d g) -> p d g", d=2), scalar1=sgn, scalar2=None,
                op0=ALU.mult), r=[st(slot_)], w=[t_ab])
        for ri, cj in ((0, CRE), (1, CIM)):
            for d in range(2):
                for blk in range(2):
                    b = 7
                    P.op("pe", lambda e, b=b, ri=ri, d=d, blk=blk: e.transpose(
                        psb[b][:, 0:128], cnat[:, ri, d, blk, :, :].rearrange("p q c -> p (q c)"), identf[:]),
                        r=[t_cnat, t_id], w=pst(b))
                    P.op("dve", lambda e, b=b, cj=cj, d=d, blk=blk: e.tensor_copy(
                        out=bg(cj)[:, d * 256 + blk * 128:d * 256 + (blk + 1) * 128], in_=psb[b][:, 0:128]),
                        r=pst(b), w=[bgt(cj)])
        def bc(slot):
            return _ap(sc(slot), 0, [[1, 32], [0, 16]])
        def b3(j):
            return bg(j).rearrange("p (g h) -> p g h", h=16)
        def cmul_bc(o_r, o_i, a_r, a_i, x_r, x_i):
            P.op("dve", lambda e: e.tensor_tensor(out=b3(T0), in0=b3(x_r), in1=bc(a_r), op=ALU.mult), r=[bgt(x_r), st(a_r)], w=[bgt(T0)])
            P.op("dve", lambda e: e.tensor_tensor(out=b3(T1), in0=b3(x_i), in1=bc(a_i), op=ALU.mult), r=[bgt(x_i), st(a_i)], w=[bgt(T1)])
            P.op("dve", lambda e: e.tensor_tensor(out=b3(T2), in0=b3(x_i), in1=bc(a_r), op=ALU.mult), r=[bgt(x_i), st(a_r)], w=[bgt(T2)])
            P.op("dve", lambda e: e.tensor_tensor(out=b3(T3), in0=b3(x_r), in1=bc(a_i), op=ALU.mult), r=[bgt(x_r), st(a_i)], w=[bgt(T3)])
            P.op("dve", lambda e: e.tensor_tensor(out=bg(o_r), in0=bg(T0), in1=bg(T1), op=ALU.subtract), r=[bgt(T0), bgt(T1)], w=[bgt(o_r)])
            P.op("dve", lambda e: e.tensor_tensor(out=bg(o_i), in0=bg(T2), in1=bg(T3), op=ALU.add), r=[bgt(T2), bgt(T3)], w=[bgt(o_i)])
        so16 = RB[:, 0:8192].rearrange("p (d r g t h) -> p d r g t h", d=2, r=2, g=16, t=8)
        t_so = P.rtile("RB", "so16", "so")
        cur = (CRE, CIM)
        nxt = [(XR, XI), (YR, YI)]
        for k in range(1, 9):
            o = nxt[k % 2]
            cmul_bc(o[0], o[1], Are, Aim, cur[0], cur[1])
            cur = o
            for d in range(2):
                slot = (k - 1) if d == 0 else (8 - k)
                for ri in range(2):
                    P.op("dve", lambda e, d=d, ri=ri, slot=slot, cur=cur: e.tensor_scalar(
                        out=so16[:, d, ri, :, slot, :], in0=b3(cur[ri])[:, d * 16:(d + 1) * 16, :],
                        scalar1=(1.0 if ri == 0 else -1.0), scalar2=None, op0=ALU.mult), r=[bgt(cur[ri])], w=[t_so])
        P.dma("sp", scr_so.rearrange("p d r g n -> p (d r g n)"), RB[:, 0:8192], r=[t_so], w=[t_scr], stream=s_scr)
        cpa = RA[:, 14336:15360].rearrange("p (r d g h) -> p r d g h", r=2, d=2, g=16)
        t_cpa = P.rtile("RAx", "cpa", "cpa")
        for ri, cj in ((0, CRE), (1, CIM)):
            P.op("dve", lambda e, ri=ri, cj=cj: e.tensor_scalar(out=cpa[:, ri].rearrange("p d g h -> p (d g) h"), in0=b3(cj),
                                                             scalar1=(1.0 if ri == 0 else -1.0), scalar2=None, op0=ALU.mult),
                 r=[bgt(cj)], w=[t_cpa])
        wa16 = RB[:, 0:8192].rearrange("p (r d g t h) -> p r d g t h", r=2, d=2, g=16, t=8)
        t_wa = P.rtile("RB", "wa16", "wa")
        cmul_bc(XR, XI, cre, cim, BRE, BIM)
        cur = (XR, XI)
        nxt = [(YR, YI), (XR, XI)]
        for tau in range(8):
            for d in range(2):
                slot = (7 - tau) if d == 0 else tau
                for ri in range(2):
                    P.op("dve", lambda e, d=d, ri=ri, slot=slot, cur=cur: e.tensor_copy(
                        out=wa16[:, ri, d, :, slot, :], in_=b3(cur[ri])[:, d * 16:(d + 1) * 16, :]),
                        r=[bgt(cur[ri])], w=[t_wa])
            if tau < 7:
                o = nxt[tau % 2]
                cmul_bc(o[0], o[1], Are, Aim, cur[0], cur[1])
                cur = o
        setup_mark.append(len(P.deferred))
        V3 = "pe_stage"
        ww = RC[:, 0:7680].rearrange("p (b d g j h) -> p b d g j h", b=2, d=2, g=8, j=15)
        cpb = RC[:, 7680:8704].rearrange("p (d g h) -> p d g h", d=2, g=32)
        mgst = RC[:, 8704:10752].rearrange("p (b g n) -> p b g n", b=2, g=8)
        szst = RD[:, 0:4096].rearrange("p (b d g n) -> p b d g n", b=2, d=2, g=8)
        t_cpb = P.rtile("RC", V3, "cpb")
        s_asm = P.stream("asm", final_only=True)
        s_asmw = [P.stream("asmw%d" % i) for i in range(2)]
        for bi in range(2):
            P.op("pool", lambda e, bi=bi: e.memset(ww[:, bi].rearrange("p d g j h -> p (d g j h)"), 0.0),
                 w=[P.rtile("RC", V3, "ww", bi)])
        for ri in range(2):
            for par in range(2):
                P.dma("sp", cpb[ri * 64:(ri + 1) * 64, :, par * 16:(par + 1) * 16, :].rearrange("p d g h -> p d (g h)"),
                      cpa[par * 64:(par + 1) * 64, ri].rearrange("p d g h -> p d (g h)"), r=[t_cpa], w=[t_cpb], stream=s_asm)
        for blk in range(4):
            bi = blk % 2
            par, gp0 = blk // 2, (blk % 2) * 8
            t_ww = P.rtile("RC", V3, "ww", bi)
            for d in range(2):
                j0 = 0 if d == 0 else 7
                for ri in range(2):
                    P.dma("sp", ww[ri * 64:(ri + 1) * 64, bi, d, :, j0:j0 + 8, :].rearrange("p g j h -> p g (j h)"),
                          wa16[par * 64:(par + 1) * 64, ri, d, gp0:gp0 + 8, :, :].rearrange("p g t h -> p g (t h)"),
                          r=[t_wa], w=[t_ww], stream=s_asmw[bi])
            t_mg = P.rtile("RC", V3, "mgst", bi)
            t_sz = P.rtile("RD", V3, "szst", bi)
            for gl in range(8):
                g = blk * 8 + gl
                b = 4 + (g % 2)
                for t in range(8):
                    for d in range(2):
                        P.op("pe", lambda e, b=b, t=t, d=d, gl=gl, bi=bi, g=g: e.matmul(
                            psb[b][:, t * 16:(t + 1) * 16],
                            lhsT=ww[:, bi, d, gl, 7 - t:15 - t, :].rearrange("p j h -> p (j h)"),
                            rhs=cpb[:, d, g, :], start=(d == 0), stop=(d == 1)), r=[t_ww, t_cpb], w=pst(b))
                P.op("dve", lambda e, b=b, bi=bi, gl=gl, g=g: e.scalar_tensor_tensor(
                    out=mgst[:, bi, gl, :], in0=identb[:], scalar=dcol[:, g:g + 1], in1=psb[b][:, 0:128],
                    op0=ALU.mult, op1=ALU.add), r=pst(b) + [t_id, P.tile("dcol")], w=[t_mg])
                for d in range(2):
                    j0 = 0 if d == 0 else 7
                    b2 = 6 + d
                    P.op("pe", lambda e, b2=b2, bi=bi, d=d, gl=gl, j0=j0: e.transpose(
                        psb[b2][:, 0:64].bitcast(BF16), ww[:, bi, d, gl, j0:j0 + 8, :].rearrange("p j h -> p (j h)"), identb[:]),
                        r=[t_ww, t_id], w=pst(b2))
                    P.op("act", lambda e, b2=b2, bi=bi, d=d, gl=gl: e.activation(
                        out=szst[:, bi, d, gl, :], in_=psb[b2][:, 0:64].bitcast(BF16), func=AF.Copy), r=pst(b2), w=[t_sz])
            P.dma("sp", scr_mg[:, blk * 8:(blk + 1) * 8, :], mgst[:, bi], r=[t_mg], w=[t_scr], stream=s_scr)
            P.dma("sp", scr_sz[:, :, blk * 8:(blk + 1) * 8, :], szst[:, bi], r=[t_sz], w=[t_scr], stream=s_scr)
        tap("amat", amat[:], [128, 2, 2, 16, 2], r=[t_ab])
        tap("ssc", ssc[:], [128, 24, 32], r=[st(i) for i in range(24)])

    if "nossm" not in taps:
        ssm_setup()
        P.parked = P.deferred[:setup_mark[0]]
        setup_b.extend(P.deferred[setup_mark[0]:])
        P.deferred = None

    def ffn(idx, wg_d, wu_d, wd_d, between=None):
        for half in range(2):
            tbs = [2 * half, 2 * half + 1]
            xn = RA[:, 0:8192].rearrange("p (k t) -> p k t", k=KD)
            h1 = RA[:, 8192:12288].rearrange("p (f t) -> p f t", f=4)
            vname = "ffn%d_%d" % (idx, half)
            for tl, tb in enumerate(tbs):
                xnt = P.rtile("RA", vname, "xn", tl)
                rmsnorm(lambda k, tb=tb: xT[:, k, tb * TB:(tb + 1) * TB],
                        lambda k, tb=tb: [P.tile("xT", k, tb)],
                        lambda k, tl=tl: xn[:, k, tl * TB:(tl + 1) * TB],
                        lambda k, xnt=xnt: [xnt], KD, float(D),
                        lambda k: gcols[:, 0 if idx == 1 else 2, k:k + 1])
            if between is not None:
                between(half)
            for (f0, f1) in FGROUPS:
                nf = f1 - f0
                wg, twg = ring_load([(lambda s, nf=nf: s[:, 0:KD * nf * 128].rearrange("p (k n) -> p k n", k=KD),
                                      wg_d[:, f0 * 128:f1 * 128].rearrange("(k p) n -> p k n", p=128))])
                wu, twu = ring_load([(lambda s, nf=nf: s[:, 0:KD * nf * 128].rearrange("p (k n) -> p k n", k=KD),
                                      wu_d[:, f0 * 128:f1 * 128].rearrange("(k p) n -> p k n", p=128))])
                wd, twd = ring_load([(lambda s, nf=nf: s[:, 0:nf * D].rearrange("p (k n) -> p k n", k=nf),
                                      wd_d[f0 * 128:f1 * 128, :].rearrange("(k p) n -> p k n", p=128))])
                wgv = wg[:, 0:KD * nf * 128].rearrange("p (k n) -> p k n", k=KD)
                wuv = wu[:, 0:KD * nf * 128].rearrange("p (k n) -> p k n", k=KD)
                wdv = wd[:, 0:nf * D].rearrange("p (k n) -> p k n", k=nf)
                for fi in range(nf):
                    for tl, tb in enumerate(tbs):
                        bg = (psrot["gu"] % 2) * 2
                        psrot["gu"] += 1
                        xnt = P.rtile("RA", vname, "xn", tl)
                        for which, (wv, tw, bb) in enumerate([(wgv, twg, bg), (wuv, twu, bg + 1)]):
                            for k in range(KD):
                                P.op("pe", lambda e, wv=wv, k=k, fi=fi, tl=tl, bb=bb: e.matmul(
                                    psb[bb][:], lhsT=wv[:, k, fi * 128:(fi + 1) * 128],
                                    rhs=xn[:, k, tl * TB:(tl + 1) * TB], start=(k == 0), stop=(k == KD - 1)),
                                    r=[tw, xnt], w=pst(bb))
                        ts = tf[1 + (psrot["gu"] % 2)]
                        tts = P.tile("tf", 1 + (psrot["gu"] % 2))
                        P.op("act", lambda e, ts=ts, bg=bg: e.activation(out=ts[:], in_=psb[bg][:], func=AF.Silu),
                             r=pst(bg), w=[tts])
                        h1t = P.rtile("RA", vname, "h1", fi, tl)
                        P.op("dve", lambda e, ts=ts, bg=bg, fi=fi, tl=tl: e.tensor_tensor(
                            out=h1[:, fi, tl * TB:(tl + 1) * TB], in0=ts[:], in1=psb[bg + 1][:], op=ALU.mult),
                            r=[tts] + pst(bg + 1), w=[h1t])
                        P.drain(2)
                for m in range(KD):
                    for tl, tb in enumerate(tbs):
                        b = dnbank()
                        for fi in range(nf):
                            P.op("pe", lambda e, fi=fi, m=m, tl=tl, b=b, wdv=wdv, nf=nf: e.matmul(
                                psb[b][:], lhsT=wdv[:, fi, m * 128:(m + 1) * 128],
                                rhs=h1[:, fi, tl * TB:(tl + 1) * TB], start=(fi == 0), stop=(fi == nf - 1)),
                                r=[twd, P.rtile("RA", vname, "h1", fi, tl)], w=pst(b))
                        xt = P.tile("xT", m, tb)
                        P.op("dve", lambda e, m=m, tb=tb, b=b: e.scalar_tensor_tensor(
                            out=xT[:, m, tb * TB:(tb + 1) * TB], in0=psb[b][:], scalar=0.5,
                            in1=xT[:, m, tb * TB:(tb + 1) * TB], op0=ALU.mult, op1=ALU.add),
                            r=pst(b) + [xt], w=[xt])
                        P.drain(2)

    if stop_after != "load" and "noffn1" not in taps:
        dnpool[:] = [4, 5, 6]
        ffn(1, w_d["ffn1_w_gate"], w_d["ffn1_w_up"], w_d["ffn1_w_down"])
        dnpool[:] = [4, 5, 6, 7]
    P.drain()
    P.parked = list(setup_b)
    P.drain()
    tap("x1", xT[:], [128, KD, T], r=[P.tile("xT", k, tb) for k in range(KD) for tb in range(NTB)])


    def middle():
        hn = RA[:, 0:8192].rearrange("p (b k t) -> p b k t", b=2, k=KD)
        uT = RA[:, 8192:16384].rearrange("p (c s n) -> p c s n", c=4, s=8)
        qT = RC[:, 0:8192].rearrange("p (c t) -> p c t", c=4)
        kT = RC[:, 8192:10240]
        Vt = RC[:, 10240:12288].rearrange("p (b d) -> p b d", b=16)
        Et = RD[:, 0:1536].rearrange("p (i n) -> p i n", i=3)
        Pt = RD[:, 1536:4608].rearrange("p (i n) -> p i n", i=6)
        attnb = RD[:, 4608:6656].rearrange("p (c n) -> p c n", c=4)
        w_in = w_d["w_in"]
        def wq_parts():
            parts = []
            for kv in range(2):
                for c in range(4):
                    parts.append((lambda s, kv=kv, c=c: s.rearrange("p (k c v d) -> p k c v d", k=KD, c=4, v=2)[:, :, c, kv, :],
                                  w_in[:, kv * 256 + c * 64:kv * 256 + (c + 1) * 64].rearrange("(k p) d -> p k d", p=128)))
            return parts
        wq, twq = ring_load(wq_parts())
        wkv, twkv = ring_load([(lambda s: s[:, 0:2048].rearrange("p (k n) -> p k n", k=KD),
                                w_in[:, 512:768].rearrange("(k p) n -> p k n", p=128))])
        wu, twu = ring_load([(lambda s: s.rearrange("p (k n) -> p k n", k=KD),
                              w_in[:, 768:1280].rearrange("(k p) n -> p k n", p=128))])
        wqv = wq.rearrange("p (k n) -> p k n", k=KD)
        wkvv = wkv[:, 0:2048].rearrange("p (k n) -> p k n", k=KD)
        wuv = wu.rearrange("p (k n) -> p k n", k=KD)
        evac = {"n": 0}

        def evacuate(out_ap, in_ap, r, w):
            P.drain(evac.get("k", 0))
            evac["n"] += 1
            if evac["n"] % 2:
                P.op("act", lambda e: e.activation(out=out_ap, in_=in_ap, func=AF.Copy), r=r, w=w)
            else:
                P.op("dve", lambda e: e.tensor_copy(out=out_ap, in_=in_ap), r=r, w=w)

        def nextbank(pool=None, key="dn"):
            pool = tuple(dnpool) if pool is None else pool
            b = pool[psrot.setdefault(key, 0) % len(pool)]
            psrot[key] += 1
            return b

        def mixnorm(tb):
            hb = tb % 2
            hnt = P.rtile("RA", "mid", "hn", hb)
            rmsnorm(lambda k, tb=tb: xT[:, k, tb * TB:(tb + 1) * TB],
                    lambda k, tb=tb: [P.tile("xT", k, tb)],
                    lambda k, hb=hb: hn[:, hb, k, :], lambda k, hnt=hnt: [hnt], KD, float(D),
                    lambda k: gcols[:, 1, k:k + 1], bank=3)

        mixnorm(0)
        for tb in range(NTB):
            hb = tb % 2
            hnt = P.rtile("RA", "mid", "hn", hb)
            if tb + 1 < NTB:
                mixnorm(tb + 1)
            for c in range(4):
                b = nextbank()
                for k in range(KD):
                    P.op("pe", lambda e, b=b, k=k, c=c, hb=hb: e.matmul(
                        psb[b][:], lhsT=wqv[:, k, c * 128:(c + 1) * 128], rhs=hn[:, hb, k, :],
                        start=(k == 0), stop=(k == KD - 1)), r=[twq, hnt], w=pst(b))
                evacuate(qT[:, c, tb * TB:(tb + 1) * TB], psb[b][:], pst(b), [P.rtile("RC", "qkv", "q", tb)])
            b = nextbank()
            for k in range(KD):
                P.op("pe", lambda e, b=b, k=k, hb=hb: e.matmul(
                    psb[b][:], lhsT=wkvv[:, k, 0:128], rhs=hn[:, hb, k, :],
                    start=(k == 0), stop=(k == KD - 1)), r=[twkv, hnt], w=pst(b))
            evacuate(kT[:, tb * TB:(tb + 1) * TB], psb[b][:], pst(b), [P.rtile("RC", "qkv", "k", tb)])
            b = nextbank()
            for sub in range(4):
                for k in range(KD):
                    P.op("pe", lambda e, b=b, k=k, sub=sub, hb=hb: e.matmul(
                        psb[b][:, sub * 128:(sub + 1) * 128], lhsT=hn[:, hb, k, sub * 128:(sub + 1) * 128],
                        rhs=wkvv[:, k, 128:256], start=(k == 0), stop=(k == KD - 1)), r=[twkv, hnt], w=pst(b))
            evacuate(Vt[:, tb * 4:(tb + 1) * 4, :], psb[b][:].rearrange("p (s d) -> p s d", s=4), pst(b),
                     [P.rtile("RC", "qkv", "v", tb)])
            for c in range(4):
                b = nextbank()
                for k in range(KD):
                    P.op("pe", lambda e, b=b, k=k, c=c, hb=hb: e.matmul(
                        psb[b][:], lhsT=wuv[:, k, c * 128:(c + 1) * 128], rhs=hn[:, hb, k, :],
                        start=(k == 0), stop=(k == KD - 1)), r=[twu, hnt], w=pst(b))
                evacuate(uT[:, c, :, tb * 64:(tb + 1) * 64], psb[b][:].rearrange("p (n s) -> p s n", s=8), pst(b),
                         [P.rtile("RA", "mid", "u", c, tb), P.rtile("RAx", "mid", "u")])
        tap("qT", qT, [128, 4, T], BF16, r=[P.rtile("RC", "qkv", "q", tb) for tb in range(NTB)])
        tap("kT", kT, [128, T], BF16, r=[P.rtile("RC", "qkv", "k", tb) for tb in range(NTB)])
        tap("Vt", Vt, [128, 16, 128], BF16, r=[P.rtile("RC", "qkv", "v", tb) for tb in range(NTB)])

        w_out = w_d["w_out"]
        woa, twoa = ring_load([(lambda s, kv=kv: s.rearrange("p (c n) -> p c n", c=4)[kv * 64:(kv + 1) * 64],
                                w_out[kv * 256:(kv + 1) * 256, :].rearrange("(c d) n -> d c n", d=64)) for kv in range(2)])
        woav = woa.rearrange("p (c n) -> p c n", c=4)

        pump_state = {"acc": 0.0, "step": 0, "on": False}

        def pump():
            if not pump_state["on"]:
                return
            pump_state["acc"] += NCH / 96.0
            while pump_state["acc"] >= 1.0 and pump_state["step"] < NCH:
                scan_step(pump_state["step"])
                pump_state["step"] += 1
                pump_state["acc"] -= 1.0

        def attn_block(n):
            tb, nl = n // 4, n % 4
            bn = 4 + 2 * (n % 2)
            bd = bn + 1
            js = [j for j in (n - 1, n, n + 1) if 0 <= j < 16]
            tiles_ = [(kv, ji, j) for kv in range(2) for ji, j in enumerate(js)]
            pis = {}

            def front(kv, ji, j):
                dl = j - n + 1
                bs = (psrot["gu"] % 3)
                psrot["gu"] += 1
                ei = psrot["gu"] % 3
                pi = psrot["gu"] % 6
                pis[(kv, ji)] = pi
                P.op("pe", lambda e, bs=bs, kv=kv, j=j, n=n: e.matmul(
                    psb[bs][:], lhsT=kT[kv * 64:(kv + 1) * 64, j * 128:(j + 1) * 128],
                    rhs=qT[kv * 64:(kv + 1) * 64, :, n * 128:(n + 1) * 128], start=True, stop=True),
                    r=[P.rtile("RC", "qkv", "k", j // 4), P.rtile("RC", "qkv", "q", tb)], w=pst(bs))
                te = P.rtile("RD", "attn", "E", ei)
                P.op("act", lambda e, bs=bs, ei=ei: e.activation(out=Et[:, ei, :], in_=psb[bs][:], func=AF.Exp, scale=0.125),
                     r=pst(bs), w=[te])
                tp = P.rtile("RD", "attn", "P", pi)
                P.op("pool", lambda e, ei=ei, pi=pi, kv=kv, dl=dl: e.tensor_tensor(
                    out=Pt[:, pi, :], in0=Et[:, ei, :],
                    in1=ebt[:, (kv * 3 + dl) * 4:(kv * 3 + dl) * 4 + 4, :].rearrange("p a q -> p (a q)"), op=ALU.mult),
                    r=[te, t_eb], w=[tp])

            def back(kv, ji, j):
                pi = pis[(kv, ji)]
                tp = P.rtile("RD", "attn", "P", pi)
                P.op("pe", lambda e, bn=bn, kv=kv, j=j, pi=pi, ji=ji: e.matmul(
                    psb[bn][kv * 64:(kv + 1) * 64, :], lhsT=Vt[:, j, kv * 64:(kv + 1) * 64], rhs=Pt[:, pi, :],
                    start=(ji == 0), stop=(ji == len(js) - 1)),
                    r=[tp, P.rtile("RC", "qkv", "v", j // 4)], w=pst(bn))
                P.op("pe", lambda e, bd=bd, kv=kv, pi=pi, ji=ji: e.matmul(
                    psb[bd][kv * 64:(kv + 1) * 64, :], lhsT=ones[:, 0:64], rhs=Pt[:, pi, :],
                    start=(ji == 0), stop=(ji == len(js) - 1)), r=[tp, t_consts], w=pst(bd))

            DEPTH = 2
            for i in range(len(tiles_) + DEPTH):
                if i < len(tiles_):
                    front(*tiles_[i])
                if i >= DEPTH:
                    back(*tiles_[i - DEPTH])
            return (n, bn, bd)

        def attn_finish(n, bn, bd):
            tb, nl = n // 4, n % 4
            tt = P.tile("tf", 0)
            pump()
            P.op("dve", lambda e, bd=bd: e.tensor_tensor(
                out=tf[0][:].rearrange("p (a q) -> p a q", a=4), in0=psb[bd][:].rearrange("p (a q) -> p a q", a=4),
                in1=_ap(pcols[:, 12:16], 0, [[1, 4], [0, 128]]), op=ALU.add), r=pst(bd) + [t_pc], w=[tt])
            P.op("dve", lambda e: e.reciprocal(out=tf[0][:], in_=tf[0][:]), r=[tt], w=[tt])
            pump()
            ta = P.rtile("RD", "attn", "attnb", nl)
            P.op("dve", lambda e, bn=bn, nl=nl: e.tensor_tensor(
                out=attnb[:, :, nl * 128:(nl + 1) * 128], in0=psb[bn][:].rearrange("p (a q) -> p a q", a=4),
                in1=tf[0][:].rearrange("p (a q) -> p a q", a=4), op=ALU.mult), r=pst(bn) + [tt], w=[ta])

        def attn_tb_out(tb):
            tas = [P.rtile("RD", "attn", "attnb", nl) for nl in range(4)]
            tap("attn%d" % tb, attnb, [128, 4, 512], BF16, r=tas)
            rmsnorm(lambda c: attnb[:, c, :], lambda c: tas, lambda c: attnb[:, c, :], lambda c: tas, 4, 512.0,
                    lambda c: pcols[:, c:c + 1], bank=3, pre_dve=pump)
            for m in range(KD):
                b = 3
                for c in range(4):
                    P.op("pe", lambda e, b=b, c=c, m=m: e.matmul(
                        psb[b][:], lhsT=woav[:, c, m * 128:(m + 1) * 128], rhs=attnb[:, c, :],
                        start=(c == 0), stop=(c == 3)), r=[twoa] + tas, w=pst(b))
                xt = P.tile("xT", m, tb)
                P.op("dve", lambda e, b=b, m=m, tb=tb: e.tensor_tensor(
                    out=xT[:, m, tb * TB:(tb + 1) * TB], in0=psb[b][:], in1=xT[:, m, tb * TB:(tb + 1) * TB], op=ALU.add),
                    r=pst(b) + [xt], w=[xt])

        U8 = RB[:, 0:8192].rearrange("p (g c) -> p g c", g=32)
        ZX = RA[:, 0:16384].rearrange("p (d r g c) -> p d r g c", d=2, r=2, g=16)
        hb_state = {"n": 0}

        def halfbank(pool=(4, 5, 6, 7)):
            i = hb_state["n"]
            hb_state["n"] += 1
            return pool[(i // 2) % len(pool)], i % 2

        def u8t(g):
            return P.rtile("RB", "u8", g)

        do_ssm = "nossm" not in taps
        if do_ssm:
            for g in range(32):
                blk, gl = g // 8, g % 8
                b, h = halfbank()
                for s_ in range(8):
                    P.op("pe", lambda e, b=b, h=h, gl=gl, s_=s_, blk=blk: e.matmul(
                        psb[b][:, h * 256:(h + 1) * 256], lhsT=rsel[:, gl, (7 - s_) * 16:(15 - s_) * 16],
                        rhs=uT[:, blk, s_, :], start=(s_ == 0), stop=(s_ == 7)),
                        r=[t_rsel] + [P.rtile("RA", "mid", "u", blk, tb) for tb in range(NTB)], w=psth(b, h))
                evacuate(U8[:, g, :], psb[b][:, h * 256:(h + 1) * 256], psth(b, h), [u8t(g)])
            tap("U8", U8, [128, 32, 256], BF16, r=[u8t(g) for g in range(32)])
            zxall = [P.rtile("RAx", "zx", "all")]
            def zxc(d, c):
                return P.rtile("RA", "zx", d, c)
            for d in range(2):
                szs, tsz = ring_load([(lambda s: s.rearrange("p (g n) -> p g n", g=32), scr_sz[:, d])], q="sp", r=[t_scr])
                szv = szs.rearrange("p (g n) -> p g n", g=32)
                for gp in range(16):
                    for ri in range(2):
                        b, h = halfbank()
                        for par in range(2):
                            g = 16 * par + gp
                            P.op("pe", lambda e, b=b, h=h, par=par, g=g, ri=ri, szv=szv: e.matmul(
                                psb[b][par * 64:(par + 1) * 64, h * 256:(h + 1) * 256],
                                lhsT=szv[:, g, ri * 64:(ri + 1) * 64], rhs=U8[:, g, :], start=True, stop=True),
                                r=[tsz, u8t(g)], w=psth(b, h))
                        P.op("act", lambda e, b=b, h=h, d=d, ri=ri, gp=gp: e.activation(
                            out=ZX[:, d, ri, gp, :], in_=psb[b][:, h * 256:(h + 1) * 256], func=AF.Copy),
                            r=psth(b, h), w=[zxc(d, c) for c in range(NCH)] + zxall)
            tap("Z", ZX, [128, 2, 2, 16, 256], BF16, r=[zxc(d, c) for d in range(2) for c in range(NCH)])
            P.op("pool", lambda e: e.memset(ssc[:, 14:16, :].rearrange("p a b -> p (a b)"), 0.0), w=[P.tile("sr", 15)])
        RAb = RA[:, 0:16384]
        ring2 = ssc[:, 0:16, :].rearrange("p (s a) b -> p s (a b)", s=8)

        def rslot(i):
            i = i % 16
            return sring[:, i, :] if i < 8 else ring2[:, i - 8, :]

        def scan_step(step):
            c0, c1 = step, NCH - 1 - step
            ip, inw, tb_ = (step - 1) % 16, step % 16, step % 2
            tp_, tn_ = P.tile("sr", ip), P.tile("sr", inw)
            tt_ = P.tile("stt", tb_)
            zt = [zxc(0, c0), zxc(1, c1)]
            zap = _ap(RAb, c0, [[8192 + c1 - c0, 2], [256, 16], [4096, 2]])
            P.op("dve", lambda e, ip=ip, tb_=tb_: e.tensor_tensor(
                out=stt[:, tb_, :].rearrange("p (d r b) -> p d r b", d=2, r=2),
                in0=_ap(rslot(ip), 0, [[32, 2], [0, 2], [1, 32]]),
                in1=amat[:].rearrange("p d r g i -> p d r (g i)"), op=ALU.mult), r=[tp_, t_ab], w=[tt_])
            P.op("dve", lambda e, inw=inw, tb_=tb_: e.tensor_tensor(
                out=_ap(rslot(inw), 0, [[32, 2], [1, 2], [2, 16]]),
                in0=_ap(stt[:, tb_, :], 0, [[64, 2], [32, 2], [2, 16]]),
                in1=_ap(stt[:, tb_, :], 1, [[64, 2], [32, 2], [2, 16]]), op=ALU.add), r=[tt_], w=[tn_])
            P.op("dve", lambda e, inw=inw, zap=zap: e.tensor_tensor(
                out=rslot(inw).rearrange("p (d g i) -> p d g i", d=2, i=2),
                in0=rslot(inw).rearrange("p (d g i) -> p d g i", d=2, i=2), in1=zap, op=ALU.add),
                r=[tn_] + zt, w=[tn_])
            if step % 8 == 7:
                k8 = step - 7
                base = rslot(k8)
                for d in range(2):
                    if d == 0:
                        oap = _ap(RAb, k8, [[1, 8], [256, 16], [4096, 2]])
                        cols = [zxc(0, k8 + j) for j in range(8)]
                    else:
                        oap = _ap(RAb, 8192 + NCH - 1 - k8, [[-1, 8], [256, 16], [4096, 2]])
                        cols = [zxc(1, NCH - 1 - k8 - j) for j in range(8)]
                    P.op("act", lambda e, d=d, oap=oap, base=base: e.activation(
                        out=oap, in_=_ap(base, d * 32, [[64, 8], [2, 16], [1, 2]]), func=AF.Copy),
                        r=[P.tile("sr", (k8 + j) % 16) for j in range(8)], w=cols)

        pend = []
        pump_state["on"] = do_ssm
        for n in range(16):
            pend.append(attn_block(n))
            if len(pend) > 1:
                attn_finish(*pend.pop(0))
            if n % 4 == 0 and n > 0:
                attn_tb_out(n // 4 - 1)
        attn_finish(*pend.pop(0))
        attn_tb_out(3)
        while do_ssm and pump_state["step"] < NCH:
            scan_step(pump_state["step"])
            pump_state["step"] += 1
        if do_ssm:
            tap("X", ZX, [128, 2, 2, 16, 256], BF16, r=[zxc(d, c) for d in range(2) for c in range(NCH)])
            ygT = RC[:, 0:8192].rearrange("p (c t) -> p c t", c=4)
            Yact = RC[:, 8192:12288].rearrange("p (b g c) -> p b g c", b=2, g=8)
            mgs, tmg = ring_load([(lambda s: s.rearrange("p (g n) -> p g n", g=32), scr_mg)], q="sp", r=[t_scr])
            mgv = mgs.rearrange("p (g n) -> p g n", g=32)
            sov, tso = [], []
            for d in range(2):
                sl, tt_ = ring_load([(lambda s: s.rearrange("p (r g n) -> p r g n", r=2, g=16), scr_so[:, d])], q="sp", r=[t_scr])
                sov.append(sl.rearrange("p (r g n) -> p r g n", r=2, g=16))
                tso.append(tt_)
            zr = [[zxc(d, c) for c in range(NCH)] for d in range(2)]
            for blk in range(4):
                yb = blk % 2
                tya = P.rtile("RC", "back", "yact", yb)
                for gl in range(8):
                    g = blk * 8 + gl
                    par, gp = g // 16, g % 16
                    b, h = nextbank((0, 1, 2, 3, 4, 5), "yb"), 0
                    c0 = h * 256
                    pr = slice(par * 64, (par + 1) * 64)
                    P.op("pe", lambda e, b=b, c0=c0, g=g: e.matmul(psb[b][:, c0:c0 + 256], lhsT=mgv[:, g, :], rhs=U8[:, g, :],
                                                                   start=True, stop=False), r=[tmg, u8t(g)], w=psth(b, h))
                    for ri in range(2):
                        P.op("pe", lambda e, b=b, c0=c0, pr=pr, ri=ri, gp=gp: e.matmul(
                            psb[b][:, c0 + 1:c0 + 256], lhsT=sov[0][pr, ri, gp, :], rhs=ZX[pr, 0, ri, gp, 0:255],
                            start=False, stop=False), r=[tso[0]] + zr[0], w=psth(b, h))
                    for ri in range(2):
                        P.op("pe", lambda e, b=b, c0=c0, pr=pr, ri=ri, gp=gp: e.matmul(
                            psb[b][:, c0:c0 + 255], lhsT=sov[1][pr, ri, gp, :], rhs=ZX[pr, 1, ri, gp, 1:256],
                            start=False, stop=(ri == 1)), r=[tso[1]] + zr[1], w=psth(b, h))
                    if "ypre" in taps and g in (0, 5, 17, 31):
                        P.op("act", lambda e, b=b, c0=c0, g=g: e.activation(out=tf[1][:, 0:256], in_=psb[b][:, c0:c0 + 256], func=AF.Copy),
                             r=psth(b, h), w=[P.tile("tf", 1)])
                        taps.append("ypre%d" % g)
                        tap("ypre%d" % g, tf[1][:, 0:256], [128, 256], r=[P.tile("tf", 1)])
                    P.op("act", lambda e, b=b, c0=c0, yb=yb, gl=gl: e.activation(
                        out=Yact[:, yb, gl, :], in_=psb[b][:, c0:c0 + 256], func=AF.Gelu_apprx_tanh), r=psth(b, h), w=[tya])
                for t in range(8):
                    b, h = nextbank((6, 7, 0, 1, 2, 3, 4, 5), "rs"), 0
                    c0 = h * 256
                    for gl in range(8):
                        P.op("pe", lambda e, b=b, c0=c0, t=t, gl=gl, yb=yb: e.matmul(
                            psb[b][:, c0:c0 + 256], lhsT=rsel[:, t, (7 - gl) * 16:(15 - gl) * 16], rhs=Yact[:, yb, gl, :],
                            start=(gl == 0), stop=(gl == 7)), r=[t_rsel, tya], w=psth(b, h))
                    evacuate(_ap(ygT[:, blk, :], t, [[8, 256]]), psb[b][:, c0:c0 + 256], psth(b, h), [P.rtile("RC", "back", "yg", blk)])
            tyg = [P.rtile("RC", "back", "yg", blk) for blk in range(4)]
            tap("ygT", ygT, [128, 4, T], BF16, r=tyg)
            glus, tglu = ring_load([(lambda s: s[:, 0:2048].rearrange("p (k n) -> p k n", k=4),
                                     w_d["ssm_glu_w"].rearrange("(k p) n -> p k n", p=128))])
            gluv = glus[:, 0:2048].rearrange("p (k n) -> p k n", k=4)
            wos, twos = ring_load([(lambda s: s.rearrange("p (c n) -> p c n", c=4),
                                    w_out[512:1024, :].rearrange("(c p) n -> p c n", p=128))])
            wosv = wos.rearrange("p (c n) -> p c n", c=4)
            ssmb2 = [attnb, RD[:, 0:2048].rearrange("p (c n) -> p c n", c=4)]
            for tb in range(NTB):
                sb_ = ssmb2[tb % 2]
                tas = [P.rtile("RD", "back", "ssmb", tb % 2)]
                for m in range(4):
                    b = nextbank()
                    for k in range(4):
                        P.op("pe", lambda e, b=b, k=k, m=m, tb=tb: e.matmul(
                            psb[b][:], lhsT=gluv[:, k, m * 128:(m + 1) * 128], rhs=ygT[:, k, tb * TB:(tb + 1) * TB],
                            start=(k == 0), stop=(k == 3)), r=[tglu] + tyg, w=pst(b))
                    ti = 1 + (m % 2)
                    tt1 = P.tile("tf", ti)
                    P.op("act", lambda e, b=b, m=m, ti=ti: e.activation(out=tf[ti][:], in_=psb[b][:], func=AF.Sigmoid,
                                                                       bias=pcols[:, 8 + m:9 + m]), r=pst(b) + [t_pc], w=[tt1])
                    P.op("dve", lambda e, m=m, tb=tb, ti=ti, sb_=sb_: e.tensor_tensor(
                        out=sb_[:, m, :], in0=ygT[:, m, tb * TB:(tb + 1) * TB], in1=tf[ti][:], op=ALU.mult),
                        r=[tt1] + tyg, w=tas)
                tap("ssm%d" % tb, sb_, [128, 4, 512], BF16, r=tas)
                rmsnorm(lambda c, sb_=sb_: sb_[:, c, :], lambda c, tas=tas: tas, lambda c, sb_=sb_: sb_[:, c, :],
                        lambda c, tas=tas: tas, 4, 512.0, lambda c: pcols[:, 4 + c:5 + c])
                for m in range(KD):
                    b = nextbank()
                    for c in range(4):
                        P.op("pe", lambda e, b=b, c=c, m=m, sb_=sb_: e.matmul(
                            psb[b][:], lhsT=wosv[:, c, m * 128:(m + 1) * 128], rhs=sb_[:, c, :],
                            start=(c == 0), stop=(c == 3)), r=[twos] + tas, w=pst(b))
                    xt = P.tile("xT", m, tb)
                    P.op("dve", lambda e, b=b, m=m, tb=tb: e.tensor_tensor(
                        out=xT[:, m, tb * TB:(tb + 1) * TB], in0=psb[b][:], in1=xT[:, m, tb * TB:(tb + 1) * TB], op=ALU.add),
                        r=pst(b) + [xt], w=[xt])
        tap("x2", xT[:], [128, KD, T], r=[P.tile("xT", k, tb) for k in range(KD) for tb in range(NTB)])

    if stop_after not in ("ffn1", "load"):
        middle()

    def final_out(tbs):
        for tb in tbs:
            rmsnorm(lambda k, tb=tb: xT[:, k, tb * TB:(tb + 1) * TB],
                    lambda k, tb=tb: [P.tile("xT", k, tb)],
                    lambda k, tb=tb: xT[:, k, tb * TB:(tb + 1) * TB],
                    lambda k, tb=tb: [P.tile("xT", k, tb)], KD, float(D),
                    lambda k: gcols[:, 3, k:k + 1])
            for k in range(KD):
                P.dma("sp", outT_d[k * 128:(k + 1) * 128, tb * TB:(tb + 1) * TB], xT[:, k, tb * TB:(tb + 1) * TB],
                      r=[P.tile("xT", k, tb)], stream=s_out)

    if stop_after not in ("ffn1", "load", "attn"):
        ffn(2, w_d["ffn2_w_gate"], w_d["ffn2_w_up"], w_d["ffn2_w_down"],
            between=lambda half: final_out([0, 1]) if half == 1 else None)
        final_out([2, 3])
    else:
        final_out([0, 1, 2, 3])

    P.emit(es, [s_out] + tap_streams)
    es.close()
    return nc, list(tap_out.keys())


def _alibi_eb():
    slopes = 2.0 ** (-8.0 * (np.arange(8) + 1) / 8.0)
    sp = np.arange(128)[:, None]
    tq = np.arange(128)[None, :]
    eb = np.zeros((128, 2, 3, 4, 128), np.float64)
    for kv in range(2):
        for dl in range(3):
            rel = 128 * (dl - 1) + sp - tq
            for hh in range(4):
                v = np.exp(-slopes[kv * 4 + hh] * np.abs(rel))
                eb[:, kv, dl, hh, :] = np.where(np.abs(rel) <= 128, v, 0.0)
    return eb.reshape(128, 24 * 128).astype(np.float32)


_PROG_CACHE = {}


def _in_maps(inputs, taps=(), stop_after=None, cores=NCORES):
    x = np.asarray(inputs["x"], np.float32)
    shared = {}
    for nm in ["ffn1_w_gate", "ffn1_w_up", "ffn1_w_down", "ffn2_w_gate", "ffn2_w_up", "ffn2_w_down",
               "w_in", "w_out", "ssm_glu_w", "norm_ffn1", "norm_mix", "norm_ffn2", "attn_out_norm",
               "ssm_out_norm", "ssm_glu_b", "attn_sinks", "ssm_lambda_re", "ssm_lambda_im", "ssm_log_dt",
               "ssm_b_re", "ssm_b_im", "ssm_c_re", "ssm_c_im", "ssm_d"]:
        a = np.asarray(inputs[nm], np.float32)
        shared[nm] = np.ascontiguousarray(a.reshape(a.shape[1:]))
    shared["final_norm"] = np.ascontiguousarray(np.asarray(inputs["final_norm"], np.float32))
    shared["c_eb"] = _alibi_eb()
    maps = []
    for c in range(cores):
        m = dict(shared)
        m["xT"] = np.ascontiguousarray(x[c].T)
        maps.append(m)
    return maps


def kernel(**inputs):
    key = "main"
    if key not in _PROG_CACHE:
        _PROG_CACHE[key] = build_program()
    nc, _ = _PROG_CACHE[key]
    maps = _in_maps(inputs)
    res = run_bass_kernel_spmd(nc, maps, core_ids=list(range(NCORES)))
    out = np.stack([np.ascontiguousarray(r["outT"].T) for r in res.results], axis=0)
    return out.astype(np.float32)
```

```python
import numpy as np
from contextlib import ExitStack
import concourse.bass as bass
import concourse.mybir as mybir
from concourse.bass_utils import run_bass_kernel_spmd

F32 = mybir.dt.float32
BF16 = mybir.dt.bfloat16
I32 = mybir.dt.int32
AF = mybir.ActivationFunctionType
ALU = mybir.AluOpType

NCORES = 8
T = 2048
TB = 512
NTB = 4
D = 1024
KD = 8
FF = 2816
NF = 22
EPS = 1e-6
FGROUPS = [(0, 4), (4, 8), (8, 12), (12, 16), (16, 20), (20, 22)]
NCH = 256


class _Tile:
    __slots__ = ("name", "last_w", "rd_eng", "rd_dma")

    def __init__(self, name):
        self.name = name
        self.last_w = None
        self.rd_eng = {}
        self.rd_dma = []


class _Stream:
    def __init__(self, name, final_only=False):
        self.name = name
        self.final_only = final_only
        self.count = 0
        self.sem = None


class _Op:
    __slots__ = ("eng", "fn", "deps", "signal", "sigval", "stream", "sidx", "is_dma")

    def __init__(self, eng, fn, stream=None):
        self.eng = eng
        self.fn = fn
        self.deps = []
        self.signal = False
        self.sigval = 0
        self.stream = stream
        self.sidx = 0
        self.is_dma = stream is not None


class _Lazy:
    __slots__ = ("region", "view", "key")

    def __init__(self, region, view, key):
        self.region = region
        self.view = view
        self.key = key


class Prog:
    ENGS = ("pe", "act", "dve", "pool", "sp")

    def __init__(self, nc):
        self.nc = nc
        self.ops = {e: [] for e in self.ENGS}
        self.tiles = {}
        self.streams = []
        self.regions = {}
        self.deferred = None
        self.parked = []

    def tile(self, *key):
        t = self.tiles.get(key)
        if t is None:
            t = _Tile(key)
            self.tiles[key] = t
        return t

    def rtile(self, region, view, *key):
        return _Lazy(region, view, key)

    def _res(self, t):
        return self._rtile(t.region, t.view, *t.key) if isinstance(t, _Lazy) else t

    def drain(self, n=None):
        d = self.parked
        if not d:
            return
        assert self.deferred is None
        k = len(d) if n is None else min(n, len(d))
        for ent in d[:k]:
            if ent[0] == "op":
                self.op(*ent[1:])
            else:
                self.dma(*ent[1:6], stream=ent[6], **ent[7])
        del d[:k]

    def _rtile(self, region, view, *key):
        reg = self.regions.setdefault(region, {"view": None, "tiles": {}, "carry": ({}, [], [])})
        if reg["view"] != view:
            eng_last = dict(reg["carry"][0])
            dmas = list(reg["carry"][1])
            writers = list(reg["carry"][2])
            for t in reg["tiles"].values():
                if t.last_w is not None:
                    writers.append(t.last_w)
                for e, o in t.rd_eng.items():
                    if e not in eng_last or eng_last[e].sidx < o.sidx:
                        eng_last[e] = o
                dmas.extend(t.rd_dma)
            reg["view"] = view
            reg["tiles"] = {}
            reg["carry"] = (eng_last, dmas, writers)
        t = reg["tiles"].get(key)
        if t is None:
            t = _Tile((region, view) + key)
            eng_last, dmas, writers = reg["carry"]
            t.rd_eng = dict(eng_last)
            t.rd_dma = list(dmas) + list(writers)
            reg["tiles"][key] = t
        return t

    def stream(self, name, final_only=False):
        s = _Stream(name, final_only)
        self.streams.append(s)
        return s

    def _add(self, o, r, w):
        r = [self._res(t) for t in r]
        w = [self._res(t) for t in w]
        raw = set()
        deps = set()
        for t in r:
            if t.last_w is not None:
                raw.add(t.last_w)
        for t in w:
            if t.last_w is not None:
                deps.add(t.last_w)
            deps.update(t.rd_eng.values())
            deps.update(t.rd_dma)
        o.sidx = len(self.ops[o.eng])
        for t in r:
            if o.is_dma:
                t.rd_dma.append(o)
            else:
                t.rd_eng[o.eng] = o
        for t in w:
            t.last_w = o
            t.rd_eng = {}
            t.rd_dma = []
        raw.discard(o)
        deps.discard(o)
        for d in raw:
            o.deps.append(d)
        for d in deps:
            if d in raw:
                continue
            if d.is_dma and o.is_dma and d.stream is o.stream and d.stream.final_only:
                continue
            if d.is_dma or o.is_dma or d.eng != o.eng or o.eng != "pe":
                o.deps.append(d)
        self.ops[o.eng].append(o)
        return o

    def op(self, eng, fn, r=(), w=()):
        if self.deferred is not None:
            self.deferred.append(("op", eng, fn, list(r), list(w)))
            return None
        return self._add(_Op(eng, fn), list(r), list(w))

    def dma(self, q, out, in_, r=(), w=(), stream=None, **kw):
        if stream is None:
            stream = self.stream("anon")
        if self.deferred is not None:
            self.deferred.append(("dma", q, out, in_, list(r), list(w), stream, kw))
            return None
        o = _Op(q, lambda e, out=out, in_=in_, kw=kw: e.dma_start(out=out, in_=in_, **kw), stream)
        stream.count += 1
        self._add(o, list(r), list(w))
        o.sigval = 16 * stream.count
        return o

    def emit(self, es: ExitStack, out_streams):
        nc = self.nc
        for e in self.ENGS:
            for o in self.ops[e]:
                for d in o.deps:
                    if not d.is_dma:
                        d.signal = True
        for e in self.ENGS:
            c = 0
            for o in self.ops[e]:
                if not o.is_dma and o.signal:
                    c += 1
                    o.sigval = c
        esem = {e: es.enter_context(nc.semaphore("sem_" + e)) for e in self.ENGS}
        for i, s in enumerate(self.streams):
            if s.count > 0:
                s.sem = es.enter_context(nc.semaphore("ds%d_%s" % (i, s.name)))
        block = es.enter_context(nc.Block())
        handles = {"pe": block.tensor, "act": block.scalar, "dve": block.vector,
                   "pool": block.gpsimd, "sp": block.sync}

        def run(engname, e):
            known = {}
            for o in self.ops[engname]:
                waits = {}
                for d in o.deps:
                    if d.is_dma:
                        sem = d.stream.sem
                        val = 16 * d.stream.count if d.stream.final_only else d.sigval
                    else:
                        sem = esem[d.eng]
                        val = d.sigval
                    k = id(sem)
                    if k not in waits or waits[k][1] < val:
                        waits[k] = (sem, val)
                for k, (sem, val) in waits.items():
                    if known.get(k, 0) < val:
                        e.wait_ge(sem, val)
                        known[k] = val
                ins = o.fn(e)
                if o.is_dma:
                    ins.then_inc(o.stream.sem, 16)
                elif o.signal:
                    ins.then_inc(esem[engname], 1)
            if engname == "sp":
                for s in out_streams:
                    if s.count > 0:
                        e.wait_ge(s.sem, 16 * s.count)

        for engname in self.ENGS:
            def _f(e, engname=engname):
                run(engname, e)
            handles[engname](_f)


def _ap(base, offset_elems, dims):
    return bass.AP(base.tensor, base.offset + offset_elems, [list(base.ap[0])] + [list(d) for d in dims])


def build_program(taps=(), stop_after=None):
    nc = bass.Bass("TRN2", target_bir_lowering=False)
    es = ExitStack()
    P = Prog(nc)
    taps = list(taps)
    tap_out = {}

    def din(name, shape, dt=F32):
        return nc.dram_tensor(name, list(shape), dt, kind="ExternalInput").ap()

    xT_d = din("xT", [D, T])
    outT_d = nc.dram_tensor("outT", [D, T], F32, kind="ExternalOutput").ap()
    w_d = {}
    for nm, shp in [("ffn1_w_gate", [D, FF]), ("ffn1_w_up", [D, FF]), ("ffn1_w_down", [FF, D]),
                    ("ffn2_w_gate", [D, FF]), ("ffn2_w_up", [D, FF]), ("ffn2_w_down", [FF, D]),
                    ("w_in", [D, 1280]), ("w_out", [D, D]), ("ssm_glu_w", [512, 512])]:
        w_d[nm] = din(nm, shp)
    p_d = {}
    for nm, shp in [("norm_ffn1", [D]), ("norm_mix", [D]), ("norm_ffn2", [D]), ("final_norm", [D]),
                    ("attn_out_norm", [512]), ("ssm_out_norm", [512]), ("ssm_glu_b", [512]),
                    ("attn_sinks", [8]), ("ssm_lambda_re", [2, 32, 64]), ("ssm_lambda_im", [2, 32, 64]),
                    ("ssm_log_dt", [2, 32]), ("ssm_b_re", [2, 32, 64, 16]), ("ssm_b_im", [2, 32, 64, 16]),
                    ("ssm_c_re", [2, 32, 16, 64]), ("ssm_c_im", [2, 32, 16, 64]), ("ssm_d", [32, 16]),
                    ("c_eb", [128, 24 * 128])]:
        p_d[nm] = din(nm, shp)

    def sb(name, shape, dt):
        return es.enter_context(nc.sbuf_tensor(name, list(shape), dt))

    xT = sb("xT_sb", [128, KD, T], F32)
    ring = sb("ring", [128, 4, 4096], BF16)
    RA = sb("RA", [128, 16384], BF16)
    RB = sb("RB", [128, 8192], BF16)
    RC = sb("RC", [128, 12288], BF16)
    RD = sb("RD", [128, 7168], BF16)
    gcols = sb("gcols", [128, 4, KD], F32)
    ones = sb("ones", [128, 128], BF16)
    tf = [sb("tf%d" % i, [128, TB], F32) for i in range(3)]
    tb16 = [sb("tb16_%d" % i, [128, TB], BF16) for i in range(2)]
    psb = [es.enter_context(nc.psum_tensor("ps%d" % b, [128, 512], F32)) for b in range(8)]

    def pst(b):
        return [P.tile("ps", b, 0), P.tile("ps", b, 1)]

    def psth(b, h):
        return [P.tile("ps", b, 0), P.tile("ps", b, 1)]

    s_small = P.stream("small", final_only=True)
    s_out = P.stream("out")
    tap_streams = []

    def tap(name, ap, shape, dt=F32, r=()):
        if name not in taps:
            return
        d = nc.dram_tensor("tap_" + name, list(shape), dt, kind="ExternalOutput").ap()
        tap_out[name] = d
        ts_ = P.stream("tap_" + name)
        tap_streams.append(ts_)
        P.dma("sp", d, ap, r=list(r), stream=ts_)

    t_consts = P.tile("consts")
    P.op("pool", lambda e: e.memset(ones[:], 1.0), w=[t_consts])
    nc_allow = es.enter_context(nc.allow_non_contiguous_dma(reason="tiny parameter loads"))
    for i, nm in enumerate(["norm_ffn1", "norm_mix", "norm_ffn2", "final_norm"]):
        P.dma("sp", gcols[:, i, :], p_d[nm].rearrange("(k p) -> p k", p=128), w=[t_consts], stream=s_small)

    xs = [P.stream("x%d" % k) for k in range(KD)]
    for k in range(KD):
        P.dma("sp", xT[:, k, :], xT_d[k * 128:(k + 1) * 128, :],
              w=[P.tile("xT", k, tb) for tb in range(NTB)], stream=xs[k])

    ring_state = {"n": 0}
    rstreams = {q: [P.stream("ring%s%d" % (q, i)) for i in range(4)] for q in ("pool", "sp")}

    def ring_load(parts, q="pool", r=()):
        s = ring_state["n"] % 4
        ring_state["n"] += 1
        t = P.tile("ring", s)
        slot = ring[:, s, :]
        for dst_fn, src in parts:
            P.dma(q, dst_fn(slot), src, r=list(r), w=[t], stream=rstreams[q][s])
        return slot, t

    psrot = {"gu": 0, "dn": 0}
    dnpool = [4, 5, 6, 7]

    def dnbank():
        b = dnpool[psrot["dn"] % len(dnpool)]
        psrot["dn"] += 1
        return b

    def rmsnorm(src_fn, src_tiles_fn, dst_fn, dst_tiles_fn, nchunk, width, gcol_fn, ones_ap=None, bank=None, pre_dve=None):
        b = dnbank() if bank is None else bank
        ps = psb[b]
        oa = ones[:] if ones_ap is None else ones_ap
        for k in range(nchunk):
            sq = tb16[k % 2]
            tq = P.tile("tb16", k % 2)
            P.op("act", lambda e, sq=sq, k=k: e.activation(out=sq[:], in_=src_fn(k), func=AF.Square),
                 r=src_tiles_fn(k), w=[tq])
            P.op("pe", lambda e, sq=sq, k=k, ps=ps: e.matmul(ps[:], lhsT=oa, rhs=sq[:],
                                                             start=(k == 0), stop=(k == nchunk - 1)),
                 r=[tq, t_consts], w=pst(b))
        ttf = P.tile("tf", 0)
        P.op("act", lambda e, ps=ps: e.activation(out=tf[0][:], in_=ps[:], func=AF.Sqrt,
                                                  scale=1.0 / width, bias=epsc[:, 0:1]),
             r=pst(b) + [t_consts], w=[ttf])
        if pre_dve is not None:
            pre_dve()
        P.op("dve", lambda e, ps=ps: e.reciprocal(out=ps[:], in_=tf[0][:]), r=[ttf], w=pst(b))
        for k in range(nchunk):
            if pre_dve is not None:
                pre_dve()
            P.op("dve", lambda e, k=k, ps=ps: e.scalar_tensor_tensor(
                out=dst_fn(k), in0=src_fn(k), scalar=gcol_fn(k), in1=ps[:], op0=ALU.mult, op1=ALU.mult),
                r=src_tiles_fn(k) + pst(b) + [t_consts], w=dst_tiles_fn(k))

    epsc = sb("epsc", [128, 1], F32)
    P.op("pool", lambda e: e.memset(epsc[:], EPS), w=[t_consts])

    ebt = sb("ebt", [128, 24, 128], BF16)
    pcols = sb("pcols", [128, 16], F32)
    s_eb = P.stream("eb")
    t_eb = P.tile("ebt")
    P.dma("pool", ebt[:], p_d["c_eb"].rearrange("p (a q) -> p a q", a=24), w=[t_eb], stream=s_eb, max_dma_last_dim=4096)
    t_pc = P.tile("pcols")
    for kv in range(2):
        P.dma("sp", pcols[kv * 64:(kv + 1) * 64, 0:4],
              p_d["attn_out_norm"][kv * 256:(kv + 1) * 256].rearrange("(c d) -> d c", d=64), w=[t_pc], stream=s_small)
        sk = p_d["attn_sinks"]
        P.dma("sp", pcols[kv * 64:(kv + 1) * 64, 12:16],
              bass.AP(sk.tensor, sk.offset + 4 * kv, [[0, 64], [1, 4]]), w=[t_pc], stream=s_small)
    P.dma("sp", pcols[:, 4:8], p_d["ssm_out_norm"].rearrange("(c p) -> p c", p=128), w=[t_pc], stream=s_small)
    P.dma("sp", pcols[:, 8:12], p_d["ssm_glu_b"].rearrange("(c p) -> p c", p=128), w=[t_pc], stream=s_small)
    P.op("act", lambda e: e.activation(out=pcols[:, 12:16], in_=pcols[:, 12:16], func=AF.Exp), r=[t_pc], w=[t_pc])

    ssc = sb("ssc", [128, 24, 32], F32)
    ssci = sb("ssci", [128, 2, 32], I32)
    identb = sb("identb", [128, 128], BF16)
    identf = sb("identf", [128, 128], F32)
    rsel = sb("rsel", [128, 8, 240], BF16)
    dcol = sb("dcol", [128, 32], F32)
    amat = sb("amat", [128, 2, 2, 16, 2], F32)
    sring = sb("sring", [128, 8, 64], F32)
    stt = sb("stt", [128, 2, 128], F32)
    scr_mg = nc.dram_tensor("scr_mg", [128, 32, 128], BF16, kind="Internal").ap()
    scr_sz = nc.dram_tensor("scr_sz", [128, 2, 32, 128], BF16, kind="Internal").ap()
    scr_so = nc.dram_tensor("scr_so", [128, 2, 2, 16, 128], BF16, kind="Internal").ap()
    t_id = P.tile("ident")
    for idt in (identb, identf):
        P.op("pool", lambda e, idt=idt: e.memset(idt[:], 0.0), w=[t_id])
        P.op("pool", lambda e, idt=idt: e.affine_select(out=idt[:], in_=idt[:], pattern=[[-1, 128]],
                                                       compare_op=ALU.not_equal, fill=1.0, base=0, channel_multiplier=1),
             r=[t_id], w=[t_id])
    t_rsel = P.tile("rsel")
    P.op("pool", lambda e: e.memset(rsel[:], 0.0), w=[t_rsel])
    for a0 in range(8):
        P.op("pool", lambda e, a0=a0: e.tensor_copy(out=rsel[:, a0, 112:128], in_=identb[:, a0 * 16:(a0 + 1) * 16]),
             r=[t_id], w=[t_rsel])
    t_ab = P.tile("arb")
    s_scr = P.stream("scr")
    t_scr = P.tile("scr")

    setup_mark = []
    setup_b = []

    def ssm_setup():
        VW = "setup"
        def sc(i):
            return ssc[:, i, :]
        def st(i):
            return P.tile("ssc", i)
        def vtt(o, a, b, op):
            P.op("dve", lambda e: e.tensor_tensor(out=sc(o), in0=sc(a), in1=sc(b), op=op), r=[st(a), st(b)], w=[st(o)])
        def vts(o, a, s1, s2, op0, op1=None):
            if op1 is None:
                P.op("dve", lambda e: e.tensor_scalar(out=sc(o), in0=sc(a), scalar1=s1, scalar2=None, op0=op0), r=[st(a)], w=[st(o)])
            else:
                P.op("dve", lambda e: e.tensor_scalar(out=sc(o), in0=sc(a), scalar1=s1, scalar2=s2, op0=op0, op1=op1), r=[st(a)], w=[st(o)])
        def vstt(o, a, scal, b, op0, op1):
            P.op("dve", lambda e: e.scalar_tensor_tensor(out=sc(o), in0=sc(a), scalar=scal, in1=sc(b), op0=op0, op1=op1),
                 r=[st(a), st(b)], w=[st(o)])
        LR, LI, LDT, lr, dt, z, th, mag, sn, cs, Are, Aim = range(12)
        t0, t1, t2, t3, t4, t5, cre, cim, p2r, p2i, t6, t7 = range(12, 24)
        for par in range(2):
            for slot, nm in ((LR, "ssm_lambda_re"), (LI, "ssm_lambda_im")):
                src = p_d[nm]
                for d in range(2):
                    P.dma("sp", ssc[par * 64:(par + 1) * 64, slot, d * 16:(d + 1) * 16],
                          bass.AP(src.tensor, src.offset + d * 2048 + par * 16 * 64, [[1, 64], [64, 16]]),
                          w=[st(slot)], stream=s_small)
            src = p_d["ssm_log_dt"]
            P.dma("sp", ssc[par * 64:(par + 1) * 64, LDT, :].rearrange("p (d g) -> p d g", d=2),
                  bass.AP(src.tensor, src.offset + par * 16, [[0, 64], [32, 2], [1, 16]]), w=[st(LDT)], stream=s_small)
        for s_ in range(8):
            src = p_d["ssm_d"]
            P.dma("sp", dcol[s_ * 16:(s_ + 1) * 16, :], bass.AP(src.tensor, src.offset, [[1, 16], [16, 32]]),
                  w=[P.tile("dcol")], stream=s_small)
        big = RC[:, 0:12288].bitcast(F32).rearrange("p (j n) -> p j n", j=12)
        def bg(j):
            return big[:, j, :]
        def bgt(j):
            return P.rtile("RC", VW, "big", j)
        BRE, BIM, CRE, CIM, XR, XI, YR, YI, T0, T1, T2, T3 = range(12)
        for par in range(2):
            for d in range(2):
                for j, nm in ((BRE, "ssm_b_re"), (BIM, "ssm_b_im")):
                    src = p_d[nm]
                    P.dma("sp", big[par * 64:(par + 1) * 64, j, d * 256:(d + 1) * 256].rearrange("p (g h) -> p g h", g=16),
                          bass.AP(src.tensor, src.offset + d * 32768 + par * 16 * 1024, [[16, 64], [1024, 16], [1, 16]]),
                          w=[bgt(j)], stream=s_small)
        cnat = RA[:, 12288:14336].bitcast(F32).rearrange("p (r d b q c) -> p r d b q c", r=2, d=2, b=2, q=2)
        t_cnat = P.rtile("RAx", VW, "cnat")
        for ri, nm in enumerate(("ssm_c_re", "ssm_c_im")):
            src = p_d[nm]
            for d in range(2):
                for blk in range(2):
                    P.dma("sp", cnat[:, ri, d, blk, :, :],
                          bass.AP(src.tensor, src.offset + d * 32768 + blk * 8 * 1024, [[64, 128], [16384, 2], [1, 64]]),
                          w=[t_cnat], stream=s_small)
        P.deferred = []
        vts(lr, LR, -1e-4, None, ALU.min)
        vts(t0, LDT, 1.4426950408889634, None, ALU.mult)
        P.op("dve", lambda e: e.tensor_copy(out=ssci[:, 0, :], in_=sc(t0)), r=[st(t0)], w=[P.tile("ssci", 0)])
        P.op("dve", lambda e: e.tensor_copy(out=sc(t1), in_=ssci[:, 0, :]), r=[P.tile("ssci", 0)], w=[st(t1)])
        vstt(t2, t1, -0.693145751953125, LDT, ALU.mult, ALU.add)
        vstt(t2, t1, -1.42860682030941723212e-6, t2, ALU.mult, ALU.add)
        vts(t3, t2, 1.0 / 362880.0, None, ALU.mult)
        for c in (1.0 / 40320, 1.0 / 5040, 1.0 / 720, 1.0 / 120, 1.0 / 24, 1.0 / 6, 0.5, 1.0):
            vstt(t3, t3, c, t2, ALU.add, ALU.mult)
        vts(t3, t3, 1.0, None, ALU.add)
        P.op("dve", lambda e: e.tensor_scalar(out=ssci[:, 1, :], in0=sc(t1), scalar1=127.0, scalar2=8388608.0,
                                              op0=ALU.add, op1=ALU.mult), r=[st(t1)], w=[P.tile("ssci", 1)])
        P.op("dve", lambda e: e.tensor_tensor(out=sc(dt), in0=sc(t3), in1=ssci[:, 1, :].bitcast(F32), op=ALU.mult),
             r=[st(t3), P.tile("ssci", 1)], w=[st(dt)])
        vtt(z, lr, dt, ALU.mult)
        vtt(th, LI, dt, ALU.mult)
        vts(t3, z, 1.0 / 5040.0, None, ALU.mult)
        for c in (1.0 / 720, 1.0 / 120, 1.0 / 24, 1.0 / 6, 0.5, 1.0):
            vstt(t3, t3, c, z, ALU.add, ALU.mult)
        vts(mag, t3, 1.0, None, ALU.add)
        PI_LO = 3.1415925
        vts(t0, th, 0.15915494309189535, None, ALU.mult)
        P.op("dve", lambda e: e.tensor_copy(out=ssci[:, 0, :], in_=sc(t0)), r=[st(t0)], w=[P.tile("ssci", 0)])
        P.op("dve", lambda e: e.tensor_copy(out=sc(t1), in_=ssci[:, 0, :]), r=[P.tile("ssci", 0)], w=[st(t1)])
        vstt(t2, t1, -6.28125, th, ALU.mult, ALU.add)
        vstt(t2, t1, -1.9353071795864769e-3, t2, ALU.mult, ALU.add)
        vts(t4, t2, -PI_LO, PI_LO, ALU.max, ALU.min)
        P.op("act", lambda e: e.activation(out=sc(sn), in_=sc(t4), func=AF.Sin), r=[st(t4)], w=[st(sn)])
        vts(t5, t2, 1.5707963267948966, None, ALU.add)
        vts(t0, t5, PI_LO, None, ALU.is_gt)
        vstt(t5, t0, -6.283185307179586, t5, ALU.mult, ALU.add)
        vts(t5, t5, -PI_LO, PI_LO, ALU.max, ALU.min)
        P.op("act", lambda e: e.activation(out=sc(cs), in_=sc(t5), func=AF.Sin), r=[st(t5)], w=[st(cs)])
        vtt(Are, mag, cs, ALU.mult)
        vtt(Aim, mag, sn, ALU.mult)
        vts(t0, Are, -1.0, None, ALU.add)
        vtt(t1, t0, lr, ALU.mult)
        vtt(t2, Aim, LI, ALU.mult)
        vtt(t1, t1, t2, ALU.add)
        vtt(t2, Aim, lr, ALU.mult)
        vtt(t3, t0, LI, ALU.mult)
        vtt(t2, t2, t3, ALU.subtract)
        vtt(t3, lr, lr, ALU.mult)
        vtt(t4, LI, LI, ALU.mult)
        vtt(t3, t3, t4, ALU.add)
        P.op("dve", lambda e: e.reciprocal(out=sc(t3), in_=sc(t3)), r=[st(t3)], w=[st(t3)])
        vtt(cre, t1, t3, ALU.mult)
        vtt(cim, t2, t3, ALU.mult)
        def csq(o_r, o_i, a_r, a_i):
            vtt(t0, a_r, a_r, ALU.mult)
            vtt(t1, a_i, a_i, ALU.mult)
            vtt(t2, a_r, a_i, ALU.mult)
            vtt(o_r, t0, t1, ALU.subtract)
            vts(o_i, t2, 2.0, None, ALU.mult)
        csq(p2r, p2i, Are, Aim)
        csq(t6, t7, p2r, p2i)
        csq(p2r, p2i, t6, t7)
        for ro, ri_, slot_, sgn in ((0, 0, p2r, 1.0), (1, 1, p2r, 1.0), (0, 1, p2i, -1.0), (1, 0, p2i, 1.0)):
            P.op("dve", lambda e, ro=ro, ri_=ri_, slot_=slot_, sgn=sgn: e.tensor_scalar(
                out=amat[:, :, ro, :, ri_], in0=sc(slot_).rearrange("p (d g) -> p d g", d=2), scalar1=sgn, scalar2=None,
                op0=ALU.mult), r=[st(slot_)], w=[t_ab])
        for ri, cj in ((0, CRE), (1, CIM)):
            for d in range(2):
                for blk in range(2):
                    b = 7
                    P.op("pe", lambda e, b=b, ri=ri, d=d, blk=blk: e.transpose(
                        psb[b][:, 0:128], cnat[:, ri, d, blk, :, :].rearrange("p q c -> p (q c)"), identf[:]),
                        r=[t_cnat, t_id], w=pst(b))
                    P.op("dve", lambda e, b=b, cj=cj, d=d, blk=blk: e.tensor_copy(
                        out=bg(cj)[:, d * 256 + blk * 128:d * 256 + (blk + 1) * 128], in_=psb[b][:, 0:128]),
                        r=pst(b), w=[bgt(cj)])
        def bc(slot):
            return _ap(sc(slot), 0, [[1, 32], [0, 16]])
        def b3(j):
            return bg(j).rearrange("p (g h) -> p g h", h=16)
        def cmul_bc(o_r, o_i, a_r, a_i, x_r, x_i):
            P.op("dve", lambda e: e.tensor_tensor(out=b3(T0), in0=b3(x_r), in1=bc(a_r), op=ALU.mult), r=[bgt(x_r), st(a_r)], w=[bgt(T0)])
            P.op("dve", lambda e: e.tensor_tensor(out=b3(T1), in0=b3(x_i), in1=bc(a_i), op=ALU.mult), r=[bgt(x_i), st(a_i)], w=[bgt(T1)])
            P.op("dve", lambda e: e.tensor_tensor(out=b3(T2), in0=b3(x_i), in1=bc(a_r), op=ALU.mult), r=[bgt(x_i), st(a_r)], w=[bgt(T2)])
            P.op("dve", lambda e: e.tensor_tensor(out=b3(T3), in0=b3(x_r), in1=bc(a_i), op=ALU.mult), r=[bgt(x_r), st(a_i)], w=[bgt(T3)])
            P.op("dve", lambda e: e.tensor_tensor(out=bg(o_r), in0=bg(T0), in1=bg(T1), op=ALU.subtract), r=[bgt(T0), bgt(T1)], w=[bgt(o_r)])
            P.op("dve", lambda e: e.tensor_tensor(out=bg(o_i), in0=bg(T2), in1=bg(T3), op=ALU.add), r=[bgt(T2), bgt(T3)], w=[bgt(o_i)])
        so16 = RB[:, 0:8192].rearrange("p (d r g t h) -> p d r g t h", d=2, r=2, g=16, t=8)
        t_so = P.rtile("RB", "so16", "so")
        cur = (CRE, CIM)
        nxt = [(XR, XI), (YR, YI)]
        for k in range(1, 9):
            o = nxt[k % 2]
            cmul_bc(o[0], o[1], Are, Aim, cur[0], cur[1])
            cur = o
            for d in range(2):
                slot = (k - 1) if d == 0 else (8 - k)
                for ri in range(2):
                    P.op("dve", lambda e, d=d, ri=ri, slot=slot, cur=cur: e.tensor_scalar(
                        out=so16[:, d, ri, :, slot, :], in0=b3(cur[ri])[:, d * 16:(d + 1) * 16, :],
                        scalar1=(1.0 if ri == 0 else -1.0), scalar2=None, op0=ALU.mult), r=[bgt(cur[ri])], w=[t_so])
        P.dma("sp", scr_so.rearrange("p d r g n -> p (d r g n)"), RB[:, 0:8192], r=[t_so], w=[t_scr], stream=s_scr)
        cpa = RA[:, 14336:15360].rearrange("p (r d g h) -> p r d g h", r=2, d=2, g=16)
        t_cpa = P.rtile("RAx", "cpa", "cpa")
        for ri, cj in ((0, CRE), (1, CIM)):
            P.op("dve", lambda e, ri=ri, cj=cj: e.tensor_scalar(out=cpa[:, ri].rearrange("p d g h -> p (d g) h"), in0=b3(cj),
                                                             scalar1=(1.0 if ri == 0 else -1.0), scalar2=None, op0=ALU.mult),
                 r=[bgt(cj)], w=[t_cpa])
        wa16 = RB[:, 0:8192].rearrange("p (r d g t h) -> p r d g t h", r=2, d=2, g=16, t=8)
        t_wa = P.rtile("RB", "wa16", "wa")
        cmul_bc(XR, XI, cre, cim, BRE, BIM)
        cur = (XR, XI)
        nxt = [(YR, YI), (XR, XI)]
        for tau in range(8):
            for d in range(2):
                slot = (7 - tau) if d == 0 else tau
                for ri in range(2):
                    P.op("dve", lambda e, d=d, ri=ri, slot=slot, cur=cur: e.tensor_copy(
                        out=wa16[:, ri, d, :, slot, :], in_=b3(cur[ri])[:, d * 16:(d + 1) * 16, :]),
                        r=[bgt(cur[ri])], w=[t_wa])
            if tau < 7:
                o = nxt[tau % 2]
                cmul_bc(o[0], o[1], Are, Aim, cur[0], cur[1])
                cur = o
        setup_mark.append(len(P.deferred))
        V3 = "pe_stage"
        ww = RC[:, 0:7680].rearrange("p (b d g j h) -> p b d g j h", b=2, d=2, g=8, j=15)
        cpb = RC[:, 7680:8704].rearrange("p (d g h) -> p d g h", d=2, g=32)
        mgst = RC[:, 8704:10752].rearrange("p (b g n) -> p b g n", b=2, g=8)
        szst = RD[:, 0:4096].rearrange("p (b d g n) -> p b d g n", b=2, d=2, g=8)
        t_cpb = P.rtile("RC", V3, "cpb")
        s_asm = P.stream("asm", final_only=True)
        s_asmw = [P.stream("asmw%d" % i, final_only=True) for i in range(4)]
        for bi in range(2):
            P.op("pool", lambda e, bi=bi: e.memset(ww[:, bi].rearrange("p d g j h -> p (d g j h)"), 0.0),
                 w=[P.rtile("RC", V3, "ww", bi)])
        for ri in range(2):
            for par in range(2):
                P.dma("sp", cpb[ri * 64:(ri + 1) * 64, :, par * 16:(par + 1) * 16, :].rearrange("p d g h -> p d (g h)"),
                      cpa[par * 64:(par + 1) * 64, ri].rearrange("p d g h -> p d (g h)"), r=[t_cpa], w=[t_cpb], stream=s_asm)

        def assemble(blk):
            bi = blk % 2
            par, gp0 = blk // 2, (blk % 2) * 8
            for d in range(2):
                j0 = 0 if d == 0 else 7
                for ri in range(2):
                    P.dma("sp", ww[ri * 64:(ri + 1) * 64, bi, d, :, j0:j0 + 8, :].rearrange("p g j h -> p g (j h)"),
                          wa16[par * 64:(par + 1) * 64, ri, d, gp0:gp0 + 8, :, :].rearrange("p g t h -> p g (t h)"),
                          r=[t_wa], w=[P.rtile("RC", V3, "ww", bi)], stream=s_asmw[blk])

        assemble(0)
        assemble(1)
        for blk in range(4):
            bi = blk % 2
            par, gp0 = blk // 2, (blk % 2) * 8
            t_ww = P.rtile("RC", V3, "ww", bi)
            t_mg = P.rtile("RC", V3, "mgst", bi)
            t_sz = P.rtile("RD", V3, "szst", bi)
            for gl in range(8):
                g = blk * 8 + gl
                b = 4 + (g % 2)
                for t in range(8):
                    for d in range(2):
                        P.op("pe", lambda e, b=b, t=t, d=d, gl=gl, bi=bi, g=g: e.matmul(
                            psb[b][:, t * 16:(t + 1) * 16],
                            lhsT=ww[:, bi, d, gl, 7 - t:15 - t, :].rearrange("p j h -> p (j h)"),
                            rhs=cpb[:, d, g, :], start=(d == 0), stop=(d == 1)), r=[t_ww, t_cpb], w=pst(b))
                P.op("dve", lambda e, b=b, bi=bi, gl=gl, g=g: e.scalar_tensor_tensor(
                    out=mgst[:, bi, gl, :], in0=identb[:], scalar=dcol[:, g:g + 1], in1=psb[b][:, 0:128],
                    op0=ALU.mult, op1=ALU.add), r=pst(b) + [t_id, P.tile("dcol")], w=[t_mg])
                for d in range(2):
                    j0 = 0 if d == 0 else 7
                    b2 = 6 + d
                    P.op("pe", lambda e, b2=b2, bi=bi, d=d, gl=gl, j0=j0: e.transpose(
                        psb[b2][:, 0:64].bitcast(BF16), ww[:, bi, d, gl, j0:j0 + 8, :].rearrange("p j h -> p (j h)"), identb[:]),
                        r=[t_ww, t_id], w=pst(b2))
                    P.op("act", lambda e, b2=b2, bi=bi, d=d, gl=gl: e.activation(
                        out=szst[:, bi, d, gl, :], in_=psb[b2][:, 0:64].bitcast(BF16), func=AF.Copy), r=pst(b2), w=[t_sz])
            if blk + 2 < 4:
                assemble(blk + 2)
            P.dma("sp", scr_mg[:, blk * 8:(blk + 1) * 8, :], mgst[:, bi], r=[t_mg], w=[t_scr], stream=s_scr)
            P.dma("sp", scr_sz[:, :, blk * 8:(blk + 1) * 8, :], szst[:, bi], r=[t_sz], w=[t_scr], stream=s_scr)
        tap("amat", amat[:], [128, 2, 2, 16, 2], r=[t_ab])
        tap("ssc", ssc[:], [128, 24, 32], r=[st(i) for i in range(24)])

    if "nossm" not in taps:
        ssm_setup()
        P.parked = P.deferred[:setup_mark[0]]
        setup_b.extend(P.deferred[setup_mark[0]:])
        P.deferred = None

    def ffn(idx, wg_d, wu_d, wd_d):
        for half in range(2):
            tbs = [2 * half, 2 * half + 1]
            xn = RA[:, 0:8192].rearrange("p (k t) -> p k t", k=KD)
            h1 = RA[:, 8192:12288].rearrange("p (f t) -> p f t", f=4)
            vname = "ffn%d_%d" % (idx, half)
            for tl, tb in enumerate(tbs):
                xnt = P.rtile("RA", vname, "xn", tl)
                rmsnorm(lambda k, tb=tb: xT[:, k, tb * TB:(tb + 1) * TB],
                        lambda k, tb=tb: [P.tile("xT", k, tb)],
                        lambda k, tl=tl: xn[:, k, tl * TB:(tl + 1) * TB],
                        lambda k, xnt=xnt: [xnt], KD, float(D),
                        lambda k: gcols[:, 0 if idx == 1 else 2, k:k + 1])
            for (f0, f1) in FGROUPS:
                nf = f1 - f0
                wg, twg = ring_load([(lambda s, nf=nf: s[:, 0:KD * nf * 128].rearrange("p (k n) -> p k n", k=KD),
                                      wg_d[:, f0 * 128:f1 * 128].rearrange("(k p) n -> p k n", p=128))])
                wu, twu = ring_load([(lambda s, nf=nf: s[:, 0:KD * nf * 128].rearrange("p (k n) -> p k n", k=KD),
                                      wu_d[:, f0 * 128:f1 * 128].rearrange("(k p) n -> p k n", p=128))])
                wd, twd = ring_load([(lambda s, nf=nf: s[:, 0:nf * D].rearrange("p (k n) -> p k n", k=nf),
                                      wd_d[f0 * 128:f1 * 128, :].rearrange("(k p) n -> p k n", p=128))])
                wgv = wg[:, 0:KD * nf * 128].rearrange("p (k n) -> p k n", k=KD)
                wuv = wu[:, 0:KD * nf * 128].rearrange("p (k n) -> p k n", k=KD)
                wdv = wd[:, 0:nf * D].rearrange("p (k n) -> p k n", k=nf)
                for fi in range(nf):
                    for tl, tb in enumerate(tbs):
                        bg = (psrot["gu"] % 2) * 2
                        psrot["gu"] += 1
                        xnt = P.rtile("RA", vname, "xn", tl)
                        for which, (wv, tw, bb) in enumerate([(wgv, twg, bg), (wuv, twu, bg + 1)]):
                            for k in range(KD):
                                P.op("pe", lambda e, wv=wv, k=k, fi=fi, tl=tl, bb=bb: e.matmul(
                                    psb[bb][:], lhsT=wv[:, k, fi * 128:(fi + 1) * 128],
                                    rhs=xn[:, k, tl * TB:(tl + 1) * TB], start=(k == 0), stop=(k == KD - 1)),
                                    r=[tw, xnt], w=pst(bb))
                        ts = tf[1 + (psrot["gu"] % 2)]
                        tts = P.tile("tf", 1 + (psrot["gu"] % 2))
                        P.op("act", lambda e, ts=ts, bg=bg: e.activation(out=ts[:], in_=psb[bg][:], func=AF.Silu),
                             r=pst(bg), w=[tts])
                        h1t = P.rtile("RA", vname, "h1", fi, tl)
                        P.op("dve", lambda e, ts=ts, bg=bg, fi=fi, tl=tl: e.tensor_tensor(
                            out=h1[:, fi, tl * TB:(tl + 1) * TB], in0=ts[:], in1=psb[bg + 1][:], op=ALU.mult),
                            r=[tts] + pst(bg + 1), w=[h1t])
                        P.drain(2)
                for m in range(KD):
                    for tl, tb in enumerate(tbs):
                        b = dnbank()
                        for fi in range(nf):
                            P.op("pe", lambda e, fi=fi, m=m, tl=tl, b=b, wdv=wdv, nf=nf: e.matmul(
                                psb[b][:], lhsT=wdv[:, fi, m * 128:(m + 1) * 128],
                                rhs=h1[:, fi, tl * TB:(tl + 1) * TB], start=(fi == 0), stop=(fi == nf - 1)),
                                r=[twd, P.rtile("RA", vname, "h1", fi, tl)], w=pst(b))
                        xt = P.tile("xT", m, tb)
                        P.op("dve", lambda e, m=m, tb=tb, b=b: e.scalar_tensor_tensor(
                            out=xT[:, m, tb * TB:(tb + 1) * TB], in0=psb[b][:], scalar=0.5,
                            in1=xT[:, m, tb * TB:(tb + 1) * TB], op0=ALU.mult, op1=ALU.add),
                            r=pst(b) + [xt], w=[xt])
                        P.drain(2)

    if stop_after != "load" and "noffn1" not in taps:
        dnpool[:] = [4, 5, 6]
        ffn(1, w_d["ffn1_w_gate"], w_d["ffn1_w_up"], w_d["ffn1_w_down"])
        dnpool[:] = [4, 5, 6, 7]
    P.drain()
    P.parked = list(setup_b)
    P.drain()
    tap("x1", xT[:], [128, KD, T], r=[P.tile("xT", k, tb) for k in range(KD) for tb in range(NTB)])


    def middle():
        hn = RA[:, 0:8192].rearrange("p (b k t) -> p b k t", b=2, k=KD)
        uT = RA[:, 8192:16384].rearrange("p (c s n) -> p c s n", c=4, s=8)
        qT = RC[:, 0:8192].rearrange("p (c t) -> p c t", c=4)
        kT = RC[:, 8192:10240]
        Vt = RC[:, 10240:12288].rearrange("p (b d) -> p b d", b=16)
        Et = RD[:, 0:1536].rearrange("p (i n) -> p i n", i=3)
        Pt = RD[:, 1536:4608].rearrange("p (i n) -> p i n", i=6)
        attnb = RD[:, 4608:6656].rearrange("p (c n) -> p c n", c=4)
        w_in = w_d["w_in"]
        def wq_parts():
            parts = []
            for kv in range(2):
                for c in range(4):
                    parts.append((lambda s, kv=kv, c=c: s.rearrange("p (k c v d) -> p k c v d", k=KD, c=4, v=2)[:, :, c, kv, :],
                                  w_in[:, kv * 256 + c * 64:kv * 256 + (c + 1) * 64].rearrange("(k p) d -> p k d", p=128)))
            return parts
        wq, twq = ring_load(wq_parts())
        wkv, twkv = ring_load([(lambda s: s[:, 0:2048].rearrange("p (k n) -> p k n", k=KD),
                                w_in[:, 512:768].rearrange("(k p) n -> p k n", p=128))])
        wu, twu = ring_load([(lambda s: s.rearrange("p (k n) -> p k n", k=KD),
                              w_in[:, 768:1280].rearrange("(k p) n -> p k n", p=128))])
        wqv = wq.rearrange("p (k n) -> p k n", k=KD)
        wkvv = wkv[:, 0:2048].rearrange("p (k n) -> p k n", k=KD)
        wuv = wu.rearrange("p (k n) -> p k n", k=KD)
        evac = {"n": 0}

        def evacuate(out_ap, in_ap, r, w):
            P.drain(evac.get("k", 0))
            evac["n"] += 1
            if evac["n"] % 2:
                P.op("act", lambda e: e.activation(out=out_ap, in_=in_ap, func=AF.Copy), r=r, w=w)
            else:
                P.op("dve", lambda e: e.tensor_copy(out=out_ap, in_=in_ap), r=r, w=w)

        def nextbank(pool=None, key="dn"):
            pool = tuple(dnpool) if pool is None else pool
            b = pool[psrot.setdefault(key, 0) % len(pool)]
            psrot[key] += 1
            return b

        def mixnorm(tb):
            hb = tb % 2
            hnt = P.rtile("RA", "mid", "hn", hb)
            rmsnorm(lambda k, tb=tb: xT[:, k, tb * TB:(tb + 1) * TB],
                    lambda k, tb=tb: [P.tile("xT", k, tb)],
                    lambda k, hb=hb: hn[:, hb, k, :], lambda k, hnt=hnt: [hnt], KD, float(D),
                    lambda k: gcols[:, 1, k:k + 1], bank=3)

        mixnorm(0)
        for tb in range(NTB):
            hb = tb % 2
            hnt = P.rtile("RA", "mid", "hn", hb)
            if tb + 1 < NTB:
                mixnorm(tb + 1)
            for c in range(4):
                b = nextbank()
                for k in range(KD):
                    P.op("pe", lambda e, b=b, k=k, c=c, hb=hb: e.matmul(
                        psb[b][:], lhsT=wqv[:, k, c * 128:(c + 1) * 128], rhs=hn[:, hb, k, :],
                        start=(k == 0), stop=(k == KD - 1)), r=[twq, hnt], w=pst(b))
                evacuate(qT[:, c, tb * TB:(tb + 1) * TB], psb[b][:], pst(b), [P.rtile("RC", "qkv", "q", tb)])
            b = nextbank()
            for k in range(KD):
                P.op("pe", lambda e, b=b, k=k, hb=hb: e.matmul(
                    psb[b][:], lhsT=wkvv[:, k, 0:128], rhs=hn[:, hb, k, :],
                    start=(k == 0), stop=(k == KD - 1)), r=[twkv, hnt], w=pst(b))
            evacuate(kT[:, tb * TB:(tb + 1) * TB], psb[b][:], pst(b), [P.rtile("RC", "qkv", "k", tb)])
            b = nextbank()
            for sub in range(4):
                for k in range(KD):
                    P.op("pe", lambda e, b=b, k=k, sub=sub, hb=hb: e.matmul(
                        psb[b][:, sub * 128:(sub + 1) * 128], lhsT=hn[:, hb, k, sub * 128:(sub + 1) * 128],
                        rhs=wkvv[:, k, 128:256], start=(k == 0), stop=(k == KD - 1)), r=[twkv, hnt], w=pst(b))
            evacuate(Vt[:, tb * 4:(tb + 1) * 4, :], psb[b][:].rearrange("p (s d) -> p s d", s=4), pst(b),
                     [P.rtile("RC", "qkv", "v", tb)])
            for c in range(4):
                b = nextbank()
                for k in range(KD):
                    P.op("pe", lambda e, b=b, k=k, c=c, hb=hb: e.matmul(
                        psb[b][:], lhsT=wuv[:, k, c * 128:(c + 1) * 128], rhs=hn[:, hb, k, :],
                        start=(k == 0), stop=(k == KD - 1)), r=[twu, hnt], w=pst(b))
                evacuate(uT[:, c, :, tb * 64:(tb + 1) * 64], psb[b][:].rearrange("p (n s) -> p s n", s=8), pst(b),
                         [P.rtile("RA", "mid", "u", c, tb), P.rtile("RAx", "mid", "u")])
        tap("qT", qT, [128, 4, T], BF16, r=[P.rtile("RC", "qkv", "q", tb) for tb in range(NTB)])
        tap("kT", kT, [128, T], BF16, r=[P.rtile("RC", "qkv", "k", tb) for tb in range(NTB)])
        tap("Vt", Vt, [128, 16, 128], BF16, r=[P.rtile("RC", "qkv", "v", tb) for tb in range(NTB)])

        w_out = w_d["w_out"]
        woa, twoa = ring_load([(lambda s, kv=kv: s.rearrange("p (c n) -> p c n", c=4)[kv * 64:(kv + 1) * 64],
                                w_out[kv * 256:(kv + 1) * 256, :].rearrange("(c d) n -> d c n", d=64)) for kv in range(2)])
        woav = woa.rearrange("p (c n) -> p c n", c=4)

        pump_state = {"acc": 0.0, "step": 0, "on": False}

        def pump():
            if not pump_state["on"]:
                return
            pump_state["acc"] += NCH / 96.0
            while pump_state["acc"] >= 1.0 and pump_state["step"] < NCH:
                scan_step(pump_state["step"])
                pump_state["step"] += 1
                pump_state["acc"] -= 1.0

        def attn_block(n):
            tb, nl = n // 4, n % 4
            bn = 4 + 2 * (n % 2)
            bd = bn + 1
            js = [j for j in (n - 1, n, n + 1) if 0 <= j < 16]
            tiles_ = [(kv, ji, j) for kv in range(2) for ji, j in enumerate(js)]
            pis = {}

            def front(kv, ji, j):
                dl = j - n + 1
                bs = (psrot["gu"] % 3)
                psrot["gu"] += 1
                ei = psrot["gu"] % 3
                pi = psrot["gu"] % 6
                pis[(kv, ji)] = pi
                P.op("pe", lambda e, bs=bs, kv=kv, j=j, n=n: e.matmul(
                    psb[bs][:], lhsT=kT[kv * 64:(kv + 1) * 64, j * 128:(j + 1) * 128],
                    rhs=qT[kv * 64:(kv + 1) * 64, :, n * 128:(n + 1) * 128], start=True, stop=True),
                    r=[P.rtile("RC", "qkv", "k", j // 4), P.rtile("RC", "qkv", "q", tb)], w=pst(bs))
                te = P.rtile("RD", "attn", "E", ei)
                P.op("act", lambda e, bs=bs, ei=ei: e.activation(out=Et[:, ei, :], in_=psb[bs][:], func=AF.Exp, scale=0.125),
                     r=pst(bs), w=[te])
                tp = P.rtile("RD", "attn", "P", pi)
                P.op("pool", lambda e, ei=ei, pi=pi, kv=kv, dl=dl: e.tensor_tensor(
                    out=Pt[:, pi, :], in0=Et[:, ei, :],
                    in1=ebt[:, (kv * 3 + dl) * 4:(kv * 3 + dl) * 4 + 4, :].rearrange("p a q -> p (a q)"), op=ALU.mult),
                    r=[te, t_eb], w=[tp])

            def back(kv, ji, j):
                pi = pis[(kv, ji)]
                tp = P.rtile("RD", "attn", "P", pi)
                P.op("pe", lambda e, bn=bn, kv=kv, j=j, pi=pi, ji=ji: e.matmul(
                    psb[bn][kv * 64:(kv + 1) * 64, :], lhsT=Vt[:, j, kv * 64:(kv + 1) * 64], rhs=Pt[:, pi, :],
                    start=(ji == 0), stop=(ji == len(js) - 1)),
                    r=[tp, P.rtile("RC", "qkv", "v", j // 4)], w=pst(bn))
                P.op("pe", lambda e, bd=bd, kv=kv, pi=pi, ji=ji: e.matmul(
                    psb[bd][kv * 64:(kv + 1) * 64, :], lhsT=ones[:, 0:64], rhs=Pt[:, pi, :],
                    start=(ji == 0), stop=(ji == len(js) - 1)), r=[tp, t_consts], w=pst(bd))

            DEPTH = 2
            for i in range(len(tiles_) + DEPTH):
                if i < len(tiles_):
                    front(*tiles_[i])
                if i >= DEPTH:
                    back(*tiles_[i - DEPTH])
            return (n, bn, bd)

        def attn_finish(n, bn, bd):
            tb, nl = n // 4, n % 4
            tt = P.tile("tf", 0)
            pump()
            P.op("dve", lambda e, bd=bd: e.tensor_tensor(
                out=tf[0][:].rearrange("p (a q) -> p a q", a=4), in0=psb[bd][:].rearrange("p (a q) -> p a q", a=4),
                in1=_ap(pcols[:, 12:16], 0, [[1, 4], [0, 128]]), op=ALU.add), r=pst(bd) + [t_pc], w=[tt])
            P.op("dve", lambda e: e.reciprocal(out=tf[0][:], in_=tf[0][:]), r=[tt], w=[tt])
            pump()
            ta = P.rtile("RD", "attn", "attnb", nl)
            P.op("dve", lambda e, bn=bn, nl=nl: e.tensor_tensor(
                out=attnb[:, :, nl * 128:(nl + 1) * 128], in0=psb[bn][:].rearrange("p (a q) -> p a q", a=4),
                in1=tf[0][:].rearrange("p (a q) -> p a q", a=4), op=ALU.mult), r=pst(bn) + [tt], w=[ta])

        def attn_tb_out(tb):
            tas = [P.rtile("RD", "attn", "attnb", nl) for nl in range(4)]
            tap("attn%d" % tb, attnb, [128, 4, 512], BF16, r=tas)
            rmsnorm(lambda c: attnb[:, c, :], lambda c: tas, lambda c: attnb[:, c, :], lambda c: tas, 4, 512.0,
                    lambda c: pcols[:, c:c + 1], bank=3, pre_dve=pump)
            for m in range(KD):
                b = 3
                for c in range(4):
                    P.op("pe", lambda e, b=b, c=c, m=m: e.matmul(
                        psb[b][:], lhsT=woav[:, c, m * 128:(m + 1) * 128], rhs=attnb[:, c, :],
                        start=(c == 0), stop=(c == 3)), r=[twoa] + tas, w=pst(b))
                xt = P.tile("xT", m, tb)
                P.op("dve", lambda e, b=b, m=m, tb=tb: e.tensor_tensor(
                    out=xT[:, m, tb * TB:(tb + 1) * TB], in0=psb[b][:], in1=xT[:, m, tb * TB:(tb + 1) * TB], op=ALU.add),
                    r=pst(b) + [xt], w=[xt])

        U8 = RB[:, 0:8192].rearrange("p (g c) -> p g c", g=32)
        ZX = RA[:, 0:16384].rearrange("p (d r g c) -> p d r g c", d=2, r=2, g=16)
        hb_state = {"n": 0}

        def halfbank(pool=(4, 5, 6, 7)):
            i = hb_state["n"]
            hb_state["n"] += 1
            return pool[(i // 2) % len(pool)], i % 2

        def u8t(g):
            return P.rtile("RB", "u8", g)

        do_ssm = "nossm" not in taps
        if do_ssm:
            for g in range(32):
                blk, gl = g // 8, g % 8
                b, h = halfbank()
                for s_ in range(8):
                    P.op("pe", lambda e, b=b, h=h, gl=gl, s_=s_, blk=blk: e.matmul(
                        psb[b][:, h * 256:(h + 1) * 256], lhsT=rsel[:, gl, (7 - s_) * 16:(15 - s_) * 16],
                        rhs=uT[:, blk, s_, :], start=(s_ == 0), stop=(s_ == 7)),
                        r=[t_rsel] + [P.rtile("RA", "mid", "u", blk, tb) for tb in range(NTB)], w=psth(b, h))
                evacuate(U8[:, g, :], psb[b][:, h * 256:(h + 1) * 256], psth(b, h), [u8t(g)])
            tap("U8", U8, [128, 32, 256], BF16, r=[u8t(g) for g in range(32)])
            zxall = [P.rtile("RAx", "zx", "all")]
            def zxc(d, c):
                return P.rtile("RA", "zx", d, c)
            for d in range(2):
                szs, tsz = ring_load([(lambda s: s.rearrange("p (g n) -> p g n", g=32), scr_sz[:, d])], q="sp", r=[t_scr])
                szv = szs.rearrange("p (g n) -> p g n", g=32)
                for gp in range(16):
                    for ri in range(2):
                        b, h = halfbank()
                        for par in range(2):
                            g = 16 * par + gp
                            P.op("pe", lambda e, b=b, h=h, par=par, g=g, ri=ri, szv=szv: e.matmul(
                                psb[b][par * 64:(par + 1) * 64, h * 256:(h + 1) * 256],
                                lhsT=szv[:, g, ri * 64:(ri + 1) * 64], rhs=U8[:, g, :], start=True, stop=True),
                                r=[tsz, u8t(g)], w=psth(b, h))
                        P.op("act", lambda e, b=b, h=h, d=d, ri=ri, gp=gp: e.activation(
                            out=ZX[:, d, ri, gp, :], in_=psb[b][:, h * 256:(h + 1) * 256], func=AF.Copy),
                            r=psth(b, h), w=[zxc(d, c) for c in range(NCH)] + zxall)
            tap("Z", ZX, [128, 2, 2, 16, 256], BF16, r=[zxc(d, c) for d in range(2) for c in range(NCH)])
            P.op("pool", lambda e: e.memset(ssc[:, 14:16, :].rearrange("p a b -> p (a b)"), 0.0), w=[P.tile("sr", 15)])
        RAb = RA[:, 0:16384]
        ring2 = ssc[:, 0:16, :].rearrange("p (s a) b -> p s (a b)", s=8)

        def rslot(i):
            i = i % 16
            return sring[:, i, :] if i < 8 else ring2[:, i - 8, :]

        def scan_step(step):
            c0, c1 = step, NCH - 1 - step
            ip, inw, tb_ = (step - 1) % 16, step % 16, step % 2
            tp_, tn_ = P.tile("sr", ip), P.tile("sr", inw)
            tt_ = P.tile("stt", tb_)
            zt = [zxc(0, c0), zxc(1, c1)]
            zap = _ap(RAb, c0, [[8192 + c1 - c0, 2], [256, 16], [4096, 2]])
            P.op("dve", lambda e, ip=ip, tb_=tb_: e.tensor_tensor(
                out=stt[:, tb_, :].rearrange("p (d r b) -> p d r b", d=2, r=2),
                in0=_ap(rslot(ip), 0, [[32, 2], [0, 2], [1, 32]]),
                in1=amat[:].rearrange("p d r g i -> p d r (g i)"), op=ALU.mult), r=[tp_, t_ab], w=[tt_])
            P.op("dve", lambda e, inw=inw, tb_=tb_: e.tensor_tensor(
                out=_ap(rslot(inw), 0, [[32, 2], [1, 2], [2, 16]]),
                in0=_ap(stt[:, tb_, :], 0, [[64, 2], [32, 2], [2, 16]]),
                in1=_ap(stt[:, tb_, :], 1, [[64, 2], [32, 2], [2, 16]]), op=ALU.add), r=[tt_], w=[tn_])
            P.op("dve", lambda e, inw=inw, zap=zap: e.tensor_tensor(
                out=rslot(inw).rearrange("p (d g i) -> p d g i", d=2, i=2),
                in0=rslot(inw).rearrange("p (d g i) -> p d g i", d=2, i=2), in1=zap, op=ALU.add),
                r=[tn_] + zt, w=[tn_])
            if step % 8 == 7:
                k8 = step - 7
                base = rslot(k8)
                for d in range(2):
                    if d == 0:
                        oap = _ap(RAb, k8, [[1, 8], [256, 16], [4096, 2]])
                        cols = [zxc(0, k8 + j) for j in range(8)]
                    else:
                        oap = _ap(RAb, 8192 + NCH - 1 - k8, [[-1, 8], [256, 16], [4096, 2]])
                        cols = [zxc(1, NCH - 1 - k8 - j) for j in range(8)]
                    P.op("act", lambda e, d=d, oap=oap, base=base: e.activation(
                        out=oap, in_=_ap(base, d * 32, [[64, 8], [2, 16], [1, 2]]), func=AF.Copy),
                        r=[P.tile("sr", (k8 + j) % 16) for j in range(8)], w=cols)

        pend = []
        pump_state["on"] = do_ssm
        for n in range(16):
            pend.append(attn_block(n))
            if len(pend) > 1:
                attn_finish(*pend.pop(0))
            if n % 4 == 0 and n > 0:
                attn_tb_out(n // 4 - 1)
        attn_finish(*pend.pop(0))
        attn_tb_out(3)
        while do_ssm and pump_state["step"] < NCH:
            scan_step(pump_state["step"])
            pump_state["step"] += 1
        if do_ssm:
            tap("X", ZX, [128, 2, 2, 16, 256], BF16, r=[zxc(d, c) for d in range(2) for c in range(NCH)])
            ygT = RC[:, 0:8192].rearrange("p (c t) -> p c t", c=4)
            Yact = RC[:, 8192:12288].rearrange("p (b g c) -> p b g c", b=2, g=8)
            mgs, tmg = ring_load([(lambda s: s.rearrange("p (g n) -> p g n", g=32), scr_mg)], q="sp", r=[t_scr])
            mgv = mgs.rearrange("p (g n) -> p g n", g=32)
            sov, tso = [], []
            for d in range(2):
                sl, tt_ = ring_load([(lambda s: s.rearrange("p (r g n) -> p r g n", r=2, g=16), scr_so[:, d])], q="sp", r=[t_scr])
                sov.append(sl.rearrange("p (r g n) -> p r g n", r=2, g=16))
                tso.append(tt_)
            zr = [[zxc(d, c) for c in range(NCH)] for d in range(2)]
            for blk in range(4):
                yb = blk % 2
                tya = P.rtile("RC", "back", "yact", yb)
                for gl in range(8):
                    g = blk * 8 + gl
                    par, gp = g // 16, g % 16
                    b, h = nextbank((0, 1, 2, 3, 4, 5), "yb"), 0
                    c0 = h * 256
                    pr = slice(par * 64, (par + 1) * 64)
                    P.op("pe", lambda e, b=b, c0=c0, g=g: e.matmul(psb[b][:, c0:c0 + 256], lhsT=mgv[:, g, :], rhs=U8[:, g, :],
                                                                   start=True, stop=False), r=[tmg, u8t(g)], w=psth(b, h))
                    for ri in range(2):
                        P.op("pe", lambda e, b=b, c0=c0, pr=pr, ri=ri, gp=gp: e.matmul(
                            psb[b][:, c0 + 1:c0 + 256], lhsT=sov[0][pr, ri, gp, :], rhs=ZX[pr, 0, ri, gp, 0:255],
                            start=False, stop=False), r=[tso[0]] + zr[0], w=psth(b, h))
                    for ri in range(2):
                        P.op("pe", lambda e, b=b, c0=c0, pr=pr, ri=ri, gp=gp: e.matmul(
                            psb[b][:, c0:c0 + 255], lhsT=sov[1][pr, ri, gp, :], rhs=ZX[pr, 1, ri, gp, 1:256],
                            start=False, stop=(ri == 1)), r=[tso[1]] + zr[1], w=psth(b, h))
                    if "ypre" in taps and g in (0, 5, 17, 31):
                        P.op("act", lambda e, b=b, c0=c0, g=g: e.activation(out=tf[1][:, 0:256], in_=psb[b][:, c0:c0 + 256], func=AF.Copy),
                             r=psth(b, h), w=[P.tile("tf", 1)])
                        taps.append("ypre%d" % g)
                        tap("ypre%d" % g, tf[1][:, 0:256], [128, 256], r=[P.tile("tf", 1)])
                    P.op("act", lambda e, b=b, c0=c0, yb=yb, gl=gl: e.activation(
                        out=Yact[:, yb, gl, :], in_=psb[b][:, c0:c0 + 256], func=AF.Gelu_apprx_tanh), r=psth(b, h), w=[tya])
                for t in range(8):
                    b, h = nextbank((6, 7, 0, 1, 2, 3, 4, 5), "rs"), 0
                    c0 = h * 256
                    for gl in range(8):
                        P.op("pe", lambda e, b=b, c0=c0, t=t, gl=gl, yb=yb: e.matmul(
                            psb[b][:, c0:c0 + 256], lhsT=rsel[:, t, (7 - gl) * 16:(15 - gl) * 16], rhs=Yact[:, yb, gl, :],
                            start=(gl == 0), stop=(gl == 7)), r=[t_rsel, tya], w=psth(b, h))
                    evacuate(_ap(ygT[:, blk, :], t, [[8, 256]]), psb[b][:, c0:c0 + 256], psth(b, h), [P.rtile("RC", "back", "yg", blk)])
            tyg = [P.rtile("RC", "back", "yg", blk) for blk in range(4)]
            tap("ygT", ygT, [128, 4, T], BF16, r=tyg)
            glus, tglu = ring_load([(lambda s: s[:, 0:2048].rearrange("p (k n) -> p k n", k=4),
                                     w_d["ssm_glu_w"].rearrange("(k p) n -> p k n", p=128))])
            gluv = glus[:, 0:2048].rearrange("p (k n) -> p k n", k=4)
            wos, twos = ring_load([(lambda s: s.rearrange("p (c n) -> p c n", c=4),
                                    w_out[512:1024, :].rearrange("(c p) n -> p c n", p=128))])
            wosv = wos.rearrange("p (c n) -> p c n", c=4)
            ssmb2 = [attnb, RD[:, 0:2048].rearrange("p (c n) -> p c n", c=4)]
            for tb in range(NTB):
                sb_ = ssmb2[tb % 2]
                tas = [P.rtile("RD", "back", "ssmb", tb % 2)]
                for m in range(4):
                    b = nextbank()
                    for k in range(4):
                        P.op("pe", lambda e, b=b, k=k, m=m, tb=tb: e.matmul(
                            psb[b][:], lhsT=gluv[:, k, m * 128:(m + 1) * 128], rhs=ygT[:, k, tb * TB:(tb + 1) * TB],
                            start=(k == 0), stop=(k == 3)), r=[tglu] + tyg, w=pst(b))
                    ti = 1 + (m % 2)
                    tt1 = P.tile("tf", ti)
                    P.op("act", lambda e, b=b, m=m, ti=ti: e.activation(out=tf[ti][:], in_=psb[b][:], func=AF.Sigmoid,
                                                                       bias=pcols[:, 8 + m:9 + m]), r=pst(b) + [t_pc], w=[tt1])
                    P.op("dve", lambda e, m=m, tb=tb, ti=ti, sb_=sb_: e.tensor_tensor(
                        out=sb_[:, m, :], in0=ygT[:, m, tb * TB:(tb + 1) * TB], in1=tf[ti][:], op=ALU.mult),
                        r=[tt1] + tyg, w=tas)
                tap("ssm%d" % tb, sb_, [128, 4, 512], BF16, r=tas)
                rmsnorm(lambda c, sb_=sb_: sb_[:, c, :], lambda c, tas=tas: tas, lambda c, sb_=sb_: sb_[:, c, :],
                        lambda c, tas=tas: tas, 4, 512.0, lambda c: pcols[:, 4 + c:5 + c])
                for m in range(KD):
                    b = nextbank()
                    for c in range(4):
                        P.op("pe", lambda e, b=b, c=c, m=m, sb_=sb_: e.matmul(
                            psb[b][:], lhsT=wosv[:, c, m * 128:(m + 1) * 128], rhs=sb_[:, c, :],
                            start=(c == 0), stop=(c == 3)), r=[twos] + tas, w=pst(b))
                    xt = P.tile("xT", m, tb)
                    P.op("dve", lambda e, b=b, m=m, tb=tb: e.tensor_tensor(
                        out=xT[:, m, tb * TB:(tb + 1) * TB], in0=psb[b][:], in1=xT[:, m, tb * TB:(tb + 1) * TB], op=ALU.add),
                        r=pst(b) + [xt], w=[xt])
        tap("x2", xT[:], [128, KD, T], r=[P.tile("xT", k, tb) for k in range(KD) for tb in range(NTB)])

    if stop_after not in ("ffn1", "load"):
        middle()

    if stop_after not in ("ffn1", "load", "attn"):
        ffn(2, w_d["ffn2_w_gate"], w_d["ffn2_w_up"], w_d["ffn2_w_down"])

    for tb in range(NTB):
        rmsnorm(lambda k, tb=tb: xT[:, k, tb * TB:(tb + 1) * TB],
                lambda k, tb=tb: [P.tile("xT", k, tb)],
                lambda k, tb=tb: xT[:, k, tb * TB:(tb + 1) * TB],
                lambda k, tb=tb: [P.tile("xT", k, tb)], KD, float(D),
                lambda k: gcols[:, 3, k:k + 1])
        for k in range(KD):
            P.dma("sp", outT_d[k * 128:(k + 1) * 128, tb * TB:(tb + 1) * TB], xT[:, k, tb * TB:(tb + 1) * TB],
                  r=[P.tile("xT", k, tb)], stream=s_out)

    P.emit(es, [s_out] + tap_streams)
    es.close()
    return nc, list(tap_out.keys())


def _alibi_eb():
    slopes = 2.0 ** (-8.0 * (np.arange(8) + 1) / 8.0)
    sp = np.arange(128)[:, None]
    tq = np.arange(128)[None, :]
    eb = np.zeros((128, 2, 3, 4, 128), np.float64)
    for kv in range(2):
        for dl in range(3):
            rel = 128 * (dl - 1) + sp - tq
            for hh in range(4):
                v = np.exp(-slopes[kv * 4 + hh] * np.abs(rel))
                eb[:, kv, dl, hh, :] = np.where(np.abs(rel) <= 128, v, 0.0)
    return eb.reshape(128, 24 * 128).astype(np.float32)


_PROG_CACHE = {}


def _in_maps(inputs, taps=(), stop_after=None, cores=NCORES):
    x = np.asarray(inputs["x"], np.float32)
    shared = {}
    for nm in ["ffn1_w_gate", "ffn1_w_up", "ffn1_w_down", "ffn2_w_gate", "ffn2_w_up", "ffn2_w_down",
               "w_in", "w_out", "ssm_glu_w", "norm_ffn1", "norm_mix", "norm_ffn2", "attn_out_norm",
               "ssm_out_norm", "ssm_glu_b", "attn_sinks", "ssm_lambda_re", "ssm_lambda_im", "ssm_log_dt",
               "ssm_b_re", "ssm_b_im", "ssm_c_re", "ssm_c_im", "ssm_d"]:
        a = np.asarray(inputs[nm], np.float32)
        shared[nm] = np.ascontiguousarray(a.reshape(a.shape[1:]))
    shared["final_norm"] = np.ascontiguousarray(np.asarray(inputs["final_norm"], np.float32))
    shared["c_eb"] = _alibi_eb()
    maps = []
    for c in range(cores):
        m = dict(shared)
        m["xT"] = np.ascontiguousarray(x[c].T)
        maps.append(m)
    return maps


def kernel(**inputs):
    key = "main"
    if key not in _PROG_CACHE:
        _PROG_CACHE[key] = build_program()
    nc, _ = _PROG_CACHE[key]
    maps = _in_maps(inputs)
    res = run_bass_kernel_spmd(nc, maps, core_ids=list(range(NCORES)))
    out = np.stack([np.ascontiguousarray(r["outT"].T) for r in res.results], axis=0)
    return out.astype(np.float32)
```
